# Optimizing a Trainium2 kernel written in Bass

```python
import math
import jax
import jax.numpy as jnp
from jax import lax
import numpy as np

D_MODEL = 2048
BATCH = 8
SEQ = 2048
DEPTH = 2

GRID_W = 64
Q_BLOCK = 128
EPS = 1e-6
ROPE_THETA = 10000.0
N_BRANCH = 4
FFN_DIM = 5632
RWKV_HEADS = 8
RWKV_HEAD = 64
RWKV_WIDTH = RWKV_HEADS * RWKV_HEAD
DECAY_LORA = 96
ICLR_LORA = 96
GATE_LORA = 256
GN_EPS = 64e-5
MLA_HEADS = 4
MLA_NOPE = 128
MLA_ROPE = 64
MLA_V = 128
MLA_Q_RANK = 384
MLA_KV_RANK = 128
MLA_WIDTH = MLA_HEADS * MLA_V
HY_WIDTH = 512
HY_ORDER = 2
HY_EMB = 33
HY_FILTER_HIDDEN = 64
HY_TARGET = 1e-2
HY_FAST = 0.3
HY_SLOW = 1.5
GQA_Q_HEADS = 4
GQA_KV_HEADS = 2
GQA_HEAD = 128
GQA_WIDTH = GQA_Q_HEADS * GQA_HEAD
BRANCH_WIDTH = 512

RWKV_COLS = 3 * RWKV_WIDTH + 2 * DECAY_LORA + 2 * ICLR_LORA + GATE_LORA
MLA_COLS = MLA_Q_RANK + MLA_KV_RANK + MLA_ROPE
HY_COLS = (HY_ORDER + 1) * HY_WIDTH
GQA_COLS = (GQA_Q_HEADS + 2 * GQA_KV_HEADS) * GQA_HEAD
GATE_COLS = N_BRANCH * D_MODEL
IN_COLS = RWKV_COLS + MLA_COLS + HY_COLS + GQA_COLS + GATE_COLS

kernel_name = 'hybrid_gated_quad_mixer_encoder'


def _offsets(widths):
    return [int(v) for v in np.cumsum(widths)[:-1]]


def rmsnorm(x, g, eps=EPS):
    xf = x.astype(jnp.float32)
    y = xf * lax.rsqrt(jnp.mean(xf * xf, axis=-1, keepdims=True) + eps)
    return (y * g.astype(jnp.float32)).astype(x.dtype)


def swiglu(x, w_gate, w_up, w_down):
    return (jax.nn.silu(x @ w_gate) * (x @ w_up)) @ w_down


def shift_prev(x):
    return jnp.pad(x[:, :-1], ((0, 0), (1, 0), (0, 0)))


def shift_next(x):
    return jnp.pad(x[:, 1:], ((0, 0), (0, 1), (0, 0)))


def rope_table(pos, dim):
    inv = ROPE_THETA ** (-jnp.arange(0, dim, 2, dtype=jnp.float32) / dim)
    ang = pos.astype(jnp.float32)[:, None] * inv[None, :]
    ang = jnp.concatenate([ang, ang], axis=-1)
    return jnp.cos(ang), jnp.sin(ang)


def apply_rope(x, cos, sin):
    half = x.shape[-1] // 2
    xf = x.astype(jnp.float32)
    rot = jnp.concatenate([-xf[..., half:], xf[..., :half]], axis=-1)
    return (xf * cos + rot * sin).astype(x.dtype)


def blocked_attention(q, k, v, scale):
    b, hk, g, s, dk = q.shape
    nb = s // Q_BLOCK
    qb = q.reshape(b, hk, g, nb, Q_BLOCK, dk).transpose(3, 0, 1, 2, 4, 5)

    def one_block(q_blk):
        sc = jnp.einsum('bhgqd,bhkd->bhgqk', q_blk, k).astype(jnp.float32) * scale
        pr = jax.nn.softmax(sc, axis=-1).astype(v.dtype)
        return jnp.einsum('bhgqk,bhkd->bhgqd', pr, v)

    o = lax.map(one_block, qb)
    return o.transpose(1, 2, 3, 0, 4, 5).reshape(b, hk, g, s, v.shape[-1])


def rwkv7_dual_scan(r, w, kk, a, k, v):
    def step(state, inp):
        r_t, w_t, kk_t, a_t, k_t, v_t = inp
        sa = jnp.einsum('...vk,...k->...v', state, -kk_t)
        state = (state * w_t[..., None, :]
                 + sa[..., :, None] * (kk_t * a_t)[..., None, :]
                 + v_t[..., :, None] * k_t[..., None, :])
        return state, jnp.einsum('...vk,...k->...v', state, r_t)

    s0 = jnp.zeros(r.shape[1:] + (RWKV_HEAD,), jnp.float32)
    _, ys = lax.scan(step, s0, (r, w, kk, a, k, v))
    return ys


def rwkv7_mixer(p, mu_prev, mu_next, w0, w2, a0, a2, g2, k_k, k_a, r_k, ln_w, ln_b):
    b, s, _ = p.shape
    p = p + mu_prev * (shift_prev(p) - p) + mu_next * (shift_next(p) - p)
    r, k, v, lw, la, lg = jnp.split(
        p, _offsets([RWKV_WIDTH] * 3 + [2 * DECAY_LORA, 2 * ICLR_LORA, GATE_LORA]), axis=-1)
    lw = lw.reshape(b, s, 2, DECAY_LORA)
    la = la.reshape(b, s, 2, ICLR_LORA)
    wlog = -jax.nn.softplus(-(w0 + jnp.einsum('bsdl,dlc->bsdc', jnp.tanh(lw), w2))) - 0.5
    decay = jnp.exp(-jnp.exp(wlog.astype(jnp.float32)))
    a = jax.nn.sigmoid(a0 + jnp.einsum('bsdl,dlc->bsdc', la, a2))
    g = jax.nn.sigmoid(lg) @ g2
    k_dir = k[:, :, None, :] * (1.0 + (a - 1.0) * k_a)

    def heads(t):
        return t.reshape(t.shape[:-1] + (RWKV_HEADS, RWKV_HEAD))

    rh, vh = heads(r), heads(v)
    kkf = heads(k * k_k).astype(jnp.float32)
    kkf = kkf / jnp.maximum(jnp.sqrt(jnp.sum(kkf * kkf, axis=-1, keepdims=True)), 1e-12)
    kh_dir = heads(k_dir)

    def both(t):
        return jnp.broadcast_to(t[:, :, None], (b, s, 2, RWKV_HEADS, RWKV_HEAD))

    def dir_major(t):
        t = jnp.moveaxis(t.astype(jnp.float32), 0, 2)
        return jnp.stack([t[:, 0], t[::-1, 1]], axis=1)

    ys = rwkv7_dual_scan(dir_major(both(rh)), dir_major(heads(decay)), dir_major(both(kkf)),
                         dir_major(heads(a)), dir_major(kh_dir), dir_major(both(vh)))
    y = jnp.moveaxis(ys[:, 0] + ys[::-1, 1], 0, 1)
    mu = jnp.mean(y, axis=-1, keepdims=True)
    var = jnp.mean(jnp.square(y - mu), axis=-1, keepdims=True)
    yn = ((y - mu) * lax.rsqrt(var + GN_EPS)).reshape(b, s, RWKV_WIDTH)
    yn = yn * ln_w.astype(jnp.float32) + ln_b.astype(jnp.float32)
    coef = jnp.sum(rh[:, :, None] * kh_dir * heads(r_k), axis=-1, keepdims=True).sum(axis=2)
    bonus = (coef * vh).reshape(b, s, RWKV_WIDTH)
    return ((yn.astype(p.dtype) + bonus) * g).astype(p.dtype)


def mla_mixer(p, cos1, sin1, q_norm, w_q_up, kv_norm, w_kv_up):
    b, s, _ = p.shape
    q_c, kv_c, k_r = jnp.split(p, _offsets([MLA_Q_RANK, MLA_KV_RANK, MLA_ROPE]), axis=-1)
    q = (rmsnorm(q_c, q_norm) @ w_q_up).reshape(b, s, MLA_HEADS, MLA_NOPE + MLA_ROPE)
    q = jnp.concatenate([q[..., :MLA_NOPE], apply_rope(q[..., MLA_NOPE:], cos1[:, None], sin1[:, None])], axis=-1)
    kv = (rmsnorm(kv_c, kv_norm) @ w_kv_up).reshape(b, s, MLA_HEADS, MLA_NOPE + MLA_V)
    k_rope = apply_rope(k_r, cos1, sin1)
    k = jnp.concatenate(
        [kv[..., :MLA_NOPE], jnp.broadcast_to(k_rope[:, :, None], (b, s, MLA_HEADS, MLA_ROPE))], axis=-1)
    v = kv[..., MLA_NOPE:]
    o = blocked_attention(q.transpose(0, 2, 1, 3)[:, :, None], k.transpose(0, 2, 1, 3),
                          v.transpose(0, 2, 1, 3), (MLA_NOPE + MLA_ROPE) ** -0.5)
    return o[:, :, 0].transpose(0, 2, 1, 3).reshape(b, s, MLA_WIDTH)


def hyena_filters(seq_len, w1, b1, w2, b2, w3, b3, w4, freq):
    f32 = jnp.float32
    t = jnp.linspace(0.0, 1.0, seq_len, dtype=f32)[:, None]
    bands = (HY_EMB - 1) // 2
    w_ang = 2.0 * math.pi * jnp.arange(seq_len, dtype=f32) / seq_len
    fr = jnp.linspace(1e-4, bands - 1, bands, dtype=f32)
    ang = w_ang[:, None] * fr[None, :]
    z = jnp.concatenate([t, jnp.cos(ang), -jnp.sin(ang)], axis=-1)
    fq = freq.astype(f32)
    hd = jnp.sin(fq[0] * (z @ w1.astype(f32) + b1.astype(f32)))
    hd = jnp.sin(fq[1] * (hd @ w2.astype(f32) + b2.astype(f32)))
    hd = jnp.sin(fq[2] * (hd @ w3.astype(f32) + b3.astype(f32)))
    h = (hd @ w4.astype(f32)).reshape(seq_len, 2, HY_ORDER, HY_WIDTH)
    deltas = jnp.asarray(np.abs(np.linspace(math.log(HY_TARGET) / HY_SLOW,
                                            math.log(HY_TARGET) / HY_FAST, HY_WIDTH)), dtype=f32)
    h = h * jnp.exp(-t * deltas[None, :])[:, None, None, :]
    h_circ = jnp.concatenate([h[:, 0], jnp.zeros((1, HY_ORDER, HY_WIDTH), f32), h[:0:-1, 1]], axis=0)
    return jnp.fft.rfft(h_circ, axis=0)


def fft_longconv(u, hf, bias):
    seq_len = u.shape[1]
    uf = u.astype(jnp.float32)
    spec = jnp.fft.rfft(uf, n=2 * seq_len, axis=1)
    y = jnp.fft.irfft(spec * hf[None], n=2 * seq_len, axis=1)[:, :seq_len]
    return (y + uf * bias.astype(jnp.float32)).astype(u.dtype)


def hyena_mixer(p, short_w, short_b, w1, b1, w2, b2, w3, b3, w4, freq, hy_bias):
    u = short_w[0] * shift_prev(p) + short_w[1] * p + short_w[2] * shift_next(p) + short_b
    x1, x2, v = jnp.split(u, _offsets([HY_WIDTH] * 3), axis=-1)
    hf = hyena_filters(p.shape[1], w1, b1, w2, b2, w3, b3, w4, freq)
    z = x1 * fft_longconv(v, hf[:, 0], hy_bias[0])
    return x2 * fft_longconv(z, hf[:, 1], hy_bias[1])


def apply_axial_rope(x, cos_r, sin_r, cos_c, sin_c):
    half = GQA_HEAD // 2
    return jnp.concatenate([apply_rope(x[..., :half], cos_r[:, None], sin_r[:, None]),
                            apply_rope(x[..., half:], cos_c[:, None], sin_c[:, None])], axis=-1)


def gqa_mixer(p, cos_r, sin_r, cos_c, sin_c, q_norm, k_norm):
    b, s, _ = p.shape
    q, k, v = jnp.split(p, _offsets([GQA_Q_HEADS * GQA_HEAD, GQA_KV_HEADS * GQA_HEAD,
                                     GQA_KV_HEADS * GQA_HEAD]), axis=-1)
    q = rmsnorm(q.reshape(b, s, GQA_Q_HEADS, GQA_HEAD), q_norm)
    k = rmsnorm(k.reshape(b, s, GQA_KV_HEADS, GQA_HEAD), k_norm)
    v = v.reshape(b, s, GQA_KV_HEADS, GQA_HEAD)
    q = apply_axial_rope(q, cos_r, sin_r, cos_c, sin_c)
    k = apply_axial_rope(k, cos_r, sin_r, cos_c, sin_c)
    groups = GQA_Q_HEADS // GQA_KV_HEADS
    qg = q.reshape(b, s, GQA_KV_HEADS, groups, GQA_HEAD).transpose(0, 2, 3, 1, 4)
    o = blocked_attention(qg, k.transpose(0, 2, 1, 3), v.transpose(0, 2, 1, 3), GQA_HEAD ** -0.5)
    return o.transpose(0, 3, 1, 2, 4).reshape(b, s, GQA_WIDTH)


def setup_inputs(seed: int = 0) -> dict:
    key = jax.random.key(seed)
    ks = iter(jax.random.split(key, 64))
    f32 = jnp.float32

    def nrm(shape, scale):
        return jax.random.normal(next(ks), (DEPTH,) + shape, f32) * scale

    def gain(n):
        return 1.0 + nrm((n,), 0.02)

    def unif(shape, lo, hi):
        return jax.random.uniform(next(ks), (DEPTH,) + shape, f32, lo, hi)

    D, F = D_MODEL, FFN_DIM
    return {
        'x': jax.random.normal(next(ks), (BATCH, SEQ, D_MODEL), f32),
        'ffn1_norm': gain(D),
        'ffn1_w_gate': nrm((D, F), D ** -0.5),
        'ffn1_w_up': nrm((D, F), D ** -0.5),
        'ffn1_w_down': nrm((F, D), F ** -0.5),
        'mix_norm': gain(D),
        'w_in': nrm((D, IN_COLS), D ** -0.5),
        'rwkv_mu_prev': unif((RWKV_COLS,), 0.0, 0.5),
        'rwkv_mu_next': unif((RWKV_COLS,), 0.0, 0.5),
        'rwkv_w0': unif((2, RWKV_WIDTH), -5.0, -0.5),
        'rwkv_w2': nrm((2, DECAY_LORA, RWKV_WIDTH), 0.1 * DECAY_LORA ** -0.5),
        'rwkv_a0': nrm((2, RWKV_WIDTH), 0.1),
        'rwkv_a2': nrm((2, ICLR_LORA, RWKV_WIDTH), ICLR_LORA ** -0.5),
        'rwkv_g2': nrm((GATE_LORA, RWKV_WIDTH), GATE_LORA ** -0.5),
        'rwkv_k_k': 0.85 + nrm((RWKV_WIDTH,), 0.02),
        'rwkv_k_a': 1.0 + nrm((RWKV_WIDTH,), 0.02),
        'rwkv_r_k': nrm((RWKV_WIDTH,), 0.1),
        'rwkv_ln_w': gain(RWKV_WIDTH),
        'rwkv_ln_b': nrm((RWKV_WIDTH,), 0.02),
        'mla_q_norm': gain(MLA_Q_RANK),
        'mla_w_q_up': nrm((MLA_Q_RANK, MLA_HEADS * (MLA_NOPE + MLA_ROPE)), MLA_Q_RANK ** -0.5),
        'mla_kv_norm': gain(MLA_KV_RANK),
        'mla_w_kv_up': nrm((MLA_KV_RANK, MLA_HEADS * (MLA_NOPE + MLA_V)), MLA_KV_RANK ** -0.5),
        'hy_short_w': nrm((3, HY_COLS), 0.6),
        'hy_short_b': nrm((HY_COLS,), 0.02),
        'hy_w1': nrm((HY_EMB, HY_FILTER_HIDDEN), HY_EMB ** -0.5),
        'hy_b1': nrm((HY_FILTER_HIDDEN,), 0.1),
        'hy_w2': nrm((HY_FILTER_HIDDEN, HY_FILTER_HIDDEN), HY_FILTER_HIDDEN ** -0.5),
        'hy_b2': nrm((HY_FILTER_HIDDEN,), 0.1),
        'hy_w3': nrm((HY_FILTER_HIDDEN, HY_FILTER_HIDDEN), HY_FILTER_HIDDEN ** -0.5),
        'hy_b3': nrm((HY_FILTER_HIDDEN,), 0.1),
        'hy_w4': nrm((HY_FILTER_HIDDEN, 2 * HY_ORDER * HY_WIDTH), 0.01),
        'hy_freq': 1.0 + nrm((3, HY_FILTER_HIDDEN), 0.1),
        'hy_bias': nrm((HY_ORDER, HY_WIDTH), 1.0),
        'gqa_q_norm': gain(GQA_HEAD),
        'gqa_k_norm': gain(GQA_HEAD),
        'w_branch': nrm((N_BRANCH, BRANCH_WIDTH, D), BRANCH_WIDTH ** -0.5),
        'w_out': nrm((D, D), D ** -0.5),
        'ffn2_norm': gain(D),
        'ffn2_w_gate': nrm((D, F), D ** -0.5),
        'ffn2_w_up': nrm((D, F), D ** -0.5),
        'ffn2_w_down': nrm((F, D), F ** -0.5),
        'final_norm': 1.0 + 0.02 * jax.random.normal(next(ks), (D,), f32),
    }


def reference(x, ffn1_norm, ffn1_w_gate, ffn1_w_up, ffn1_w_down, mix_norm, w_in,
              rwkv_mu_prev, rwkv_mu_next, rwkv_w0, rwkv_w2, rwkv_a0, rwkv_a2, rwkv_g2,
              rwkv_k_k, rwkv_k_a, rwkv_r_k, rwkv_ln_w, rwkv_ln_b,
              mla_q_norm, mla_w_q_up, mla_kv_norm, mla_w_kv_up,
              hy_short_w, hy_short_b, hy_w1, hy_b1, hy_w2, hy_b2, hy_w3, hy_b3, hy_w4, hy_freq, hy_bias,
              gqa_q_norm, gqa_k_norm, w_branch, w_out,
              ffn2_norm, ffn2_w_gate, ffn2_w_up, ffn2_w_down, final_norm):
    b, s, d = x.shape
    pos = jnp.arange(s, dtype=jnp.int32)
    rows = s // GRID_W
    row_pos = jnp.repeat(jnp.arange(rows, dtype=jnp.int32), GRID_W)
    col_pos = jnp.tile(jnp.arange(GRID_W, dtype=jnp.int32), rows)
    cos1, sin1 = rope_table(pos, MLA_ROPE)
    cos_r, sin_r = rope_table(row_pos, GQA_HEAD // 2)
    cos_c, sin_c = rope_table(col_pos, GQA_HEAD // 2)
    split_at = _offsets([RWKV_COLS, MLA_COLS, HY_COLS, GQA_COLS, GATE_COLS])

    for l in range(DEPTH):
        x = x + 0.5 * swiglu(rmsnorm(x, ffn1_norm[l]), ffn1_w_gate[l], ffn1_w_up[l], ffn1_w_down[l])
        h = rmsnorm(x, mix_norm[l])
        p = h @ w_in[l]
        p_a, p_b, p_c, p_d, p_g = jnp.split(p, split_at, axis=-1)
        y_a = rwkv7_mixer(p_a, rwkv_mu_prev[l], rwkv_mu_next[l], rwkv_w0[l], rwkv_w2[l],
                          rwkv_a0[l], rwkv_a2[l], rwkv_g2[l], rwkv_k_k[l], rwkv_k_a[l],
                          rwkv_r_k[l], rwkv_ln_w[l], rwkv_ln_b[l])
        y_b = mla_mixer(p_b, cos1, sin1, mla_q_norm[l], mla_w_q_up[l], mla_kv_norm[l], mla_w_kv_up[l])
        y_c = hyena_mixer(p_c, hy_short_w[l], hy_short_b[l], hy_w1[l], hy_b1[l], hy_w2[l], hy_b2[l],
                          hy_w3[l], hy_b3[l], hy_w4[l], hy_freq[l], hy_bias[l])
        y_d = gqa_mixer(p_d, cos_r, sin_r, cos_c, sin_c, gqa_q_norm[l], gqa_k_norm[l])
        gates = jax.nn.sigmoid(p_g).reshape(b, s, N_BRANCH, d)
        wb = w_branch[l]
        merged = gates[:, :, 0] * (y_a @ wb[0])
        merged = merged + gates[:, :, 1] * (y_b @ wb[1])
        merged = merged + gates[:, :, 2] * (y_c @ wb[2])
        merged = merged + gates[:, :, 3] * (y_d @ wb[3])
        x = x + merged @ w_out[l]
        x = x + 0.5 * swiglu(rmsnorm(x, ffn2_norm[l]), ffn2_w_gate[l], ffn2_w_up[l], ffn2_w_down[l])
    return rmsnorm(x, final_norm)
```

```python
import math
from contextlib import ExitStack

import numpy as np
import ml_dtypes
import concourse.bass as bass
import concourse.mybir as mybir
from concourse.bass_utils import run_bass_kernel_spmd

F32 = mybir.dt.float32
BF16 = mybir.dt.bfloat16
ALU = mybir.AluOpType
AF = mybir.ActivationFunctionType
AX = mybir.AxisListType

D = 2048
S_LEN = 2048
DEPTH = 2
FFN = 5632
EPS = 1e-6
NT = 512
RW_COLS, MLA_COLS, HY_COLS, GQA_COLS = 2176, 576, 1536, 1024
OFF_RW, OFF_MLA, OFF_HY, OFF_GQA, OFF_GATE = 0, 2176, 2752, 4288, 5312
IN_COLS = 13504
PM_ROWS = 5056


class Buf:
    __slots__ = ("w", "r", "name", "excl")

    def __init__(self, name="", excl=False):
        self.w = None
        self.r = []
        self.name = name
        self.excl = excl


class Rec:
    __slots__ = ("eng", "fn", "deps", "dma", "semkey", "val", "needs_inc", "idx")


class Sched:
    ENG = ("pe", "act", "dve", "pool", "sp")
    BLK = {"pe": "tensor", "act": "scalar", "dve": "vector", "pool": "gpsimd", "sp": "sync"}
    CAP = 8000
    NRING = 8

    def __init__(self, nc):
        self.nc = nc
        self.streams = {e: [] for e in self.ENG}
        self.all = []
        self.ndma = {e: 0 for e in self.ENG}
        self.ring_last = {}
        self.es = ExitStack()

    def sbuf(self, name, shape, dt):
        return self.es.enter_context(self.nc.sbuf_tensor(name, list(shape), dt))

    def psum(self, name, shape, dt=F32):
        return self.es.enter_context(self.nc.psum_tensor(name, list(shape), dt))

    def release(self):
        self.es.close()
        self.es = ExitStack()

    def op(self, eng, fn, r=(), w=(), dma=False):
        rec = Rec()
        rec.eng, rec.fn, rec.dma = eng, fn, dma
        rec.needs_inc = False
        rec.val = None
        rec.semkey = None
        rec.idx = len(self.all)
        deps = {}
        for b in r:
            if b.w is not None:
                deps[id(b.w)] = b.w
            if b.excl:
                for rr in b.r:
                    if rr.eng != eng:
                        deps[id(rr)] = rr
        for b in w:
            if b.w is not None:
                lw = b.w
                if not (eng == "pe" and lw.eng == "pe" and not lw.dma and not dma):
                    deps[id(lw)] = lw
            for rr in b.r:
                if rr.eng != eng or rr.dma or dma:
                    deps[id(rr)] = rr
        if dma:
            i = self.ndma[eng]
            self.ndma[eng] += 1
            slot = i % self.NRING
            rec.semkey = ("dma", eng, slot)
            rec.val = 16 * (i // self.NRING + 1)
            prev = self.ring_last.get((eng, slot))
            if prev is not None:
                deps[id(prev)] = prev
            self.ring_last[(eng, slot)] = rec
        for d in deps.values():
            d.needs_inc = True
        rec.deps = list(deps.values())
        for b in r:
            if not dma:
                b.r = [x for x in b.r if x.dma or x.eng != eng]
            b.r.append(rec)
        for b in w:
            b.w = rec
            b.r = []
        self.streams[eng].append(rec)
        self.all.append(rec)
        return rec

    def barrier(self):
        lasts = []
        for e in self.ENG:
            for rec in reversed(self.streams[e]):
                if rec.fn is not None:
                    lasts.append(rec)
                    break
        for (e, slot), rec in self.ring_last.items():
            lasts.append(rec)
        fence = Buf("fence")
        for e in self.ENG:
            rec = Rec()
            rec.eng, rec.fn, rec.dma = e, None, False
            rec.needs_inc = False
            rec.val = None
            rec.semkey = None
            rec.idx = len(self.all)
            rec.deps = [d for d in lasts]
            for d in lasts:
                d.needs_inc = True
            self.streams[e].append(rec)
            self.all.append(rec)

    def finalize(self):
        nc = self.nc
        cnt = {e: 0 for e in self.ENG}
        for rec in self.all:
            if rec.dma or not rec.needs_inc or rec.fn is None:
                continue
            c = cnt[rec.eng]
            rec.semkey = ("eng", rec.eng, c // self.CAP)
            rec.val = c % self.CAP + 1
            cnt[rec.eng] = c + 1
        keys = set()
        for rec in self.all:
            if rec.semkey is not None:
                keys.add(rec.semkey)
        with ExitStack() as es:
            sems = {}
            for k in sorted(keys):
                sems[k] = es.enter_context(nc.semaphore("s_%s_%s_%d" % k))
            block = es.enter_context(nc.Block())
            for e in self.ENG:
                stream = self.streams[e]

                def body(eng, stream=stream):
                    waited = {}
                    for rec in stream:
                        for d in rec.deps:
                            if d.semkey is None:
                                continue
                            if waited.get(d.semkey, 0) >= d.val:
                                continue
                            eng.wait_ge(sems[d.semkey], d.val)
                            waited[d.semkey] = d.val
                        if rec.fn is None:
                            continue
                        ins = rec.fn(eng)
                        if rec.dma:
                            ins.then_inc(sems[rec.semkey], 16)
                        elif rec.needs_inc:
                            ins.then_inc(sems[rec.semkey], 1)

                getattr(block, self.BLK[e])(body)
        self.es.close()


class T:
    def __init__(self, h, excl=False):
        self.h = h
        self.bufs = {}
        self.excl = excl

    def __getitem__(self, idx):
        return self.h[idx]

    def b(self, key=0):
        bb = self.bufs.get(key)
        if bb is None:
            bb = self.bufs[key] = Buf(excl=self.excl)
        return bb


def sb(S, name, shape, dt=F32):
    return T(S.sbuf(name, shape, dt))


def ps(S, name, shape, dt=F32):
    return T(S.psum(name, shape, dt), excl=True)


class DT:
    def __init__(self, nc, name, shape, dt, kind="Internal"):
        self.t = nc.dram_tensor(name, list(shape), dt, kind=kind)
        self.ap = self.t.ap()
        self.bufs = {}

    def __getitem__(self, idx):
        return self.ap[idx]

    def b(self, key=0):
        bb = self.bufs.get(key)
        if bb is None:
            bb = self.bufs[key] = Buf()
        return bb


def mm(S, out, lhsT, rhs, start=True, stop=True, r=(), w=()):
    return S.op("pe", lambda e: e.matmul(out, lhsT, rhs, start=start, stop=stop), r=r, w=w)


def tr(S, out, in_, ident, r=(), w=()):
    return S.op("pe", lambda e: e.transpose(out, in_, ident), r=r, w=w)


def act(S, out, in_, func, bias=None, scale=None, accum_out=None, r=(), w=(), eng="act"):
    kw = {}
    if bias is not None:
        kw["bias"] = bias
    if scale is not None:
        kw["scale"] = scale
    if accum_out is not None:
        kw["accum_out"] = accum_out
    return S.op(eng, lambda e: e.activation(out, in_, func, **kw), r=r, w=w)


def tt(S, out, in0, in1, op, r=(), w=(), eng="dve"):
    return S.op(eng, lambda e: e.tensor_tensor(out, in0, in1, op), r=r, w=w)


def ts(S, out, in0, s1, s2, op0, op1=None, r=(), w=(), eng="dve", accum_out=None):
    if op1 is None:
        return S.op(eng, lambda e: e.tensor_single_scalar(out, in0, s1, op0), r=r, w=w)
    if accum_out is not None:
        return S.op(eng, lambda e: e.tensor_scalar(out, in0, s1, s2, op0, op1, accum_out), r=r, w=w)
    return S.op(eng, lambda e: e.tensor_scalar(out, in0, s1, s2, op0, op1), r=r, w=w)


def stt(S, out, in0, scalar, in1, op0, op1, r=(), w=(), eng="dve"):
    return S.op(eng, lambda e: e.scalar_tensor_tensor(out, in0, scalar, in1, op0, op1), r=r, w=w)


def rsqrt(S, out, in_, scale, bias, r=(), w=()):
    S.op("act", lambda e: e.activation(out, in_, AF.Sqrt, bias=bias, scale=scale), r=r, w=w)
    S.op("dve", lambda e: e.reciprocal(out, out), r=w, w=w)


def cp(S, out, in_, r=(), w=(), eng="dve"):
    if eng == "act":
        return S.op(eng, lambda e: e.copy(out, in_), r=r, w=w)
    return S.op(eng, lambda e: e.tensor_copy(out, in_), r=r, w=w)


def mset(S, ap, val, w=(), eng="dve"):
    return S.op(eng, lambda e: e.memset(ap, val), w=w)


def dma(S, out, in_, r=(), w=(), q="sp"):
    return S.op(q, lambda e: e.dma_start(out=out, in_=in_), r=r, w=w, dma=True)


class Ctx:
    pass


def load_consts(C):
    S = C.S
    nc = S.nc
    C.cst = ExitStack()
    C.ident = T(C.cst.enter_context(nc.sbuf_tensor("ident", [128, 128], F32)))
    C.identb = T(C.cst.enter_context(nc.sbuf_tensor("identb", [128, 128], BF16)))
    C.ones = T(C.cst.enter_context(nc.sbuf_tensor("ones", [128, 128], F32)))
    dma(S, C.ident[:], C.d_ident[:], w=[C.ident.b()])
    mset(S, C.ones[:], 1.0, w=[C.ones.b()])
    C.epsc = T(C.cst.enter_context(nc.sbuf_tensor("epsc", [128, 4], F32)))
    mset(S, C.epsc[:, 0:1], EPS, w=[C.epsc.b()])
    mset(S, C.epsc[:, 1:2], 0.0, w=[C.epsc.b()])
    cp(S, C.identb[:], C.ident[:], r=[C.ident.b()], w=[C.identb.b()])


def phase_load_x(C):
    S = C.S
    xt = [sb(S, "lx_xt%d" % i, [128, 4, D]) for i in range(2)]
    st = [sb(S, "lx_st%d" % i, [128, NT]) for i in range(3)]
    pp = [ps(S, "lx_ps%d" % i, [128, NT]) for i in range(3)]
    k = 0
    for tg in range(S_LEN // NT):
        x_t = xt[tg % 2]
        for j in range(4):
            t0 = tg * NT + j * 128
            dma(S, x_t[:, j, :], C.x[t0:t0 + 128, :], w=[x_t.b(j)], q="sp" if j % 2 == 0 else "act")
        for dc in range(D // 128):
            p = pp[k % 3]
            s = st[k % 3]
            for j in range(4):
                tr(S, p[:, j * 128:(j + 1) * 128], x_t[:, j, dc * 128:(dc + 1) * 128], C.ident[:],
                   r=[x_t.b(j), C.ident.b()], w=[p.b()])
            cp(S, s[:], p[:], r=[p.b()], w=[s.b()], eng="dve" if k % 2 == 0 else "act")
            dma(S, C.xres[dc * 128:(dc + 1) * 128, tg * NT:(tg + 1) * NT], s[:], r=[s.b()],
                w=[C.xres.b((dc, tg // 2))], q="sp")
            k += 1
    S.barrier()
    S.release()


def phase_final_norm(C):
    S = C.S
    gb = sb(S, "fn_g", [128, D])
    dma(S, gb[:], C.final_norm.ap.partition_broadcast(128), w=[gb.b()])
    xin = [sb(S, "fn_xin%d" % i, [128, 16, 128]) for i in range(2)]
    xt = [sb(S, "fn_xt%d" % i, [128, D]) for i in range(2)]
    sq = sb(S, "fn_sq", [128, D])
    ot = [sb(S, "fn_ot%d" % i, [128, D]) for i in range(2)]
    ss = [sb(S, "fn_ss%d" % i, [128, 1]) for i in range(2)]
    pp = [ps(S, "fn_ps%d" % i, [128, NT]) for i in range(4)]
    for tc in range(S_LEN // 128):
        xi = xin[tc % 2]
        x_t = xt[tc % 2]
        o_t = ot[tc % 2]
        s_ = ss[tc % 2]
        dma(S, xi[:], C.xres.ap[:, tc * 128:(tc + 1) * 128].rearrange("(c p) t -> p c t", p=128),
            r=[C.xres.b((dc, tc // 8)) for dc in range(16)], w=[xi.b()], q="sp" if tc % 2 == 0 else "act")
        for g in range(4):
            p = pp[g]
            for j in range(4):
                dc = g * 4 + j
                tr(S, p[:, j * 128:(j + 1) * 128], xi[:, dc, :], C.ident[:], r=[xi.b(), C.ident.b()], w=[p.b()])
            cp(S, x_t[:, g * NT:(g + 1) * NT], p[:], r=[p.b()], w=[x_t.b(g)], eng="dve" if g % 2 == 0 else "act")
        act(S, sq[:], x_t[:], AF.Square, accum_out=s_[:], r=[x_t.b(g) for g in range(4)], w=[sq.b(), s_.b()])
        rsqrt(S, s_[:], s_[:], 1.0 / D, C.epsc[:, 0:1], r=[s_.b(), C.epsc.b()], w=[s_.b()])
        stt(S, o_t[:], x_t[:], s_[:, 0:1], gb[:], ALU.mult, ALU.mult,
            r=[x_t.b(g) for g in range(4)] + [s_.b(), gb.b()], w=[o_t.b()])
        dma(S, C.out[tc * 128:(tc + 1) * 128, :], o_t[:], r=[o_t.b()], w=[C.out.b(tc)], q="sp")
    S.barrier()
    S.release()


def phase_ffn(C, l, which):
    S = C.S
    nm = "f%d%d_" % (l, which)
    g_d = C.ffn_norm[which][l]
    wg_d, wu_d, wd_d = C.ffn_wg[which].ap[l], C.ffn_wu[which].ap[l], C.ffn_wd[which].ap[l]
    TT = 1024
    G = 2
    NG = FFN // (128 * G)
    gcol = sb(S, nm + "gcol", [128, 16])
    S.op("sp", lambda e: e.dma_start(out=gcol[:], in_=g_d.rearrange("(c p) -> p c", p=128),
                                     allow_slow_non_contiguous=True), w=[gcol.b()], dma=True)
    xa = sb(S, nm + "xa", [128, 16, TT])
    hT = sb(S, nm + "hT", [128, 16, TT], BF16)
    rstd = sb(S, nm + "rstd", [128, TT])
    sq = [sb(S, nm + "sq%d" % i, [128, NT]) for i in range(2)]
    wg = [sb(S, nm + "wg%d" % i, [128, 16, 128 * G], BF16) for i in range(2)]
    wu = [sb(S, nm + "wu%d" % i, [128, 16, 128 * G], BF16) for i in range(2)]
    wd = [sb(S, nm + "wd%d" % i, [128, G, D], BF16) for i in range(2)]
    aT = [sb(S, nm + "aT%d" % i, [128, G, TT], BF16) for i in range(2)]
    sg = [sb(S, nm + "sg%d" % i, [128, NT]) for i in range(2)]
    p_g = [ps(S, nm + "pg%d" % i, [128, NT]) for i in range(2)]
    p_u = [ps(S, nm + "pu%d" % i, [128, NT]) for i in range(2)]
    p_d = [ps(S, nm + "pd%d" % i, [128, NT]) for i in range(2)]
    p_s = ps(S, nm + "pss", [128, NT])
    NSUB = TT // NT
    it = 0
    for tt_i in range(S_LEN // TT):
        t0 = tt_i * TT
        for c in range(16):
            dma(S, xa[:, c, :], C.xres[c * 128:(c + 1) * 128, t0:t0 + TT], r=[C.xres.b((c, tt_i))],
                w=[xa.b(c)], q="sp" if c % 2 == 0 else "act")
        for s_i in range(NSUB):
            for c in range(16):
                q_ = sq[c % 2]
                act(S, q_[:], xa[:, c, s_i * NT:(s_i + 1) * NT], AF.Square, r=[xa.b(c)], w=[q_.b()])
                mm(S, p_s[:], C.ones[:], q_[:], start=(c == 0), stop=(c == 15), r=[C.ones.b(), q_.b()], w=[p_s.b()])
            rsqrt(S, rstd[:, s_i * NT:(s_i + 1) * NT], p_s[:], 1.0 / D, C.epsc[:, 0:1], r=[p_s.b(), C.epsc.b()],
                  w=[rstd.b(s_i)])
        for c in range(16):
            stt(S, hT[:, c, :], xa[:, c, :], gcol[:, c:c + 1], rstd[:], ALU.mult, ALU.mult,
                r=[xa.b(c), gcol.b()] + [rstd.b(i) for i in range(NSUB)], w=[hT.b(c)])
        hT_r = [hT.b(c) for c in range(16)]
        for fg in range(NG):
            wg_t, wu_t, wd_t, a_t = wg[it % 2], wu[it % 2], wd[it % 2], aT[it % 2]
            it += 1
            f0 = fg * 128 * G
            dma(S, wg_t[:], wg_d[:, f0:f0 + 128 * G].rearrange("(c p) f -> p c f", p=128), w=[wg_t.b()], q="pool")
            dma(S, wu_t[:], wu_d[:, f0:f0 + 128 * G].rearrange("(c p) f -> p c f", p=128), w=[wu_t.b()], q="pool")
            dma(S, wd_t[:], wd_d[f0:f0 + 128 * G, :].rearrange("(c p) d -> p c d", p=128), w=[wd_t.b()], q="pool")
            k = 0
            for fc in range(G):
                for s_i in range(NSUB):
                    pg, pu, sg_t = p_g[k % 2], p_u[k % 2], sg[k % 2]
                    k += 1
                    tsl = slice(s_i * NT, (s_i + 1) * NT)
                    for c in range(16):
                        mm(S, pg[:], wg_t[:, c, fc * 128:(fc + 1) * 128], hT[:, c, tsl], start=(c == 0), stop=(c == 15),
                           r=[wg_t.b(), hT_r[c]], w=[pg.b()])
                    for c in range(16):
                        mm(S, pu[:], wu_t[:, c, fc * 128:(fc + 1) * 128], hT[:, c, tsl], start=(c == 0), stop=(c == 15),
                           r=[wu_t.b(), hT_r[c]], w=[pu.b()])
                    act(S, sg_t[:], pg[:], AF.Silu, r=[pg.b()], w=[sg_t.b()])
                    tt(S, a_t[:, fc, tsl], sg_t[:], pu[:], ALU.mult, r=[sg_t.b(), pu.b()], w=[a_t.b((fc, s_i))])
            k = 0
            for dc in range(16):
                for s_i in range(NSUB):
                    pd = p_d[k % 2]
                    k += 1
                    tsl = slice(s_i * NT, (s_i + 1) * NT)
                    for fc in range(G):
                        mm(S, pd[:], wd_t[:, fc, dc * 128:(dc + 1) * 128], a_t[:, fc, tsl], start=(fc == 0), stop=(fc == G - 1),
                           r=[wd_t.b(), a_t.b((fc, s_i))], w=[pd.b()])
                    stt(S, xa[:, dc, tsl], pd[:], 0.5, xa[:, dc, tsl], ALU.mult, ALU.add, r=[pd.b(), xa.b(dc)], w=[xa.b(dc)])
        for c in range(16):
            dma(S, C.xres[c * 128:(c + 1) * 128, t0:t0 + TT], xa[:, c, :], r=[xa.b(c)], w=[C.xres.b((c, tt_i))],
                q="sp" if c % 2 == 0 else "act")
    S.barrier()
    S.release()


PARAM_SHAPES = {
    'ffn1_norm': (DEPTH, D), 'ffn1_w_gate': (DEPTH, D, FFN), 'ffn1_w_up': (DEPTH, D, FFN), 'ffn1_w_down': (DEPTH, FFN, D),
    'mix_norm': (DEPTH, D), 'w_in': (DEPTH, D, IN_COLS),
    'rwkv_mu_prev': (DEPTH, RW_COLS), 'rwkv_mu_next': (DEPTH, RW_COLS), 'rwkv_w0': (DEPTH, 2, 512),
    'rwkv_w2': (DEPTH, 2, 96, 512), 'rwkv_a0': (DEPTH, 2, 512), 'rwkv_a2': (DEPTH, 2, 96, 512),
    'rwkv_g2': (DEPTH, 256, 512), 'rwkv_k_k': (DEPTH, 512), 'rwkv_k_a': (DEPTH, 512), 'rwkv_r_k': (DEPTH, 512),
    'rwkv_ln_w': (DEPTH, 512), 'rwkv_ln_b': (DEPTH, 512),
    'mla_q_norm': (DEPTH, 384), 'mla_w_q_up': (DEPTH, 384, 768), 'mla_kv_norm': (DEPTH, 128),
    'mla_w_kv_up': (DEPTH, 128, 1024),
    'hy_short_w': (DEPTH, 3, 1536), 'hy_short_b': (DEPTH, 1536), 'hy_w1': (DEPTH, 33, 64), 'hy_b1': (DEPTH, 64),
    'hy_w2': (DEPTH, 64, 64), 'hy_b2': (DEPTH, 64), 'hy_w3': (DEPTH, 64, 64), 'hy_b3': (DEPTH, 64),
    'hy_w4': (DEPTH, 64, 2048), 'hy_freq': (DEPTH, 3, 64), 'hy_bias': (DEPTH, 2, 512),
    'gqa_q_norm': (DEPTH, 128), 'gqa_k_norm': (DEPTH, 128), 'w_branch': (DEPTH, 4, 512, D), 'w_out': (DEPTH, D, D),
    'ffn2_norm': (DEPTH, D), 'ffn2_w_gate': (DEPTH, D, FFN), 'ffn2_w_up': (DEPTH, D, FFN), 'ffn2_w_down': (DEPTH, FFN, D),
    'final_norm': (D,),
}


def rope_tab(pos, dim):
    inv = (10000.0 ** (-np.arange(0, dim, 2, dtype=np.float32) / np.float32(dim))).astype(np.float32)
    ang = pos.astype(np.float32)[:, None] * inv[None, :]
    ang = np.concatenate([ang, ang], axis=-1)
    return np.cos(ang).astype(np.float32), np.sin(ang).astype(np.float32)


def rot_lhsT(blocks):
    n = sum(b for b in blocks)
    Rm = np.zeros((n, n), np.float32)
    o = 0
    for size in blocks:
        half = size // 2
        for i in range(half):
            Rm[o + i, o + i + half] = -1.0
            Rm[o + i + half, o + i] = 1.0
        o += size
    return np.ascontiguousarray(Rm.T)


def host_consts():
    c = {}
    c['c_ident'] = np.eye(128, dtype=np.float32)
    pos = np.arange(S_LEN)
    cr, sr = rope_tab(pos // 64, 64)
    cc, sc = rope_tab(pos % 64, 64)
    c['c_gq_cos'] = np.ascontiguousarray(np.concatenate([cr, cc], axis=1).T)
    c['c_gq_sin'] = np.ascontiguousarray(np.concatenate([sr, sc], axis=1).T)
    c['c_gq_RT'] = rot_lhsT([64, 64])
    c1, s1 = rope_tab(pos, 64)
    c['c_ml_cos'] = np.ascontiguousarray(c1.T)
    c['c_ml_sin'] = np.ascontiguousarray(s1.T)
    c['c_ml_RT'] = rot_lhsT([64])
    c.update(hy_consts())
    c.update(rw_consts())
    return c


SCRATCH = {
    'xres': ([D, S_LEN], F32), 'hT_d': ([D, S_LEN], BF16), 'pm_d': ([PM_ROWS, S_LEN], F32),
    'pdv_d': ([S_LEN, 256], BF16), 'ya_d': ([512, S_LEN], BF16), 'yb_d': ([512, S_LEN], BF16),
    'yc_d': ([512, S_LEN], BF16), 'yd_d': ([512, S_LEN], BF16), 'hyH_d': ([S_LEN, 2048], F32),
    'rw_d': ([11, 512, S_LEN], F32),
}


def build(phases=("load", "ffn1", "proj", "rwkv", "mla", "hyena", "gqa", "merge", "ffn2", "final"), depth=DEPTH,
          inject=(), expose=()):
    nc = bass.Bass("TRN2", target_bir_lowering=False)
    C = Ctx()
    C.nc = nc
    C.S = Sched(nc)
    C.x = DT(nc, "x", [S_LEN, D], F32, kind="ExternalInput")
    C.P = {}
    for k, shp in PARAM_SHAPES.items():
        C.P[k] = DT(nc, k, shp, F32, kind="ExternalInput")
    hc = host_consts()
    for k, v in hc.items():
        dt_ = F32 if v.dtype == np.float32 else BF16
        setattr(C, k, DT(nc, k, list(v.shape), dt_, kind="ExternalInput"))
    C.d_ident = C.c_ident
    C.out = DT(nc, "out", [S_LEN, D], F32, kind="ExternalOutput")
    for k, (shp, dt_) in SCRATCH.items():
        kind = "ExternalInput" if k in inject else ("ExternalOutput" if k in expose else "Internal")
        setattr(C, k, DT(nc, k, shp, dt_, kind=kind))
    C.final_norm = C.P['final_norm']
    C.ffn_norm = {1: C.P['ffn1_norm'].ap, 2: C.P['ffn2_norm'].ap}
    C.ffn_wg = {1: C.P['ffn1_w_gate'], 2: C.P['ffn2_w_gate']}
    C.ffn_wu = {1: C.P['ffn1_w_up'], 2: C.P['ffn2_w_up']}
    C.ffn_wd = {1: C.P['ffn1_w_down'], 2: C.P['ffn2_w_down']}
    load_consts(C)
    C.rw_stop = RW_STOP
    if "load" in phases:
        phase_load_x(C)
    for l in range(depth):
        if "ffn1" in phases:
            phase_ffn(C, l, 1)
        if "proj" in phases:
            phase_proj(C, l)
        if "rwkv" in phases:
            phase_rwkv(C, l)
        if "mla" in phases:
            phase_mla(C, l)
        if "hyena" in phases:
            phase_hyena(C, l)
        if "gqa" in phases:
            phase_gqa(C, l)
        if "merge" in phases:
            phase_merge(C, l)
        if "ffn2" in phases:
            phase_ffn(C, l, 2)
    if "final" in phases:
        phase_final_norm(C)
    C.S.barrier()
    C.S.finalize()
    return nc, hc


def kernel(**inputs):
    nc, hc = build()
    x = np.ascontiguousarray(np.asarray(inputs['x'], dtype=np.float32))
    n = x.shape[0]
    base = {k: np.ascontiguousarray(np.asarray(inputs[k], dtype=np.float32)) for k in PARAM_SHAPES}
    base.update(hc)
    in_maps = []
    for b in range(n):
        m = dict(base)
        m['x'] = x[b]
        in_maps.append(m)
    res = run_bass_kernel_spmd(nc, in_maps, core_ids=list(range(n)))
    return np.stack([r['out'] for r in res.results], axis=0)


def norm_tile(C, xa, hT, gcol, rstd, sq, p_s, nsub):
    S = C.S
    for s_i in range(nsub):
        for c in range(16):
            q_ = sq[c % 2]
            act(S, q_[:], xa[:, c, s_i * NT:(s_i + 1) * NT], AF.Square, r=[xa.b(c)], w=[q_.b()])
            mm(S, p_s[:], C.ones[:], q_[:], start=(c == 0), stop=(c == 15), r=[C.ones.b(), q_.b()], w=[p_s.b()])
        rsqrt(S, rstd[:, s_i * NT:(s_i + 1) * NT], p_s[:], 1.0 / D, C.epsc[:, 0:1], r=[p_s.b(), C.epsc.b()],
              w=[rstd.b(s_i)])
    for c in range(16):
        stt(S, hT[:, c, :], xa[:, c, :], gcol[:, c:c + 1], rstd[:], ALU.mult, ALU.mult,
            r=[xa.b(c), gcol.b()] + [rstd.b(i) for i in range(nsub)], w=[hT.b(c)])


def phase_proj(C, l):
    S = C.S
    nm = "pj%d_" % l
    TT = 1024
    NSUB = TT // NT
    win = C.P['w_in'].ap[l]
    gcol = sb(S, nm + "gcol", [128, 16])
    S.op("sp", lambda e: e.dma_start(out=gcol[:], in_=C.P['mix_norm'].ap[l].rearrange("(c p) -> p c", p=128),
                                     allow_slow_non_contiguous=True), w=[gcol.b()], dma=True)
    xa = sb(S, nm + "xa", [128, 16, TT])
    hT = sb(S, nm + "hT", [128, 16, TT], BF16)
    rstd = sb(S, nm + "rstd", [128, TT])
    sq = [sb(S, nm + "sq%d" % i, [128, NT]) for i in range(2)]
    wt = [sb(S, nm + "wt%d" % i, [128, 16, 512], BF16) for i in range(2)]
    stg = [sb(S, nm + "stg%d" % i, [128, TT]) for i in range(2)]
    stv = [sb(S, nm + "stv%d" % i, [128, 256], BF16) for i in range(2)]
    pp = [ps(S, nm + "pp%d" % i, [128, NT]) for i in range(4)]
    p_s = ps(S, nm + "pss", [128, NT])
    groups = []
    c0 = 0
    while c0 < PM_ROWS:
        n = min(512, PM_ROWS - c0)
        groups.append((c0, n))
        c0 += n
    it = 0
    kk = 0
    ks = 0
    for tt_i in range(S_LEN // TT):
        t0 = tt_i * TT
        for c in range(16):
            dma(S, xa[:, c, :], C.xres[c * 128:(c + 1) * 128, t0:t0 + TT], r=[C.xres.b((c, tt_i))],
                w=[xa.b(c)], q="sp" if c % 2 == 0 else "act")
        norm_tile(C, xa, hT, gcol, rstd, sq, p_s, NSUB)
        hT_r = [hT.b(c) for c in range(16)]
        for c in range(16):
            dma(S, C.hT_d[c * 128:(c + 1) * 128, t0:t0 + TT], hT[:, c, :], r=[hT.b(c)], w=[C.hT_d.b((c, tt_i))], q="sp")
        for (c0, n) in groups:
            w_t = wt[it % 2]
            it += 1
            dma(S, w_t[:, :, 0:n], win[:, c0:c0 + n].rearrange("(c p) f -> p c f", p=128), w=[w_t.b()], q="pool")
            for j in range((n + 127) // 128):
                m = min(128, n - j * 128)
                st_ = stg[ks % 2]
                ks += 1
                for s_i in range(NSUB):
                    p = pp[kk % 4]
                    kk += 1
                    tsl = slice(s_i * NT, (s_i + 1) * NT)
                    for c in range(16):
                        mm(S, p[0:m, :], w_t[:, c, j * 128:j * 128 + m], hT[:, c, tsl], start=(c == 0), stop=(c == 15),
                           r=[w_t.b(), hT_r[c]], w=[p.b()])
                    cp(S, st_[0:m, tsl], p[0:m, :], r=[p.b()], w=[st_.b()], eng="dve" if kk % 2 == 0 else "act")
                dma(S, C.pm_d[c0 + j * 128:c0 + j * 128 + m, t0:t0 + TT], st_[0:m, :], r=[st_.b()],
                    w=[C.pm_d.b((c0 + j * 128, tt_i))], q="sp")
        w_t = wt[it % 2]
        it += 1
        dma(S, w_t[:, :, 0:256], win[:, PM_ROWS:PM_ROWS + 256].rearrange("(c p) f -> p c f", p=128), w=[w_t.b()], q="pool")
        for tc in range(TT // 128):
            p = pp[kk % 4]
            kk += 1
            sv = stv[tc % 2]
            for c in range(16):
                mm(S, p[:, 0:256], hT[:, c, tc * 128:(tc + 1) * 128], w_t[:, c, 0:256], start=(c == 0), stop=(c == 15),
                   r=[w_t.b(), hT_r[c]], w=[p.b()])
            cp(S, sv[:], p[:, 0:256], r=[p.b()], w=[sv.b()], eng="dve" if tc % 2 == 0 else "act")
            dma(S, C.pdv_d[t0 + tc * 128:t0 + (tc + 1) * 128, :], sv[:], r=[sv.b()], w=[C.pdv_d.b(tt_i * 8 + tc)], q="sp")
    S.barrier()
    S.release()


def attn_core(C, nm, kq_parts, v_t, v_off, scale, y_d, row0, onesb, bufs):
    S = C.S
    p_sc, p_o, p_r, pT, rinv, yst = bufs
    ksc = 0
    for ti in range(S_LEN // NT):
        tsl = slice(ti * NT, (ti + 1) * NT)
        po = p_o[ti % 2]
        pr = p_r[ti % 2]
        for sc in range(16):
            psc = p_sc[ksc % 2]
            p_t = pT[ksc % 3]
            ksc += 1
            for i, (kT, qT, K) in enumerate(kq_parts):
                mm(S, psc[:], kT[0:K, sc * 128:(sc + 1) * 128], qT[0:K, tsl], start=(i == 0), stop=(i == len(kq_parts) - 1),
                   r=[kT.b(), qT.b()], w=[psc.b()])
            act(S, p_t[:], psc[:], AF.Exp, scale=scale, r=[psc.b()], w=[p_t.b()])
            mm(S, po[:], v_t[:, sc, v_off:v_off + 128], p_t[:], start=(sc == 0), stop=(sc == 15), r=[v_t.b(), p_t.b()], w=[po.b()])
            mm(S, pr[:], onesb[:], p_t[:], start=(sc == 0), stop=(sc == 15), r=[onesb.b(), p_t.b()], w=[pr.b()])
        ri = rinv[ti % 2]
        ys = yst[ti % 2]
        S.op("dve", lambda e, ri=ri, pr=pr: e.reciprocal(ri[:], pr[:]), r=[pr.b()], w=[ri.b()])
        tt(S, ys[:], po[:], ri[:], ALU.mult, r=[po.b(), ri.b()], w=[ys.b()])
        dma(S, y_d[row0:row0 + 128, tsl], ys[:], r=[ys.b()], w=[y_d.b((row0, ti))], q="sp")


def attn_bufs(S, nm):
    p_sc = [ps(S, nm + "psc%d" % i, [128, NT]) for i in range(2)]
    p_o = [ps(S, nm + "po%d" % i, [128, NT]) for i in range(2)]
    p_r = [ps(S, nm + "pr%d" % i, [128, NT]) for i in range(2)]
    pT = [sb(S, nm + "pT%d" % i, [128, NT], BF16) for i in range(3)]
    rinv = [sb(S, nm + "ri%d" % i, [128, NT]) for i in range(2)]
    yst = [sb(S, nm + "ys%d" % i, [128, NT], BF16) for i in range(2)]
    return p_sc, p_o, p_r, pT, rinv, yst


def rope_norm_head(C, nm, src_rows, nrow, gain_col, cosT, sinT, RT, dst, tmp, p_a, p_b, do_norm):
    S = C.S
    xin, xn, t1, rs = tmp
    dma(S, xin[0:nrow, :], C.pm_d[src_rows:src_rows + nrow, :], w=[xin.b()], q="act")
    for ti in range(S_LEN // NT):
        tsl = slice(ti * NT, (ti + 1) * NT)
        if do_norm:
            act(S, t1[0:nrow, :], xin[0:nrow, tsl], AF.Square, r=[xin.b()], w=[t1.b()])
            mm(S, p_a[0:nrow, :], C.ones[0:nrow, 0:nrow], t1[0:nrow, :], r=[C.ones.b(), t1.b()], w=[p_a.b()])
            rsqrt(S, rs[0:nrow, :], p_a[0:nrow, :], 1.0 / nrow, C.epsc[0:nrow, 0:1], r=[p_a.b(), C.epsc.b()], w=[rs.b()])
            stt(S, xn[0:nrow, :], xin[0:nrow, tsl], gain_col[0:nrow, 0:1], rs[0:nrow, :], ALU.mult, ALU.mult,
                r=[xin.b(), gain_col.b(), rs.b()], w=[xn.b()])
            src = xn[0:nrow, :]
        else:
            cp(S, xn[0:nrow, :], xin[0:nrow, tsl], r=[xin.b()], w=[xn.b()], eng="pool")
            src = xn[0:nrow, :]
        mm(S, p_b[0:nrow, :], RT[0:nrow, 0:nrow], src, r=[RT.b(), xn.b()], w=[p_b.b()])
        tt(S, t1[0:nrow, :], p_b[0:nrow, :], sinT[0:nrow, tsl], ALU.mult, r=[p_b.b(), sinT.b()], w=[t1.b()])
        tt(S, xn[0:nrow, :], src, cosT[0:nrow, tsl], ALU.mult, r=[xn.b(), cosT.b()], w=[xn.b()], eng="pool")
        tt(S, dst[0:nrow, tsl], xn[0:nrow, :], t1[0:nrow, :], ALU.add, r=[xn.b(), t1.b()], w=[dst.b()])


def phase_gqa(C, l):
    S = C.S
    nm = "gq%d_" % l
    cosT = sb(S, nm + "cos", [128, S_LEN])
    sinT = sb(S, nm + "sin", [128, S_LEN])
    RT = sb(S, nm + "RT", [128, 128])
    dma(S, cosT[:], C.c_gq_cos[:], w=[cosT.b()])
    dma(S, sinT[:], C.c_gq_sin[:], w=[sinT.b()], q="act")
    dma(S, RT[:], C.c_gq_RT[:], w=[RT.b()])
    gq = sb(S, nm + "gq", [128, 2])
    S.op("sp", lambda e: e.dma_start(out=gq[:, 0:1], in_=C.P['gqa_q_norm'].ap[l].rearrange("(p o) -> p o", o=1),
                                     allow_slow_non_contiguous=True), w=[gq.b()], dma=True)
    gk = sb(S, nm + "gk", [128, 2])
    S.op("sp", lambda e: e.dma_start(out=gk[:, 0:1], in_=C.P['gqa_k_norm'].ap[l].rearrange("(p o) -> p o", o=1),
                                     allow_slow_non_contiguous=True), w=[gk.b()], dma=True)
    onesb = sb(S, nm + "onesb", [128, 128], BF16)
    mset(S, onesb[:], 1.0, w=[onesb.b()])
    tmp = (sb(S, nm + "xin", [128, S_LEN]), sb(S, nm + "xn", [128, NT]), sb(S, nm + "t1", [128, NT]), sb(S, nm + "rs", [128, NT]))
    p_a = ps(S, nm + "pa", [128, NT])
    p_b = ps(S, nm + "pb", [128, NT])
    qT = [sb(S, nm + "qT%d" % h, [128, S_LEN], BF16) for h in range(4)]
    kT = [sb(S, nm + "kT%d" % g, [128, S_LEN], BF16) for g in range(2)]
    v_t = sb(S, nm + "v", [128, 16, 256], BF16)
    dma(S, v_t[:], C.pdv_d.ap.rearrange("(c p) f -> p c f", p=128), w=[v_t.b()], q="act")
    for h in range(4):
        rope_norm_head(C, nm, OFF_GQA + h * 128, 128, gq, cosT, sinT, RT, qT[h], tmp, p_a, p_b, True)
    for g in range(2):
        rope_norm_head(C, nm, OFF_GQA + 512 + g * 128, 128, gk, cosT, sinT, RT, kT[g], tmp, p_a, p_b, True)
    bufs = attn_bufs(S, nm)
    for h in range(4):
        g = h // 2
        attn_core(C, nm, [(kT[g], qT[h], 128)], v_t, g * 128, 128.0 ** -0.5, C.yd_d, h * 128, onesb, bufs)
    S.barrier()
    S.release()


def phase_mla(C, l):
    S = C.S
    nm = "ml%d_" % l
    cosT = sb(S, nm + "cos", [64, S_LEN])
    sinT = sb(S, nm + "sin", [64, S_LEN])
    RT = sb(S, nm + "RT", [64, 64])
    dma(S, cosT[:], C.c_ml_cos[:], w=[cosT.b()])
    dma(S, sinT[:], C.c_ml_sin[:], w=[sinT.b()], q="act")
    dma(S, RT[:], C.c_ml_RT[:], w=[RT.b()])
    wq = sb(S, nm + "wq", [128, 3, 768], BF16)
    dma(S, wq[:], C.P['mla_w_q_up'].ap[l].rearrange("(c p) f -> p c f", p=128), w=[wq.b()], q="pool")
    wkv = sb(S, nm + "wkv", [128, 1024], BF16)
    dma(S, wkv[:], C.P['mla_w_kv_up'].ap[l], w=[wkv.b()], q="pool")
    gq = sb(S, nm + "gq", [128, 4])
    S.op("sp", lambda e: e.dma_start(out=gq[:, 0:3], in_=C.P['mla_q_norm'].ap[l].rearrange("(c p) -> p c", p=128),
                                     allow_slow_non_contiguous=True), w=[gq.b()], dma=True)
    S.op("sp", lambda e: e.dma_start(out=gq[:, 3:4], in_=C.P['mla_kv_norm'].ap[l].rearrange("(p o) -> p o", o=1),
                                     allow_slow_non_contiguous=True), w=[gq.b()], dma=True)
    onesb = sb(S, nm + "onesb", [128, 128], BF16)
    mset(S, onesb[:], 1.0, w=[onesb.b()])
    xq = sb(S, nm + "xq", [128, 3, S_LEN])
    dma(S, xq[:], C.pm_d.ap[OFF_MLA:OFF_MLA + 384, :].rearrange("(c p) t -> p c t", p=128), w=[xq.b()])
    xkv = sb(S, nm + "xkv", [128, S_LEN])
    dma(S, xkv[:], C.pm_d[OFF_MLA + 384:OFF_MLA + 512, :], w=[xkv.b()], q="act")
    qn = sb(S, nm + "qn", [128, 3, S_LEN], BF16)
    kvn = sb(S, nm + "kvn", [128, S_LEN], BF16)
    t1 = sb(S, nm + "t1", [128, NT])
    t2 = sb(S, nm + "t2", [128, NT])
    xn = sb(S, nm + "xn", [128, NT])
    rs = sb(S, nm + "rs", [128, NT])
    p_a = ps(S, nm + "pa", [128, NT])
    p_b = ps(S, nm + "pb", [128, NT])
    nti = S_LEN // NT
    for ti in range(nti):
        tsl = slice(ti * NT, (ti + 1) * NT)
        for c in range(3):
            act(S, t1[:], xq[:, c, tsl], AF.Square, r=[xq.b()], w=[t1.b()])
            mm(S, p_a[:], C.ones[:], t1[:], start=(c == 0), stop=(c == 2), r=[C.ones.b(), t1.b()], w=[p_a.b()])
        rsqrt(S, rs[:], p_a[:], 1.0 / 384, C.epsc[:, 0:1], r=[p_a.b(), C.epsc.b()], w=[rs.b()])
        for c in range(3):
            stt(S, qn[:, c, tsl], xq[:, c, tsl], gq[:, c:c + 1], rs[:], ALU.mult, ALU.mult, r=[xq.b(), gq.b(), rs.b()], w=[qn.b()])
        act(S, t1[:], xkv[:, tsl], AF.Square, r=[xkv.b()], w=[t1.b()])
        mm(S, p_a[:], C.ones[:], t1[:], r=[C.ones.b(), t1.b()], w=[p_a.b()])
        rsqrt(S, rs[:], p_a[:], 1.0 / 128, C.epsc[:, 0:1], r=[p_a.b(), C.epsc.b()], w=[rs.b()])
        stt(S, kvn[:, tsl], xkv[:, tsl], gq[:, 3:4], rs[:], ALU.mult, ALU.mult, r=[xkv.b(), gq.b(), rs.b()], w=[kvn.b()])
    qnope = [sb(S, nm + "qnope%d" % h, [128, S_LEN], BF16) for h in range(4)]
    qrope = [sb(S, nm + "qrope%d" % h, [64, S_LEN], BF16) for h in range(4)]
    knope = [sb(S, nm + "knope%d" % h, [128, S_LEN], BF16) for h in range(4)]
    krope = sb(S, nm + "krope", [64, S_LEN], BF16)
    v_t = sb(S, nm + "v", [128, 16, 512], BF16)
    k = 0
    for h in range(4):
        for ti in range(nti):
            tsl = slice(ti * NT, (ti + 1) * NT)
            for c in range(3):
                mm(S, p_a[:], wq[:, c, h * 192:h * 192 + 128], qn[:, c, tsl], start=(c == 0), stop=(c == 2), r=[wq.b(), qn.b()], w=[p_a.b()])
            cp(S, qnope[h][:, tsl], p_a[:], r=[p_a.b()], w=[qnope[h].b()], eng="act")
            mm(S, p_a[:], wkv[:, h * 256:h * 256 + 128], kvn[:, tsl], r=[wkv.b(), kvn.b()], w=[p_a.b()])
            cp(S, knope[h][:, tsl], p_a[:], r=[p_a.b()], w=[knope[h].b()], eng="act")
            for c in range(3):
                mm(S, p_b[0:64, :], wq[:, c, h * 192 + 128:h * 192 + 192], qn[:, c, tsl], start=(c == 0), stop=(c == 2), r=[wq.b(), qn.b()], w=[p_b.b()])
            cp(S, xn[0:64, :], p_b[0:64, :], r=[p_b.b()], w=[xn.b()])
            mm(S, p_b[0:64, :], RT[:, :], xn[0:64, :], r=[RT.b(), xn.b()], w=[p_b.b()])
            tt(S, t1[0:64, :], p_b[0:64, :], sinT[:, tsl], ALU.mult, r=[p_b.b(), sinT.b()], w=[t1.b()])
            tt(S, t2[0:64, :], xn[0:64, :], cosT[:, tsl], ALU.mult, r=[xn.b(), cosT.b()], w=[t2.b()], eng="pool")
            tt(S, qrope[h][:, tsl], t2[0:64, :], t1[0:64, :], ALU.add, r=[t2.b(), t1.b()], w=[qrope[h].b()])
    xkr = sb(S, nm + "xkr", [64, S_LEN])
    dma(S, xkr[:], C.pm_d[OFF_MLA + 512:OFF_MLA + 576, :], w=[xkr.b()])
    for ti in range(nti):
        tsl = slice(ti * NT, (ti + 1) * NT)
        mm(S, p_b[0:64, :], RT[:, :], xkr[:, tsl], r=[RT.b(), xkr.b()], w=[p_b.b()])
        tt(S, t1[0:64, :], p_b[0:64, :], sinT[:, tsl], ALU.mult, r=[p_b.b(), sinT.b()], w=[t1.b()])
        tt(S, t2[0:64, :], xkr[:, tsl], cosT[:, tsl], ALU.mult, r=[xkr.b(), cosT.b()], w=[t2.b()], eng="pool")
        tt(S, krope[:, tsl], t2[0:64, :], t1[0:64, :], ALU.add, r=[t2.b(), t1.b()], w=[krope.b()])
    for sc in range(16):
        for h in range(4):
            mm(S, p_a[:, h * 128:(h + 1) * 128], kvn[:, sc * 128:(sc + 1) * 128], wkv[:, h * 256 + 128:h * 256 + 256],
               r=[wkv.b(), kvn.b()], w=[p_a.b()])
        cp(S, v_t[:, sc, :], p_a[:], r=[p_a.b()], w=[v_t.b()], eng="dve" if sc % 2 == 0 else "act")
    bufs = attn_bufs(S, nm)
    for h in range(4):
        attn_core(C, nm, [(knope[h], qnope[h], 128), (krope, qrope[h], 64)], v_t, h * 128, 192.0 ** -0.5, C.yb_d, h * 128, onesb, bufs)
    S.barrier()
    S.release()


def phase_merge(C, l):
    S = C.S
    nm = "mg%d_" % l
    TT = 1024
    NSUB = TT // NT
    win = C.P['w_in'].ap[l]
    wbr = C.P['w_branch'].ap[l]
    wout = C.P['w_out'].ap[l]
    ys_d = [C.ya_d, C.yb_d, C.yc_d, C.yd_d]
    hT = sb(S, nm + "hT", [128, 16, TT], BF16)
    yT = [sb(S, nm + "yT%d" % n, [128, 4, TT], BF16) for n in range(4)]
    mT = sb(S, nm + "mT", [128, 16, TT], BF16)
    C.mg_acc = {(j, s_i): sb(S, nm + "acc%d_%d" % (j, s_i), [128, NT]) for j in range(4) for s_i in range(NSUB)}
    gw = [sb(S, nm + "gw%d" % i, [128, 16, 512], BF16) for i in range(2)]
    wb = [sb(S, nm + "wb%d" % i, [128, 4, 512], BF16) for i in range(2)]
    sg = [sb(S, nm + "sg%d" % i, [128, NT]) for i in range(2)]
    tmp = [sb(S, nm + "tmp%d" % i, [128, NT]) for i in range(2)]
    xa = [sb(S, nm + "xa%d" % i, [128, TT]) for i in range(2)]
    p_g = [ps(S, nm + "pg%d" % i, [128, NT]) for i in range(2)]
    p_b = [ps(S, nm + "pb%d" % i, [128, NT]) for i in range(2)]
    p_o = [ps(S, nm + "po%d" % i, [128, NT]) for i in range(2)]
    it = 0
    kk = 0
    for tt_i in range(S_LEN // TT):
        t0 = tt_i * TT
        for c in range(16):
            dma(S, hT[:, c, :], C.hT_d[c * 128:(c + 1) * 128, t0:t0 + TT], r=[C.hT_d.b((c, tt_i))], w=[hT.b()],
                q="sp" if c % 2 == 0 else "act")
        for n in range(4):
            dma(S, yT[n][:], ys_d[n].ap[:, t0:t0 + TT].rearrange("(c p) t -> p c t", p=128), w=[yT[n].b()], q="act")
        for dg in range(4):
            gws = []
            for n in range(4):
                g_t, b_t = gw[it % 2], wb[it % 2]
                it += 1
                col = OFF_GATE + n * D + dg * 512
                dma(S, g_t[:], win[:, col:col + 512].rearrange("(c p) f -> p c f", p=128), w=[g_t.b()], q="pool")
                dma(S, b_t[:], wbr[n][:, dg * 512:(dg + 1) * 512].rearrange("(c p) f -> p c f", p=128), w=[b_t.b()], q="pool")
                for j in range(4):
                    dc = dg * 4 + j
                    for s_i in range(NSUB):
                        tsl = slice(s_i * NT, (s_i + 1) * NT)
                        pg, pb = p_g[kk % 2], p_b[kk % 2]
                        sg_t, tm_t = sg[kk % 2], tmp[kk % 2]
                        kk += 1
                        for c in range(16):
                            mm(S, pg[:], g_t[:, c, j * 128:(j + 1) * 128], hT[:, c, tsl], start=(c == 0), stop=(c == 15),
                               r=[g_t.b(), hT.b()], w=[pg.b()])
                        for c in range(4):
                            mm(S, pb[:], b_t[:, c, j * 128:(j + 1) * 128], yT[n][:, c, tsl], start=(c == 0), stop=(c == 3),
                               r=[b_t.b(), yT[n].b()], w=[pb.b()])
                        act(S, sg_t[:], pg[:], AF.Sigmoid, r=[pg.b()], w=[sg_t.b()])
                        acc = C.mg_acc[(j, s_i)]
                        if n == 0:
                            tt(S, acc[:], sg_t[:], pb[:], ALU.mult, r=[sg_t.b(), pb.b()], w=[acc.b()])
                        else:
                            tt(S, tm_t[:], sg_t[:], pb[:], ALU.mult, r=[sg_t.b(), pb.b()], w=[tm_t.b()])
                            if n < 3:
                                tt(S, acc[:], acc[:], tm_t[:], ALU.add, r=[acc.b(), tm_t.b()], w=[acc.b()], eng="pool")
                            else:
                                tt(S, mT[:, dc, tsl], acc[:], tm_t[:], ALU.add, r=[acc.b(), tm_t.b()], w=[mT.b(dc)], eng="pool")
        for og in range(4):
            w_t = gw[it % 2]
            it += 1
            dma(S, w_t[:], wout[:, og * 512:(og + 1) * 512].rearrange("(c p) f -> p c f", p=128), w=[w_t.b()], q="pool")
            for j in range(4):
                dc = og * 4 + j
                x_t = xa[dc % 2]
                dma(S, x_t[:], C.xres[dc * 128:(dc + 1) * 128, t0:t0 + TT], r=[C.xres.b((dc, tt_i))], w=[x_t.b()], q="sp")
                for s_i in range(NSUB):
                    tsl = slice(s_i * NT, (s_i + 1) * NT)
                    po = p_o[kk % 2]
                    kk += 1
                    for c in range(16):
                        mm(S, po[:], w_t[:, c, j * 128:(j + 1) * 128], mT[:, c, tsl], start=(c == 0), stop=(c == 15),
                           r=[w_t.b(), mT.b(c)], w=[po.b()])
                    tt(S, x_t[:, tsl], x_t[:, tsl], po[:], ALU.add, r=[x_t.b(), po.b()], w=[x_t.b()])
                dma(S, C.xres[dc * 128:(dc + 1) * 128, t0:t0 + TT], x_t[:], r=[x_t.b()], w=[C.xres.b((dc, tt_i))], q="sp")
    S.barrier()
    S.release()


def hy_consts():
    L = S_LEN
    c = {}
    t = np.linspace(0.0, 1.0, L, dtype=np.float32)[:, None]
    w_ang = (2.0 * math.pi * np.arange(L, dtype=np.float32) / L).astype(np.float32)
    fr = np.linspace(1e-4, 15.0, 16, dtype=np.float32)
    ang = w_ang[:, None] * fr[None, :]
    z = np.concatenate([t, np.cos(ang), -np.sin(ang)], axis=-1).astype(np.float32)
    c['c_hy_z'] = np.ascontiguousarray(z.T)
    deltas = np.abs(np.linspace(math.log(1e-2) / 1.5, math.log(1e-2) / 0.3, 512)).astype(np.float32)
    win = np.exp(-t * deltas[None, :]).astype(np.float32)
    c['c_hy_win'] = np.ascontiguousarray(win.reshape(16, 128, 512).transpose(1, 0, 2))
    n = np.arange(L, dtype=np.int64)
    k = np.mod(np.outer(n, 2 * n + 1), 8192)
    angm = (2.0 * math.pi / 8192.0) * k.astype(np.float64)
    Cm = np.cos(angm)
    Sm = np.sin(angm)
    bf = ml_dtypes.bfloat16

    def t_major(M):
        return np.ascontiguousarray(M.reshape(16, 128, 16, 128).transpose(2, 1, 0, 3).reshape(16, 128, 2048).astype(bf))

    def f_major(M):
        MT = M.T
        return np.ascontiguousarray(MT.reshape(16, 128, 16, 128).transpose(2, 1, 0, 3).reshape(16, 128, 2048).astype(bf))

    c['c_hy_Ct'] = t_major(Cm)
    c['c_hy_St'] = t_major(Sm)
    c['c_hy_Cf'] = f_major(Cm)
    c['c_hy_Sf'] = f_major(Sm)
    return c


def sin_act(S, out, arg, tmp, bufs_r, w):
    s4, s8, q = tmp
    act(S, s4, arg, AF.Sin, scale=0.25, r=bufs_r, w=[w[1]])
    act(S, s8, arg, AF.Sin, scale=0.125, r=bufs_r, w=[w[2]])
    tt(S, q, s8, s8, ALU.mult, r=[w[2]], w=[w[3]])
    ts(S, q, q, -2.0, 1.0, ALU.mult, ALU.add, r=[w[3]], w=[w[3]])
    tt(S, s8, s4, q, ALU.mult, r=[w[1], w[3]], w=[w[2]])
    tt(S, q, s4, s4, ALU.mult, r=[w[1]], w=[w[3]])
    ts(S, q, q, -2.0, 1.0, ALU.mult, ALU.add, r=[w[3]], w=[w[3]])
    stt(S, out, s8, 4.0, q, ALU.mult, ALU.mult, r=[w[2], w[3]], w=[w[0]])


def phase_hyena(C, l):
    S = C.S
    nm = "hy%d_" % l
    P = C.P
    nti = S_LEN // NT
    zT = sb(S, nm + "zT", [33, S_LEN])
    dma(S, zT[:], C.c_hy_z[:], w=[zT.b()])
    w1 = sb(S, nm + "w1", [33, 64])
    dma(S, w1[:], P['hy_w1'].ap[l], w=[w1.b()])
    w2 = sb(S, nm + "w2", [64, 64])
    dma(S, w2[:], P['hy_w2'].ap[l], w=[w2.b()])
    w3 = sb(S, nm + "w3", [64, 64])
    dma(S, w3[:], P['hy_w3'].ap[l], w=[w3.b()])
    w4 = sb(S, nm + "w4", [64, 2048])
    dma(S, w4[:], P['hy_w4'].ap[l], w=[w4.b()], q="act")
    cols = sb(S, nm + "cols", [64, 8])
    for i, k in enumerate(['hy_b1', 'hy_b2', 'hy_b3']):
        S.op("sp", lambda e, i=i, k=k: e.dma_start(out=cols[:, i:i + 1], in_=P[k].ap[l].rearrange("(p o) -> p o", o=1),
                                                   allow_slow_non_contiguous=True), w=[cols.b()], dma=True)
    S.op("sp", lambda e: e.dma_start(out=cols[:, 3:6], in_=P['hy_freq'].ap[l].rearrange("k c -> c k"),
                                     allow_slow_non_contiguous=True), w=[cols.b()], dma=True)
    bias_s = sb(S, nm + "bias", [128, 1024])
    dma(S, bias_s[:], P['hy_bias'].ap[l].rearrange("o c -> (o c)").partition_broadcast(128), w=[bias_s.b()])
    ts(S, bias_s[:], bias_s[:], 1.0 / 2048.0, None, ALU.mult, r=[bias_s.b()], w=[bias_s.b()])
    win = sb(S, nm + "win", [128, 16, 512])
    dma(S, win[:], C.c_hy_win[:], w=[win.b()], q="act")
    hA = sb(S, nm + "hA", [64, S_LEN])
    hB = sb(S, nm + "hB", [64, S_LEN])
    arg = sb(S, nm + "arg", [64, NT])
    s4 = sb(S, nm + "s4", [64, NT])
    s8 = sb(S, nm + "s8", [64, NT])
    qq = sb(S, nm + "qq", [64, NT])
    p_a = ps(S, nm + "pa", [128, NT])
    p_b = ps(S, nm + "pb", [128, NT])
    p_c = ps(S, nm + "pc", [128, NT])
    p_d = ps(S, nm + "pd", [128, NT])
    layers = [(w1, zT, 33, hA, 0), (w2, hA, 64, hB, 1), (w3, hB, 64, hA, 2)]
    for (w_, src, K, dst, li) in layers:
        for ti in range(nti):
            tsl = slice(ti * NT, (ti + 1) * NT)
            mm(S, p_a[0:64, :], w_[0:K, :], src[0:K, tsl], r=[w_.b(), src.b()], w=[p_a.b()])
            ts(S, arg[:], p_a[0:64, :], cols[:, li:li + 1], cols[:, 3 + li:4 + li], ALU.add, ALU.mult,
               r=[p_a.b(), cols.b()], w=[arg.b()])
            sin_act(S, dst[:, tsl], arg[:], (s4[:], s8[:], qq[:]), [arg.b()], [dst.b(), s4.b(), s8.b(), qq.b()])
    h3 = hA
    hs = sb(S, nm + "hs", [128, 16, 1024], BF16)
    hd = sb(S, nm + "hd", [128, 16, 1024], BF16)
    f0 = sb(S, nm + "f0", [128, 1024])
    f1 = sb(S, nm + "f1", [128, 1024])
    pg = [p_a, p_b, p_c, p_d]
    for tc in range(16):
        for g in range(4):
            mm(S, pg[g][:], h3[:, tc * 128:(tc + 1) * 128], w4[:, g * 512:(g + 1) * 512], r=[h3.b(), w4.b()], w=[pg[g].b()])
        for o in range(2):
            tt(S, f0[:, o * 512:(o + 1) * 512], pg[o][:], win[:, tc, :], ALU.mult, r=[pg[o].b(), win.b()], w=[f0.b()])
            tt(S, f1[:, o * 512:(o + 1) * 512], pg[2 + o][:], win[:, tc, :], ALU.mult, r=[pg[2 + o].b(), win.b()], w=[f1.b()])
        if tc == 0:
            mset(S, f1[0:1, :], 0.0, w=[f1.b()])
        tt(S, hs[:, tc, :], f0[:], f1[:], ALU.add, r=[f0.b(), f1.b()], w=[hs.b()], eng="pool")
        tt(S, hd[:, tc, :], f0[:], f1[:], ALU.subtract, r=[f0.b(), f1.b()], w=[hd.b()])
    ct = [sb(S, nm + "ct%d" % i, [128, 2048], BF16) for i in range(2)]
    st = [sb(S, nm + "st%d" % i, [128, 2048], BF16) for i in range(2)]
    hst = [sb(S, nm + "hst%d" % i, [128, 4, 512]) for i in range(2)]
    for fc in range(16):
        c_t, s_t, h_t = ct[fc % 2], st[fc % 2], hst[fc % 2]
        dma(S, c_t[:], C.c_hy_Ct.ap[fc], w=[c_t.b()], q="sp")
        dma(S, s_t[:], C.c_hy_St.ap[fc], w=[s_t.b()], q="act")
        for o in range(2):
            pr, pi = pg[o * 2], pg[o * 2 + 1]
            for tc in range(16):
                mm(S, pr[:], c_t[:, tc * 128:(tc + 1) * 128], hs[:, tc, o * 512:(o + 1) * 512], start=(tc == 0), stop=(tc == 15),
                   r=[c_t.b(), hs.b()], w=[pr.b()])
            for tc in range(16):
                mm(S, pi[:], s_t[:, tc * 128:(tc + 1) * 128], hd[:, tc, o * 512:(o + 1) * 512], start=(tc == 0), stop=(tc == 15),
                   r=[s_t.b(), hd.b()], w=[pi.b()])
            stt(S, h_t[:, o * 2, :], pr[:], 1.0 / 2048.0, bias_s[:, o * 512:(o + 1) * 512], ALU.mult, ALU.add,
                r=[pr.b(), bias_s.b()], w=[h_t.b()])
            act(S, h_t[:, o * 2 + 1, :], pi[:], AF.Copy, scale=1.0 / 2048.0, r=[pi.b()], w=[h_t.b()])
        dma(S, C.hyH_d.ap[fc * 128:(fc + 1) * 128, :].rearrange("p (k c) -> p k c", k=4), h_t[:], r=[h_t.b()],
            w=[C.hyH_d.b(fc)], q="sp")
    S.barrier()
    S.release()
    swc = sb(S, nm + "swc", [128, 12, 4])
    for k in range(3):
        S.op("sp", lambda e, k=k: e.dma_start(out=swc[:, :, k:k + 1],
                                              in_=P['hy_short_w'].ap[l][k].rearrange("(c p o) -> p c o", p=128, o=1),
                                              allow_slow_non_contiguous=True), w=[swc.b()], dma=True)
    S.op("sp", lambda e: e.dma_start(out=swc[:, :, 3:4], in_=P['hy_short_b'].ap[l].rearrange("(c p o) -> p c o", p=128, o=1),
                                     allow_slow_non_contiguous=True), w=[swc.b()], dma=True)
    x1_tm = sb(S, nm + "x1tm", [128, 16, 512])
    x2T = sb(S, nm + "x2T", [128, 4, S_LEN])
    v_tm = sb(S, nm + "vtm", [128, 16, 512], BF16)
    Yr = sb(S, nm + "Yr", [128, 16, 512], BF16)
    Ys = sb(S, nm + "Ys", [128, 16, 512], BF16)
    ycT = sb(S, nm + "ycT", [128, 4, S_LEN], BF16)
    pin = [sb(S, nm + "pin%d" % i, [128, S_LEN]) for i in range(2)]
    u = sb(S, nm + "u", [128, S_LEN])
    pt = [ps(S, nm + "pt%d" % i, [128, NT]) for i in range(2)]
    kk = 0
    for ch in range(12):
        p_in = pin[ch % 2]
        dma(S, p_in[:], C.pm_d[OFF_HY + ch * 128:OFF_HY + (ch + 1) * 128, :], w=[p_in.b()], q="sp" if ch % 2 == 0 else "act")
        dst = x2T[:, ch - 4, :] if 4 <= ch < 8 else u[:]
        dbuf = x2T.b() if 4 <= ch < 8 else u.b()
        ts(S, dst, p_in[:], swc[:, ch, 1:2], swc[:, ch, 3:4], ALU.mult, ALU.add, r=[p_in.b(), swc.b()], w=[dbuf])
        d1 = x2T[:, ch - 4, 1:S_LEN] if 4 <= ch < 8 else u[:, 1:S_LEN]
        d2 = x2T[:, ch - 4, 0:S_LEN - 1] if 4 <= ch < 8 else u[:, 0:S_LEN - 1]
        stt(S, d1, p_in[:, 0:S_LEN - 1], swc[:, ch, 0:1], d1, ALU.mult, ALU.add, r=[p_in.b(), swc.b(), dbuf], w=[dbuf])
        stt(S, d2, p_in[:, 1:S_LEN], swc[:, ch, 2:3], d2, ALU.mult, ALU.add, r=[p_in.b(), swc.b(), dbuf], w=[dbuf])
        if ch < 4 or ch >= 8:
            cc = ch if ch < 4 else ch - 8
            tgt = x1_tm if ch < 4 else v_tm
            for tc in range(16):
                p = pt[kk % 2]
                kk += 1
                tr(S, p[:, 0:128], u[:, tc * 128:(tc + 1) * 128], C.ident[:], r=[u.b(), C.ident.b()], w=[p.b()])
                cp(S, tgt[:, tc, cc * 128:(cc + 1) * 128], p[:, 0:128], r=[p.b()], w=[tgt.b()], eng="dve" if kk % 2 == 0 else "act")
    ct = [sb(S, nm + "dct%d" % i, [128, 2048], BF16) for i in range(2)]
    st = [sb(S, nm + "dst%d" % i, [128, 2048], BF16) for i in range(2)]
    hst = [sb(S, nm + "dhst%d" % i, [128, 2, 512]) for i in range(2)]
    ur = [sb(S, nm + "ur%d" % i, [128, 512]) for i in range(2)]
    us = [sb(S, nm + "us%d" % i, [128, 512]) for i in range(2)]
    m1 = sb(S, nm + "m1", [128, 512])
    m2 = sb(S, nm + "m2", [128, 512])
    m3 = sb(S, nm + "m3", [128, 512])
    m4 = sb(S, nm + "m4", [128, 512])
    p_r = [ps(S, nm + "pr%d" % i, [128, NT]) for i in range(2)]
    p_s = [ps(S, nm + "psn%d" % i, [128, NT]) for i in range(2)]
    p_y = [ps(S, nm + "py%d" % i, [128, NT]) for i in range(2)]
    it = 0
    for o in range(2):
        src_tm = v_tm
        for fc in range(16):
            c_t, s_t, h_t = ct[it % 2], st[it % 2], hst[it % 2]
            u_r, u_s = ur[it % 2], us[it % 2]
            pr, pi = p_r[it % 2], p_s[it % 2]
            it += 1
            dma(S, c_t[:], C.c_hy_Ct.ap[fc], w=[c_t.b()], q="sp")
            dma(S, s_t[:], C.c_hy_St.ap[fc], w=[s_t.b()], q="act")
            dma(S, h_t[:], C.hyH_d.ap[fc * 128:(fc + 1) * 128, o * 1024:(o + 1) * 1024].rearrange("p (k c) -> p k c", k=2),
                r=[C.hyH_d.b(fc)], w=[h_t.b()], q="sp")
            for tc in range(16):
                mm(S, pr[:], c_t[:, tc * 128:(tc + 1) * 128], src_tm[:, tc, :], start=(tc == 0), stop=(tc == 15),
                   r=[c_t.b(), src_tm.b()], w=[pr.b()])
            for tc in range(16):
                mm(S, pi[:], s_t[:, tc * 128:(tc + 1) * 128], src_tm[:, tc, :], start=(tc == 0), stop=(tc == 15),
                   r=[s_t.b(), src_tm.b()], w=[pi.b()])
            cp(S, u_r[:], pr[:], r=[pr.b()], w=[u_r.b()], eng="act")
            cp(S, u_s[:], pi[:], r=[pi.b()], w=[u_s.b()], eng="act")
            tt(S, m1[:], u_r[:], h_t[:, 0, :], ALU.mult, r=[u_r.b(), h_t.b()], w=[m1.b()])
            tt(S, m2[:], u_s[:], h_t[:, 1, :], ALU.mult, r=[u_s.b(), h_t.b()], w=[m2.b()], eng="pool")
            tt(S, Yr[:, fc, :], m1[:], m2[:], ALU.subtract, r=[m1.b(), m2.b()], w=[Yr.b()])
            tt(S, m3[:], u_r[:], h_t[:, 1, :], ALU.mult, r=[u_r.b(), h_t.b()], w=[m3.b()], eng="pool")
            tt(S, m4[:], u_s[:], h_t[:, 0, :], ALU.mult, r=[u_s.b(), h_t.b()], w=[m4.b()])
            tt(S, Ys[:, fc, :], m3[:], m4[:], ALU.add, r=[m3.b(), m4.b()], w=[Ys.b()], eng="pool")
        for tc in range(16):
            c_t, s_t = ct[it % 2], st[it % 2]
            it += 1
            dma(S, c_t[:], C.c_hy_Cf.ap[tc], w=[c_t.b()], q="sp")
            dma(S, s_t[:], C.c_hy_Sf.ap[tc], w=[s_t.b()], q="act")
            if o == 0:
                py = p_y[tc % 2]
                for fc in range(16):
                    mm(S, py[:], c_t[:, fc * 128:(fc + 1) * 128], Yr[:, fc, :], start=(fc == 0), stop=False,
                       r=[c_t.b(), Yr.b()], w=[py.b()])
                for fc in range(16):
                    mm(S, py[:], s_t[:, fc * 128:(fc + 1) * 128], Ys[:, fc, :], start=False, stop=(fc == 15),
                       r=[s_t.b(), Ys.b()], w=[py.b()])
                tt(S, v_tm[:, tc, :], x1_tm[:, tc, :], py[:], ALU.mult, r=[x1_tm.b(), py.b()], w=[v_tm.b()])
            else:
                py = p_y[tc % 2]
                for cc in range(4):
                    for fc in range(16):
                        mm(S, py[:, cc * 128:(cc + 1) * 128], Yr[:, fc, cc * 128:(cc + 1) * 128], c_t[:, fc * 128:(fc + 1) * 128],
                           start=(fc == 0), stop=False, r=[c_t.b(), Yr.b()], w=[py.b()])
                    for fc in range(16):
                        mm(S, py[:, cc * 128:(cc + 1) * 128], Ys[:, fc, cc * 128:(cc + 1) * 128], s_t[:, fc * 128:(fc + 1) * 128],
                           start=False, stop=(fc == 15), r=[s_t.b(), Ys.b()], w=[py.b()])
                for cc in range(4):
                    tt(S, ycT[:, cc, tc * 128:(tc + 1) * 128], x2T[:, cc, tc * 128:(tc + 1) * 128], py[:, cc * 128:(cc + 1) * 128],
                       ALU.mult, r=[x2T.b(), py.b()], w=[ycT.b()], eng="dve")
    dma(S, C.yc_d.ap.rearrange("(c p) t -> p c t", p=128), ycT[:], r=[ycT.b()], w=[C.yc_d.b()], q="sp")
    S.barrier()
    S.release()


RW_R, RW_V, RW_KK, RW_G, RW_BONUS = 0, 1, 2, 3, 4
RW_E, RW_B, RW_KD = 5, 7, 9
GN_EPS = 64e-5
RW_STOP = 0


def rw_consts():
    c = {}
    j = np.arange(128)[:, None]
    t = np.arange(128)[None, :]
    MU_s = (t > j).astype(np.float32)
    ML_s = (t < j).astype(np.float32)
    MU_i = (t >= j).astype(np.float32)
    ML_i = (t <= j).astype(np.float32)
    c['c_rw_m4'] = np.ascontiguousarray(np.stack([np.concatenate([MU_s, MU_s, ML_s, ML_s], 1),
                                                  np.concatenate([ML_s, ML_s, MU_s, MU_s], 1)], 0))
    c['c_rw_m3'] = np.ascontiguousarray(np.stack([np.concatenate([MU_s, MU_i, MU_i], 1),
                                                  np.concatenate([ML_s, ML_i, ML_i], 1)], 0))
    c['c_rw_tri'] = np.ascontiguousarray(np.stack([MU_i, ML_i], 0))
    blk = np.zeros((128, 128), np.float32)
    blk[:64, :64] = 1.0
    blk[64:, 64:] = 1.0
    c['c_rw_blk'] = blk
    return c


def phase_rwkv(C, l):
    S = C.S
    P = C.P
    nm = "rw%d_" % l
    T_ = S_LEN
    nti = T_ // NT
    rw = C.rw_d

    def col(dst, src_ap, n):
        S.op("sp", lambda e: e.dma_start(out=dst, in_=src_ap.rearrange("(p o) -> p o", o=1), allow_slow_non_contiguous=True),
             w=[], dma=True)

    blk = sb(S, nm + "blk", [128, 128])
    dma(S, blk[:], C.c_rw_blk[:], w=[blk.b()])
    pin = [sb(S, nm + "pin%d" % i, [128, T_]) for i in range(2)]
    mcol = [sb(S, nm + "mcol%d" % i, [128, 4]) for i in range(2)]
    npiece = [0]

    def shift_piece(row0, nrows, dst, dbuf):
        i = npiece[0] % 2
        npiece[0] += 1
        p_in, mc = pin[i], mcol[i]
        dma(S, p_in[0:nrows, :], C.pm_d[row0:row0 + nrows, :], w=[p_in.b()], q="sp" if i == 0 else "act")
        S.op("sp", lambda e: e.dma_start(out=mc[0:nrows, 0:1], in_=P['rwkv_mu_prev'].ap[l][row0:row0 + nrows].rearrange("(p o) -> p o", o=1),
                                         allow_slow_non_contiguous=True), w=[mc.b()], dma=True)
        S.op("sp", lambda e: e.dma_start(out=mc[0:nrows, 1:2], in_=P['rwkv_mu_next'].ap[l][row0:row0 + nrows].rearrange("(p o) -> p o", o=1),
                                         allow_slow_non_contiguous=True), w=[mc.b()], dma=True)
        tt(S, mc[0:nrows, 2:3], mc[0:nrows, 0:1], mc[0:nrows, 1:2], ALU.add, r=[mc.b()], w=[mc.b()])
        ts(S, mc[0:nrows, 2:3], mc[0:nrows, 2:3], -1.0, 1.0, ALU.mult, ALU.add, r=[mc.b()], w=[mc.b()])
        ts(S, dst[0:nrows, :], p_in[0:nrows, :], mc[0:nrows, 2:3], None, ALU.mult, r=[p_in.b(), mc.b()], w=[dbuf])
        stt(S, dst[0:nrows, 1:T_], p_in[0:nrows, 0:T_ - 1], mc[0:nrows, 0:1], dst[0:nrows, 1:T_], ALU.mult, ALU.add,
            r=[p_in.b(), mc.b(), dbuf], w=[dbuf])
        stt(S, dst[0:nrows, 0:T_ - 1], p_in[0:nrows, 1:T_], mc[0:nrows, 1:2], dst[0:nrows, 0:T_ - 1], ALU.mult, ALU.add,
            r=[p_in.b(), mc.b(), dbuf], w=[dbuf])

    lw = [sb(S, nm + "lw%d" % d, [96, T_]) for d in range(2)]
    la = [sb(S, nm + "la%d" % d, [96, T_]) for d in range(2)]
    lg = [sb(S, nm + "lg%d" % i, [128, T_]) for i in range(2)]
    for d in range(2):
        shift_piece(1536 + 96 * d, 96, lw[d], lw[d].b())
        act(S, lw[d][:], lw[d][:], AF.Tanh, r=[lw[d].b()], w=[lw[d].b()])
        shift_piece(1728 + 96 * d, 96, la[d], la[d].b())
    for i in range(2):
        shift_piece(1920 + 128 * i, 128, lg[i], lg[i].b())
        act(S, lg[i][:], lg[i][:], AF.Sigmoid, r=[lg[i].b()], w=[lg[i].b()])
    w2 = sb(S, nm + "w2", [96, 2, 512])
    a2 = sb(S, nm + "a2", [96, 2, 512])
    g2 = sb(S, nm + "g2", [128, 2, 512])
    dma(S, w2[:], P['rwkv_w2'].ap[l].rearrange("d k c -> k d c"), w=[w2.b()])
    dma(S, a2[:], P['rwkv_a2'].ap[l].rearrange("d k c -> k d c"), w=[a2.b()], q="act")
    dma(S, g2[:], P['rwkv_g2'].ap[l].rearrange("(i p) c -> p i c", p=128), w=[g2.b()])
    pc = sb(S, nm + "pc", [128, 4, 12])
    srcs = [P['rwkv_w0'].ap[l][0], P['rwkv_w0'].ap[l][1], P['rwkv_a0'].ap[l][0], P['rwkv_a0'].ap[l][1], P['rwkv_k_k'].ap[l],
            P['rwkv_k_a'].ap[l], P['rwkv_r_k'].ap[l], P['rwkv_ln_w'].ap[l], P['rwkv_ln_b'].ap[l]]
    for j, s_ap in enumerate(srcs):
        S.op("sp", lambda e, j=j, s_ap=s_ap: e.dma_start(out=pc[:, :, j:j + 1], in_=s_ap.rearrange("(c p o) -> p c o", p=128, o=1),
                                                         allow_slow_non_contiguous=True), w=[pc.b()], dma=True)
    ts(S, pc[:, :, 9:10], pc[:, :, 5:6], -1.0, 1.0, ALU.mult, ALU.add, r=[pc.b()], w=[pc.b()])
    rs_ = sb(S, nm + "rs", [128, T_])
    ks_ = sb(S, nm + "ks", [128, T_])
    vs_ = sb(S, nm + "vs", [128, T_])
    kk_ = sb(S, nm + "kk", [128, T_])
    tl = {k: sb(S, nm + "tl_" + k, [128, NT]) for k in ["sq", "den", "e0", "e1", "a0", "a1", "t0", "kd0", "kd1", "b0", "b1", "kds", "pr", "bo", "g"]}
    pp = [ps(S, nm + "pp%d" % i, [128, NT]) for i in range(6)]
    kq = [0]

    def nps():
        kq[0] += 1
        return pp[kq[0] % 6]

    for cc in range(4):
        shift_piece(cc * 128, 128, rs_, rs_.b())
        shift_piece(512 + cc * 128, 128, ks_, ks_.b())
        shift_piece(1024 + cc * 128, 128, vs_, vs_.b())
        csl = slice(cc * 128, (cc + 1) * 128)
        dma(S, rw.ap[RW_R][csl, :], rs_[:], r=[rs_.b()], w=[rw.b((RW_R, cc))], q="sp")
        dma(S, rw.ap[RW_V][csl, :], vs_[:], r=[vs_.b()], w=[rw.b((RW_V, cc))], q="act")
        for ti in range(nti):
            tsl = slice(ti * NT, (ti + 1) * NT)
            ts(S, kk_[:, tsl], ks_[:, tsl], pc[:, cc, 4:5], None, ALU.mult, r=[ks_.b(), pc.b()], w=[kk_.b()])
            act(S, tl["sq"][:], kk_[:, tsl], AF.Square, r=[kk_.b()], w=[tl["sq"].b()])
            p = nps()
            mm(S, p[:], blk[:], tl["sq"][:], r=[blk.b(), tl["sq"].b()], w=[p.b()])
            act(S, tl["den"][:], p[:], AF.Sqrt, r=[p.b()], w=[tl["den"].b()])
            ts(S, tl["den"][:], tl["den"][:], 1e-12, None, ALU.max, r=[tl["den"].b()], w=[tl["den"].b()])
            S.op("dve", lambda e: e.reciprocal(tl["den"][:], tl["den"][:]), r=[tl["den"].b()], w=[tl["den"].b()])
            tt(S, kk_[:, tsl], kk_[:, tsl], tl["den"][:], ALU.mult, r=[kk_.b(), tl["den"].b()], w=[kk_.b()])
            for d in range(2):
                e_t, a_t, kd_t, b_t = tl["e%d" % d], tl["a%d" % d], tl["kd%d" % d], tl["b%d" % d]
                p = nps()
                mm(S, p[:], w2[:, d, csl], lw[d][:, tsl], r=[w2.b(), lw[d].b()], w=[p.b()])
                act(S, e_t[:], p[:], AF.Sigmoid, bias=pc[:, cc, d:d + 1], r=[p.b(), pc.b()], w=[e_t.b()])
                ts(S, e_t[:], e_t[:], -math.exp(-0.5), None, ALU.mult, r=[e_t.b()], w=[e_t.b()], eng="pool")
                dma(S, rw.ap[RW_E + d][csl, tsl], e_t[:], r=[e_t.b()], w=[rw.b((RW_E + d, cc))], q="sp")
                p = nps()
                mm(S, p[:], a2[:, d, csl], la[d][:, tsl], r=[a2.b(), la[d].b()], w=[p.b()])
                act(S, a_t[:], p[:], AF.Sigmoid, bias=pc[:, cc, 2 + d:3 + d], r=[p.b(), pc.b()], w=[a_t.b()])
                ts(S, tl["t0"][:], a_t[:], pc[:, cc, 5:6], pc[:, cc, 9:10], ALU.mult, ALU.add, r=[a_t.b(), pc.b()], w=[tl["t0"].b()])
                tt(S, kd_t[:], ks_[:, tsl], tl["t0"][:], ALU.mult, r=[ks_.b(), tl["t0"].b()], w=[kd_t.b()])
                dma(S, rw.ap[RW_KD + d][csl, tsl], kd_t[:], r=[kd_t.b()], w=[rw.b((RW_KD + d, cc))], q="act")
                tt(S, b_t[:], a_t[:], kk_[:, tsl], ALU.mult, r=[a_t.b(), kk_.b()], w=[b_t.b()], eng="pool")
                dma(S, rw.ap[RW_B + d][csl, tsl], b_t[:], r=[b_t.b()], w=[rw.b((RW_B + d, cc))], q="sp")
            tt(S, tl["kds"][:], tl["kd0"][:], tl["kd1"][:], ALU.add, r=[tl["kd0"].b(), tl["kd1"].b()], w=[tl["kds"].b()], eng="pool")
            stt(S, tl["pr"][:], rs_[:, tsl], pc[:, cc, 6:7], tl["kds"][:], ALU.mult, ALU.mult, r=[rs_.b(), pc.b(), tl["kds"].b()], w=[tl["pr"].b()])
            p = nps()
            mm(S, p[:], blk[:], tl["pr"][:], r=[blk.b(), tl["pr"].b()], w=[p.b()])
            tt(S, tl["bo"][:], p[:], vs_[:, tsl], ALU.mult, r=[p.b(), vs_.b()], w=[tl["bo"].b()])
            dma(S, rw.ap[RW_BONUS][csl, tsl], tl["bo"][:], r=[tl["bo"].b()], w=[rw.b((RW_BONUS, cc))], q="act")
            p = nps()
            for i in range(2):
                mm(S, p[:], g2[:, i, csl], lg[i][:, tsl], start=(i == 0), stop=(i == 1), r=[g2.b(), lg[i].b()], w=[p.b()])
            cp(S, tl["g"][:], p[:], r=[p.b()], w=[tl["g"].b()], eng="act")
            dma(S, rw.ap[RW_G][csl, tsl], tl["g"][:], r=[tl["g"].b()], w=[rw.b((RW_G, cc))], q="sp")
        dma(S, rw.ap[RW_KK][csl, :], kk_[:], r=[kk_.b()], w=[rw.b((RW_KK, cc))], q="sp")
    S.barrier()
    S.release()
    if getattr(C, "rw_stop", 0) == 1:
        return
    CH = 128
    NCH = T_ // CH
    m4 = [sb(S, nm + "m4_%d" % d, [128, 512]) for d in range(2)]
    m3 = [sb(S, nm + "m3_%d" % d, [128, 384]) for d in range(2)]
    tri = [sb(S, nm + "tri%d" % d, [128, 128]) for d in range(2)]
    for d in range(2):
        dma(S, m4[d][:], C.c_rw_m4.ap[d], w=[m4[d].b()])
        dma(S, m3[d][:], C.c_rw_m3.ap[d], w=[m3[d].b()], q="act")
        dma(S, tri[d][:], C.c_rw_tri.ap[d], w=[tri[d].b()])
    hmk = sb(S, nm + "hmk", [128, 128])
    dma(S, hmk[:], C.c_rw_blk[:], w=[hmk.b()])
    Yacc = sb(S, nm + "Yacc", [128, NCH, 512])
    ST = {}
    for d in range(2):
        for cc in range(4):
            ST[(d, cc)] = [sb(S, nm + "ST%d%d%d" % (d, cc, i), [64, 2, 64]) for i in range(2)]
            mset(S, ST[(d, cc)][0][:], 0.0, w=[ST[(d, cc)][0].b()])
    NSLOT = 4
    slots = []
    for s_ in range(NSLOT):
        sl = {}
        sn = nm + "s%d_" % s_
        sl["in"] = sb(S, sn + "in", [128, 6, CH])
        sl["etm"] = sb(S, sn + "etm", [128, 128])
        sl["cx"] = sb(S, sn + "cx", [128, 128])
        sl["gp"] = sb(S, sn + "gp", [128, 128])
        sl["gn"] = sb(S, sn + "gn", [128, 128])
        sl["gx"] = sb(S, sn + "gx", [128, 128])
        sl["cm"] = sb(S, sn + "cm", [128, 6, 128])
        sl["tm"] = sb(S, sn + "tm", [128, 4, 128])
        sl["cmm"] = sb(S, sn + "cmm", [128, 3, 2, 128])
        sl["XX"] = [sb(S, sn + "XX%d" % i, [128, 4, 128]) for i in range(2)]
        sl["TT"] = [sb(S, sn + "TT%d" % i, [128, 2, 128]) for i in range(2)]
        sl["L3"] = sb(S, sn + "L3", [128, 2, 384])
        sl["W1A"] = sb(S, sn + "W1A", [128, 2, 128])
        sl["QP"] = sb(S, sn + "QP", [128, 2, 128])
        sl["GT"] = sb(S, sn + "GT", [64, 2, 64])
        sl["RyT"] = sb(S, sn + "RyT", [64, 2, 128])
        sl["Dg"] = sb(S, sn + "Dg", [128, 128])
        slots.append(sl)
    bank = [ps(S, nm + "bk%d" % i, [128, NT]) for i in range(8)]
    kb = [0]

    def nb():
        kb[0] += 1
        return bank[kb[0] % 8]

    touched = set()
    par = {}
    for step in range(NCH):
        for d in range(2):
            n = step if d == 0 else NCH - 1 - step
            t0 = n * CH
            units = [(cc, slots[cc]) for cc in range(4)]
            for cc, sl in units:
                csl = slice(cc * 128, (cc + 1) * 128)
                srcs = [RW_R, RW_KK, RW_V, RW_E + d, RW_B + d, RW_KD + d]
                for j, k_ in enumerate(srcs):
                    dma(S, sl["in"][:, j, :], rw.ap[k_][csl, t0:t0 + CH], r=[rw.b((k_, cc))], w=[sl["in"].b()],
                        q="sp" if j % 2 == 0 else "act")
            for cc, sl in units:
                inb = sl["in"]
                p = nb()
                tr(S, p[:, 0:128], inb[:, 3, :], C.ident[:], r=[inb.b(), C.ident.b()], w=[p.b()])
                cp(S, sl["etm"][:], p[:, 0:128], r=[p.b()], w=[sl["etm"].b()], eng="act")
                p2 = nb()
                mm(S, p2[:, 0:128], sl["etm"][:], tri[d][:], r=[sl["etm"].b(), tri[d].b()], w=[p2.b()])
                act(S, sl["gp"][:], p2[:, 0:128], AF.Exp, r=[p2.b()], w=[sl["gp"].b()])
                act(S, sl["gn"][:], p2[:, 0:128], AF.Exp, scale=-1.0, r=[p2.b()], w=[sl["gn"].b()])
                tt(S, sl["cx"][:], p2[:, 0:128], inb[:, 3, :], ALU.subtract, r=[p2.b(), inb.b()], w=[sl["cx"].b()])
                act(S, sl["gx"][:], sl["cx"][:], AF.Exp, r=[sl["cx"].b()], w=[sl["gx"].b()])
                cm = sl["cm"]
                stt(S, cm[:, 0, :], inb[:, 1, :], -1.0, sl["gx"][:], ALU.mult, ALU.mult, r=[inb.b(), sl["gx"].b()], w=[cm.b(0)])
                tt(S, cm[:, 1, :], inb[:, 4, :], sl["gn"][:], ALU.mult, r=[inb.b(), sl["gn"].b()], w=[cm.b(1)], eng="pool")
                tt(S, cm[:, 2, :], inb[:, 5, :], sl["gn"][:], ALU.mult, r=[inb.b(), sl["gn"].b()], w=[cm.b(2)])
                tt(S, cm[:, 3, :], inb[:, 0, :], sl["gp"][:], ALU.mult, r=[inb.b(), sl["gp"].b()], w=[cm.b(3)], eng="pool")
                gcol_ap = sl["gp"][:, 127:128] if d == 0 else sl["gp"][:, 0:1]
                ts(S, cm[:, 4, :], cm[:, 1, :], gcol_ap, None, ALU.mult, r=[cm.b(1), sl["gp"].b()], w=[cm.b(4)])
                ts(S, cm[:, 5, :], cm[:, 2, :], gcol_ap, None, ALU.mult, r=[cm.b(2), sl["gp"].b()], w=[cm.b(5)])
                ts(S, sl["Dg"][:], C.ident[:], gcol_ap, None, ALU.mult, r=[C.ident.b(), sl["gp"].b()], w=[sl["Dg"].b()])
                cmm = sl["cmm"]
                for j in range(3):
                    for hp in range(2):
                        ts(S, cmm[:, j, hp, :], cm[:, j, :], hmk[:, 64 * hp:64 * hp + 1], None, ALU.mult, r=[cm.b(j), hmk.b()],
                           w=[cmm.b((j, hp))], eng="pool" if (j + hp) % 2 == 0 else "dve")
                p3 = nb()
                for j, (src, sbuf_) in enumerate([(cm[:, 0, :], cm.b(0)), (cm[:, 4, :], cm.b(4)), (cm[:, 5, :], cm.b(5)),
                                                  (inb[:, 2, :], inb.b())]):
                    tr(S, p3[:, j * 128:(j + 1) * 128], src, C.ident[:], r=[sbuf_, C.ident.b()], w=[p3.b()])
                cp(S, sl["tm"][:].rearrange("p a b -> p (a b)"), p3[:], r=[p3.b()], w=[sl["tm"].b()], eng="act")
                p4 = nb()
                for hp in range(2):
                    hs = slice(64 * hp, 64 * hp + 64)
                    mm(S, p4[:, hp * 128:(hp + 1) * 128], cmm[:, 1, hp, :], cm[:, 0, :], r=[cmm.b((1, hp)), cm.b(0)], w=[p4.b()])
                    mm(S, p4[:, (2 + hp) * 128:(3 + hp) * 128], cmm[:, 0, hp, :], cm[:, 1, :], r=[cmm.b((0, hp)), cm.b(1)], w=[p4.b()])
                XX0 = sl["XX"][0]
                tt(S, XX0[:, 2:4, :].rearrange("p a b -> p (a b)"), p4[:, 0:256], m4[d][:, 0:256], ALU.mult, r=[p4.b(), m4[d].b()], w=[XX0.b()])
                tt(S, XX0[:, 0:2, :].rearrange("p a b -> p (a b)"), p4[:, 256:512], m4[d][:, 256:512], ALU.mult, r=[p4.b(), m4[d].b()], w=[XX0.b()])
                TT0 = sl["TT"][0]
                for hp in range(2):
                    tt(S, TT0[:, hp, :], XX0[:, 2 + hp, :], C.ident[:], ALU.add, r=[XX0.b(), C.ident.b()], w=[TT0.b()], eng="pool")
                for hp in range(2):
                    hs = slice(64 * hp, 64 * hp + 64)
                    p5 = nb()
                    mm(S, p5[:, 0:128], cmm[:, 2, hp, :], cm[:, 0, :], r=[cmm.b((2, hp)), cm.b(0)], w=[p5.b()])
                    mm(S, p5[:, 128:256], cmm[:, 1, hp, :], cm[:, 3, :], r=[cmm.b((1, hp)), cm.b(3)], w=[p5.b()])
                    mm(S, p5[:, 256:384], cmm[:, 2, hp, :], cm[:, 3, :], r=[cmm.b((2, hp)), cm.b(3)], w=[p5.b()])
                    tt(S, sl["L3"][:, hp, :], p5[:, 0:384], m3[d][:], ALU.mult, r=[p5.b(), m3[d].b()], w=[sl["L3"].b(hp)])
            if C.rw_stop == 2:
                return
            for m_ in range(1, 7):
                cur, nxt = (m_ - 1) % 2, m_ % 2
                for cc, sl in units:
                    Xc, Xn = sl["XX"][cur], sl["XX"][nxt]
                    Tc, Tn = sl["TT"][cur], sl["TT"][nxt]
                    p = nb()
                    for hp in range(2):
                        mm(S, p[:, hp * 128:(hp + 1) * 128], Xc[:, 2 + hp, :], Xc[:, hp, :], r=[Xc.b()], w=[p.b()])
                        if m_ < 6:
                            mm(S, p[:, (2 + hp) * 128:(3 + hp) * 128], Xc[:, hp, :], Xc[:, 2 + hp, :], r=[Xc.b()], w=[p.b()])
                    if m_ < 6:
                        cp(S, Xn[:].rearrange("p a b -> p (a b)"), p[:], r=[p.b()], w=[Xn.b()], eng="act")
                    else:
                        cp(S, Xn[:, 0:2, :].rearrange("p a b -> p (a b)"), p[:, 0:256], r=[p.b()], w=[Xn.b()], eng="act")
                    p2 = nb()
                    for hp in range(2):
                        mm(S, p2[:, hp * 128:(hp + 1) * 128], Xn[:, hp, :], Tc[:, hp, :], r=[Xn.b(), Tc.b()], w=[p2.b()])
                    tt(S, Tn[:].rearrange("p a b -> p (a b)"), Tc[:].rearrange("p a b -> p (a b)"), p2[:, 0:256], ALU.add,
                       r=[Tc.b(), p2.b()], w=[Tn.b()])
            if C.rw_stop == 3:
                return
            for cc, sl in units:
                TTf = sl["TT"][0]
                tm, L3, W1A, QP = sl["tm"], sl["L3"], sl["W1A"], sl["QP"]
                cm = sl["cm"]
                p = nb()
                for hp in range(2):
                    fs = slice(64 * hp, 64 * hp + 64)
                    mm(S, p[:, hp * 64:(hp + 1) * 64], L3[:, hp, 0:128], tm[:, 3, fs], r=[L3.b(hp), tm.b()], w=[p.b()])
                for hp in range(2):
                    fs = slice(64 * hp, 64 * hp + 64)
                    cp(S, W1A[:, hp, 0:64], p[:, hp * 64:(hp + 1) * 64], r=[p.b()], w=[W1A.b()], eng="act")
                    cp(S, W1A[:, hp, 64:128], tm[:, 0, fs], r=[tm.b()], w=[W1A.b()], eng="pool")
                p2 = nb()
                for hp in range(2):
                    mm(S, p2[:, hp * 128:(hp + 1) * 128], TTf[:, hp, :], W1A[:, hp, :], r=[TTf.b(), W1A.b()], w=[p2.b()])
                cp(S, QP[:].rearrange("p a b -> p (a b)"), p2[:, 0:256], r=[p2.b()], w=[QP.b()], eng="act")
                p3 = nb()
                for hp in range(2):
                    fs = slice(64 * hp, 64 * hp + 64)
                    mm(S, p3[0:64, hp * 64:(hp + 1) * 64], QP[:, hp, 64:128], tm[:, 1, fs], start=True, stop=False,
                       r=[QP.b(), tm.b()], w=[p3.b()])
                    mm(S, p3[0:64, hp * 64:(hp + 1) * 64], C.ident[:, fs], sl["Dg"][:, fs], start=False, stop=True,
                       r=[C.ident.b(), sl["Dg"].b()], w=[p3.b()])
                    mm(S, p3[0:64, 128 + hp * 128:256 + hp * 128], QP[:, hp, 64:128], L3[:, hp, 128:256], start=True, stop=False,
                       r=[QP.b(), L3.b(hp)], w=[p3.b()])
                    mm(S, p3[0:64, 128 + hp * 128:256 + hp * 128], C.ident[:, fs], cm[:, 3, :], start=False, stop=True,
                       r=[C.ident.b(), cm.b(3)], w=[p3.b()])
                cp(S, sl["GT"][:].rearrange("p a b -> p (a b)"), p3[0:64, 0:128], r=[p3.b()], w=[sl["GT"].b()], eng="act")
                cp(S, sl["RyT"][:].rearrange("p a b -> p (a b)"), p3[0:64, 128:384], r=[p3.b()], w=[sl["RyT"].b()])
            if C.rw_stop == 4:
                return
            for cc, sl in units:
                tm, L3, QP = sl["tm"], sl["L3"], sl["QP"]
                k_ = par.get((d, cc), 0)
                S_old, S_new = ST[(d, cc)][k_], ST[(d, cc)][1 - k_]
                par[(d, cc)] = 1 - k_
                py = nb()
                for hp in range(2):
                    fs = slice(64 * hp, 64 * hp + 64)
                    mm(S, py[:, fs], L3[:, hp, 128:256], QP[:, hp, 0:64], start=True, stop=False, r=[L3.b(hp), QP.b()], w=[py.b()])
                    mm(S, py[:, fs], L3[:, hp, 256:384], tm[:, 3, fs], start=False, stop=False, r=[L3.b(hp), tm.b()], w=[py.b()])
                    mm(S, py[:, fs], sl["RyT"][:, hp, :], S_old[:, hp, :], start=False, stop=True, r=[sl["RyT"].b(), S_old.b()], w=[py.b()])
                if (n, cc) not in touched:
                    touched.add((n, cc))
                    cp(S, Yacc[:, n, cc * 128:(cc + 1) * 128], py[:, 0:128], r=[py.b()], w=[Yacc.b((n, cc))])
                else:
                    tt(S, Yacc[:, n, cc * 128:(cc + 1) * 128], Yacc[:, n, cc * 128:(cc + 1) * 128], py[:, 0:128], ALU.add,
                       r=[py.b(), Yacc.b((n, cc))], w=[Yacc.b((n, cc))])
                pz = nb()
                for hp in range(2):
                    fs = slice(64 * hp, 64 * hp + 64)
                    mm(S, pz[0:64, fs], sl["GT"][:, hp, :], S_old[:, hp, :], start=True, stop=False, r=[sl["GT"].b(), S_old.b()], w=[pz.b()])
                    mm(S, pz[0:64, fs], tm[:, 1, fs], QP[:, hp, 0:64], start=False, stop=False, r=[tm.b(), QP.b()], w=[pz.b()])
                    mm(S, pz[0:64, fs], tm[:, 2, fs], tm[:, 3, fs], start=False, stop=True, r=[tm.b()], w=[pz.b()])
                cp(S, S_new[:].rearrange("p a b -> p (a b)"), pz[0:64, 0:128], r=[pz.b()], w=[S_new.b()], eng="act")
            if C.rw_stop == 5:
                return
    if C.rw_stop == 6:
        return
    pcs = sb(S, nm + "pcs", [128, 4, 4])
    for j, k_ in enumerate(['rwkv_ln_w', 'rwkv_ln_b']):
        S.op("sp", lambda e, j=j, k_=k_: e.dma_start(out=pcs[:, :, j:j + 1], in_=P[k_].ap[l].rearrange("(c p o) -> p c o", p=128, o=1),
                                                     allow_slow_non_contiguous=True), w=[pcs.b()], dma=True)
    blk2 = sb(S, nm + "blk2", [128, 128])
    dma(S, blk2[:], C.c_rw_blk[:], w=[blk2.b()])
    gne = sb(S, nm + "gne", [128, 1])
    mset(S, gne[:], GN_EPS, w=[gne.b()])
    ycm = sb(S, nm + "ycm", [128, NT])
    yc2 = sb(S, nm + "yc2", [128, NT])
    sq2 = sb(S, nm + "sq2", [128, NT])
    rstd2 = sb(S, nm + "rstd2", [128, NT])
    bo_t = [sb(S, nm + "bo%d" % i, [128, NT]) for i in range(2)]
    g_t = [sb(S, nm + "gt%d" % i, [128, NT]) for i in range(2)]
    yo = [sb(S, nm + "yo%d" % i, [128, NT], BF16) for i in range(2)]
    k3 = 0
    for cc in range(4):
        csl = slice(cc * 128, (cc + 1) * 128)
        for ti in range(nti):
            tsl = slice(ti * NT, (ti + 1) * NT)
            b_t, gg, y_o = bo_t[k3 % 2], g_t[k3 % 2], yo[k3 % 2]
            k3 += 1
            dma(S, b_t[:], rw.ap[RW_BONUS][csl, tsl], r=[rw.b((RW_BONUS, cc))], w=[b_t.b()], q="sp")
            dma(S, gg[:], rw.ap[RW_G][csl, tsl], r=[rw.b((RW_G, cc))], w=[gg.b()], q="act")
            p = nb()
            for j in range(4):
                n = ti * 4 + j
                tr(S, p[:, j * 128:(j + 1) * 128], Yacc[:, n, csl], C.ident[:], r=[Yacc.b((n, cc)), C.ident.b()], w=[p.b()])
            cp(S, ycm[:], p[:], r=[p.b()], w=[ycm.b()], eng="act")
            p2 = nb()
            mm(S, p2[:], blk2[:], ycm[:], r=[blk2.b(), ycm.b()], w=[p2.b()])
            stt(S, yc2[:], p2[:], -1.0 / 64.0, ycm[:], ALU.mult, ALU.add, r=[p2.b(), ycm.b()], w=[yc2.b()])
            act(S, sq2[:], yc2[:], AF.Square, r=[yc2.b()], w=[sq2.b()])
            p3 = nb()
            mm(S, p3[:], blk2[:], sq2[:], r=[blk2.b(), sq2.b()], w=[p3.b()])
            rsqrt(S, rstd2[:], p3[:], 1.0 / 64.0, gne[:, 0:1], r=[p3.b(), gne.b()], w=[rstd2.b()])
            tt(S, yc2[:], yc2[:], rstd2[:], ALU.mult, r=[yc2.b(), rstd2.b()], w=[yc2.b()])
            ts(S, yc2[:], yc2[:], pcs[:, cc, 0:1], pcs[:, cc, 1:2], ALU.mult, ALU.add, r=[yc2.b(), pcs.b()], w=[yc2.b()])
            tt(S, yc2[:], yc2[:], b_t[:], ALU.add, r=[yc2.b(), b_t.b()], w=[yc2.b()], eng="pool")
            tt(S, y_o[:], yc2[:], gg[:], ALU.mult, r=[yc2.b(), gg.b()], w=[y_o.b()])
            dma(S, C.ya_d[csl, tsl], y_o[:], r=[y_o.b()], w=[C.ya_d.b((cc, ti))], q="sp")
    S.barrier()
    S.release()
```

```python
import math
from contextlib import ExitStack

import numpy as np
import ml_dtypes
import concourse.bass as bass
import concourse.mybir as mybir
from concourse.bass_utils import run_bass_kernel_spmd

F32 = mybir.dt.float32
BF16 = mybir.dt.bfloat16
ALU = mybir.AluOpType
AF = mybir.ActivationFunctionType
AX = mybir.AxisListType

D = 2048
S_LEN = 2048
DEPTH = 2
FFN = 5632
EPS = 1e-6
NT = 512
RW_COLS, MLA_COLS, HY_COLS, GQA_COLS = 2176, 576, 1536, 1024
OFF_RW, OFF_MLA, OFF_HY, OFF_GQA, OFF_GATE = 0, 2176, 2752, 4288, 5312
IN_COLS = 13504
PM_ROWS = 5056


class Buf:
    __slots__ = ("w", "r", "name", "excl")

    def __init__(self, name="", excl=False):
        self.w = None
        self.r = []
        self.name = name
        self.excl = excl


class Rec:
    __slots__ = ("eng", "fn", "deps", "dma", "semkey", "val", "needs_inc", "idx")


class Sched:
    ENG = ("pe", "act", "dve", "pool", "sp")
    BLK = {"pe": "tensor", "act": "scalar", "dve": "vector", "pool": "gpsimd", "sp": "sync"}
    CAP = 8000
    NRING = 8

    def __init__(self, nc):
        self.nc = nc
        self.streams = {e: [] for e in self.ENG}
        self.all = []
        self.ndma = {e: 0 for e in self.ENG}
        self.ring_last = {}
        self.es = ExitStack()

    def sbuf(self, name, shape, dt):
        return self.es.enter_context(self.nc.sbuf_tensor(name, list(shape), dt))

    def psum(self, name, shape, dt=F32):
        return self.es.enter_context(self.nc.psum_tensor(name, list(shape), dt))

    def release(self):
        self.es.close()
        self.es = ExitStack()

    def op(self, eng, fn, r=(), w=(), dma=False):
        rec = Rec()
        rec.eng, rec.fn, rec.dma = eng, fn, dma
        rec.needs_inc = False
        rec.val = None
        rec.semkey = None
        rec.idx = len(self.all)
        deps = {}
        for b in r:
            if b.w is not None:
                deps[id(b.w)] = b.w
            if b.excl:
                for rr in b.r:
                    if rr.eng != eng:
                        deps[id(rr)] = rr
        for b in w:
            if b.w is not None:
                lw = b.w
                if not (eng == "pe" and lw.eng == "pe" and not lw.dma and not dma):
                    deps[id(lw)] = lw
            for rr in b.r:
                if rr.eng != eng or rr.dma or dma:
                    deps[id(rr)] = rr
        if dma:
            i = self.ndma[eng]
            self.ndma[eng] += 1
            slot = i % self.NRING
            rec.semkey = ("dma", eng, slot)
            rec.val = 16 * (i // self.NRING + 1)
            prev = self.ring_last.get((eng, slot))
            if prev is not None:
                deps[id(prev)] = prev
            self.ring_last[(eng, slot)] = rec
        for d in deps.values():
            d.needs_inc = True
        rec.deps = list(deps.values())
        for b in r:
            if not dma:
                b.r = [x for x in b.r if x.dma or x.eng != eng]
            b.r.append(rec)
        for b in w:
            b.w = rec
            b.r = []
        self.streams[eng].append(rec)
        self.all.append(rec)
        return rec

    def barrier(self):
        lasts = []
        for e in self.ENG:
            for rec in reversed(self.streams[e]):
                if rec.fn is not None:
                    lasts.append(rec)
                    break
        for (e, slot), rec in self.ring_last.items():
            lasts.append(rec)
        fence = Buf("fence")
        for e in self.ENG:
            rec = Rec()
            rec.eng, rec.fn, rec.dma = e, None, False
            rec.needs_inc = False
            rec.val = None
            rec.semkey = None
            rec.idx = len(self.all)
            rec.deps = [d for d in lasts]
            for d in lasts:
                d.needs_inc = True
            self.streams[e].append(rec)
            self.all.append(rec)

    def finalize(self):
        nc = self.nc
        cnt = {e: 0 for e in self.ENG}
        for rec in self.all:
            if rec.dma or not rec.needs_inc or rec.fn is None:
                continue
            c = cnt[rec.eng]
            rec.semkey = ("eng", rec.eng, c // self.CAP)
            rec.val = c % self.CAP + 1
            cnt[rec.eng] = c + 1
        keys = set()
        for rec in self.all:
            if rec.semkey is not None:
                keys.add(rec.semkey)
        with ExitStack() as es:
            sems = {}
            for k in sorted(keys):
                sems[k] = es.enter_context(nc.semaphore("s_%s_%s_%d" % k))
            block = es.enter_context(nc.Block())
            for e in self.ENG:
                stream = self.streams[e]

                def body(eng, stream=stream):
                    waited = {}
                    for rec in stream:
                        for d in rec.deps:
                            if d.semkey is None:
                                continue
                            if waited.get(d.semkey, 0) >= d.val:
                                continue
                            eng.wait_ge(sems[d.semkey], d.val)
                            waited[d.semkey] = d.val
                        if rec.fn is None:
                            continue
                        ins = rec.fn(eng)
                        if rec.dma:
                            ins.then_inc(sems[rec.semkey], 16)
                        elif rec.needs_inc:
                            ins.then_inc(sems[rec.semkey], 1)

                getattr(block, self.BLK[e])(body)
        self.es.close()


class T:
    def __init__(self, h, excl=False):
        self.h = h
        self.bufs = {}
        self.excl = excl

    def __getitem__(self, idx):
        return self.h[idx]

    def b(self, key=0):
        bb = self.bufs.get(key)
        if bb is None:
            bb = self.bufs[key] = Buf(excl=self.excl)
        return bb


def sb(S, name, shape, dt=F32):
    return T(S.sbuf(name, shape, dt))


def ps(S, name, shape, dt=F32):
    return T(S.psum(name, shape, dt), excl=True)


class DT:
    def __init__(self, nc, name, shape, dt, kind="Internal"):
        self.t = nc.dram_tensor(name, list(shape), dt, kind=kind)
        self.ap = self.t.ap()
        self.bufs = {}

    def __getitem__(self, idx):
        return self.ap[idx]

    def b(self, key=0):
        bb = self.bufs.get(key)
        if bb is None:
            bb = self.bufs[key] = Buf()
        return bb


def mm(S, out, lhsT, rhs, start=True, stop=True, r=(), w=()):
    return S.op("pe", lambda e: e.matmul(out, lhsT, rhs, start=start, stop=stop), r=r, w=w)


def tr(S, out, in_, ident, r=(), w=()):
    return S.op("pe", lambda e: e.transpose(out, in_, ident), r=r, w=w)


def act(S, out, in_, func, bias=None, scale=None, accum_out=None, r=(), w=(), eng="act"):
    kw = {}
    if bias is not None:
        kw["bias"] = bias
    if scale is not None:
        kw["scale"] = scale
    if accum_out is not None:
        kw["accum_out"] = accum_out
    return S.op(eng, lambda e: e.activation(out, in_, func, **kw), r=r, w=w)


def tt(S, out, in0, in1, op, r=(), w=(), eng="dve"):
    return S.op(eng, lambda e: e.tensor_tensor(out, in0, in1, op), r=r, w=w)


def ts(S, out, in0, s1, s2, op0, op1=None, r=(), w=(), eng="dve", accum_out=None):
    if op1 is None:
        return S.op(eng, lambda e: e.tensor_single_scalar(out, in0, s1, op0), r=r, w=w)
    if accum_out is not None:
        return S.op(eng, lambda e: e.tensor_scalar(out, in0, s1, s2, op0, op1, accum_out), r=r, w=w)
    return S.op(eng, lambda e: e.tensor_scalar(out, in0, s1, s2, op0, op1), r=r, w=w)


def stt(S, out, in0, scalar, in1, op0, op1, r=(), w=(), eng="dve"):
    return S.op(eng, lambda e: e.scalar_tensor_tensor(out, in0, scalar, in1, op0, op1), r=r, w=w)


def rsqrt(S, out, in_, scale, bias, r=(), w=()):
    S.op("act", lambda e: e.activation(out, in_, AF.Sqrt, bias=bias, scale=scale), r=r, w=w)
    S.op("dve", lambda e: e.reciprocal(out, out), r=w, w=w)


def cp(S, out, in_, r=(), w=(), eng="dve"):
    if eng == "act":
        return S.op(eng, lambda e: e.copy(out, in_), r=r, w=w)
    return S.op(eng, lambda e: e.tensor_copy(out, in_), r=r, w=w)


def mset(S, ap, val, w=(), eng="dve"):
    return S.op(eng, lambda e: e.memset(ap, val), w=w)


def dma(S, out, in_, r=(), w=(), q="sp"):
    return S.op(q, lambda e: e.dma_start(out=out, in_=in_), r=r, w=w, dma=True)


class Ctx:
    pass


def load_consts(C):
    S = C.S
    nc = S.nc
    C.cst = ExitStack()
    C.ident = T(C.cst.enter_context(nc.sbuf_tensor("ident", [128, 128], F32)))
    C.identb = T(C.cst.enter_context(nc.sbuf_tensor("identb", [128, 128], BF16)))
    C.ones = T(C.cst.enter_context(nc.sbuf_tensor("ones", [128, 128], F32)))
    dma(S, C.ident[:], C.d_ident[:], w=[C.ident.b()])
    mset(S, C.ones[:], 1.0, w=[C.ones.b()])
    C.epsc = T(C.cst.enter_context(nc.sbuf_tensor("epsc", [128, 4], F32)))
    mset(S, C.epsc[:, 0:1], EPS, w=[C.epsc.b()])
    mset(S, C.epsc[:, 1:2], 0.0, w=[C.epsc.b()])
    cp(S, C.identb[:], C.ident[:], r=[C.ident.b()], w=[C.identb.b()])


def phase_load_x(C):
    S = C.S
    xt = [sb(S, "lx_xt%d" % i, [128, 4, D]) for i in range(2)]
    st = [sb(S, "lx_st%d" % i, [128, NT]) for i in range(3)]
    pp = [ps(S, "lx_ps%d" % i, [128, NT]) for i in range(3)]
    k = 0
    for tg in range(S_LEN // NT):
        x_t = xt[tg % 2]
        for j in range(4):
            t0 = tg * NT + j * 128
            dma(S, x_t[:, j, :], C.x[t0:t0 + 128, :], w=[x_t.b(j)], q="sp" if j % 2 == 0 else "act")
        for dc in range(D // 128):
            p = pp[k % 3]
            s = st[k % 3]
            for j in range(4):
                tr(S, p[:, j * 128:(j + 1) * 128], x_t[:, j, dc * 128:(dc + 1) * 128], C.ident[:],
                   r=[x_t.b(j), C.ident.b()], w=[p.b()])
            cp(S, s[:], p[:], r=[p.b()], w=[s.b()], eng="dve" if k % 2 == 0 else "act")
            dma(S, C.xres[dc * 128:(dc + 1) * 128, tg * NT:(tg + 1) * NT], s[:], r=[s.b()],
                w=[C.xres.b((dc, tg // 2))], q="sp")
            k += 1
    S.barrier()
    S.release()


def phase_final_norm(C):
    S = C.S
    gb = sb(S, "fn_g", [128, D])
    dma(S, gb[:], C.final_norm.ap.partition_broadcast(128), w=[gb.b()])
    xin = [sb(S, "fn_xin%d" % i, [128, 16, 128]) for i in range(2)]
    xt = [sb(S, "fn_xt%d" % i, [128, D]) for i in range(2)]
    sq = sb(S, "fn_sq", [128, D])
    ot = [sb(S, "fn_ot%d" % i, [128, D]) for i in range(2)]
    ss = [sb(S, "fn_ss%d" % i, [128, 1]) for i in range(2)]
    pp = [ps(S, "fn_ps%d" % i, [128, NT]) for i in range(4)]
    for tc in range(S_LEN // 128):
        xi = xin[tc % 2]
        x_t = xt[tc % 2]
        o_t = ot[tc % 2]
        s_ = ss[tc % 2]
        dma(S, xi[:], C.xres.ap[:, tc * 128:(tc + 1) * 128].rearrange("(c p) t -> p c t", p=128),
            r=[C.xres.b((dc, tc // 8)) for dc in range(16)], w=[xi.b()], q="sp" if tc % 2 == 0 else "act")
        for g in range(4):
            p = pp[g]
            for j in range(4):
                dc = g * 4 + j
                tr(S, p[:, j * 128:(j + 1) * 128], xi[:, dc, :], C.ident[:], r=[xi.b(), C.ident.b()], w=[p.b()])
            cp(S, x_t[:, g * NT:(g + 1) * NT], p[:], r=[p.b()], w=[x_t.b(g)], eng="dve" if g % 2 == 0 else "act")
        act(S, sq[:], x_t[:], AF.Square, accum_out=s_[:], r=[x_t.b(g) for g in range(4)], w=[sq.b(), s_.b()])
        rsqrt(S, s_[:], s_[:], 1.0 / D, C.epsc[:, 0:1], r=[s_.b(), C.epsc.b()], w=[s_.b()])
        stt(S, o_t[:], x_t[:], s_[:, 0:1], gb[:], ALU.mult, ALU.mult,
            r=[x_t.b(g) for g in range(4)] + [s_.b(), gb.b()], w=[o_t.b()])
        dma(S, C.out[tc * 128:(tc + 1) * 128, :], o_t[:], r=[o_t.b()], w=[C.out.b(tc)], q="sp")
    S.barrier()
    S.release()


def phase_ffn(C, l, which):
    S = C.S
    nm = "f%d%d_" % (l, which)
    g_d = C.ffn_norm[which][l]
    wg_d, wu_d, wd_d = C.ffn_wg[which].ap[l], C.ffn_wu[which].ap[l], C.ffn_wd[which].ap[l]
    TT = 1024
    G = 2
    NG = FFN // (128 * G)
    gcol = sb(S, nm + "gcol", [128, 16])
    S.op("sp", lambda e: e.dma_start(out=gcol[:], in_=g_d.rearrange("(c p) -> p c", p=128),
                                     allow_slow_non_contiguous=True), w=[gcol.b()], dma=True)
    xa = sb(S, nm + "xa", [128, 16, TT])
    hT = sb(S, nm + "hT", [128, 16, TT], BF16)
    rstd = sb(S, nm + "rstd", [128, TT])
    sq = [sb(S, nm + "sq%d" % i, [128, NT]) for i in range(2)]
    wg = [sb(S, nm + "wg%d" % i, [128, 16, 128 * G], BF16) for i in range(2)]
    wu = [sb(S, nm + "wu%d" % i, [128, 16, 128 * G], BF16) for i in range(2)]
    wd = [sb(S, nm + "wd%d" % i, [128, G, D], BF16) for i in range(2)]
    aT = [sb(S, nm + "aT%d" % i, [128, G, TT], BF16) for i in range(2)]
    sg = [sb(S, nm + "sg%d" % i, [128, NT]) for i in range(2)]
    p_g = [ps(S, nm + "pg%d" % i, [128, NT]) for i in range(2)]
    p_u = [ps(S, nm + "pu%d" % i, [128, NT]) for i in range(2)]
    p_d = [ps(S, nm + "pd%d" % i, [128, NT]) for i in range(2)]
    p_s = ps(S, nm + "pss", [128, NT])
    NSUB = TT // NT
    it = 0
    for tt_i in range(S_LEN // TT):
        t0 = tt_i * TT
        for c in range(16):
            dma(S, xa[:, c, :], C.xres[c * 128:(c + 1) * 128, t0:t0 + TT], r=[C.xres.b((c, tt_i))],
                w=[xa.b(c)], q="sp" if c % 2 == 0 else "act")
        for s_i in range(NSUB):
            for c in range(16):
                q_ = sq[c % 2]
                act(S, q_[:], xa[:, c, s_i * NT:(s_i + 1) * NT], AF.Square, r=[xa.b(c)], w=[q_.b()])
                mm(S, p_s[:], C.ones[:], q_[:], start=(c == 0), stop=(c == 15), r=[C.ones.b(), q_.b()], w=[p_s.b()])
            rsqrt(S, rstd[:, s_i * NT:(s_i + 1) * NT], p_s[:], 1.0 / D, C.epsc[:, 0:1], r=[p_s.b(), C.epsc.b()],
                  w=[rstd.b(s_i)])
        for c in range(16):
            stt(S, hT[:, c, :], xa[:, c, :], gcol[:, c:c + 1], rstd[:], ALU.mult, ALU.mult,
                r=[xa.b(c), gcol.b()] + [rstd.b(i) for i in range(NSUB)], w=[hT.b(c)])
        hT_r = [hT.b(c) for c in range(16)]
        for fg in range(NG):
            wg_t, wu_t, wd_t, a_t = wg[it % 2], wu[it % 2], wd[it % 2], aT[it % 2]
            it += 1
            f0 = fg * 128 * G
            dma(S, wg_t[:], wg_d[:, f0:f0 + 128 * G].rearrange("(c p) f -> p c f", p=128), w=[wg_t.b()], q="pool")
            dma(S, wu_t[:], wu_d[:, f0:f0 + 128 * G].rearrange("(c p) f -> p c f", p=128), w=[wu_t.b()], q="pool")
            dma(S, wd_t[:], wd_d[f0:f0 + 128 * G, :].rearrange("(c p) d -> p c d", p=128), w=[wd_t.b()], q="pool")
            k = 0
            for fc in range(G):
                for s_i in range(NSUB):
                    pg, pu, sg_t = p_g[k % 2], p_u[k % 2], sg[k % 2]
                    k += 1
                    tsl = slice(s_i * NT, (s_i + 1) * NT)
                    for c in range(16):
                        mm(S, pg[:], wg_t[:, c, fc * 128:(fc + 1) * 128], hT[:, c, tsl], start=(c == 0), stop=(c == 15),
                           r=[wg_t.b(), hT_r[c]], w=[pg.b()])
                    for c in range(16):
                        mm(S, pu[:], wu_t[:, c, fc * 128:(fc + 1) * 128], hT[:, c, tsl], start=(c == 0), stop=(c == 15),
                           r=[wu_t.b(), hT_r[c]], w=[pu.b()])
                    act(S, sg_t[:], pg[:], AF.Silu, r=[pg.b()], w=[sg_t.b()])
                    tt(S, a_t[:, fc, tsl], sg_t[:], pu[:], ALU.mult, r=[sg_t.b(), pu.b()], w=[a_t.b((fc, s_i))])
            k = 0
            for dc in range(16):
                for s_i in range(NSUB):
                    pd = p_d[k % 2]
                    k += 1
                    tsl = slice(s_i * NT, (s_i + 1) * NT)
                    for fc in range(G):
                        mm(S, pd[:], wd_t[:, fc, dc * 128:(dc + 1) * 128], a_t[:, fc, tsl], start=(fc == 0), stop=(fc == G - 1),
                           r=[wd_t.b(), a_t.b((fc, s_i))], w=[pd.b()])
                    stt(S, xa[:, dc, tsl], pd[:], 0.5, xa[:, dc, tsl], ALU.mult, ALU.add, r=[pd.b(), xa.b(dc)], w=[xa.b(dc)])
        for c in range(16):
            dma(S, C.xres[c * 128:(c + 1) * 128, t0:t0 + TT], xa[:, c, :], r=[xa.b(c)], w=[C.xres.b((c, tt_i))],
                q="sp" if c % 2 == 0 else "act")
    S.barrier()
    S.release()


PARAM_SHAPES = {
    'ffn1_norm': (DEPTH, D), 'ffn1_w_gate': (DEPTH, D, FFN), 'ffn1_w_up': (DEPTH, D, FFN), 'ffn1_w_down': (DEPTH, FFN, D),
    'mix_norm': (DEPTH, D), 'w_in': (DEPTH, D, IN_COLS),
    'rwkv_mu_prev': (DEPTH, RW_COLS), 'rwkv_mu_next': (DEPTH, RW_COLS), 'rwkv_w0': (DEPTH, 2, 512),
    'rwkv_w2': (DEPTH, 2, 96, 512), 'rwkv_a0': (DEPTH, 2, 512), 'rwkv_a2': (DEPTH, 2, 96, 512),
    'rwkv_g2': (DEPTH, 256, 512), 'rwkv_k_k': (DEPTH, 512), 'rwkv_k_a': (DEPTH, 512), 'rwkv_r_k': (DEPTH, 512),
    'rwkv_ln_w': (DEPTH, 512), 'rwkv_ln_b': (DEPTH, 512),
    'mla_q_norm': (DEPTH, 384), 'mla_w_q_up': (DEPTH, 384, 768), 'mla_kv_norm': (DEPTH, 128),
    'mla_w_kv_up': (DEPTH, 128, 1024),
    'hy_short_w': (DEPTH, 3, 1536), 'hy_short_b': (DEPTH, 1536), 'hy_w1': (DEPTH, 33, 64), 'hy_b1': (DEPTH, 64),
    'hy_w2': (DEPTH, 64, 64), 'hy_b2': (DEPTH, 64), 'hy_w3': (DEPTH, 64, 64), 'hy_b3': (DEPTH, 64),
    'hy_w4': (DEPTH, 64, 2048), 'hy_freq': (DEPTH, 3, 64), 'hy_bias': (DEPTH, 2, 512),
    'gqa_q_norm': (DEPTH, 128), 'gqa_k_norm': (DEPTH, 128), 'w_branch': (DEPTH, 4, 512, D), 'w_out': (DEPTH, D, D),
    'ffn2_norm': (DEPTH, D), 'ffn2_w_gate': (DEPTH, D, FFN), 'ffn2_w_up': (DEPTH, D, FFN), 'ffn2_w_down': (DEPTH, FFN, D),
    'final_norm': (D,),
}


def rope_tab(pos, dim):
    inv = (10000.0 ** (-np.arange(0, dim, 2, dtype=np.float32) / np.float32(dim))).astype(np.float32)
    ang = pos.astype(np.float32)[:, None] * inv[None, :]
    ang = np.concatenate([ang, ang], axis=-1)
    return np.cos(ang).astype(np.float32), np.sin(ang).astype(np.float32)


def rot_lhsT(blocks):
    n = sum(b for b in blocks)
    Rm = np.zeros((n, n), np.float32)
    o = 0
    for size in blocks:
        half = size // 2
        for i in range(half):
            Rm[o + i, o + i + half] = -1.0
            Rm[o + i + half, o + i] = 1.0
        o += size
    return np.ascontiguousarray(Rm.T)


def host_consts():
    c = {}
    c['c_ident'] = np.eye(128, dtype=np.float32)
    pos = np.arange(S_LEN)
    cr, sr = rope_tab(pos // 64, 64)
    cc, sc = rope_tab(pos % 64, 64)
    c['c_gq_cos'] = np.ascontiguousarray(np.concatenate([cr, cc], axis=1).T)
    c['c_gq_sin'] = np.ascontiguousarray(np.concatenate([sr, sc], axis=1).T)
    c['c_gq_RT'] = rot_lhsT([64, 64])
    c1, s1 = rope_tab(pos, 64)
    c['c_ml_cos'] = np.ascontiguousarray(c1.T)
    c['c_ml_sin'] = np.ascontiguousarray(s1.T)
    c['c_ml_RT'] = rot_lhsT([64])
    c.update(hy_consts())
    c.update(rw_consts())
    return c


SCRATCH = {
    'xres': ([D, S_LEN], F32), 'hT_d': ([D, S_LEN], BF16), 'pm_d': ([PM_ROWS, S_LEN], F32),
    'pdv_d': ([S_LEN, 256], BF16), 'ya_d': ([512, S_LEN], BF16), 'yb_d': ([512, S_LEN], BF16),
    'yc_d': ([512, S_LEN], BF16), 'yd_d': ([512, S_LEN], BF16), 'hyH_d': ([S_LEN, 2048], F32),
    'rw_d': ([11, 512, S_LEN], F32),
}


def build(phases=("load", "ffn1", "proj", "rwkv", "mla", "hyena", "gqa", "merge", "ffn2", "final"), depth=DEPTH,
          inject=(), expose=()):
    nc = bass.Bass("TRN2", target_bir_lowering=False)
    C = Ctx()
    C.nc = nc
    C.S = Sched(nc)
    C.x = DT(nc, "x", [S_LEN, D], F32, kind="ExternalInput")
    C.P = {}
    for k, shp in PARAM_SHAPES.items():
        C.P[k] = DT(nc, k, shp, F32, kind="ExternalInput")
    hc = host_consts()
    for k, v in hc.items():
        dt_ = F32 if v.dtype == np.float32 else BF16
        setattr(C, k, DT(nc, k, list(v.shape), dt_, kind="ExternalInput"))
    C.d_ident = C.c_ident
    C.out = DT(nc, "out", [S_LEN, D], F32, kind="ExternalOutput")
    for k, (shp, dt_) in SCRATCH.items():
        kind = "ExternalInput" if k in inject else ("ExternalOutput" if k in expose else "Internal")
        setattr(C, k, DT(nc, k, shp, dt_, kind=kind))
    C.final_norm = C.P['final_norm']
    C.ffn_norm = {1: C.P['ffn1_norm'].ap, 2: C.P['ffn2_norm'].ap}
    C.ffn_wg = {1: C.P['ffn1_w_gate'], 2: C.P['ffn2_w_gate']}
    C.ffn_wu = {1: C.P['ffn1_w_up'], 2: C.P['ffn2_w_up']}
    C.ffn_wd = {1: C.P['ffn1_w_down'], 2: C.P['ffn2_w_down']}
    load_consts(C)
    C.rw_stop = RW_STOP
    if "load" in phases:
        phase_load_x(C)
    for l in range(depth):
        if "ffn1" in phases:
            phase_ffn(C, l, 1)
        if "proj" in phases:
            phase_proj(C, l)
        if "rwkv" in phases:
            phase_rwkv(C, l)
        if "mla" in phases:
            phase_mla(C, l)
        if "hyena" in phases:
            phase_hyena(C, l)
        if "gqa" in phases:
            phase_gqa(C, l)
        if "merge" in phases:
            phase_merge(C, l)
        if "ffn2" in phases:
            phase_ffn(C, l, 2)
    if "final" in phases:
        phase_final_norm(C)
    C.S.barrier()
    C.S.finalize()
    return nc, hc


def kernel(**inputs):
    nc, hc = build()
    x = np.ascontiguousarray(np.asarray(inputs['x'], dtype=np.float32))
    n = x.shape[0]
    base = {k: np.ascontiguousarray(np.asarray(inputs[k], dtype=np.float32)) for k in PARAM_SHAPES}
    base.update(hc)
    in_maps = []
    for b in range(n):
        m = dict(base)
        m['x'] = x[b]
        in_maps.append(m)
    res = run_bass_kernel_spmd(nc, in_maps, core_ids=list(range(n)))
    return np.stack([r['out'] for r in res.results], axis=0)


def norm_tile(C, xa, hT, gcol, rstd, sq, p_s, nsub):
    S = C.S
    for s_i in range(nsub):
        for c in range(16):
            q_ = sq[c % 2]
            act(S, q_[:], xa[:, c, s_i * NT:(s_i + 1) * NT], AF.Square, r=[xa.b(c)], w=[q_.b()])
            mm(S, p_s[:], C.ones[:], q_[:], start=(c == 0), stop=(c == 15), r=[C.ones.b(), q_.b()], w=[p_s.b()])
        rsqrt(S, rstd[:, s_i * NT:(s_i + 1) * NT], p_s[:], 1.0 / D, C.epsc[:, 0:1], r=[p_s.b(), C.epsc.b()],
              w=[rstd.b(s_i)])
    for c in range(16):
        stt(S, hT[:, c, :], xa[:, c, :], gcol[:, c:c + 1], rstd[:], ALU.mult, ALU.mult,
            r=[xa.b(c), gcol.b()] + [rstd.b(i) for i in range(nsub)], w=[hT.b(c)])


def phase_proj(C, l):
    S = C.S
    nm = "pj%d_" % l
    TT = 1024
    NSUB = TT // NT
    win = C.P['w_in'].ap[l]
    gcol = sb(S, nm + "gcol", [128, 16])
    S.op("sp", lambda e: e.dma_start(out=gcol[:], in_=C.P['mix_norm'].ap[l].rearrange("(c p) -> p c", p=128),
                                     allow_slow_non_contiguous=True), w=[gcol.b()], dma=True)
    xa = sb(S, nm + "xa", [128, 16, TT])
    hT = sb(S, nm + "hT", [128, 16, TT], BF16)
    rstd = sb(S, nm + "rstd", [128, TT])
    sq = [sb(S, nm + "sq%d" % i, [128, NT]) for i in range(2)]
    wt = [sb(S, nm + "wt%d" % i, [128, 16, 512], BF16) for i in range(2)]
    stg = [sb(S, nm + "stg%d" % i, [128, TT]) for i in range(2)]
    stv = [sb(S, nm + "stv%d" % i, [128, 256], BF16) for i in range(2)]
    pp = [ps(S, nm + "pp%d" % i, [128, NT]) for i in range(4)]
    p_s = ps(S, nm + "pss", [128, NT])
    groups = []
    c0 = 0
    while c0 < PM_ROWS:
        n = min(512, PM_ROWS - c0)
        groups.append((c0, n))
        c0 += n
    it = 0
    kk = 0
    ks = 0
    for tt_i in range(S_LEN // TT):
        t0 = tt_i * TT
        for c in range(16):
            dma(S, xa[:, c, :], C.xres[c * 128:(c + 1) * 128, t0:t0 + TT], r=[C.xres.b((c, tt_i))],
                w=[xa.b(c)], q="sp" if c % 2 == 0 else "act")
        norm_tile(C, xa, hT, gcol, rstd, sq, p_s, NSUB)
        hT_r = [hT.b(c) for c in range(16)]
        for c in range(16):
            dma(S, C.hT_d[c * 128:(c + 1) * 128, t0:t0 + TT], hT[:, c, :], r=[hT.b(c)], w=[C.hT_d.b((c, tt_i))], q="sp")
        for (c0, n) in groups:
            w_t = wt[it % 2]
            it += 1
            dma(S, w_t[:, :, 0:n], win[:, c0:c0 + n].rearrange("(c p) f -> p c f", p=128), w=[w_t.b()], q="pool")
            for j in range((n + 127) // 128):
                m = min(128, n - j * 128)
                st_ = stg[ks % 2]
                ks += 1
                for s_i in range(NSUB):
                    p = pp[kk % 4]
                    kk += 1
                    tsl = slice(s_i * NT, (s_i + 1) * NT)
                    for c in range(16):
                        mm(S, p[0:m, :], w_t[:, c, j * 128:j * 128 + m], hT[:, c, tsl], start=(c == 0), stop=(c == 15),
                           r=[w_t.b(), hT_r[c]], w=[p.b()])
                    cp(S, st_[0:m, tsl], p[0:m, :], r=[p.b()], w=[st_.b()], eng="dve" if kk % 2 == 0 else "act")
                dma(S, C.pm_d[c0 + j * 128:c0 + j * 128 + m, t0:t0 + TT], st_[0:m, :], r=[st_.b()],
                    w=[C.pm_d.b((c0 + j * 128, tt_i))], q="sp")
        w_t = wt[it % 2]
        it += 1
        dma(S, w_t[:, :, 0:256], win[:, PM_ROWS:PM_ROWS + 256].rearrange("(c p) f -> p c f", p=128), w=[w_t.b()], q="pool")
        for tc in range(TT // 128):
            p = pp[kk % 4]
            kk += 1
            sv = stv[tc % 2]
            for c in range(16):
                mm(S, p[:, 0:256], hT[:, c, tc * 128:(tc + 1) * 128], w_t[:, c, 0:256], start=(c == 0), stop=(c == 15),
                   r=[w_t.b(), hT_r[c]], w=[p.b()])
            cp(S, sv[:], p[:, 0:256], r=[p.b()], w=[sv.b()], eng="dve" if tc % 2 == 0 else "act")
            dma(S, C.pdv_d[t0 + tc * 128:t0 + (tc + 1) * 128, :], sv[:], r=[sv.b()], w=[C.pdv_d.b(tt_i * 8 + tc)], q="sp")
    S.barrier()
    S.release()


def attn_core(C, nm, kq_parts, v_t, v_off, scale, y_d, row0, onesb, bufs):
    S = C.S
    p_sc, p_o, p_r, pT, rinv, yst = bufs
    ksc = 0
    for ti in range(S_LEN // NT):
        tsl = slice(ti * NT, (ti + 1) * NT)
        po = p_o[ti % 2]
        pr = p_r[ti % 2]
        for sc in range(16):
            psc = p_sc[ksc % 2]
            p_t = pT[ksc % 3]
            ksc += 1
            for i, (kT, qT, K) in enumerate(kq_parts):
                mm(S, psc[:], kT[0:K, sc * 128:(sc + 1) * 128], qT[0:K, tsl], start=(i == 0), stop=(i == len(kq_parts) - 1),
                   r=[kT.b(), qT.b()], w=[psc.b()])
            act(S, p_t[:], psc[:], AF.Exp, scale=scale, r=[psc.b()], w=[p_t.b()])
            mm(S, po[:], v_t[:, sc, v_off:v_off + 128], p_t[:], start=(sc == 0), stop=(sc == 15), r=[v_t.b(), p_t.b()], w=[po.b()])
            mm(S, pr[:], onesb[:], p_t[:], start=(sc == 0), stop=(sc == 15), r=[onesb.b(), p_t.b()], w=[pr.b()])
        ri = rinv[ti % 2]
        ys = yst[ti % 2]
        S.op("dve", lambda e, ri=ri, pr=pr: e.reciprocal(ri[:], pr[:]), r=[pr.b()], w=[ri.b()])
        tt(S, ys[:], po[:], ri[:], ALU.mult, r=[po.b(), ri.b()], w=[ys.b()])
        dma(S, y_d[row0:row0 + 128, tsl], ys[:], r=[ys.b()], w=[y_d.b((row0, ti))], q="sp")


def attn_bufs(S, nm):
    p_sc = [ps(S, nm + "psc%d" % i, [128, NT]) for i in range(2)]
    p_o = [ps(S, nm + "po%d" % i, [128, NT]) for i in range(2)]
    p_r = [ps(S, nm + "pr%d" % i, [128, NT]) for i in range(2)]
    pT = [sb(S, nm + "pT%d" % i, [128, NT], BF16) for i in range(3)]
    rinv = [sb(S, nm + "ri%d" % i, [128, NT]) for i in range(2)]
    yst = [sb(S, nm + "ys%d" % i, [128, NT], BF16) for i in range(2)]
    return p_sc, p_o, p_r, pT, rinv, yst


def rope_norm_head(C, nm, src_rows, nrow, gain_col, cosT, sinT, RT, dst, tmp, p_a, p_b, do_norm):
    S = C.S
    xin, xn, t1, rs = tmp
    dma(S, xin[0:nrow, :], C.pm_d[src_rows:src_rows + nrow, :], w=[xin.b()], q="act")
    for ti in range(S_LEN // NT):
        tsl = slice(ti * NT, (ti + 1) * NT)
        if do_norm:
            act(S, t1[0:nrow, :], xin[0:nrow, tsl], AF.Square, r=[xin.b()], w=[t1.b()])
            mm(S, p_a[0:nrow, :], C.ones[0:nrow, 0:nrow], t1[0:nrow, :], r=[C.ones.b(), t1.b()], w=[p_a.b()])
            rsqrt(S, rs[0:nrow, :], p_a[0:nrow, :], 1.0 / nrow, C.epsc[0:nrow, 0:1], r=[p_a.b(), C.epsc.b()], w=[rs.b()])
            stt(S, xn[0:nrow, :], xin[0:nrow, tsl], gain_col[0:nrow, 0:1], rs[0:nrow, :], ALU.mult, ALU.mult,
                r=[xin.b(), gain_col.b(), rs.b()], w=[xn.b()])
            src = xn[0:nrow, :]
        else:
            cp(S, xn[0:nrow, :], xin[0:nrow, tsl], r=[xin.b()], w=[xn.b()], eng="pool")
            src = xn[0:nrow, :]
        mm(S, p_b[0:nrow, :], RT[0:nrow, 0:nrow], src, r=[RT.b(), xn.b()], w=[p_b.b()])
        tt(S, t1[0:nrow, :], p_b[0:nrow, :], sinT[0:nrow, tsl], ALU.mult, r=[p_b.b(), sinT.b()], w=[t1.b()])
        tt(S, xn[0:nrow, :], src, cosT[0:nrow, tsl], ALU.mult, r=[xn.b(), cosT.b()], w=[xn.b()], eng="pool")
        tt(S, dst[0:nrow, tsl], xn[0:nrow, :], t1[0:nrow, :], ALU.add, r=[xn.b(), t1.b()], w=[dst.b()])


def phase_gqa(C, l):
    S = C.S
    nm = "gq%d_" % l
    cosT = sb(S, nm + "cos", [128, S_LEN])
    sinT = sb(S, nm + "sin", [128, S_LEN])
    RT = sb(S, nm + "RT", [128, 128])
    dma(S, cosT[:], C.c_gq_cos[:], w=[cosT.b()])
    dma(S, sinT[:], C.c_gq_sin[:], w=[sinT.b()], q="act")
    dma(S, RT[:], C.c_gq_RT[:], w=[RT.b()])
    gq = sb(S, nm + "gq", [128, 2])
    S.op("sp", lambda e: e.dma_start(out=gq[:, 0:1], in_=C.P['gqa_q_norm'].ap[l].rearrange("(p o) -> p o", o=1),
                                     allow_slow_non_contiguous=True), w=[gq.b()], dma=True)
    gk = sb(S, nm + "gk", [128, 2])
    S.op("sp", lambda e: e.dma_start(out=gk[:, 0:1], in_=C.P['gqa_k_norm'].ap[l].rearrange("(p o) -> p o", o=1),
                                     allow_slow_non_contiguous=True), w=[gk.b()], dma=True)
    onesb = sb(S, nm + "onesb", [128, 128], BF16)
    mset(S, onesb[:], 1.0, w=[onesb.b()])
    tmp = (sb(S, nm + "xin", [128, S_LEN]), sb(S, nm + "xn", [128, NT]), sb(S, nm + "t1", [128, NT]), sb(S, nm + "rs", [128, NT]))
    p_a = ps(S, nm + "pa", [128, NT])
    p_b = ps(S, nm + "pb", [128, NT])
    qT = [sb(S, nm + "qT%d" % h, [128, S_LEN], BF16) for h in range(4)]
    kT = [sb(S, nm + "kT%d" % g, [128, S_LEN], BF16) for g in range(2)]
    v_t = sb(S, nm + "v", [128, 16, 256], BF16)
    dma(S, v_t[:], C.pdv_d.ap.rearrange("(c p) f -> p c f", p=128), w=[v_t.b()], q="act")
    for h in range(4):
        rope_norm_head(C, nm, OFF_GQA + h * 128, 128, gq, cosT, sinT, RT, qT[h], tmp, p_a, p_b, True)
    for g in range(2):
        rope_norm_head(C, nm, OFF_GQA + 512 + g * 128, 128, gk, cosT, sinT, RT, kT[g], tmp, p_a, p_b, True)
    bufs = attn_bufs(S, nm)
    for h in range(4):
        g = h // 2
        attn_core(C, nm, [(kT[g], qT[h], 128)], v_t, g * 128, 128.0 ** -0.5, C.yd_d, h * 128, onesb, bufs)
    S.barrier()
    S.release()


def phase_mla(C, l):
    S = C.S
    nm = "ml%d_" % l
    cosT = sb(S, nm + "cos", [64, S_LEN])
    sinT = sb(S, nm + "sin", [64, S_LEN])
    RT = sb(S, nm + "RT", [64, 64])
    dma(S, cosT[:], C.c_ml_cos[:], w=[cosT.b()])
    dma(S, sinT[:], C.c_ml_sin[:], w=[sinT.b()], q="act")
    dma(S, RT[:], C.c_ml_RT[:], w=[RT.b()])
    wq = sb(S, nm + "wq", [128, 3, 768], BF16)
    dma(S, wq[:], C.P['mla_w_q_up'].ap[l].rearrange("(c p) f -> p c f", p=128), w=[wq.b()], q="pool")
    wkv = sb(S, nm + "wkv", [128, 1024], BF16)
    dma(S, wkv[:], C.P['mla_w_kv_up'].ap[l], w=[wkv.b()], q="pool")
    gq = sb(S, nm + "gq", [128, 4])
    S.op("sp", lambda e: e.dma_start(out=gq[:, 0:3], in_=C.P['mla_q_norm'].ap[l].rearrange("(c p) -> p c", p=128),
                                     allow_slow_non_contiguous=True), w=[gq.b()], dma=True)
    S.op("sp", lambda e: e.dma_start(out=gq[:, 3:4], in_=C.P['mla_kv_norm'].ap[l].rearrange("(p o) -> p o", o=1),
                                     allow_slow_non_contiguous=True), w=[gq.b()], dma=True)
    onesb = sb(S, nm + "onesb", [128, 128], BF16)
    mset(S, onesb[:], 1.0, w=[onesb.b()])
    xq = sb(S, nm + "xq", [128, 3, S_LEN])
    dma(S, xq[:], C.pm_d.ap[OFF_MLA:OFF_MLA + 384, :].rearrange("(c p) t -> p c t", p=128), w=[xq.b()])
    xkv = sb(S, nm + "xkv", [128, S_LEN])
    dma(S, xkv[:], C.pm_d[OFF_MLA + 384:OFF_MLA + 512, :], w=[xkv.b()], q="act")
    qn = sb(S, nm + "qn", [128, 3, S_LEN], BF16)
    kvn = sb(S, nm + "kvn", [128, S_LEN], BF16)
    t1 = sb(S, nm + "t1", [128, NT])
    t2 = sb(S, nm + "t2", [128, NT])
    xn = sb(S, nm + "xn", [128, NT])
    rs = sb(S, nm + "rs", [128, NT])
    p_a = ps(S, nm + "pa", [128, NT])
    p_b = ps(S, nm + "pb", [128, NT])
    nti = S_LEN // NT
    for ti in range(nti):
        tsl = slice(ti * NT, (ti + 1) * NT)
        for c in range(3):
            act(S, t1[:], xq[:, c, tsl], AF.Square, r=[xq.b()], w=[t1.b()])
            mm(S, p_a[:], C.ones[:], t1[:], start=(c == 0), stop=(c == 2), r=[C.ones.b(), t1.b()], w=[p_a.b()])
        rsqrt(S, rs[:], p_a[:], 1.0 / 384, C.epsc[:, 0:1], r=[p_a.b(), C.epsc.b()], w=[rs.b()])
        for c in range(3):
            stt(S, qn[:, c, tsl], xq[:, c, tsl], gq[:, c:c + 1], rs[:], ALU.mult, ALU.mult, r=[xq.b(), gq.b(), rs.b()], w=[qn.b()])
        act(S, t1[:], xkv[:, tsl], AF.Square, r=[xkv.b()], w=[t1.b()])
        mm(S, p_a[:], C.ones[:], t1[:], r=[C.ones.b(), t1.b()], w=[p_a.b()])
        rsqrt(S, rs[:], p_a[:], 1.0 / 128, C.epsc[:, 0:1], r=[p_a.b(), C.epsc.b()], w=[rs.b()])
        stt(S, kvn[:, tsl], xkv[:, tsl], gq[:, 3:4], rs[:], ALU.mult, ALU.mult, r=[xkv.b(), gq.b(), rs.b()], w=[kvn.b()])
    qnope = [sb(S, nm + "qnope%d" % h, [128, S_LEN], BF16) for h in range(4)]
    qrope = [sb(S, nm + "qrope%d" % h, [64, S_LEN], BF16) for h in range(4)]
    knope = [sb(S, nm + "knope%d" % h, [128, S_LEN], BF16) for h in range(4)]
    krope = sb(S, nm + "krope", [64, S_LEN], BF16)
    v_t = sb(S, nm + "v", [128, 16, 512], BF16)
    k = 0
    for h in range(4):
        for ti in range(nti):
            tsl = slice(ti * NT, (ti + 1) * NT)
            for c in range(3):
                mm(S, p_a[:], wq[:, c, h * 192:h * 192 + 128], qn[:, c, tsl], start=(c == 0), stop=(c == 2), r=[wq.b(), qn.b()], w=[p_a.b()])
            cp(S, qnope[h][:, tsl], p_a[:], r=[p_a.b()], w=[qnope[h].b()], eng="act")
            mm(S, p_a[:], wkv[:, h * 256:h * 256 + 128], kvn[:, tsl], r=[wkv.b(), kvn.b()], w=[p_a.b()])
            cp(S, knope[h][:, tsl], p_a[:], r=[p_a.b()], w=[knope[h].b()], eng="act")
            for c in range(3):
                mm(S, p_b[0:64, :], wq[:, c, h * 192 + 128:h * 192 + 192], qn[:, c, tsl], start=(c == 0), stop=(c == 2), r=[wq.b(), qn.b()], w=[p_b.b()])
            cp(S, xn[0:64, :], p_b[0:64, :], r=[p_b.b()], w=[xn.b()])
            mm(S, p_b[0:64, :], RT[:, :], xn[0:64, :], r=[RT.b(), xn.b()], w=[p_b.b()])
            tt(S, t1[0:64, :], p_b[0:64, :], sinT[:, tsl], ALU.mult, r=[p_b.b(), sinT.b()], w=[t1.b()])
            tt(S, t2[0:64, :], xn[0:64, :], cosT[:, tsl], ALU.mult, r=[xn.b(), cosT.b()], w=[t2.b()], eng="pool")
            tt(S, qrope[h][:, tsl], t2[0:64, :], t1[0:64, :], ALU.add, r=[t2.b(), t1.b()], w=[qrope[h].b()])
    xkr = sb(S, nm + "xkr", [64, S_LEN])
    dma(S, xkr[:], C.pm_d[OFF_MLA + 512:OFF_MLA + 576, :], w=[xkr.b()])
    for ti in range(nti):
        tsl = slice(ti * NT, (ti + 1) * NT)
        mm(S, p_b[0:64, :], RT[:, :], xkr[:, tsl], r=[RT.b(), xkr.b()], w=[p_b.b()])
        tt(S, t1[0:64, :], p_b[0:64, :], sinT[:, tsl], ALU.mult, r=[p_b.b(), sinT.b()], w=[t1.b()])
        tt(S, t2[0:64, :], xkr[:, tsl], cosT[:, tsl], ALU.mult, r=[xkr.b(), cosT.b()], w=[t2.b()], eng="pool")
        tt(S, krope[:, tsl], t2[0:64, :], t1[0:64, :], ALU.add, r=[t2.b(), t1.b()], w=[krope.b()])
    for sc in range(16):
        for h in range(4):
            mm(S, p_a[:, h * 128:(h + 1) * 128], kvn[:, sc * 128:(sc + 1) * 128], wkv[:, h * 256 + 128:h * 256 + 256],
               r=[wkv.b(), kvn.b()], w=[p_a.b()])
        cp(S, v_t[:, sc, :], p_a[:], r=[p_a.b()], w=[v_t.b()], eng="dve" if sc % 2 == 0 else "act")
    bufs = attn_bufs(S, nm)
    for h in range(4):
        attn_core(C, nm, [(knope[h], qnope[h], 128), (krope, qrope[h], 64)], v_t, h * 128, 192.0 ** -0.5, C.yb_d, h * 128, onesb, bufs)
    S.barrier()
    S.release()


def phase_merge(C, l):
    S = C.S
    nm = "mg%d_" % l
    TT = 1024
    NSUB = TT // NT
    win = C.P['w_in'].ap[l]
    wbr = C.P['w_branch'].ap[l]
    wout = C.P['w_out'].ap[l]
    ys_d = [C.ya_d, C.yb_d, C.yc_d, C.yd_d]
    hT = sb(S, nm + "hT", [128, 16, TT], BF16)
    yT = [sb(S, nm + "yT%d" % n, [128, 4, TT], BF16) for n in range(4)]
    mT = sb(S, nm + "mT", [128, 16, TT], BF16)
    C.mg_acc = {(j, s_i): sb(S, nm + "acc%d_%d" % (j, s_i), [128, NT]) for j in range(4) for s_i in range(NSUB)}
    gw = [sb(S, nm + "gw%d" % i, [128, 16, 512], BF16) for i in range(2)]
    wb = [sb(S, nm + "wb%d" % i, [128, 4, 512], BF16) for i in range(2)]
    sg = [sb(S, nm + "sg%d" % i, [128, NT]) for i in range(2)]
    tmp = [sb(S, nm + "tmp%d" % i, [128, NT]) for i in range(2)]
    xa = [sb(S, nm + "xa%d" % i, [128, TT]) for i in range(2)]
    p_g = [ps(S, nm + "pg%d" % i, [128, NT]) for i in range(2)]
    p_b = [ps(S, nm + "pb%d" % i, [128, NT]) for i in range(2)]
    p_o = [ps(S, nm + "po%d" % i, [128, NT]) for i in range(2)]
    it = 0
    kk = 0
    for tt_i in range(S_LEN // TT):
        t0 = tt_i * TT
        for c in range(16):
            dma(S, hT[:, c, :], C.hT_d[c * 128:(c + 1) * 128, t0:t0 + TT], r=[C.hT_d.b((c, tt_i))], w=[hT.b()],
                q="sp" if c % 2 == 0 else "act")
        for n in range(4):
            dma(S, yT[n][:], ys_d[n].ap[:, t0:t0 + TT].rearrange("(c p) t -> p c t", p=128), w=[yT[n].b()], q="act")
        for dg in range(4):
            gws = []
            for n in range(4):
                g_t, b_t = gw[it % 2], wb[it % 2]
                it += 1
                col = OFF_GATE + n * D + dg * 512
                dma(S, g_t[:], win[:, col:col + 512].rearrange("(c p) f -> p c f", p=128), w=[g_t.b()], q="pool")
                dma(S, b_t[:], wbr[n][:, dg * 512:(dg + 1) * 512].rearrange("(c p) f -> p c f", p=128), w=[b_t.b()], q="pool")
                for j in range(4):
                    dc = dg * 4 + j
                    for s_i in range(NSUB):
                        tsl = slice(s_i * NT, (s_i + 1) * NT)
                        pg, pb = p_g[kk % 2], p_b[kk % 2]
                        sg_t, tm_t = sg[kk % 2], tmp[kk % 2]
                        kk += 1
                        for c in range(16):
                            mm(S, pg[:], g_t[:, c, j * 128:(j + 1) * 128], hT[:, c, tsl], start=(c == 0), stop=(c == 15),
                               r=[g_t.b(), hT.b()], w=[pg.b()])
                        for c in range(4):
                            mm(S, pb[:], b_t[:, c, j * 128:(j + 1) * 128], yT[n][:, c, tsl], start=(c == 0), stop=(c == 3),
                               r=[b_t.b(), yT[n].b()], w=[pb.b()])
                        act(S, sg_t[:], pg[:], AF.Sigmoid, r=[pg.b()], w=[sg_t.b()])
                        acc = C.mg_acc[(j, s_i)]
                        if n == 0:
                            tt(S, acc[:], sg_t[:], pb[:], ALU.mult, r=[sg_t.b(), pb.b()], w=[acc.b()])
                        else:
                            tt(S, tm_t[:], sg_t[:], pb[:], ALU.mult, r=[sg_t.b(), pb.b()], w=[tm_t.b()])
                            if n < 3:
                                tt(S, acc[:], acc[:], tm_t[:], ALU.add, r=[acc.b(), tm_t.b()], w=[acc.b()], eng="pool")
                            else:
                                tt(S, mT[:, dc, tsl], acc[:], tm_t[:], ALU.add, r=[acc.b(), tm_t.b()], w=[mT.b(dc)], eng="pool")
        for og in range(4):
            w_t = gw[it % 2]
            it += 1
            dma(S, w_t[:], wout[:, og * 512:(og + 1) * 512].rearrange("(c p) f -> p c f", p=128), w=[w_t.b()], q="pool")
            for j in range(4):
                dc = og * 4 + j
                x_t = xa[dc % 2]
                dma(S, x_t[:], C.xres[dc * 128:(dc + 1) * 128, t0:t0 + TT], r=[C.xres.b((dc, tt_i))], w=[x_t.b()], q="sp")
                for s_i in range(NSUB):
                    tsl = slice(s_i * NT, (s_i + 1) * NT)
                    po = p_o[kk % 2]
                    kk += 1
                    for c in range(16):
                        mm(S, po[:], w_t[:, c, j * 128:(j + 1) * 128], mT[:, c, tsl], start=(c == 0), stop=(c == 15),
                           r=[w_t.b(), mT.b(c)], w=[po.b()])
                    tt(S, x_t[:, tsl], x_t[:, tsl], po[:], ALU.add, r=[x_t.b(), po.b()], w=[x_t.b()])
                dma(S, C.xres[dc * 128:(dc + 1) * 128, t0:t0 + TT], x_t[:], r=[x_t.b()], w=[C.xres.b((dc, tt_i))], q="sp")
    S.barrier()
    S.release()


def hy_consts():
    L = S_LEN
    c = {}
    t = np.linspace(0.0, 1.0, L, dtype=np.float32)[:, None]
    w_ang = (2.0 * math.pi * np.arange(L, dtype=np.float32) / L).astype(np.float32)
    fr = np.linspace(1e-4, 15.0, 16, dtype=np.float32)
    ang = w_ang[:, None] * fr[None, :]
    z = np.concatenate([t, np.cos(ang), -np.sin(ang)], axis=-1).astype(np.float32)
    c['c_hy_z'] = np.ascontiguousarray(z.T)
    deltas = np.abs(np.linspace(math.log(1e-2) / 1.5, math.log(1e-2) / 0.3, 512)).astype(np.float32)
    win = np.exp(-t * deltas[None, :]).astype(np.float32)
    c['c_hy_win'] = np.ascontiguousarray(win.reshape(16, 128, 512).transpose(1, 0, 2))
    n = np.arange(L, dtype=np.int64)
    k = np.mod(np.outer(n, 2 * n + 1), 8192)
    angm = (2.0 * math.pi / 8192.0) * k.astype(np.float64)
    Cm = np.cos(angm)
    Sm = np.sin(angm)
    bf = ml_dtypes.bfloat16

    def t_major(M):
        return np.ascontiguousarray(M.reshape(16, 128, 16, 128).transpose(2, 1, 0, 3).reshape(16, 128, 2048).astype(bf))

    def f_major(M):
        MT = M.T
        return np.ascontiguousarray(MT.reshape(16, 128, 16, 128).transpose(2, 1, 0, 3).reshape(16, 128, 2048).astype(bf))

    c['c_hy_Ct'] = t_major(Cm)
    c['c_hy_St'] = t_major(Sm)
    c['c_hy_Cf'] = f_major(Cm)
    c['c_hy_Sf'] = f_major(Sm)
    return c


def sin_act(S, out, arg, tmp, bufs_r, w):
    s4, s8, q = tmp
    act(S, s4, arg, AF.Sin, scale=0.25, r=bufs_r, w=[w[1]])
    act(S, s8, arg, AF.Sin, scale=0.125, r=bufs_r, w=[w[2]])
    tt(S, q, s8, s8, ALU.mult, r=[w[2]], w=[w[3]])
    ts(S, q, q, -2.0, 1.0, ALU.mult, ALU.add, r=[w[3]], w=[w[3]])
    tt(S, s8, s4, q, ALU.mult, r=[w[1], w[3]], w=[w[2]])
    tt(S, q, s4, s4, ALU.mult, r=[w[1]], w=[w[3]])
    ts(S, q, q, -2.0, 1.0, ALU.mult, ALU.add, r=[w[3]], w=[w[3]])
    stt(S, out, s8, 4.0, q, ALU.mult, ALU.mult, r=[w[2], w[3]], w=[w[0]])


def phase_hyena(C, l):
    S = C.S
    nm = "hy%d_" % l
    P = C.P
    nti = S_LEN // NT
    zT = sb(S, nm + "zT", [33, S_LEN])
    dma(S, zT[:], C.c_hy_z[:], w=[zT.b()])
    w1 = sb(S, nm + "w1", [33, 64])
    dma(S, w1[:], P['hy_w1'].ap[l], w=[w1.b()])
    w2 = sb(S, nm + "w2", [64, 64])
    dma(S, w2[:], P['hy_w2'].ap[l], w=[w2.b()])
    w3 = sb(S, nm + "w3", [64, 64])
    dma(S, w3[:], P['hy_w3'].ap[l], w=[w3.b()])
    w4 = sb(S, nm + "w4", [64, 2048])
    dma(S, w4[:], P['hy_w4'].ap[l], w=[w4.b()], q="act")
    cols = sb(S, nm + "cols", [64, 8])
    for i, k in enumerate(['hy_b1', 'hy_b2', 'hy_b3']):
        S.op("sp", lambda e, i=i, k=k: e.dma_start(out=cols[:, i:i + 1], in_=P[k].ap[l].rearrange("(p o) -> p o", o=1),
                                                   allow_slow_non_contiguous=True), w=[cols.b()], dma=True)
    S.op("sp", lambda e: e.dma_start(out=cols[:, 3:6], in_=P['hy_freq'].ap[l].rearrange("k c -> c k"),
                                     allow_slow_non_contiguous=True), w=[cols.b()], dma=True)
    bias_s = sb(S, nm + "bias", [128, 1024])
    dma(S, bias_s[:], P['hy_bias'].ap[l].rearrange("o c -> (o c)").partition_broadcast(128), w=[bias_s.b()])
    ts(S, bias_s[:], bias_s[:], 1.0 / 2048.0, None, ALU.mult, r=[bias_s.b()], w=[bias_s.b()])
    win = sb(S, nm + "win", [128, 16, 512])
    dma(S, win[:], C.c_hy_win[:], w=[win.b()], q="act")
    hA = sb(S, nm + "hA", [64, S_LEN])
    hB = sb(S, nm + "hB", [64, S_LEN])
    arg = sb(S, nm + "arg", [64, NT])
    s4 = sb(S, nm + "s4", [64, NT])
    s8 = sb(S, nm + "s8", [64, NT])
    qq = sb(S, nm + "qq", [64, NT])
    p_a = ps(S, nm + "pa", [128, NT])
    p_b = ps(S, nm + "pb", [128, NT])
    p_c = ps(S, nm + "pc", [128, NT])
    p_d = ps(S, nm + "pd", [128, NT])
    layers = [(w1, zT, 33, hA, 0), (w2, hA, 64, hB, 1), (w3, hB, 64, hA, 2)]
    for (w_, src, K, dst, li) in layers:
        for ti in range(nti):
            tsl = slice(ti * NT, (ti + 1) * NT)
            mm(S, p_a[0:64, :], w_[0:K, :], src[0:K, tsl], r=[w_.b(), src.b()], w=[p_a.b()])
            ts(S, arg[:], p_a[0:64, :], cols[:, li:li + 1], cols[:, 3 + li:4 + li], ALU.add, ALU.mult,
               r=[p_a.b(), cols.b()], w=[arg.b()])
            sin_act(S, dst[:, tsl], arg[:], (s4[:], s8[:], qq[:]), [arg.b()], [dst.b(), s4.b(), s8.b(), qq.b()])
    h3 = hA
    hs = sb(S, nm + "hs", [128, 16, 1024], BF16)
    hd = sb(S, nm + "hd", [128, 16, 1024], BF16)
    f0 = sb(S, nm + "f0", [128, 1024])
    f1 = sb(S, nm + "f1", [128, 1024])
    pg = [p_a, p_b, p_c, p_d]
    for tc in range(16):
        for g in range(4):
            mm(S, pg[g][:], h3[:, tc * 128:(tc + 1) * 128], w4[:, g * 512:(g + 1) * 512], r=[h3.b(), w4.b()], w=[pg[g].b()])
        for o in range(2):
            tt(S, f0[:, o * 512:(o + 1) * 512], pg[o][:], win[:, tc, :], ALU.mult, r=[pg[o].b(), win.b()], w=[f0.b()])
            tt(S, f1[:, o * 512:(o + 1) * 512], pg[2 + o][:], win[:, tc, :], ALU.mult, r=[pg[2 + o].b(), win.b()], w=[f1.b()])
        if tc == 0:
            mset(S, f1[0:1, :], 0.0, w=[f1.b()])
        tt(S, hs[:, tc, :], f0[:], f1[:], ALU.add, r=[f0.b(), f1.b()], w=[hs.b()], eng="pool")
        tt(S, hd[:, tc, :], f0[:], f1[:], ALU.subtract, r=[f0.b(), f1.b()], w=[hd.b()])
    ct = [sb(S, nm + "ct%d" % i, [128, 2048], BF16) for i in range(2)]
    st = [sb(S, nm + "st%d" % i, [128, 2048], BF16) for i in range(2)]
    hst = [sb(S, nm + "hst%d" % i, [128, 4, 512]) for i in range(2)]
    for fc in range(16):
        c_t, s_t, h_t = ct[fc % 2], st[fc % 2], hst[fc % 2]
        dma(S, c_t[:], C.c_hy_Ct.ap[fc], w=[c_t.b()], q="sp")
        dma(S, s_t[:], C.c_hy_St.ap[fc], w=[s_t.b()], q="act")
        for o in range(2):
            pr, pi = pg[o * 2], pg[o * 2 + 1]
            for tc in range(16):
                mm(S, pr[:], c_t[:, tc * 128:(tc + 1) * 128], hs[:, tc, o * 512:(o + 1) * 512], start=(tc == 0), stop=(tc == 15),
                   r=[c_t.b(), hs.b()], w=[pr.b()])
            for tc in range(16):
                mm(S, pi[:], s_t[:, tc * 128:(tc + 1) * 128], hd[:, tc, o * 512:(o + 1) * 512], start=(tc == 0), stop=(tc == 15),
                   r=[s_t.b(), hd.b()], w=[pi.b()])
            stt(S, h_t[:, o * 2, :], pr[:], 1.0 / 2048.0, bias_s[:, o * 512:(o + 1) * 512], ALU.mult, ALU.add,
                r=[pr.b(), bias_s.b()], w=[h_t.b()])
            act(S, h_t[:, o * 2 + 1, :], pi[:], AF.Copy, scale=1.0 / 2048.0, r=[pi.b()], w=[h_t.b()])
        dma(S, C.hyH_d.ap[fc * 128:(fc + 1) * 128, :].rearrange("p (k c) -> p k c", k=4), h_t[:], r=[h_t.b()],
            w=[C.hyH_d.b(fc)], q="sp")
    S.barrier()
    S.release()
    swc = sb(S, nm + "swc", [128, 12, 4])
    for k in range(3):
        S.op("sp", lambda e, k=k: e.dma_start(out=swc[:, :, k:k + 1],
                                              in_=P['hy_short_w'].ap[l][k].rearrange("(c p o) -> p c o", p=128, o=1),
                                              allow_slow_non_contiguous=True), w=[swc.b()], dma=True)
    S.op("sp", lambda e: e.dma_start(out=swc[:, :, 3:4], in_=P['hy_short_b'].ap[l].rearrange("(c p o) -> p c o", p=128, o=1),
                                     allow_slow_non_contiguous=True), w=[swc.b()], dma=True)
    x1_tm = sb(S, nm + "x1tm", [128, 16, 512])
    x2T = sb(S, nm + "x2T", [128, 4, S_LEN])
    v_tm = sb(S, nm + "vtm", [128, 16, 512], BF16)
    Yr = sb(S, nm + "Yr", [128, 16, 512], BF16)
    Ys = sb(S, nm + "Ys", [128, 16, 512], BF16)
    ycT = sb(S, nm + "ycT", [128, 4, S_LEN], BF16)
    pin = [sb(S, nm + "pin%d" % i, [128, S_LEN]) for i in range(2)]
    u = sb(S, nm + "u", [128, S_LEN])
    pt = [ps(S, nm + "pt%d" % i, [128, NT]) for i in range(2)]
    kk = 0
    for ch in range(12):
        p_in = pin[ch % 2]
        dma(S, p_in[:], C.pm_d[OFF_HY + ch * 128:OFF_HY + (ch + 1) * 128, :], w=[p_in.b()], q="sp" if ch % 2 == 0 else "act")
        dst = x2T[:, ch - 4, :] if 4 <= ch < 8 else u[:]
        dbuf = x2T.b() if 4 <= ch < 8 else u.b()
        ts(S, dst, p_in[:], swc[:, ch, 1:2], swc[:, ch, 3:4], ALU.mult, ALU.add, r=[p_in.b(), swc.b()], w=[dbuf])
        d1 = x2T[:, ch - 4, 1:S_LEN] if 4 <= ch < 8 else u[:, 1:S_LEN]
        d2 = x2T[:, ch - 4, 0:S_LEN - 1] if 4 <= ch < 8 else u[:, 0:S_LEN - 1]
        stt(S, d1, p_in[:, 0:S_LEN - 1], swc[:, ch, 0:1], d1, ALU.mult, ALU.add, r=[p_in.b(), swc.b(), dbuf], w=[dbuf])
        stt(S, d2, p_in[:, 1:S_LEN], swc[:, ch, 2:3], d2, ALU.mult, ALU.add, r=[p_in.b(), swc.b(), dbuf], w=[dbuf])
        if ch < 4 or ch >= 8:
            cc = ch if ch < 4 else ch - 8
            tgt = x1_tm if ch < 4 else v_tm
            for tc in range(16):
                p = pt[kk % 2]
                kk += 1
                tr(S, p[:, 0:128], u[:, tc * 128:(tc + 1) * 128], C.ident[:], r=[u.b(), C.ident.b()], w=[p.b()])
                cp(S, tgt[:, tc, cc * 128:(cc + 1) * 128], p[:, 0:128], r=[p.b()], w=[tgt.b()], eng="dve" if kk % 2 == 0 else "act")
    ct = [sb(S, nm + "dct%d" % i, [128, 2048], BF16) for i in range(2)]
    st = [sb(S, nm + "dst%d" % i, [128, 2048], BF16) for i in range(2)]
    hst = [sb(S, nm + "dhst%d" % i, [128, 2, 512]) for i in range(2)]
    ur = [sb(S, nm + "ur%d" % i, [128, 512]) for i in range(2)]
    us = [sb(S, nm + "us%d" % i, [128, 512]) for i in range(2)]
    m1 = sb(S, nm + "m1", [128, 512])
    m2 = sb(S, nm + "m2", [128, 512])
    m3 = sb(S, nm + "m3", [128, 512])
    m4 = sb(S, nm + "m4", [128, 512])
    p_r = [ps(S, nm + "pr%d" % i, [128, NT]) for i in range(2)]
    p_s = [ps(S, nm + "psn%d" % i, [128, NT]) for i in range(2)]
    p_y = [ps(S, nm + "py%d" % i, [128, NT]) for i in range(2)]
    it = 0
    for o in range(2):
        src_tm = v_tm
        for fc in range(16):
            c_t, s_t, h_t = ct[it % 2], st[it % 2], hst[it % 2]
            u_r, u_s = ur[it % 2], us[it % 2]
            pr, pi = p_r[it % 2], p_s[it % 2]
            it += 1
            dma(S, c_t[:], C.c_hy_Ct.ap[fc], w=[c_t.b()], q="sp")
            dma(S, s_t[:], C.c_hy_St.ap[fc], w=[s_t.b()], q="act")
            dma(S, h_t[:], C.hyH_d.ap[fc * 128:(fc + 1) * 128, o * 1024:(o + 1) * 1024].rearrange("p (k c) -> p k c", k=2),
                r=[C.hyH_d.b(fc)], w=[h_t.b()], q="sp")
            for tc in range(16):
                mm(S, pr[:], c_t[:, tc * 128:(tc + 1) * 128], src_tm[:, tc, :], start=(tc == 0), stop=(tc == 15),
                   r=[c_t.b(), src_tm.b()], w=[pr.b()])
            for tc in range(16):
                mm(S, pi[:], s_t[:, tc * 128:(tc + 1) * 128], src_tm[:, tc, :], start=(tc == 0), stop=(tc == 15),
                   r=[s_t.b(), src_tm.b()], w=[pi.b()])
            cp(S, u_r[:], pr[:], r=[pr.b()], w=[u_r.b()], eng="act")
            cp(S, u_s[:], pi[:], r=[pi.b()], w=[u_s.b()], eng="act")
            tt(S, m1[:], u_r[:], h_t[:, 0, :], ALU.mult, r=[u_r.b(), h_t.b()], w=[m1.b()])
            tt(S, m2[:], u_s[:], h_t[:, 1, :], ALU.mult, r=[u_s.b(), h_t.b()], w=[m2.b()], eng="pool")
            tt(S, Yr[:, fc, :], m1[:], m2[:], ALU.subtract, r=[m1.b(), m2.b()], w=[Yr.b()])
            tt(S, m3[:], u_r[:], h_t[:, 1, :], ALU.mult, r=[u_r.b(), h_t.b()], w=[m3.b()], eng="pool")
            tt(S, m4[:], u_s[:], h_t[:, 0, :], ALU.mult, r=[u_s.b(), h_t.b()], w=[m4.b()])
            tt(S, Ys[:, fc, :], m3[:], m4[:], ALU.add, r=[m3.b(), m4.b()], w=[Ys.b()], eng="pool")
        for tc in range(16):
            c_t, s_t = ct[it % 2], st[it % 2]
            it += 1
            dma(S, c_t[:], C.c_hy_Cf.ap[tc], w=[c_t.b()], q="sp")
            dma(S, s_t[:], C.c_hy_Sf.ap[tc], w=[s_t.b()], q="act")
            if o == 0:
                py = p_y[tc % 2]
                for fc in range(16):
                    mm(S, py[:], c_t[:, fc * 128:(fc + 1) * 128], Yr[:, fc, :], start=(fc == 0), stop=False,
                       r=[c_t.b(), Yr.b()], w=[py.b()])
                for fc in range(16):
                    mm(S, py[:], s_t[:, fc * 128:(fc + 1) * 128], Ys[:, fc, :], start=False, stop=(fc == 15),
                       r=[s_t.b(), Ys.b()], w=[py.b()])
                tt(S, v_tm[:, tc, :], x1_tm[:, tc, :], py[:], ALU.mult, r=[x1_tm.b(), py.b()], w=[v_tm.b()])
            else:
                py = p_y[tc % 2]
                for cc in range(4):
                    for fc in range(16):
                        mm(S, py[:, cc * 128:(cc + 1) * 128], Yr[:, fc, cc * 128:(cc + 1) * 128], c_t[:, fc * 128:(fc + 1) * 128],
                           start=(fc == 0), stop=False, r=[c_t.b(), Yr.b()], w=[py.b()])
                    for fc in range(16):
                        mm(S, py[:, cc * 128:(cc + 1) * 128], Ys[:, fc, cc * 128:(cc + 1) * 128], s_t[:, fc * 128:(fc + 1) * 128],
                           start=False, stop=(fc == 15), r=[s_t.b(), Ys.b()], w=[py.b()])
                for cc in range(4):
                    tt(S, ycT[:, cc, tc * 128:(tc + 1) * 128], x2T[:, cc, tc * 128:(tc + 1) * 128], py[:, cc * 128:(cc + 1) * 128],
                       ALU.mult, r=[x2T.b(), py.b()], w=[ycT.b()], eng="dve")
    dma(S, C.yc_d.ap.rearrange("(c p) t -> p c t", p=128), ycT[:], r=[ycT.b()], w=[C.yc_d.b()], q="sp")
    S.barrier()
    S.release()


RW_R, RW_V, RW_KK, RW_G, RW_BONUS = 0, 1, 2, 3, 4
RW_E, RW_B, RW_KD = 5, 7, 9
GN_EPS = 64e-5
RW_STOP = 0


def rw_consts():
    c = {}
    j = np.arange(128)[:, None]
    t = np.arange(128)[None, :]
    MU_s = (t > j).astype(np.float32)
    ML_s = (t < j).astype(np.float32)
    MU_i = (t >= j).astype(np.float32)
    ML_i = (t <= j).astype(np.float32)
    c['c_rw_m4'] = np.ascontiguousarray(np.stack([np.concatenate([MU_s, MU_s, ML_s, ML_s], 1),
                                                  np.concatenate([ML_s, ML_s, MU_s, MU_s], 1)], 0))
    c['c_rw_m3'] = np.ascontiguousarray(np.stack([np.concatenate([MU_s, MU_i, MU_i], 1),
                                                  np.concatenate([ML_s, ML_i, ML_i], 1)], 0))
    c['c_rw_tri'] = np.ascontiguousarray(np.stack([MU_i, ML_i], 0))
    blk = np.zeros((128, 128), np.float32)
    blk[:64, :64] = 1.0
    blk[64:, 64:] = 1.0
    c['c_rw_blk'] = blk
    return c


def phase_rwkv(C, l):
    S = C.S
    P = C.P
    nm = "rw%d_" % l
    T_ = S_LEN
    nti = T_ // NT
    rw = C.rw_d

    def col(dst, src_ap, n):
        S.op("sp", lambda e: e.dma_start(out=dst, in_=src_ap.rearrange("(p o) -> p o", o=1), allow_slow_non_contiguous=True),
             w=[], dma=True)

    blk = sb(S, nm + "blk", [128, 128])
    dma(S, blk[:], C.c_rw_blk[:], w=[blk.b()])
    pin = [sb(S, nm + "pin%d" % i, [128, T_]) for i in range(2)]
    mcol = [sb(S, nm + "mcol%d" % i, [128, 4]) for i in range(2)]
    npiece = [0]

    def shift_piece(row0, nrows, dst, dbuf):
        i = npiece[0] % 2
        npiece[0] += 1
        p_in, mc = pin[i], mcol[i]
        dma(S, p_in[0:nrows, :], C.pm_d[row0:row0 + nrows, :], w=[p_in.b()], q="sp" if i == 0 else "act")
        S.op("sp", lambda e: e.dma_start(out=mc[0:nrows, 0:1], in_=P['rwkv_mu_prev'].ap[l][row0:row0 + nrows].rearrange("(p o) -> p o", o=1),
                                         allow_slow_non_contiguous=True), w=[mc.b()], dma=True)
        S.op("sp", lambda e: e.dma_start(out=mc[0:nrows, 1:2], in_=P['rwkv_mu_next'].ap[l][row0:row0 + nrows].rearrange("(p o) -> p o", o=1),
                                         allow_slow_non_contiguous=True), w=[mc.b()], dma=True)
        tt(S, mc[0:nrows, 2:3], mc[0:nrows, 0:1], mc[0:nrows, 1:2], ALU.add, r=[mc.b()], w=[mc.b()])
        ts(S, mc[0:nrows, 2:3], mc[0:nrows, 2:3], -1.0, 1.0, ALU.mult, ALU.add, r=[mc.b()], w=[mc.b()])
        ts(S, dst[0:nrows, :], p_in[0:nrows, :], mc[0:nrows, 2:3], None, ALU.mult, r=[p_in.b(), mc.b()], w=[dbuf])
        stt(S, dst[0:nrows, 1:T_], p_in[0:nrows, 0:T_ - 1], mc[0:nrows, 0:1], dst[0:nrows, 1:T_], ALU.mult, ALU.add,
            r=[p_in.b(), mc.b(), dbuf], w=[dbuf])
        stt(S, dst[0:nrows, 0:T_ - 1], p_in[0:nrows, 1:T_], mc[0:nrows, 1:2], dst[0:nrows, 0:T_ - 1], ALU.mult, ALU.add,
            r=[p_in.b(), mc.b(), dbuf], w=[dbuf])

    lw = [sb(S, nm + "lw%d" % d, [96, T_]) for d in range(2)]
    la = [sb(S, nm + "la%d" % d, [96, T_]) for d in range(2)]
    lg = [sb(S, nm + "lg%d" % i, [128, T_]) for i in range(2)]
    for d in range(2):
        shift_piece(1536 + 96 * d, 96, lw[d], lw[d].b())
        act(S, lw[d][:], lw[d][:], AF.Tanh, r=[lw[d].b()], w=[lw[d].b()])
        shift_piece(1728 + 96 * d, 96, la[d], la[d].b())
    for i in range(2):
        shift_piece(1920 + 128 * i, 128, lg[i], lg[i].b())
        act(S, lg[i][:], lg[i][:], AF.Sigmoid, r=[lg[i].b()], w=[lg[i].b()])
    w2 = sb(S, nm + "w2", [96, 2, 512])
    a2 = sb(S, nm + "a2", [96, 2, 512])
    g2 = sb(S, nm + "g2", [128, 2, 512])
    dma(S, w2[:], P['rwkv_w2'].ap[l].rearrange("d k c -> k d c"), w=[w2.b()])
    dma(S, a2[:], P['rwkv_a2'].ap[l].rearrange("d k c -> k d c"), w=[a2.b()], q="act")
    dma(S, g2[:], P['rwkv_g2'].ap[l].rearrange("(i p) c -> p i c", p=128), w=[g2.b()])
    pc = sb(S, nm + "pc", [128, 4, 12])
    srcs = [P['rwkv_w0'].ap[l][0], P['rwkv_w0'].ap[l][1], P['rwkv_a0'].ap[l][0], P['rwkv_a0'].ap[l][1], P['rwkv_k_k'].ap[l],
            P['rwkv_k_a'].ap[l], P['rwkv_r_k'].ap[l], P['rwkv_ln_w'].ap[l], P['rwkv_ln_b'].ap[l]]
    for j, s_ap in enumerate(srcs):
        S.op("sp", lambda e, j=j, s_ap=s_ap: e.dma_start(out=pc[:, :, j:j + 1], in_=s_ap.rearrange("(c p o) -> p c o", p=128, o=1),
                                                         allow_slow_non_contiguous=True), w=[pc.b()], dma=True)
    ts(S, pc[:, :, 9:10], pc[:, :, 5:6], -1.0, 1.0, ALU.mult, ALU.add, r=[pc.b()], w=[pc.b()])
    rs_ = sb(S, nm + "rs", [128, T_])
    ks_ = sb(S, nm + "ks", [128, T_])
    vs_ = sb(S, nm + "vs", [128, T_])
    kk_ = sb(S, nm + "kk", [128, T_])
    tl = {k: sb(S, nm + "tl_" + k, [128, NT]) for k in ["sq", "den", "e0", "e1", "a0", "a1", "t0", "kd0", "kd1", "b0", "b1", "kds", "pr", "bo", "g"]}
    pp = [ps(S, nm + "pp%d" % i, [128, NT]) for i in range(6)]
    kq = [0]

    def nps():
        kq[0] += 1
        return pp[kq[0] % 6]

    for cc in range(4):
        shift_piece(cc * 128, 128, rs_, rs_.b())
        shift_piece(512 + cc * 128, 128, ks_, ks_.b())
        shift_piece(1024 + cc * 128, 128, vs_, vs_.b())
        csl = slice(cc * 128, (cc + 1) * 128)
        dma(S, rw.ap[RW_R][csl, :], rs_[:], r=[rs_.b()], w=[rw.b((RW_R, cc))], q="sp")
        dma(S, rw.ap[RW_V][csl, :], vs_[:], r=[vs_.b()], w=[rw.b((RW_V, cc))], q="act")
        for ti in range(nti):
            tsl = slice(ti * NT, (ti + 1) * NT)
            ts(S, kk_[:, tsl], ks_[:, tsl], pc[:, cc, 4:5], None, ALU.mult, r=[ks_.b(), pc.b()], w=[kk_.b()])
            act(S, tl["sq"][:], kk_[:, tsl], AF.Square, r=[kk_.b()], w=[tl["sq"].b()])
            p = nps()
            mm(S, p[:], blk[:], tl["sq"][:], r=[blk.b(), tl["sq"].b()], w=[p.b()])
            act(S, tl["den"][:], p[:], AF.Sqrt, r=[p.b()], w=[tl["den"].b()])
            ts(S, tl["den"][:], tl["den"][:], 1e-12, None, ALU.max, r=[tl["den"].b()], w=[tl["den"].b()])
            S.op("dve", lambda e: e.reciprocal(tl["den"][:], tl["den"][:]), r=[tl["den"].b()], w=[tl["den"].b()])
            tt(S, kk_[:, tsl], kk_[:, tsl], tl["den"][:], ALU.mult, r=[kk_.b(), tl["den"].b()], w=[kk_.b()])
            for d in range(2):
                e_t, a_t, kd_t, b_t = tl["e%d" % d], tl["a%d" % d], tl["kd%d" % d], tl["b%d" % d]
                p = nps()
                mm(S, p[:], w2[:, d, csl], lw[d][:, tsl], r=[w2.b(), lw[d].b()], w=[p.b()])
                act(S, e_t[:], p[:], AF.Sigmoid, bias=pc[:, cc, d:d + 1], r=[p.b(), pc.b()], w=[e_t.b()])
                ts(S, e_t[:], e_t[:], -math.exp(-0.5), None, ALU.mult, r=[e_t.b()], w=[e_t.b()], eng="pool")
                dma(S, rw.ap[RW_E + d][csl, tsl], e_t[:], r=[e_t.b()], w=[rw.b((RW_E + d, cc))], q="sp")
                p = nps()
                mm(S, p[:], a2[:, d, csl], la[d][:, tsl], r=[a2.b(), la[d].b()], w=[p.b()])
                act(S, a_t[:], p[:], AF.Sigmoid, bias=pc[:, cc, 2 + d:3 + d], r=[p.b(), pc.b()], w=[a_t.b()])
                ts(S, tl["t0"][:], a_t[:], pc[:, cc, 5:6], pc[:, cc, 9:10], ALU.mult, ALU.add, r=[a_t.b(), pc.b()], w=[tl["t0"].b()])
                tt(S, kd_t[:], ks_[:, tsl], tl["t0"][:], ALU.mult, r=[ks_.b(), tl["t0"].b()], w=[kd_t.b()])
                dma(S, rw.ap[RW_KD + d][csl, tsl], kd_t[:], r=[kd_t.b()], w=[rw.b((RW_KD + d, cc))], q="act")
                tt(S, b_t[:], a_t[:], kk_[:, tsl], ALU.mult, r=[a_t.b(), kk_.b()], w=[b_t.b()], eng="pool")
                dma(S, rw.ap[RW_B + d][csl, tsl], b_t[:], r=[b_t.b()], w=[rw.b((RW_B + d, cc))], q="sp")
            tt(S, tl["kds"][:], tl["kd0"][:], tl["kd1"][:], ALU.add, r=[tl["kd0"].b(), tl["kd1"].b()], w=[tl["kds"].b()], eng="pool")
            stt(S, tl["pr"][:], rs_[:, tsl], pc[:, cc, 6:7], tl["kds"][:], ALU.mult, ALU.mult, r=[rs_.b(), pc.b(), tl["kds"].b()], w=[tl["pr"].b()])
            p = nps()
            mm(S, p[:], blk[:], tl["pr"][:], r=[blk.b(), tl["pr"].b()], w=[p.b()])
            tt(S, tl["bo"][:], p[:], vs_[:, tsl], ALU.mult, r=[p.b(), vs_.b()], w=[tl["bo"].b()])
            dma(S, rw.ap[RW_BONUS][csl, tsl], tl["bo"][:], r=[tl["bo"].b()], w=[rw.b((RW_BONUS, cc))], q="act")
            p = nps()
            for i in range(2):
                mm(S, p[:], g2[:, i, csl], lg[i][:, tsl], start=(i == 0), stop=(i == 1), r=[g2.b(), lg[i].b()], w=[p.b()])
            cp(S, tl["g"][:], p[:], r=[p.b()], w=[tl["g"].b()], eng="act")
            dma(S, rw.ap[RW_G][csl, tsl], tl["g"][:], r=[tl["g"].b()], w=[rw.b((RW_G, cc))], q="sp")
        dma(S, rw.ap[RW_KK][csl, :], kk_[:], r=[kk_.b()], w=[rw.b((RW_KK, cc))], q="sp")
    S.barrier()
    S.release()
    if getattr(C, "rw_stop", 0) == 1:
        return
    CH = 128
    NCH = T_ // CH
    m4 = [sb(S, nm + "m4_%d" % d, [128, 512]) for d in range(2)]
    m3 = [sb(S, nm + "m3_%d" % d, [128, 384]) for d in range(2)]
    tri = [sb(S, nm + "tri%d" % d, [128, 128]) for d in range(2)]
    for d in range(2):
        dma(S, m4[d][:], C.c_rw_m4.ap[d], w=[m4[d].b()])
        dma(S, m3[d][:], C.c_rw_m3.ap[d], w=[m3[d].b()], q="act")
        dma(S, tri[d][:], C.c_rw_tri.ap[d], w=[tri[d].b()])
    hmk = sb(S, nm + "hmk", [128, 128])
    dma(S, hmk[:], C.c_rw_blk[:], w=[hmk.b()])
    Yacc = sb(S, nm + "Yacc", [128, NCH, 512])
    ST = {}
    for d in range(2):
        for cc in range(4):
            ST[(d, cc)] = [sb(S, nm + "ST%d%d%d" % (d, cc, i), [64, 2, 64], BF16) for i in range(2)]
            mset(S, ST[(d, cc)][0][:], 0.0, w=[ST[(d, cc)][0].b()])
    slots = {}
    for d in range(2):
        for cc in range(4):
            sl = {}
            sn = nm + "s%d%d_" % (d, cc)
            sl["in"] = sb(S, sn + "in", [128, 6, CH])
            sl["etm"] = sb(S, sn + "etm", [128, 128])
            sl["cx"] = sb(S, sn + "cx", [128, 128])
            sl["gp"] = sb(S, sn + "gp", [128, 128])
            sl["gn"] = sb(S, sn + "gn", [128, 128])
            sl["gx"] = sb(S, sn + "gx", [128, 128])
            sl["cm"] = sb(S, sn + "cm", [128, 7, 128], BF16)
            sl["cmm"] = sb(S, sn + "cmm", [128, 3, 2, 128], BF16)
            sl["tm"] = sb(S, sn + "tm", [128, 4, 128], BF16)
            sl["XX"] = [sb(S, sn + "XX%d" % i, [128, 4, 128], BF16) for i in range(2)]
            sl["TT"] = [sb(S, sn + "TT%d" % i, [128, 2, 128], BF16) for i in range(2)]
            sl["L3"] = sb(S, sn + "L3", [128, 2, 384], BF16)
            sl["W1A"] = sb(S, sn + "W1A", [128, 2, 128], BF16)
            sl["QP"] = sb(S, sn + "QP", [128, 2, 128], BF16)
            sl["GT"] = sb(S, sn + "GT", [64, 2, 64], BF16)
            sl["RyT"] = sb(S, sn + "RyT", [64, 2, 128], BF16)
            sl["Dg"] = sb(S, sn + "Dg", [128, 128], BF16)
            slots[(d, cc)] = sl
    bank = [ps(S, nm + "bk%d" % i, [128, NT]) for i in range(6)]
    bankb = [ps(S, nm + "bkb%d" % i, [128, 1024], BF16) for i in range(2)]
    kb = [0]

    def nb():
        kb[0] += 1
        return bank[kb[0] % 6]

    IDB = C.identb
    touched = set()
    par = {}
    for step in range(NCH):
        units = []
        for d in range(2):
            n = step if d == 0 else NCH - 1 - step
            for cc in range(4):
                units.append((d, cc, n, slots[(d, cc)]))
        for d, cc, n, sl in units:
            t0 = n * CH
            csl = slice(cc * 128, (cc + 1) * 128)
            srcs = [RW_R, RW_KK, RW_V, RW_E + d, RW_B + d, RW_KD + d]
            for j, k_ in enumerate(srcs):
                dma(S, sl["in"][:, j, :], rw.ap[k_][csl, t0:t0 + CH], r=[rw.b((k_, cc))], w=[sl["in"].b()], q="sp")
        pbank = {}
        for half in (units[0:4], units[4:8]):
            for u in half:
                d, cc, n, sl = u
                p = nb()
                pbank[(d, cc)] = p
                tr(S, p[:, 0:128], sl["in"][:, 3, :], C.ident[:], r=[sl["in"].b(), C.ident.b()], w=[p.b()])
            for u in half:
                d, cc, n, sl = u
                p = pbank[(d, cc)]
                cp(S, sl["etm"][:], p[:, 0:128], r=[p.b()], w=[sl["etm"].b()], eng="act")
            for u in half:
                d, cc, n, sl = u
                p2 = nb()
                pbank[(d, cc)] = p2
                mm(S, p2[:, 0:128], sl["etm"][:], tri[d][:], r=[sl["etm"].b(), tri[d].b()], w=[p2.b()])
            for u in half:
                d, cc, n, sl = u
                p2 = pbank[(d, cc)]
                act(S, sl["gp"][:], p2[:, 0:128], AF.Exp, r=[p2.b()], w=[sl["gp"].b()])
                act(S, sl["gn"][:], p2[:, 0:128], AF.Exp, scale=-1.0, r=[p2.b()], w=[sl["gn"].b()])
            for u in half:
                d, cc, n, sl = u
                p2 = pbank[(d, cc)]
                tt(S, sl["cx"][:], p2[:, 0:128], sl["in"][:, 3, :], ALU.subtract, r=[p2.b(), sl["in"].b()], w=[sl["cx"].b()])
            for u in half:
                d, cc, n, sl = u
                act(S, sl["gx"][:], sl["cx"][:], AF.Exp, r=[sl["cx"].b()], w=[sl["gx"].b()])
            for u in half:
                d, cc, n, sl = u
                inb, cm = sl["in"], sl["cm"]
                gcol_ap = sl["gp"][:, 127:128] if d == 0 else sl["gp"][:, 0:1]
                tt(S, cm[:, 1, :], inb[:, 4, :], sl["gn"][:], ALU.mult, r=[inb.b(), sl["gn"].b()], w=[cm.b(1)], eng="pool")
                tt(S, cm[:, 2, :], inb[:, 5, :], sl["gn"][:], ALU.mult, r=[inb.b(), sl["gn"].b()], w=[cm.b(2)])
                tt(S, cm[:, 3, :], inb[:, 0, :], sl["gp"][:], ALU.mult, r=[inb.b(), sl["gp"].b()], w=[cm.b(3)], eng="pool")
                stt(S, cm[:, 4, :], inb[:, 4, :], gcol_ap, sl["gn"][:], ALU.mult, ALU.mult, r=[inb.b(), sl["gn"].b(), sl["gp"].b()], w=[cm.b(4)])
                stt(S, cm[:, 5, :], inb[:, 5, :], gcol_ap, sl["gn"][:], ALU.mult, ALU.mult, r=[inb.b(), sl["gn"].b(), sl["gp"].b()], w=[cm.b(5)])
                cp(S, cm[:, 6, :], inb[:, 2, :], r=[inb.b()], w=[cm.b(6)], eng="pool")
                ts(S, sl["Dg"][:], C.ident[:], gcol_ap, None, ALU.mult, r=[C.ident.b(), sl["gp"].b()], w=[sl["Dg"].b()], eng="pool")
            for u in half:
                d, cc, n, sl = u
                inb, cm = sl["in"], sl["cm"]
                stt(S, cm[:, 0, :], inb[:, 1, :], -1.0, sl["gx"][:], ALU.mult, ALU.mult, r=[inb.b(), sl["gx"].b()], w=[cm.b(0)])
            for u in half:
                d, cc, n, sl = u
                cm, cmm = sl["cm"], sl["cmm"]
                for j in range(3):
                    for hp in range(2):
                        ts(S, cmm[:, j, hp, :], cm[:, j, :], hmk[:, 64 * hp:64 * hp + 1], None, ALU.mult, r=[cm.b(j), hmk.b()],
                           w=[cmm.b((j, hp))], eng="pool" if (j + hp) % 2 == 0 else "dve")
            for ui, u in enumerate(half):
                d, cc, n, sl = u
                cm = sl["cm"]
                bb = bankb[ui % 2]
                for j, jj in enumerate([0, 4, 5, 6]):
                    tr(S, bb[:, j * 128:(j + 1) * 128], cm[:, jj, :], IDB[:], r=[cm.b(jj), IDB.b()], w=[bb.b()])
                cp(S, sl["tm"][:].rearrange("p a b -> p (a b)"), bb[:, 0:512], r=[bb.b()], w=[sl["tm"].b()], eng="act")
            for u in half:
                d, cc, n, sl = u
                cm, cmm = sl["cm"], sl["cmm"]
                p4 = nb()
                pbank[(d, cc)] = p4
                for hp in range(2):
                    mm(S, p4[:, hp * 128:(hp + 1) * 128], cmm[:, 1, hp, :], cm[:, 0, :], r=[cmm.b((1, hp)), cm.b(0)], w=[p4.b()])
                    mm(S, p4[:, (2 + hp) * 128:(3 + hp) * 128], cmm[:, 0, hp, :], cm[:, 1, :], r=[cmm.b((0, hp)), cm.b(1)], w=[p4.b()])
            for u in half:
                d, cc, n, sl = u
                p4 = pbank[(d, cc)]
                XX0 = sl["XX"][0]
                tt(S, XX0[:, 2:4, :].rearrange("p a b -> p (a b)"), p4[:, 0:256], m4[d][:, 0:256], ALU.mult, r=[p4.b(), m4[d].b()], w=[XX0.b()])
                tt(S, XX0[:, 0:2, :].rearrange("p a b -> p (a b)"), p4[:, 256:512], m4[d][:, 256:512], ALU.mult, r=[p4.b(), m4[d].b()], w=[XX0.b()])
            for u in half:
                d, cc, n, sl = u
                XX0, TT0 = sl["XX"][0], sl["TT"][0]
                for hp in range(2):
                    tt(S, TT0[:, hp, :], XX0[:, 2 + hp, :], IDB[:], ALU.add, r=[XX0.b(), IDB.b()], w=[TT0.b()], eng="pool")
            for hp in range(2):
                for u in half:
                    d, cc, n, sl = u
                    cm, cmm = sl["cm"], sl["cmm"]
                    p5 = nb()
                    pbank[(d, cc)] = p5
                    mm(S, p5[:, 0:128], cmm[:, 2, hp, :], cm[:, 0, :], r=[cmm.b((2, hp)), cm.b(0)], w=[p5.b()])
                    mm(S, p5[:, 128:256], cmm[:, 1, hp, :], cm[:, 3, :], r=[cmm.b((1, hp)), cm.b(3)], w=[p5.b()])
                    mm(S, p5[:, 256:384], cmm[:, 2, hp, :], cm[:, 3, :], r=[cmm.b((2, hp)), cm.b(3)], w=[p5.b()])
                for u in half:
                    d, cc, n, sl = u
                    p5 = pbank[(d, cc)]
                    tt(S, sl["L3"][:, hp, :], p5[:, 0:384], m3[d][:], ALU.mult, r=[p5.b(), m3[d].b()], w=[sl["L3"].b(hp)])

        for m_ in range(1, 7):
            cur, nxt = (m_ - 1) % 2, m_ % 2
            for d, cc, n, sl in units:
                Xc, Xn = sl["XX"][cur], sl["XX"][nxt]
                Tc, Tn = sl["TT"][cur], sl["TT"][nxt]
                p = nb()
                for hp in range(2):
                    mm(S, p[:, hp * 128:(hp + 1) * 128], Xc[:, 2 + hp, :], Xc[:, hp, :], r=[Xc.b()], w=[p.b()])
                    if m_ < 6:
                        mm(S, p[:, (2 + hp) * 128:(3 + hp) * 128], Xc[:, hp, :], Xc[:, 2 + hp, :], r=[Xc.b()], w=[p.b()])
                if m_ < 6:
                    cp(S, Xn[:].rearrange("p a b -> p (a b)"), p[:], r=[p.b()], w=[Xn.b()], eng="act")
                else:
                    cp(S, Xn[:, 0:2, :].rearrange("p a b -> p (a b)"), p[:, 0:256], r=[p.b()], w=[Xn.b()], eng="act")
                p2 = nb()
                for hp in range(2):
                    mm(S, p2[:, hp * 128:(hp + 1) * 128], Xn[:, hp, :], Tc[:, hp, :], r=[Xn.b(), Tc.b()], w=[p2.b()])
                tt(S, Tn[:].rearrange("p a b -> p (a b)"), Tc[:].rearrange("p a b -> p (a b)"), p2[:, 0:256], ALU.add,
                   r=[Tc.b(), p2.b()], w=[Tn.b()])
        sold = {}
        for half in (units[0:4], units[4:8]):
            for u in half:
                d, cc, n, sl = u
                tm, L3 = sl["tm"], sl["L3"]
                p = nb()
                pbank[(d, cc)] = p
                for hp in range(2):
                    fs = slice(64 * hp, 64 * hp + 64)
                    mm(S, p[:, hp * 64:(hp + 1) * 64], L3[:, hp, 0:128], tm[:, 3, fs], r=[L3.b(hp), tm.b()], w=[p.b()])
            for u in half:
                d, cc, n, sl = u
                p = pbank[(d, cc)]
                tm, W1A = sl["tm"], sl["W1A"]
                for hp in range(2):
                    fs = slice(64 * hp, 64 * hp + 64)
                    cp(S, W1A[:, hp, 0:64], p[:, hp * 64:(hp + 1) * 64], r=[p.b()], w=[W1A.b()], eng="act")
                    cp(S, W1A[:, hp, 64:128], tm[:, 0, fs], r=[tm.b()], w=[W1A.b()], eng="pool")
            for u in half:
                d, cc, n, sl = u
                TTf, W1A = sl["TT"][0], sl["W1A"]
                p2 = nb()
                pbank[(d, cc)] = p2
                for hp in range(2):
                    mm(S, p2[:, hp * 128:(hp + 1) * 128], TTf[:, hp, :], W1A[:, hp, :], r=[TTf.b(), W1A.b()], w=[p2.b()])
            for u in half:
                d, cc, n, sl = u
                p2 = pbank[(d, cc)]
                cp(S, sl["QP"][:].rearrange("p a b -> p (a b)"), p2[:, 0:256], r=[p2.b()], w=[sl["QP"].b()], eng="act")
            for u in half:
                d, cc, n, sl = u
                tm, L3, QP, cm = sl["tm"], sl["L3"], sl["QP"], sl["cm"]
                p3 = nb()
                pbank[(d, cc)] = p3
                for hp in range(2):
                    fs = slice(64 * hp, 64 * hp + 64)
                    mm(S, p3[0:64, hp * 64:(hp + 1) * 64], QP[:, hp, 64:128], tm[:, 1, fs], start=True, stop=False,
                       r=[QP.b(), tm.b()], w=[p3.b()])
                    mm(S, p3[0:64, hp * 64:(hp + 1) * 64], IDB[:, fs], sl["Dg"][:, fs], start=False, stop=True,
                       r=[IDB.b(), sl["Dg"].b()], w=[p3.b()])
                    mm(S, p3[0:64, 128 + hp * 128:256 + hp * 128], QP[:, hp, 64:128], L3[:, hp, 128:256], start=True, stop=False,
                       r=[QP.b(), L3.b(hp)], w=[p3.b()])
                    mm(S, p3[0:64, 128 + hp * 128:256 + hp * 128], IDB[:, fs], cm[:, 3, :], start=False, stop=True,
                       r=[IDB.b(), cm.b(3)], w=[p3.b()])
            for u in half:
                d, cc, n, sl = u
                p3 = pbank[(d, cc)]
                cp(S, sl["GT"][:].rearrange("p a b -> p (a b)"), p3[0:64, 0:128], r=[p3.b()], w=[sl["GT"].b()], eng="act")
                cp(S, sl["RyT"][:].rearrange("p a b -> p (a b)"), p3[0:64, 128:384], r=[p3.b()], w=[sl["RyT"].b()], eng="act")
            for u in half:
                d, cc, n, sl = u
                tm, L3, QP = sl["tm"], sl["L3"], sl["QP"]
                k_ = par.get((d, cc), 0)
                S_old, S_new = ST[(d, cc)][k_], ST[(d, cc)][1 - k_]
                par[(d, cc)] = 1 - k_
                sold[(d, cc)] = (S_old, S_new)
                pz = nb()
                pbank[(d, cc)] = pz
                for hp in range(2):
                    fs = slice(64 * hp, 64 * hp + 64)
                    mm(S, pz[0:64, fs], sl["GT"][:, hp, :], S_old[:, hp, :], start=True, stop=False, r=[sl["GT"].b(), S_old.b()], w=[pz.b()])
                    mm(S, pz[0:64, fs], tm[:, 1, fs], QP[:, hp, 0:64], start=False, stop=False, r=[tm.b(), QP.b()], w=[pz.b()])
                    mm(S, pz[0:64, fs], tm[:, 2, fs], tm[:, 3, fs], start=False, stop=True, r=[tm.b()], w=[pz.b()])
                    mm(S, pz[:, 128 + 64 * hp:192 + 64 * hp], L3[:, hp, 128:256], QP[:, hp, 0:64], start=True, stop=False, r=[L3.b(hp), QP.b()], w=[pz.b()])
                    mm(S, pz[:, 128 + 64 * hp:192 + 64 * hp], L3[:, hp, 256:384], tm[:, 3, fs], start=False, stop=False, r=[L3.b(hp), tm.b()], w=[pz.b()])
                    mm(S, pz[:, 128 + 64 * hp:192 + 64 * hp], sl["RyT"][:, hp, :], S_old[:, hp, :], start=False, stop=True, r=[sl["RyT"].b(), S_old.b()], w=[pz.b()])
            for u in half:
                d, cc, n, sl = u
                pz = pbank[(d, cc)]
                S_old, S_new = sold[(d, cc)]
                cp(S, S_new[:].rearrange("p a b -> p (a b)"), pz[0:64, 0:128], r=[pz.b()], w=[S_new.b()], eng="act")
                if (n, cc) not in touched:
                    touched.add((n, cc))
                    cp(S, Yacc[:, n, cc * 128:(cc + 1) * 128], pz[:, 128:256], r=[pz.b()], w=[Yacc.b((n, cc))])
                else:
                    tt(S, Yacc[:, n, cc * 128:(cc + 1) * 128], Yacc[:, n, cc * 128:(cc + 1) * 128], pz[:, 128:256], ALU.add,
                       r=[pz.b(), Yacc.b((n, cc))], w=[Yacc.b((n, cc))])
    if C.rw_stop == 6:
        return
    pcs = sb(S, nm + "pcs", [128, 4, 4])
    for j, k_ in enumerate(['rwkv_ln_w', 'rwkv_ln_b']):
        S.op("sp", lambda e, j=j, k_=k_: e.dma_start(out=pcs[:, :, j:j + 1], in_=P[k_].ap[l].rearrange("(c p o) -> p c o", p=128, o=1),
                                                     allow_slow_non_contiguous=True), w=[pcs.b()], dma=True)
    blk2 = sb(S, nm + "blk2", [128, 128])
    dma(S, blk2[:], C.c_rw_blk[:], w=[blk2.b()])
    gne = sb(S, nm + "gne", [128, 1])
    mset(S, gne[:], GN_EPS, w=[gne.b()])
    ycm = sb(S, nm + "ycm", [128, NT])
    yc2 = sb(S, nm + "yc2", [128, NT])
    sq2 = sb(S, nm + "sq2", [128, NT])
    rstd2 = sb(S, nm + "rstd2", [128, NT])
    bo_t = [sb(S, nm + "bo%d" % i, [128, NT]) for i in range(2)]
    g_t = [sb(S, nm + "gt%d" % i, [128, NT]) for i in range(2)]
    yo = [sb(S, nm + "yo%d" % i, [128, NT], BF16) for i in range(2)]
    k3 = 0
    for cc in range(4):
        csl = slice(cc * 128, (cc + 1) * 128)
        for ti in range(nti):
            tsl = slice(ti * NT, (ti + 1) * NT)
            b_t, gg, y_o = bo_t[k3 % 2], g_t[k3 % 2], yo[k3 % 2]
            k3 += 1
            dma(S, b_t[:], rw.ap[RW_BONUS][csl, tsl], r=[rw.b((RW_BONUS, cc))], w=[b_t.b()], q="sp")
            dma(S, gg[:], rw.ap[RW_G][csl, tsl], r=[rw.b((RW_G, cc))], w=[gg.b()], q="act")
            p = nb()
            for j in range(4):
                n = ti * 4 + j
                tr(S, p[:, j * 128:(j + 1) * 128], Yacc[:, n, csl], C.ident[:], r=[Yacc.b((n, cc)), C.ident.b()], w=[p.b()])
            cp(S, ycm[:], p[:], r=[p.b()], w=[ycm.b()], eng="act")
            p2 = nb()
            mm(S, p2[:], blk2[:], ycm[:], r=[blk2.b(), ycm.b()], w=[p2.b()])
            stt(S, yc2[:], p2[:], -1.0 / 64.0, ycm[:], ALU.mult, ALU.add, r=[p2.b(), ycm.b()], w=[yc2.b()])
            act(S, sq2[:], yc2[:], AF.Square, r=[yc2.b()], w=[sq2.b()])
            p3 = nb()
            mm(S, p3[:], blk2[:], sq2[:], r=[blk2.b(), sq2.b()], w=[p3.b()])
            rsqrt(S, rstd2[:], p3[:], 1.0 / 64.0, gne[:, 0:1], r=[p3.b(), gne.b()], w=[rstd2.b()])
            tt(S, yc2[:], yc2[:], rstd2[:], ALU.mult, r=[yc2.b(), rstd2.b()], w=[yc2.b()])
            ts(S, yc2[:], yc2[:], pcs[:, cc, 0:1], pcs[:, cc, 1:2], ALU.mult, ALU.add, r=[yc2.b(), pcs.b()], w=[yc2.b()])
            tt(S, yc2[:], yc2[:], b_t[:], ALU.add, r=[yc2.b(), b_t.b()], w=[yc2.b()], eng="pool")
            tt(S, y_o[:], yc2[:], gg[:], ALU.mult, r=[yc2.b(), gg.b()], w=[y_o.b()])
            dma(S, C.ya_d[csl, tsl], y_o[:], r=[y_o.b()], w=[C.ya_d.b((cc, ti))], q="sp")
    S.barrier()
    S.release()
```

```python
import math
from contextlib import ExitStack

import numpy as np
import ml_dtypes
import concourse.bass as bass
import concourse.mybir as mybir
from concourse.bass_utils import run_bass_kernel_spmd

F32 = mybir.dt.float32
BF16 = mybir.dt.bfloat16
ALU = mybir.AluOpType
AF = mybir.ActivationFunctionType
AX = mybir.AxisListType

D = 2048
S_LEN = 2048
DEPTH = 2
FFN = 5632
EPS = 1e-6
NT = 512
RW_COLS, MLA_COLS, HY_COLS, GQA_COLS = 2176, 576, 1536, 1024
OFF_RW, OFF_MLA, OFF_HY, OFF_GQA, OFF_GATE = 0, 2176, 2752, 4288, 5312
IN_COLS = 13504
PM_ROWS = 5056


class Buf:
    __slots__ = ("w", "r", "name", "excl")

    def __init__(self, name="", excl=False):
        self.w = None
        self.r = []
        self.name = name
        self.excl = excl


class Rec:
    __slots__ = ("eng", "fn", "deps", "dma", "semkey", "val", "needs_inc", "idx")


class Sched:
    ENG = ("pe", "act", "dve", "pool", "sp")
    BLK = {"pe": "tensor", "act": "scalar", "dve": "vector", "pool": "gpsimd", "sp": "sync"}
    CAP = 8000
    NRING = 8

    def __init__(self, nc):
        self.nc = nc
        self.streams = {e: [] for e in self.ENG}
        self.all = []
        self.ndma = {e: 0 for e in self.ENG}
        self.ring_last = {}
        self.es = ExitStack()

    def sbuf(self, name, shape, dt):
        return self.es.enter_context(self.nc.sbuf_tensor(name, list(shape), dt))

    def psum(self, name, shape, dt=F32):
        return self.es.enter_context(self.nc.psum_tensor(name, list(shape), dt))

    def release(self):
        self.es.close()
        self.es = ExitStack()

    def op(self, eng, fn, r=(), w=(), dma=False):
        rec = Rec()
        rec.eng, rec.fn, rec.dma = eng, fn, dma
        rec.needs_inc = False
        rec.val = None
        rec.semkey = None
        rec.idx = len(self.all)
        deps = {}
        for b in r:
            if b.w is not None:
                deps[id(b.w)] = b.w
            if b.excl:
                for rr in b.r:
                    if rr.eng != eng:
                        deps[id(rr)] = rr
        for b in w:
            if b.w is not None:
                lw = b.w
                if not (eng == "pe" and lw.eng == "pe" and not lw.dma and not dma):
                    deps[id(lw)] = lw
            for rr in b.r:
                if rr.eng != eng or rr.dma or dma:
                    deps[id(rr)] = rr
        if dma:
            i = self.ndma[eng]
            self.ndma[eng] += 1
            slot = i % self.NRING
            rec.semkey = ("dma", eng, slot)
            rec.val = 16 * (i // self.NRING + 1)
            prev = self.ring_last.get((eng, slot))
            if prev is not None:
                deps[id(prev)] = prev
            self.ring_last[(eng, slot)] = rec
        for d in deps.values():
            d.needs_inc = True
        rec.deps = list(deps.values())
        for b in r:
            if not dma:
                b.r = [x for x in b.r if x.dma or x.eng != eng]
            b.r.append(rec)
        for b in w:
            b.w = rec
            b.r = []
        self.streams[eng].append(rec)
        self.all.append(rec)
        return rec

    def barrier(self):
        lasts = []
        for e in self.ENG:
            for rec in reversed(self.streams[e]):
                if rec.fn is not None:
                    lasts.append(rec)
                    break
        for (e, slot), rec in self.ring_last.items():
            lasts.append(rec)
        fence = Buf("fence")
        for e in self.ENG:
            rec = Rec()
            rec.eng, rec.fn, rec.dma = e, None, False
            rec.needs_inc = False
            rec.val = None
            rec.semkey = None
            rec.idx = len(self.all)
            rec.deps = [d for d in lasts]
            for d in lasts:
                d.needs_inc = True
            self.streams[e].append(rec)
            self.all.append(rec)

    def finalize(self):
        nc = self.nc
        cnt = {e: 0 for e in self.ENG}
        for rec in self.all:
            if rec.dma or not rec.needs_inc or rec.fn is None:
                continue
            c = cnt[rec.eng]
            rec.semkey = ("eng", rec.eng, c // self.CAP)
            rec.val = c % self.CAP + 1
            cnt[rec.eng] = c + 1
        keys = set()
        for rec in self.all:
            if rec.semkey is not None:
                keys.add(rec.semkey)
        with ExitStack() as es:
            sems = {}
            for k in sorted(keys):
                sems[k] = es.enter_context(nc.semaphore("s_%s_%s_%d" % k))
            block = es.enter_context(nc.Block())
            for e in self.ENG:
                stream = self.streams[e]

                def body(eng, stream=stream):
                    waited = {}
                    for rec in stream:
                        for d in rec.deps:
                            if d.semkey is None:
                                continue
                            if waited.get(d.semkey, 0) >= d.val:
                                continue
                            eng.wait_ge(sems[d.semkey], d.val)
                            waited[d.semkey] = d.val
                        if rec.fn is None:
                            continue
                        ins = rec.fn(eng)
                        if rec.dma:
                            ins.then_inc(sems[rec.semkey], 16)
                        elif rec.needs_inc:
                            ins.then_inc(sems[rec.semkey], 1)

                getattr(block, self.BLK[e])(body)
        self.es.close()


class T:
    def __init__(self, h, excl=False):
        self.h = h
        self.bufs = {}
        self.excl = excl

    def __getitem__(self, idx):
        return self.h[idx]

    def b(self, key=0):
        bb = self.bufs.get(key)
        if bb is None:
            bb = self.bufs[key] = Buf(excl=self.excl)
        return bb


def sb(S, name, shape, dt=F32):
    return T(S.sbuf(name, shape, dt))


def ps(S, name, shape, dt=F32):
    return T(S.psum(name, shape, dt), excl=True)


class DT:
    def __init__(self, nc, name, shape, dt, kind="Internal"):
        self.t = nc.dram_tensor(name, list(shape), dt, kind=kind)
        self.ap = self.t.ap()
        self.bufs = {}

    def __getitem__(self, idx):
        return self.ap[idx]

    def b(self, key=0):
        bb = self.bufs.get(key)
        if bb is None:
            bb = self.bufs[key] = Buf()
        return bb


def mm(S, out, lhsT, rhs, start=True, stop=True, r=(), w=()):
    return S.op("pe", lambda e: e.matmul(out, lhsT, rhs, start=start, stop=stop), r=r, w=w)


def tr(S, out, in_, ident, r=(), w=()):
    return S.op("pe", lambda e: e.transpose(out, in_, ident), r=r, w=w)


def act(S, out, in_, func, bias=None, scale=None, accum_out=None, r=(), w=(), eng="act"):
    kw = {}
    if bias is not None:
        kw["bias"] = bias
    if scale is not None:
        kw["scale"] = scale
    if accum_out is not None:
        kw["accum_out"] = accum_out
    return S.op(eng, lambda e: e.activation(out, in_, func, **kw), r=r, w=w)


def tt(S, out, in0, in1, op, r=(), w=(), eng="dve"):
    return S.op(eng, lambda e: e.tensor_tensor(out, in0, in1, op), r=r, w=w)


def ts(S, out, in0, s1, s2, op0, op1=None, r=(), w=(), eng="dve", accum_out=None):
    if op1 is None:
        return S.op(eng, lambda e: e.tensor_single_scalar(out, in0, s1, op0), r=r, w=w)
    if accum_out is not None:
        return S.op(eng, lambda e: e.tensor_scalar(out, in0, s1, s2, op0, op1, accum_out), r=r, w=w)
    return S.op(eng, lambda e: e.tensor_scalar(out, in0, s1, s2, op0, op1), r=r, w=w)


def stt(S, out, in0, scalar, in1, op0, op1, r=(), w=(), eng="dve"):
    return S.op(eng, lambda e: e.scalar_tensor_tensor(out, in0, scalar, in1, op0, op1), r=r, w=w)


def rsqrt(S, out, in_, scale, bias, r=(), w=()):
    S.op("act", lambda e: e.activation(out, in_, AF.Sqrt, bias=bias, scale=scale), r=r, w=w)
    S.op("dve", lambda e: e.reciprocal(out, out), r=w, w=w)


def cp(S, out, in_, r=(), w=(), eng="dve"):
    if eng == "act":
        return S.op(eng, lambda e: e.copy(out, in_), r=r, w=w)
    return S.op(eng, lambda e: e.tensor_copy(out, in_), r=r, w=w)


def mset(S, ap, val, w=(), eng="dve"):
    return S.op(eng, lambda e: e.memset(ap, val), w=w)


def dma(S, out, in_, r=(), w=(), q="sp"):
    return S.op(q, lambda e: e.dma_start(out=out, in_=in_), r=r, w=w, dma=True)


class Ctx:
    pass


def load_consts(C):
    S = C.S
    nc = S.nc
    C.cst = ExitStack()
    C.ident = T(C.cst.enter_context(nc.sbuf_tensor("ident", [128, 128], F32)))
    C.identb = T(C.cst.enter_context(nc.sbuf_tensor("identb", [128, 128], BF16)))
    C.ones = T(C.cst.enter_context(nc.sbuf_tensor("ones", [128, 128], F32)))
    dma(S, C.ident[:], C.d_ident[:], w=[C.ident.b()])
    mset(S, C.ones[:], 1.0, w=[C.ones.b()])
    C.epsc = T(C.cst.enter_context(nc.sbuf_tensor("epsc", [128, 4], F32)))
    mset(S, C.epsc[:, 0:1], EPS, w=[C.epsc.b()])
    mset(S, C.epsc[:, 1:2], 0.0, w=[C.epsc.b()])
    cp(S, C.identb[:], C.ident[:], r=[C.ident.b()], w=[C.identb.b()])


def phase_load_x(C):
    S = C.S
    xt = [sb(S, "lx_xt%d" % i, [128, 4, D]) for i in range(2)]
    st = [sb(S, "lx_st%d" % i, [128, NT]) for i in range(3)]
    pp = [ps(S, "lx_ps%d" % i, [128, NT]) for i in range(3)]
    k = 0
    for tg in range(S_LEN // NT):
        x_t = xt[tg % 2]
        for j in range(4):
            t0 = tg * NT + j * 128
            dma(S, x_t[:, j, :], C.x[t0:t0 + 128, :], w=[x_t.b(j)], q="sp" if j % 2 == 0 else "act")
        for dc in range(D // 128):
            p = pp[k % 3]
            s = st[k % 3]
            for j in range(4):
                tr(S, p[:, j * 128:(j + 1) * 128], x_t[:, j, dc * 128:(dc + 1) * 128], C.ident[:],
                   r=[x_t.b(j), C.ident.b()], w=[p.b()])
            cp(S, s[:], p[:], r=[p.b()], w=[s.b()], eng="dve" if k % 2 == 0 else "act")
            dma(S, C.xres[dc * 128:(dc + 1) * 128, tg * NT:(tg + 1) * NT], s[:], r=[s.b()],
                w=[C.xres.b((dc, tg // 2))], q="sp")
            k += 1
    S.barrier()
    S.release()


def phase_final_norm(C):
    S = C.S
    gb = sb(S, "fn_g", [128, D])
    dma(S, gb[:], C.final_norm.ap.partition_broadcast(128), w=[gb.b()])
    xin = [sb(S, "fn_xin%d" % i, [128, 16, 128]) for i in range(2)]
    xt = [sb(S, "fn_xt%d" % i, [128, D]) for i in range(2)]
    sq = sb(S, "fn_sq", [128, D])
    ot = [sb(S, "fn_ot%d" % i, [128, D]) for i in range(2)]
    ss = [sb(S, "fn_ss%d" % i, [128, 1]) for i in range(2)]
    pp = [ps(S, "fn_ps%d" % i, [128, NT]) for i in range(4)]
    for tc in range(S_LEN // 128):
        xi = xin[tc % 2]
        x_t = xt[tc % 2]
        o_t = ot[tc % 2]
        s_ = ss[tc % 2]
        dma(S, xi[:], C.xres.ap[:, tc * 128:(tc + 1) * 128].rearrange("(c p) t -> p c t", p=128),
            r=[C.xres.b((dc, tc // 8)) for dc in range(16)], w=[xi.b()], q="sp" if tc % 2 == 0 else "act")
        for g in range(4):
            p = pp[g]
            for j in range(4):
                dc = g * 4 + j
                tr(S, p[:, j * 128:(j + 1) * 128], xi[:, dc, :], C.ident[:], r=[xi.b(), C.ident.b()], w=[p.b()])
            cp(S, x_t[:, g * NT:(g + 1) * NT], p[:], r=[p.b()], w=[x_t.b(g)], eng="dve" if g % 2 == 0 else "act")
        act(S, sq[:], x_t[:], AF.Square, accum_out=s_[:], r=[x_t.b(g) for g in range(4)], w=[sq.b(), s_.b()])
        rsqrt(S, s_[:], s_[:], 1.0 / D, C.epsc[:, 0:1], r=[s_.b(), C.epsc.b()], w=[s_.b()])
        stt(S, o_t[:], x_t[:], s_[:, 0:1], gb[:], ALU.mult, ALU.mult,
            r=[x_t.b(g) for g in range(4)] + [s_.b(), gb.b()], w=[o_t.b()])
        dma(S, C.out[tc * 128:(tc + 1) * 128, :], o_t[:], r=[o_t.b()], w=[C.out.b(tc)], q="sp")
    S.barrier()
    S.release()


def phase_ffn(C, l, which):
    S = C.S
    nm = "f%d%d_" % (l, which)
    g_d = C.ffn_norm[which][l]
    wg_d, wu_d, wd_d = C.ffn_wg[which].ap[l], C.ffn_wu[which].ap[l], C.ffn_wd[which].ap[l]
    TT = 1024
    G = 2
    NG = FFN // (128 * G)
    gcol = sb(S, nm + "gcol", [128, 16])
    S.op("sp", lambda e: e.dma_start(out=gcol[:], in_=g_d.rearrange("(c p) -> p c", p=128),
                                     allow_slow_non_contiguous=True), w=[gcol.b()], dma=True)
    xa = sb(S, nm + "xa", [128, 16, TT])
    hT = sb(S, nm + "hT", [128, 16, TT], BF16)
    rstd = sb(S, nm + "rstd", [128, TT])
    sq = [sb(S, nm + "sq%d" % i, [128, NT]) for i in range(2)]
    wg = [sb(S, nm + "wg%d" % i, [128, 16, 128 * G], BF16) for i in range(2)]
    wu = [sb(S, nm + "wu%d" % i, [128, 16, 128 * G], BF16) for i in range(2)]
    wd = [sb(S, nm + "wd%d" % i, [128, G, D], BF16) for i in range(2)]
    aT = [sb(S, nm + "aT%d" % i, [128, G, TT], BF16) for i in range(2)]
    sg = [sb(S, nm + "sg%d" % i, [128, NT]) for i in range(2)]
    p_g = [ps(S, nm + "pg%d" % i, [128, NT]) for i in range(2)]
    p_u = [ps(S, nm + "pu%d" % i, [128, NT]) for i in range(2)]
    p_d = [ps(S, nm + "pd%d" % i, [128, NT]) for i in range(3)]
    dtmp = [sb(S, nm + "dtmp%d" % i, [128, NT]) for i in range(2)]
    kt = 0
    p_s = ps(S, nm + "pss", [128, NT])
    NSUB = TT // NT
    it = 0
    for tt_i in range(S_LEN // TT):
        t0 = tt_i * TT
        for c in range(16):
            dma(S, xa[:, c, :], C.xres[c * 128:(c + 1) * 128, t0:t0 + TT], r=[C.xres.b((c, tt_i))],
                w=[xa.b((c, s)) for s in range(NSUB)], q="sp" if c % 2 == 0 else "act")
        for s_i in range(NSUB):
            for c in range(16):
                q_ = sq[c % 2]
                act(S, q_[:], xa[:, c, s_i * NT:(s_i + 1) * NT], AF.Square, r=[xa.b((c, s_i))], w=[q_.b()])
                mm(S, p_s[:], C.ones[:], q_[:], start=(c == 0), stop=(c == 15), r=[C.ones.b(), q_.b()], w=[p_s.b()])
            rsqrt(S, rstd[:, s_i * NT:(s_i + 1) * NT], p_s[:], 1.0 / D, C.epsc[:, 0:1], r=[p_s.b(), C.epsc.b()],
                  w=[rstd.b(s_i)])
        for c in range(16):
            stt(S, hT[:, c, :], xa[:, c, :], gcol[:, c:c + 1], rstd[:], ALU.mult, ALU.mult,
                r=[xa.b((c, s)) for s in range(NSUB)] + [gcol.b()] + [rstd.b(i) for i in range(NSUB)], w=[hT.b(c)])
        hT_r = [hT.b(c) for c in range(16)]
        for fg in range(NG):
            wg_t, wu_t, wd_t, a_t = wg[it % 2], wu[it % 2], wd[it % 2], aT[it % 2]
            it += 1
            f0 = fg * 128 * G
            dma(S, wg_t[:], wg_d[:, f0:f0 + 128 * G].rearrange("(c p) f -> p c f", p=128), w=[wg_t.b()], q="pool")
            dma(S, wu_t[:], wu_d[:, f0:f0 + 128 * G].rearrange("(c p) f -> p c f", p=128), w=[wu_t.b()], q="pool")
            dma(S, wd_t[:], wd_d[f0:f0 + 128 * G, :].rearrange("(c p) d -> p c d", p=128), w=[wd_t.b()], q="pool")
            k = 0
            for fc in range(G):
                for s_i in range(NSUB):
                    pg, pu, sg_t = p_g[k % 2], p_u[k % 2], sg[k % 2]
                    k += 1
                    tsl = slice(s_i * NT, (s_i + 1) * NT)
                    for c in range(16):
                        mm(S, pg[:], wg_t[:, c, fc * 128:(fc + 1) * 128], hT[:, c, tsl], start=(c == 0), stop=(c == 15),
                           r=[wg_t.b(), hT_r[c]], w=[pg.b()])
                    for c in range(16):
                        mm(S, pu[:], wu_t[:, c, fc * 128:(fc + 1) * 128], hT[:, c, tsl], start=(c == 0), stop=(c == 15),
                           r=[wu_t.b(), hT_r[c]], w=[pu.b()])
                    act(S, sg_t[:], pg[:], AF.Silu, r=[pg.b()], w=[sg_t.b()])
                    tt(S, a_t[:, fc, tsl], sg_t[:], pu[:], ALU.mult, r=[sg_t.b(), pu.b()], w=[a_t.b((fc, s_i))])
            k = 0
            for dc in range(16):
                for s_i in range(NSUB):
                    pd = p_d[k % 3]
                    k += 1
                    tsl = slice(s_i * NT, (s_i + 1) * NT)
                    for fc in range(G):
                        mm(S, pd[:], wd_t[:, fc, dc * 128:(dc + 1) * 128], a_t[:, fc, tsl], start=(fc == 0), stop=(fc == G - 1),
                           r=[wd_t.b(), a_t.b((fc, s_i))], w=[pd.b()])
                    if s_i % 2 == 0:
                        stt(S, xa[:, dc, tsl], pd[:], 0.5, xa[:, dc, tsl], ALU.mult, ALU.add, r=[pd.b(), xa.b((dc, s_i))],
                            w=[xa.b((dc, s_i))])
                    else:
                        t_ = dtmp[kt % 2]
                        kt += 1
                        act(S, t_[:], pd[:], AF.Copy, scale=0.5, r=[pd.b()], w=[t_.b()])
                        tt(S, xa[:, dc, tsl], xa[:, dc, tsl], t_[:], ALU.add, r=[xa.b((dc, s_i)), t_.b()], w=[xa.b((dc, s_i))])
        for c in range(16):
            dma(S, C.xres[c * 128:(c + 1) * 128, t0:t0 + TT], xa[:, c, :], r=[xa.b((c, s)) for s in range(NSUB)],
                w=[C.xres.b((c, tt_i))], q="sp" if c % 2 == 0 else "act")
    S.barrier()
    S.release()


PARAM_SHAPES = {
    'ffn1_norm': (DEPTH, D), 'ffn1_w_gate': (DEPTH, D, FFN), 'ffn1_w_up': (DEPTH, D, FFN), 'ffn1_w_down': (DEPTH, FFN, D),
    'mix_norm': (DEPTH, D), 'w_in': (DEPTH, D, IN_COLS),
    'rwkv_mu_prev': (DEPTH, RW_COLS), 'rwkv_mu_next': (DEPTH, RW_COLS), 'rwkv_w0': (DEPTH, 2, 512),
    'rwkv_w2': (DEPTH, 2, 96, 512), 'rwkv_a0': (DEPTH, 2, 512), 'rwkv_a2': (DEPTH, 2, 96, 512),
    'rwkv_g2': (DEPTH, 256, 512), 'rwkv_k_k': (DEPTH, 512), 'rwkv_k_a': (DEPTH, 512), 'rwkv_r_k': (DEPTH, 512),
    'rwkv_ln_w': (DEPTH, 512), 'rwkv_ln_b': (DEPTH, 512),
    'mla_q_norm': (DEPTH, 384), 'mla_w_q_up': (DEPTH, 384, 768), 'mla_kv_norm': (DEPTH, 128),
    'mla_w_kv_up': (DEPTH, 128, 1024),
    'hy_short_w': (DEPTH, 3, 1536), 'hy_short_b': (DEPTH, 1536), 'hy_w1': (DEPTH, 33, 64), 'hy_b1': (DEPTH, 64),
    'hy_w2': (DEPTH, 64, 64), 'hy_b2': (DEPTH, 64), 'hy_w3': (DEPTH, 64, 64), 'hy_b3': (DEPTH, 64),
    'hy_w4': (DEPTH, 64, 2048), 'hy_freq': (DEPTH, 3, 64), 'hy_bias': (DEPTH, 2, 512),
    'gqa_q_norm': (DEPTH, 128), 'gqa_k_norm': (DEPTH, 128), 'w_branch': (DEPTH, 4, 512, D), 'w_out': (DEPTH, D, D),
    'ffn2_norm': (DEPTH, D), 'ffn2_w_gate': (DEPTH, D, FFN), 'ffn2_w_up': (DEPTH, D, FFN), 'ffn2_w_down': (DEPTH, FFN, D),
    'final_norm': (D,),
}


def rope_tab(pos, dim):
    inv = (10000.0 ** (-np.arange(0, dim, 2, dtype=np.float32) / np.float32(dim))).astype(np.float32)
    ang = pos.astype(np.float32)[:, None] * inv[None, :]
    ang = np.concatenate([ang, ang], axis=-1)
    return np.cos(ang).astype(np.float32), np.sin(ang).astype(np.float32)


def rot_lhsT(blocks):
    n = sum(b for b in blocks)
    Rm = np.zeros((n, n), np.float32)
    o = 0
    for size in blocks:
        half = size // 2
        for i in range(half):
            Rm[o + i, o + i + half] = -1.0
            Rm[o + i + half, o + i] = 1.0
        o += size
    return np.ascontiguousarray(Rm.T)


def host_consts():
    c = {}
    c['c_ident'] = np.eye(128, dtype=np.float32)
    pos = np.arange(S_LEN)
    cr, sr = rope_tab(pos // 64, 64)
    cc, sc = rope_tab(pos % 64, 64)
    c['c_gq_cos'] = np.ascontiguousarray(np.concatenate([cr, cc], axis=1).T)
    c['c_gq_sin'] = np.ascontiguousarray(np.concatenate([sr, sc], axis=1).T)
    c['c_gq_RT'] = rot_lhsT([64, 64])
    c1, s1 = rope_tab(pos, 64)
    c['c_ml_cos'] = np.ascontiguousarray(c1.T)
    c['c_ml_sin'] = np.ascontiguousarray(s1.T)
    c['c_ml_RT'] = rot_lhsT([64])
    c.update(hy_consts())
    c.update(rw_consts())
    return c


SCRATCH = {
    'xres': ([D, S_LEN], F32), 'hT_d': ([D, S_LEN], BF16), 'pm_d': ([PM_ROWS, S_LEN], F32),
    'pdv_d': ([S_LEN, 256], BF16), 'ya_d': ([512, S_LEN], BF16), 'yb_d': ([512, S_LEN], BF16),
    'yc_d': ([512, S_LEN], BF16), 'yd_d': ([512, S_LEN], BF16), 'hyH_d': ([S_LEN, 2048], F32),
    'rw_d': ([11, 512, S_LEN], F32),
}


def build(phases=("load", "ffn1", "proj", "rwkv", "mla", "hyena", "gqa", "merge", "ffn2", "final"), depth=DEPTH,
          inject=(), expose=()):
    nc = bass.Bass("TRN2", target_bir_lowering=False)
    C = Ctx()
    C.nc = nc
    C.S = Sched(nc)
    C.x = DT(nc, "x", [S_LEN, D], F32, kind="ExternalInput")
    C.P = {}
    for k, shp in PARAM_SHAPES.items():
        C.P[k] = DT(nc, k, shp, F32, kind="ExternalInput")
    hc = host_consts()
    for k, v in hc.items():
        dt_ = F32 if v.dtype == np.float32 else BF16
        setattr(C, k, DT(nc, k, list(v.shape), dt_, kind="ExternalInput"))
    C.d_ident = C.c_ident
    C.out = DT(nc, "out", [S_LEN, D], F32, kind="ExternalOutput")
    for k, (shp, dt_) in SCRATCH.items():
        kind = "ExternalInput" if k in inject else ("ExternalOutput" if k in expose else "Internal")
        setattr(C, k, DT(nc, k, shp, dt_, kind=kind))
    C.final_norm = C.P['final_norm']
    C.ffn_norm = {1: C.P['ffn1_norm'].ap, 2: C.P['ffn2_norm'].ap}
    C.ffn_wg = {1: C.P['ffn1_w_gate'], 2: C.P['ffn2_w_gate']}
    C.ffn_wu = {1: C.P['ffn1_w_up'], 2: C.P['ffn2_w_up']}
    C.ffn_wd = {1: C.P['ffn1_w_down'], 2: C.P['ffn2_w_down']}
    load_consts(C)
    C.rw_stop = RW_STOP
    if "load" in phases:
        phase_load_x(C)
    for l in range(depth):
        if "ffn1" in phases:
            phase_ffn(C, l, 1)
        if "proj" in phases:
            phase_proj(C, l)
        if "rwkv" in phases:
            phase_rwkv(C, l)
        if "mla" in phases:
            phase_mla(C, l)
        if "hyena" in phases:
            phase_hyena(C, l)
        if "gqa" in phases:
            phase_gqa(C, l)
        if "merge" in phases:
            phase_merge(C, l)
        if "ffn2" in phases:
            phase_ffn(C, l, 2)
    if "final" in phases:
        phase_final_norm(C)
    C.S.barrier()
    C.S.finalize()
    return nc, hc


def kernel(**inputs):
    nc, hc = build()
    x = np.ascontiguousarray(np.asarray(inputs['x'], dtype=np.float32))
    n = x.shape[0]
    base = {k: np.ascontiguousarray(np.asarray(inputs[k], dtype=np.float32)) for k in PARAM_SHAPES}
    base.update(hc)
    in_maps = []
    for b in range(n):
        m = dict(base)
        m['x'] = x[b]
        in_maps.append(m)
    res = run_bass_kernel_spmd(nc, in_maps, core_ids=list(range(n)))
    return np.stack([r['out'] for r in res.results], axis=0)


def norm_tile(C, xa, hT, gcol, rstd, sq, p_s, nsub):
    S = C.S
    for s_i in range(nsub):
        for c in range(16):
            q_ = sq[c % 2]
            act(S, q_[:], xa[:, c, s_i * NT:(s_i + 1) * NT], AF.Square, r=[xa.b(c)], w=[q_.b()])
            mm(S, p_s[:], C.ones[:], q_[:], start=(c == 0), stop=(c == 15), r=[C.ones.b(), q_.b()], w=[p_s.b()])
        rsqrt(S, rstd[:, s_i * NT:(s_i + 1) * NT], p_s[:], 1.0 / D, C.epsc[:, 0:1], r=[p_s.b(), C.epsc.b()],
              w=[rstd.b(s_i)])
    for c in range(16):
        stt(S, hT[:, c, :], xa[:, c, :], gcol[:, c:c + 1], rstd[:], ALU.mult, ALU.mult,
            r=[xa.b(c), gcol.b()] + [rstd.b(i) for i in range(nsub)], w=[hT.b(c)])


def phase_proj(C, l):
    S = C.S
    nm = "pj%d_" % l
    TT = 1024
    NSUB = TT // NT
    win = C.P['w_in'].ap[l]
    gcol = sb(S, nm + "gcol", [128, 16])
    S.op("sp", lambda e: e.dma_start(out=gcol[:], in_=C.P['mix_norm'].ap[l].rearrange("(c p) -> p c", p=128),
                                     allow_slow_non_contiguous=True), w=[gcol.b()], dma=True)
    xa = sb(S, nm + "xa", [128, 16, TT])
    hT = sb(S, nm + "hT", [128, 16, TT], BF16)
    rstd = sb(S, nm + "rstd", [128, TT])
    sq = [sb(S, nm + "sq%d" % i, [128, NT]) for i in range(2)]
    wt = [sb(S, nm + "wt%d" % i, [128, 16, 512], BF16) for i in range(2)]
    stg = [sb(S, nm + "stg%d" % i, [128, TT]) for i in range(2)]
    stv = [sb(S, nm + "stv%d" % i, [128, 256], BF16) for i in range(2)]
    pp = [ps(S, nm + "pp%d" % i, [128, NT]) for i in range(4)]
    p_s = ps(S, nm + "pss", [128, NT])
    groups = []
    c0 = 0
    while c0 < PM_ROWS:
        n = min(512, PM_ROWS - c0)
        groups.append((c0, n))
        c0 += n
    it = 0
    kk = 0
    ks = 0
    for tt_i in range(S_LEN // TT):
        t0 = tt_i * TT
        for c in range(16):
            dma(S, xa[:, c, :], C.xres[c * 128:(c + 1) * 128, t0:t0 + TT], r=[C.xres.b((c, tt_i))],
                w=[xa.b(c)], q="sp" if c % 2 == 0 else "act")
        norm_tile(C, xa, hT, gcol, rstd, sq, p_s, NSUB)
        hT_r = [hT.b(c) for c in range(16)]
        for c in range(16):
            dma(S, C.hT_d[c * 128:(c + 1) * 128, t0:t0 + TT], hT[:, c, :], r=[hT.b(c)], w=[C.hT_d.b((c, tt_i))], q="sp")
        for (c0, n) in groups:
            w_t = wt[it % 2]
            it += 1
            dma(S, w_t[:, :, 0:n], win[:, c0:c0 + n].rearrange("(c p) f -> p c f", p=128), w=[w_t.b()], q="pool")
            for j in range((n + 127) // 128):
                m = min(128, n - j * 128)
                st_ = stg[ks % 2]
                ks += 1
                for s_i in range(NSUB):
                    p = pp[kk % 4]
                    kk += 1
                    tsl = slice(s_i * NT, (s_i + 1) * NT)
                    for c in range(16):
                        mm(S, p[0:m, :], w_t[:, c, j * 128:j * 128 + m], hT[:, c, tsl], start=(c == 0), stop=(c == 15),
                           r=[w_t.b(), hT_r[c]], w=[p.b()])
                    cp(S, st_[0:m, tsl], p[0:m, :], r=[p.b()], w=[st_.b()], eng="dve" if kk % 2 == 0 else "act")
                dma(S, C.pm_d[c0 + j * 128:c0 + j * 128 + m, t0:t0 + TT], st_[0:m, :], r=[st_.b()],
                    w=[C.pm_d.b((c0 + j * 128, tt_i))], q="sp")
        w_t = wt[it % 2]
        it += 1
        dma(S, w_t[:, :, 0:256], win[:, PM_ROWS:PM_ROWS + 256].rearrange("(c p) f -> p c f", p=128), w=[w_t.b()], q="pool")
        for tc in range(TT // 128):
            p = pp[kk % 4]
            kk += 1
            sv = stv[tc % 2]
            for c in range(16):
                mm(S, p[:, 0:256], hT[:, c, tc * 128:(tc + 1) * 128], w_t[:, c, 0:256], start=(c == 0), stop=(c == 15),
                   r=[w_t.b(), hT_r[c]], w=[p.b()])
            cp(S, sv[:], p[:, 0:256], r=[p.b()], w=[sv.b()], eng="dve" if tc % 2 == 0 else "act")
            dma(S, C.pdv_d[t0 + tc * 128:t0 + (tc + 1) * 128, :], sv[:], r=[sv.b()], w=[C.pdv_d.b(tt_i * 8 + tc)], q="sp")
    S.barrier()
    S.release()


def attn_core(C, nm, kq_parts, v_t, v_off, scale, y_d, row0, onesb, bufs):
    S = C.S
    p_sc, p_o, p_r, pT, rinv, yst = bufs
    ksc = 0
    for ti in range(S_LEN // NT):
        tsl = slice(ti * NT, (ti + 1) * NT)
        po = p_o[ti % 2]
        pr = p_r[ti % 2]
        for sc in range(16):
            psc = p_sc[ksc % 2]
            p_t = pT[ksc % 3]
            ksc += 1
            for i, (kT, qT, K) in enumerate(kq_parts):
                mm(S, psc[:], kT[0:K, sc * 128:(sc + 1) * 128], qT[0:K, tsl], start=(i == 0), stop=(i == len(kq_parts) - 1),
                   r=[kT.b(), qT.b()], w=[psc.b()])
            act(S, p_t[:], psc[:], AF.Exp, scale=scale, r=[psc.b()], w=[p_t.b()])
            mm(S, po[:], v_t[:, sc, v_off:v_off + 128], p_t[:], start=(sc == 0), stop=(sc == 15), r=[v_t.b(), p_t.b()], w=[po.b()])
            mm(S, pr[:], onesb[:], p_t[:], start=(sc == 0), stop=(sc == 15), r=[onesb.b(), p_t.b()], w=[pr.b()])
        ri = rinv[ti % 2]
        ys = yst[ti % 2]
        S.op("dve", lambda e, ri=ri, pr=pr: e.reciprocal(ri[:], pr[:]), r=[pr.b()], w=[ri.b()])
        tt(S, ys[:], po[:], ri[:], ALU.mult, r=[po.b(), ri.b()], w=[ys.b()])
        dma(S, y_d[row0:row0 + 128, tsl], ys[:], r=[ys.b()], w=[y_d.b((row0, ti))], q="sp")


def attn_bufs(S, nm):
    p_sc = [ps(S, nm + "psc%d" % i, [128, NT]) for i in range(2)]
    p_o = [ps(S, nm + "po%d" % i, [128, NT]) for i in range(2)]
    p_r = [ps(S, nm + "pr%d" % i, [128, NT]) for i in range(2)]
    pT = [sb(S, nm + "pT%d" % i, [128, NT], BF16) for i in range(3)]
    rinv = [sb(S, nm + "ri%d" % i, [128, NT]) for i in range(2)]
    yst = [sb(S, nm + "ys%d" % i, [128, NT], BF16) for i in range(2)]
    return p_sc, p_o, p_r, pT, rinv, yst


def rope_norm_head(C, nm, src_rows, nrow, gain_col, cosT, sinT, RT, dst, tmp, p_a, p_b, do_norm):
    S = C.S
    xin, xn, t1, rs = tmp
    dma(S, xin[0:nrow, :], C.pm_d[src_rows:src_rows + nrow, :], w=[xin.b()], q="act")
    for ti in range(S_LEN // NT):
        tsl = slice(ti * NT, (ti + 1) * NT)
        if do_norm:
            act(S, t1[0:nrow, :], xin[0:nrow, tsl], AF.Square, r=[xin.b()], w=[t1.b()])
            mm(S, p_a[0:nrow, :], C.ones[0:nrow, 0:nrow], t1[0:nrow, :], r=[C.ones.b(), t1.b()], w=[p_a.b()])
            rsqrt(S, rs[0:nrow, :], p_a[0:nrow, :], 1.0 / nrow, C.epsc[0:nrow, 0:1], r=[p_a.b(), C.epsc.b()], w=[rs.b()])
            stt(S, xn[0:nrow, :], xin[0:nrow, tsl], gain_col[0:nrow, 0:1], rs[0:nrow, :], ALU.mult, ALU.mult,
                r=[xin.b(), gain_col.b(), rs.b()], w=[xn.b()])
            src = xn[0:nrow, :]
        else:
            cp(S, xn[0:nrow, :], xin[0:nrow, tsl], r=[xin.b()], w=[xn.b()], eng="pool")
            src = xn[0:nrow, :]
        mm(S, p_b[0:nrow, :], RT[0:nrow, 0:nrow], src, r=[RT.b(), xn.b()], w=[p_b.b()])
        tt(S, t1[0:nrow, :], p_b[0:nrow, :], sinT[0:nrow, tsl], ALU.mult, r=[p_b.b(), sinT.b()], w=[t1.b()])
        tt(S, xn[0:nrow, :], src, cosT[0:nrow, tsl], ALU.mult, r=[xn.b(), cosT.b()], w=[xn.b()], eng="pool")
        tt(S, dst[0:nrow, tsl], xn[0:nrow, :], t1[0:nrow, :], ALU.add, r=[xn.b(), t1.b()], w=[dst.b()])


def phase_gqa(C, l):
    S = C.S
    nm = "gq%d_" % l
    cosT = sb(S, nm + "cos", [128, S_LEN])
    sinT = sb(S, nm + "sin", [128, S_LEN])
    RT = sb(S, nm + "RT", [128, 128])
    dma(S, cosT[:], C.c_gq_cos[:], w=[cosT.b()])
    dma(S, sinT[:], C.c_gq_sin[:], w=[sinT.b()], q="act")
    dma(S, RT[:], C.c_gq_RT[:], w=[RT.b()])
    gq = sb(S, nm + "gq", [128, 2])
    S.op("sp", lambda e: e.dma_start(out=gq[:, 0:1], in_=C.P['gqa_q_norm'].ap[l].rearrange("(p o) -> p o", o=1),
                                     allow_slow_non_contiguous=True), w=[gq.b()], dma=True)
    gk = sb(S, nm + "gk", [128, 2])
    S.op("sp", lambda e: e.dma_start(out=gk[:, 0:1], in_=C.P['gqa_k_norm'].ap[l].rearrange("(p o) -> p o", o=1),
                                     allow_slow_non_contiguous=True), w=[gk.b()], dma=True)
    onesb = sb(S, nm + "onesb", [128, 128], BF16)
    mset(S, onesb[:], 1.0, w=[onesb.b()])
    tmp = (sb(S, nm + "xin", [128, S_LEN]), sb(S, nm + "xn", [128, NT]), sb(S, nm + "t1", [128, NT]), sb(S, nm + "rs", [128, NT]))
    p_a = ps(S, nm + "pa", [128, NT])
    p_b = ps(S, nm + "pb", [128, NT])
    qT = [sb(S, nm + "qT%d" % h, [128, S_LEN], BF16) for h in range(4)]
    kT = [sb(S, nm + "kT%d" % g, [128, S_LEN], BF16) for g in range(2)]
    v_t = sb(S, nm + "v", [128, 16, 256], BF16)
    dma(S, v_t[:], C.pdv_d.ap.rearrange("(c p) f -> p c f", p=128), w=[v_t.b()], q="act")
    for h in range(4):
        rope_norm_head(C, nm, OFF_GQA + h * 128, 128, gq, cosT, sinT, RT, qT[h], tmp, p_a, p_b, True)
    for g in range(2):
        rope_norm_head(C, nm, OFF_GQA + 512 + g * 128, 128, gk, cosT, sinT, RT, kT[g], tmp, p_a, p_b, True)
    bufs = attn_bufs(S, nm)
    for h in range(4):
        g = h // 2
        attn_core(C, nm, [(kT[g], qT[h], 128)], v_t, g * 128, 128.0 ** -0.5, C.yd_d, h * 128, onesb, bufs)
    S.barrier()
    S.release()


def phase_mla(C, l):
    S = C.S
    nm = "ml%d_" % l
    cosT = sb(S, nm + "cos", [64, S_LEN])
    sinT = sb(S, nm + "sin", [64, S_LEN])
    RT = sb(S, nm + "RT", [64, 64])
    dma(S, cosT[:], C.c_ml_cos[:], w=[cosT.b()])
    dma(S, sinT[:], C.c_ml_sin[:], w=[sinT.b()], q="act")
    dma(S, RT[:], C.c_ml_RT[:], w=[RT.b()])
    wq = sb(S, nm + "wq", [128, 3, 768], BF16)
    dma(S, wq[:], C.P['mla_w_q_up'].ap[l].rearrange("(c p) f -> p c f", p=128), w=[wq.b()], q="pool")
    wkv = sb(S, nm + "wkv", [128, 1024], BF16)
    dma(S, wkv[:], C.P['mla_w_kv_up'].ap[l], w=[wkv.b()], q="pool")
    gq = sb(S, nm + "gq", [128, 4])
    S.op("sp", lambda e: e.dma_start(out=gq[:, 0:3], in_=C.P['mla_q_norm'].ap[l].rearrange("(c p) -> p c", p=128),
                                     allow_slow_non_contiguous=True), w=[gq.b()], dma=True)
    S.op("sp", lambda e: e.dma_start(out=gq[:, 3:4], in_=C.P['mla_kv_norm'].ap[l].rearrange("(p o) -> p o", o=1),
                                     allow_slow_non_contiguous=True), w=[gq.b()], dma=True)
    onesb = sb(S, nm + "onesb", [128, 128], BF16)
    mset(S, onesb[:], 1.0, w=[onesb.b()])
    xq = sb(S, nm + "xq", [128, 3, S_LEN])
    dma(S, xq[:], C.pm_d.ap[OFF_MLA:OFF_MLA + 384, :].rearrange("(c p) t -> p c t", p=128), w=[xq.b()])
    xkv = sb(S, nm + "xkv", [128, S_LEN])
    dma(S, xkv[:], C.pm_d[OFF_MLA + 384:OFF_MLA + 512, :], w=[xkv.b()], q="act")
    qn = sb(S, nm + "qn", [128, 3, S_LEN], BF16)
    kvn = sb(S, nm + "kvn", [128, S_LEN], BF16)
    t1 = sb(S, nm + "t1", [128, NT])
    t2 = sb(S, nm + "t2", [128, NT])
    xn = sb(S, nm + "xn", [128, NT])
    rs = sb(S, nm + "rs", [128, NT])
    p_a = ps(S, nm + "pa", [128, NT])
    p_b = ps(S, nm + "pb", [128, NT])
    nti = S_LEN // NT
    for ti in range(nti):
        tsl = slice(ti * NT, (ti + 1) * NT)
        for c in range(3):
            act(S, t1[:], xq[:, c, tsl], AF.Square, r=[xq.b()], w=[t1.b()])
            mm(S, p_a[:], C.ones[:], t1[:], start=(c == 0), stop=(c == 2), r=[C.ones.b(), t1.b()], w=[p_a.b()])
        rsqrt(S, rs[:], p_a[:], 1.0 / 384, C.epsc[:, 0:1], r=[p_a.b(), C.epsc.b()], w=[rs.b()])
        for c in range(3):
            stt(S, qn[:, c, tsl], xq[:, c, tsl], gq[:, c:c + 1], rs[:], ALU.mult, ALU.mult, r=[xq.b(), gq.b(), rs.b()], w=[qn.b()])
        act(S, t1[:], xkv[:, tsl], AF.Square, r=[xkv.b()], w=[t1.b()])
        mm(S, p_a[:], C.ones[:], t1[:], r=[C.ones.b(), t1.b()], w=[p_a.b()])
        rsqrt(S, rs[:], p_a[:], 1.0 / 128, C.epsc[:, 0:1], r=[p_a.b(), C.epsc.b()], w=[rs.b()])
        stt(S, kvn[:, tsl], xkv[:, tsl], gq[:, 3:4], rs[:], ALU.mult, ALU.mult, r=[xkv.b(), gq.b(), rs.b()], w=[kvn.b()])
    qnope = [sb(S, nm + "qnope%d" % h, [128, S_LEN], BF16) for h in range(4)]
    qrope = [sb(S, nm + "qrope%d" % h, [64, S_LEN], BF16) for h in range(4)]
    knope = [sb(S, nm + "knope%d" % h, [128, S_LEN], BF16) for h in range(4)]
    krope = sb(S, nm + "krope", [64, S_LEN], BF16)
    v_t = sb(S, nm + "v", [128, 16, 512], BF16)
    k = 0
    for h in range(4):
        for ti in range(nti):
            tsl = slice(ti * NT, (ti + 1) * NT)
            for c in range(3):
                mm(S, p_a[:], wq[:, c, h * 192:h * 192 + 128], qn[:, c, tsl], start=(c == 0), stop=(c == 2), r=[wq.b(), qn.b()], w=[p_a.b()])
            cp(S, qnope[h][:, tsl], p_a[:], r=[p_a.b()], w=[qnope[h].b()], eng="act")
            mm(S, p_a[:], wkv[:, h * 256:h * 256 + 128], kvn[:, tsl], r=[wkv.b(), kvn.b()], w=[p_a.b()])
            cp(S, knope[h][:, tsl], p_a[:], r=[p_a.b()], w=[knope[h].b()], eng="act")
            for c in range(3):
                mm(S, p_b[0:64, :], wq[:, c, h * 192 + 128:h * 192 + 192], qn[:, c, tsl], start=(c == 0), stop=(c == 2), r=[wq.b(), qn.b()], w=[p_b.b()])
            cp(S, xn[0:64, :], p_b[0:64, :], r=[p_b.b()], w=[xn.b()])
            mm(S, p_b[0:64, :], RT[:, :], xn[0:64, :], r=[RT.b(), xn.b()], w=[p_b.b()])
            tt(S, t1[0:64, :], p_b[0:64, :], sinT[:, tsl], ALU.mult, r=[p_b.b(), sinT.b()], w=[t1.b()])
            tt(S, t2[0:64, :], xn[0:64, :], cosT[:, tsl], ALU.mult, r=[xn.b(), cosT.b()], w=[t2.b()], eng="pool")
            tt(S, qrope[h][:, tsl], t2[0:64, :], t1[0:64, :], ALU.add, r=[t2.b(), t1.b()], w=[qrope[h].b()])
    xkr = sb(S, nm + "xkr", [64, S_LEN])
    dma(S, xkr[:], C.pm_d[OFF_MLA + 512:OFF_MLA + 576, :], w=[xkr.b()])
    for ti in range(nti):
        tsl = slice(ti * NT, (ti + 1) * NT)
        mm(S, p_b[0:64, :], RT[:, :], xkr[:, tsl], r=[RT.b(), xkr.b()], w=[p_b.b()])
        tt(S, t1[0:64, :], p_b[0:64, :], sinT[:, tsl], ALU.mult, r=[p_b.b(), sinT.b()], w=[t1.b()])
        tt(S, t2[0:64, :], xkr[:, tsl], cosT[:, tsl], ALU.mult, r=[xkr.b(), cosT.b()], w=[t2.b()], eng="pool")
        tt(S, krope[:, tsl], t2[0:64, :], t1[0:64, :], ALU.add, r=[t2.b(), t1.b()], w=[krope.b()])
    for sc in range(16):
        for h in range(4):
            mm(S, p_a[:, h * 128:(h + 1) * 128], kvn[:, sc * 128:(sc + 1) * 128], wkv[:, h * 256 + 128:h * 256 + 256],
               r=[wkv.b(), kvn.b()], w=[p_a.b()])
        cp(S, v_t[:, sc, :], p_a[:], r=[p_a.b()], w=[v_t.b()], eng="dve" if sc % 2 == 0 else "act")
    bufs = attn_bufs(S, nm)
    for h in range(4):
        attn_core(C, nm, [(knope[h], qnope[h], 128), (krope, qrope[h], 64)], v_t, h * 128, 192.0 ** -0.5, C.yb_d, h * 128, onesb, bufs)
    S.barrier()
    S.release()


def phase_merge(C, l):
    S = C.S
    nm = "mg%d_" % l
    TT = 1024
    NSUB = TT // NT
    win = C.P['w_in'].ap[l]
    wbr = C.P['w_branch'].ap[l]
    wout = C.P['w_out'].ap[l]
    ys_d = [C.ya_d, C.yb_d, C.yc_d, C.yd_d]
    hT = sb(S, nm + "hT", [128, 16, TT], BF16)
    yT = [sb(S, nm + "yT%d" % n, [128, 4, TT], BF16) for n in range(4)]
    mT = sb(S, nm + "mT", [128, 16, TT], BF16)
    C.mg_acc = {(j, s_i): sb(S, nm + "acc%d_%d" % (j, s_i), [128, NT]) for j in range(4) for s_i in range(NSUB)}
    gw = [sb(S, nm + "gw%d" % i, [128, 16, 512], BF16) for i in range(2)]
    wb = [sb(S, nm + "wb%d" % i, [128, 4, 512], BF16) for i in range(2)]
    sg = [sb(S, nm + "sg%d" % i, [128, NT]) for i in range(2)]
    tmp = [sb(S, nm + "tmp%d" % i, [128, NT]) for i in range(2)]
    xa = [sb(S, nm + "xa%d" % i, [128, TT]) for i in range(2)]
    p_g = [ps(S, nm + "pg%d" % i, [128, NT]) for i in range(2)]
    p_b = [ps(S, nm + "pb%d" % i, [128, NT]) for i in range(2)]
    p_o = [ps(S, nm + "po%d" % i, [128, NT]) for i in range(2)]
    it = 0
    kk = 0
    for tt_i in range(S_LEN // TT):
        t0 = tt_i * TT
        for c in range(16):
            dma(S, hT[:, c, :], C.hT_d[c * 128:(c + 1) * 128, t0:t0 + TT], r=[C.hT_d.b((c, tt_i))], w=[hT.b()],
                q="sp" if c % 2 == 0 else "act")
        for n in range(4):
            dma(S, yT[n][:], ys_d[n].ap[:, t0:t0 + TT].rearrange("(c p) t -> p c t", p=128), w=[yT[n].b()], q="act")
        for dg in range(4):
            gws = []
            for n in range(4):
                g_t, b_t = gw[it % 2], wb[it % 2]
                it += 1
                col = OFF_GATE + n * D + dg * 512
                dma(S, g_t[:], win[:, col:col + 512].rearrange("(c p) f -> p c f", p=128), w=[g_t.b()], q="pool")
                dma(S, b_t[:], wbr[n][:, dg * 512:(dg + 1) * 512].rearrange("(c p) f -> p c f", p=128), w=[b_t.b()], q="pool")
                for j in range(4):
                    dc = dg * 4 + j
                    for s_i in range(NSUB):
                        tsl = slice(s_i * NT, (s_i + 1) * NT)
                        pg, pb = p_g[kk % 2], p_b[kk % 2]
                        sg_t, tm_t = sg[kk % 2], tmp[kk % 2]
                        kk += 1
                        for c in range(16):
                            mm(S, pg[:], g_t[:, c, j * 128:(j + 1) * 128], hT[:, c, tsl], start=(c == 0), stop=(c == 15),
                               r=[g_t.b(), hT.b()], w=[pg.b()])
                        for c in range(4):
                            mm(S, pb[:], b_t[:, c, j * 128:(j + 1) * 128], yT[n][:, c, tsl], start=(c == 0), stop=(c == 3),
                               r=[b_t.b(), yT[n].b()], w=[pb.b()])
                        act(S, sg_t[:], pg[:], AF.Sigmoid, r=[pg.b()], w=[sg_t.b()])
                        acc = C.mg_acc[(j, s_i)]
                        if n == 0:
                            tt(S, acc[:], sg_t[:], pb[:], ALU.mult, r=[sg_t.b(), pb.b()], w=[acc.b()])
                        else:
                            tt(S, tm_t[:], sg_t[:], pb[:], ALU.mult, r=[sg_t.b(), pb.b()], w=[tm_t.b()])
                            if n < 3:
                                tt(S, acc[:], acc[:], tm_t[:], ALU.add, r=[acc.b(), tm_t.b()], w=[acc.b()], eng="pool")
                            else:
                                tt(S, mT[:, dc, tsl], acc[:], tm_t[:], ALU.add, r=[acc.b(), tm_t.b()], w=[mT.b(dc)], eng="pool")
        for og in range(4):
            w_t = gw[it % 2]
            it += 1
            dma(S, w_t[:], wout[:, og * 512:(og + 1) * 512].rearrange("(c p) f -> p c f", p=128), w=[w_t.b()], q="pool")
            for j in range(4):
                dc = og * 4 + j
                x_t = xa[dc % 2]
                dma(S, x_t[:], C.xres[dc * 128:(dc + 1) * 128, t0:t0 + TT], r=[C.xres.b((dc, tt_i))], w=[x_t.b()], q="sp")
                for s_i in range(NSUB):
                    tsl = slice(s_i * NT, (s_i + 1) * NT)
                    po = p_o[kk % 2]
                    kk += 1
                    for c in range(16):
                        mm(S, po[:], w_t[:, c, j * 128:(j + 1) * 128], mT[:, c, tsl], start=(c == 0), stop=(c == 15),
                           r=[w_t.b(), mT.b(c)], w=[po.b()])
                    tt(S, x_t[:, tsl], x_t[:, tsl], po[:], ALU.add, r=[x_t.b(), po.b()], w=[x_t.b()])
                dma(S, C.xres[dc * 128:(dc + 1) * 128, t0:t0 + TT], x_t[:], r=[x_t.b()], w=[C.xres.b((dc, tt_i))], q="sp")
    S.barrier()
    S.release()


def hy_consts():
    L = S_LEN
    c = {}
    t = np.linspace(0.0, 1.0, L, dtype=np.float32)[:, None]
    w_ang = (2.0 * math.pi * np.arange(L, dtype=np.float32) / L).astype(np.float32)
    fr = np.linspace(1e-4, 15.0, 16, dtype=np.float32)
    ang = w_ang[:, None] * fr[None, :]
    z = np.concatenate([t, np.cos(ang), -np.sin(ang)], axis=-1).astype(np.float32)
    c['c_hy_z'] = np.ascontiguousarray(z.T)
    deltas = np.abs(np.linspace(math.log(1e-2) / 1.5, math.log(1e-2) / 0.3, 512)).astype(np.float32)
    win = np.exp(-t * deltas[None, :]).astype(np.float32)
    c['c_hy_win'] = np.ascontiguousarray(win.reshape(16, 128, 512).transpose(1, 0, 2))
    n = np.arange(L, dtype=np.int64)
    k = np.mod(np.outer(n, 2 * n + 1), 8192)
    angm = (2.0 * math.pi / 8192.0) * k.astype(np.float64)
    Cm = np.cos(angm)
    Sm = np.sin(angm)
    bf = ml_dtypes.bfloat16

    def t_major(M):
        return np.ascontiguousarray(M.reshape(16, 128, 16, 128).transpose(2, 1, 0, 3).reshape(16, 128, 2048).astype(bf))

    def f_major(M):
        MT = M.T
        return np.ascontiguousarray(MT.reshape(16, 128, 16, 128).transpose(2, 1, 0, 3).reshape(16, 128, 2048).astype(bf))

    c['c_hy_Ct'] = t_major(Cm)
    c['c_hy_St'] = t_major(Sm)
    c['c_hy_Cf'] = f_major(Cm)
    c['c_hy_Sf'] = f_major(Sm)
    return c


def sin_act(S, out, arg, tmp, bufs_r, w):
    s4, s8, q = tmp
    act(S, s4, arg, AF.Sin, scale=0.25, r=bufs_r, w=[w[1]])
    act(S, s8, arg, AF.Sin, scale=0.125, r=bufs_r, w=[w[2]])
    tt(S, q, s8, s8, ALU.mult, r=[w[2]], w=[w[3]])
    ts(S, q, q, -2.0, 1.0, ALU.mult, ALU.add, r=[w[3]], w=[w[3]])
    tt(S, s8, s4, q, ALU.mult, r=[w[1], w[3]], w=[w[2]])
    tt(S, q, s4, s4, ALU.mult, r=[w[1]], w=[w[3]])
    ts(S, q, q, -2.0, 1.0, ALU.mult, ALU.add, r=[w[3]], w=[w[3]])
    stt(S, out, s8, 4.0, q, ALU.mult, ALU.mult, r=[w[2], w[3]], w=[w[0]])


def phase_hyena(C, l):
    S = C.S
    nm = "hy%d_" % l
    P = C.P
    nti = S_LEN // NT
    zT = sb(S, nm + "zT", [33, S_LEN])
    dma(S, zT[:], C.c_hy_z[:], w=[zT.b()])
    w1 = sb(S, nm + "w1", [33, 64])
    dma(S, w1[:], P['hy_w1'].ap[l], w=[w1.b()])
    w2 = sb(S, nm + "w2", [64, 64])
    dma(S, w2[:], P['hy_w2'].ap[l], w=[w2.b()])
    w3 = sb(S, nm + "w3", [64, 64])
    dma(S, w3[:], P['hy_w3'].ap[l], w=[w3.b()])
    w4 = sb(S, nm + "w4", [64, 2048])
    dma(S, w4[:], P['hy_w4'].ap[l], w=[w4.b()], q="act")
    cols = sb(S, nm + "cols", [64, 8])
    for i, k in enumerate(['hy_b1', 'hy_b2', 'hy_b3']):
        S.op("sp", lambda e, i=i, k=k: e.dma_start(out=cols[:, i:i + 1], in_=P[k].ap[l].rearrange("(p o) -> p o", o=1),
                                                   allow_slow_non_contiguous=True), w=[cols.b()], dma=True)
    S.op("sp", lambda e: e.dma_start(out=cols[:, 3:6], in_=P['hy_freq'].ap[l].rearrange("k c -> c k"),
                                     allow_slow_non_contiguous=True), w=[cols.b()], dma=True)
    bias_s = sb(S, nm + "bias", [128, 1024])
    dma(S, bias_s[:], P['hy_bias'].ap[l].rearrange("o c -> (o c)").partition_broadcast(128), w=[bias_s.b()])
    ts(S, bias_s[:], bias_s[:], 1.0 / 2048.0, None, ALU.mult, r=[bias_s.b()], w=[bias_s.b()])
    win = sb(S, nm + "win", [128, 16, 512])
    dma(S, win[:], C.c_hy_win[:], w=[win.b()], q="act")
    hA = sb(S, nm + "hA", [64, S_LEN])
    hB = sb(S, nm + "hB", [64, S_LEN])
    arg = sb(S, nm + "arg", [64, NT])
    s4 = sb(S, nm + "s4", [64, NT])
    s8 = sb(S, nm + "s8", [64, NT])
    qq = sb(S, nm + "qq", [64, NT])
    p_a = ps(S, nm + "pa", [128, NT])
    p_b = ps(S, nm + "pb", [128, NT])
    p_c = ps(S, nm + "pc", [128, NT])
    p_d = ps(S, nm + "pd", [128, NT])
    layers = [(w1, zT, 33, hA, 0), (w2, hA, 64, hB, 1), (w3, hB, 64, hA, 2)]
    for (w_, src, K, dst, li) in layers:
        for ti in range(nti):
            tsl = slice(ti * NT, (ti + 1) * NT)
            mm(S, p_a[0:64, :], w_[0:K, :], src[0:K, tsl], r=[w_.b(), src.b()], w=[p_a.b()])
            ts(S, arg[:], p_a[0:64, :], cols[:, li:li + 1], cols[:, 3 + li:4 + li], ALU.add, ALU.mult,
               r=[p_a.b(), cols.b()], w=[arg.b()])
            sin_act(S, dst[:, tsl], arg[:], (s4[:], s8[:], qq[:]), [arg.b()], [dst.b(), s4.b(), s8.b(), qq.b()])
    h3 = hA
    hs = sb(S, nm + "hs", [128, 16, 1024], BF16)
    hd = sb(S, nm + "hd", [128, 16, 1024], BF16)
    f0 = sb(S, nm + "f0", [128, 1024])
    f1 = sb(S, nm + "f1", [128, 1024])
    pg = [p_a, p_b, p_c, p_d]
    for tc in range(16):
        for g in range(4):
            mm(S, pg[g][:], h3[:, tc * 128:(tc + 1) * 128], w4[:, g * 512:(g + 1) * 512], r=[h3.b(), w4.b()], w=[pg[g].b()])
        for o in range(2):
            tt(S, f0[:, o * 512:(o + 1) * 512], pg[o][:], win[:, tc, :], ALU.mult, r=[pg[o].b(), win.b()], w=[f0.b()])
            tt(S, f1[:, o * 512:(o + 1) * 512], pg[2 + o][:], win[:, tc, :], ALU.mult, r=[pg[2 + o].b(), win.b()], w=[f1.b()])
        if tc == 0:
            mset(S, f1[0:1, :], 0.0, w=[f1.b()])
        tt(S, hs[:, tc, :], f0[:], f1[:], ALU.add, r=[f0.b(), f1.b()], w=[hs.b()], eng="pool")
        tt(S, hd[:, tc, :], f0[:], f1[:], ALU.subtract, r=[f0.b(), f1.b()], w=[hd.b()])
    ct = [sb(S, nm + "ct%d" % i, [128, 2048], BF16) for i in range(2)]
    st = [sb(S, nm + "st%d" % i, [128, 2048], BF16) for i in range(2)]
    hst = [sb(S, nm + "hst%d" % i, [128, 4, 512]) for i in range(2)]
    for fc in range(16):
        c_t, s_t, h_t = ct[fc % 2], st[fc % 2], hst[fc % 2]
        dma(S, c_t[:], C.c_hy_Ct.ap[fc], w=[c_t.b()], q="sp")
        dma(S, s_t[:], C.c_hy_St.ap[fc], w=[s_t.b()], q="act")
        for o in range(2):
            pr, pi = pg[o * 2], pg[o * 2 + 1]
            for tc in range(16):
                mm(S, pr[:], c_t[:, tc * 128:(tc + 1) * 128], hs[:, tc, o * 512:(o + 1) * 512], start=(tc == 0), stop=(tc == 15),
                   r=[c_t.b(), hs.b()], w=[pr.b()])
            for tc in range(16):
                mm(S, pi[:], s_t[:, tc * 128:(tc + 1) * 128], hd[:, tc, o * 512:(o + 1) * 512], start=(tc == 0), stop=(tc == 15),
                   r=[s_t.b(), hd.b()], w=[pi.b()])
            stt(S, h_t[:, o * 2, :], pr[:], 1.0 / 2048.0, bias_s[:, o * 512:(o + 1) * 512], ALU.mult, ALU.add,
                r=[pr.b(), bias_s.b()], w=[h_t.b()])
            act(S, h_t[:, o * 2 + 1, :], pi[:], AF.Copy, scale=1.0 / 2048.0, r=[pi.b()], w=[h_t.b()])
        dma(S, C.hyH_d.ap[fc * 128:(fc + 1) * 128, :].rearrange("p (k c) -> p k c", k=4), h_t[:], r=[h_t.b()],
            w=[C.hyH_d.b(fc)], q="sp")
    S.barrier()
    S.release()
    swc = sb(S, nm + "swc", [128, 12, 4])
    for k in range(3):
        S.op("sp", lambda e, k=k: e.dma_start(out=swc[:, :, k:k + 1],
                                              in_=P['hy_short_w'].ap[l][k].rearrange("(c p o) -> p c o", p=128, o=1),
                                              allow_slow_non_contiguous=True), w=[swc.b()], dma=True)
    S.op("sp", lambda e: e.dma_start(out=swc[:, :, 3:4], in_=P['hy_short_b'].ap[l].rearrange("(c p o) -> p c o", p=128, o=1),
                                     allow_slow_non_contiguous=True), w=[swc.b()], dma=True)
    x1_tm = sb(S, nm + "x1tm", [128, 16, 512])
    x2T = sb(S, nm + "x2T", [128, 4, S_LEN])
    v_tm = sb(S, nm + "vtm", [128, 16, 512], BF16)
    Yr = sb(S, nm + "Yr", [128, 16, 512], BF16)
    Ys = sb(S, nm + "Ys", [128, 16, 512], BF16)
    ycT = sb(S, nm + "ycT", [128, 4, S_LEN], BF16)
    pin = [sb(S, nm + "pin%d" % i, [128, S_LEN]) for i in range(2)]
    u = sb(S, nm + "u", [128, S_LEN])
    pt = [ps(S, nm + "pt%d" % i, [128, NT]) for i in range(2)]
    kk = 0
    for ch in range(12):
        p_in = pin[ch % 2]
        dma(S, p_in[:], C.pm_d[OFF_HY + ch * 128:OFF_HY + (ch + 1) * 128, :], w=[p_in.b()], q="sp" if ch % 2 == 0 else "act")
        dst = x2T[:, ch - 4, :] if 4 <= ch < 8 else u[:]
        dbuf = x2T.b() if 4 <= ch < 8 else u.b()
        ts(S, dst, p_in[:], swc[:, ch, 1:2], swc[:, ch, 3:4], ALU.mult, ALU.add, r=[p_in.b(), swc.b()], w=[dbuf])
        d1 = x2T[:, ch - 4, 1:S_LEN] if 4 <= ch < 8 else u[:, 1:S_LEN]
        d2 = x2T[:, ch - 4, 0:S_LEN - 1] if 4 <= ch < 8 else u[:, 0:S_LEN - 1]
        stt(S, d1, p_in[:, 0:S_LEN - 1], swc[:, ch, 0:1], d1, ALU.mult, ALU.add, r=[p_in.b(), swc.b(), dbuf], w=[dbuf])
        stt(S, d2, p_in[:, 1:S_LEN], swc[:, ch, 2:3], d2, ALU.mult, ALU.add, r=[p_in.b(), swc.b(), dbuf], w=[dbuf])
        if ch < 4 or ch >= 8:
            cc = ch if ch < 4 else ch - 8
            tgt = x1_tm if ch < 4 else v_tm
            for tc in range(16):
                p = pt[kk % 2]
                kk += 1
                tr(S, p[:, 0:128], u[:, tc * 128:(tc + 1) * 128], C.ident[:], r=[u.b(), C.ident.b()], w=[p.b()])
                cp(S, tgt[:, tc, cc * 128:(cc + 1) * 128], p[:, 0:128], r=[p.b()], w=[tgt.b()], eng="dve" if kk % 2 == 0 else "act")
    ct = [sb(S, nm + "dct%d" % i, [128, 2048], BF16) for i in range(2)]
    st = [sb(S, nm + "dst%d" % i, [128, 2048], BF16) for i in range(2)]
    hst = [sb(S, nm + "dhst%d" % i, [128, 2, 512]) for i in range(2)]
    ur = [sb(S, nm + "ur%d" % i, [128, 512]) for i in range(2)]
    us = [sb(S, nm + "us%d" % i, [128, 512]) for i in range(2)]
    m1 = sb(S, nm + "m1", [128, 512])
    m2 = sb(S, nm + "m2", [128, 512])
    m3 = sb(S, nm + "m3", [128, 512])
    m4 = sb(S, nm + "m4", [128, 512])
    p_r = [ps(S, nm + "pr%d" % i, [128, NT]) for i in range(2)]
    p_s = [ps(S, nm + "psn%d" % i, [128, NT]) for i in range(2)]
    p_y = [ps(S, nm + "py%d" % i, [128, NT]) for i in range(2)]
    it = 0
    for o in range(2):
        src_tm = v_tm
        for fc in range(16):
            c_t, s_t, h_t = ct[it % 2], st[it % 2], hst[it % 2]
            u_r, u_s = ur[it % 2], us[it % 2]
            pr, pi = p_r[it % 2], p_s[it % 2]
            it += 1
            dma(S, c_t[:], C.c_hy_Ct.ap[fc], w=[c_t.b()], q="sp")
            dma(S, s_t[:], C.c_hy_St.ap[fc], w=[s_t.b()], q="act")
            dma(S, h_t[:], C.hyH_d.ap[fc * 128:(fc + 1) * 128, o * 1024:(o + 1) * 1024].rearrange("p (k c) -> p k c", k=2),
                r=[C.hyH_d.b(fc)], w=[h_t.b()], q="sp")
            for tc in range(16):
                mm(S, pr[:], c_t[:, tc * 128:(tc + 1) * 128], src_tm[:, tc, :], start=(tc == 0), stop=(tc == 15),
                   r=[c_t.b(), src_tm.b()], w=[pr.b()])
            for tc in range(16):
                mm(S, pi[:], s_t[:, tc * 128:(tc + 1) * 128], src_tm[:, tc, :], start=(tc == 0), stop=(tc == 15),
                   r=[s_t.b(), src_tm.b()], w=[pi.b()])
            cp(S, u_r[:], pr[:], r=[pr.b()], w=[u_r.b()], eng="act")
            cp(S, u_s[:], pi[:], r=[pi.b()], w=[u_s.b()], eng="act")
            tt(S, m1[:], u_r[:], h_t[:, 0, :], ALU.mult, r=[u_r.b(), h_t.b()], w=[m1.b()])
            tt(S, m2[:], u_s[:], h_t[:, 1, :], ALU.mult, r=[u_s.b(), h_t.b()], w=[m2.b()], eng="pool")
            tt(S, Yr[:, fc, :], m1[:], m2[:], ALU.subtract, r=[m1.b(), m2.b()], w=[Yr.b()])
            tt(S, m3[:], u_r[:], h_t[:, 1, :], ALU.mult, r=[u_r.b(), h_t.b()], w=[m3.b()], eng="pool")
            tt(S, m4[:], u_s[:], h_t[:, 0, :], ALU.mult, r=[u_s.b(), h_t.b()], w=[m4.b()])
            tt(S, Ys[:, fc, :], m3[:], m4[:], ALU.add, r=[m3.b(), m4.b()], w=[Ys.b()], eng="pool")
        for tc in range(16):
            c_t, s_t = ct[it % 2], st[it % 2]
            it += 1
            dma(S, c_t[:], C.c_hy_Cf.ap[tc], w=[c_t.b()], q="sp")
            dma(S, s_t[:], C.c_hy_Sf.ap[tc], w=[s_t.b()], q="act")
            if o == 0:
                py = p_y[tc % 2]
                for fc in range(16):
                    mm(S, py[:], c_t[:, fc * 128:(fc + 1) * 128], Yr[:, fc, :], start=(fc == 0), stop=False,
                       r=[c_t.b(), Yr.b()], w=[py.b()])
                for fc in range(16):
                    mm(S, py[:], s_t[:, fc * 128:(fc + 1) * 128], Ys[:, fc, :], start=False, stop=(fc == 15),
                       r=[s_t.b(), Ys.b()], w=[py.b()])
                tt(S, v_tm[:, tc, :], x1_tm[:, tc, :], py[:], ALU.mult, r=[x1_tm.b(), py.b()], w=[v_tm.b()])
            else:
                py = p_y[tc % 2]
                for cc in range(4):
                    for fc in range(16):
                        mm(S, py[:, cc * 128:(cc + 1) * 128], Yr[:, fc, cc * 128:(cc + 1) * 128], c_t[:, fc * 128:(fc + 1) * 128],
                           start=(fc == 0), stop=False, r=[c_t.b(), Yr.b()], w=[py.b()])
                    for fc in range(16):
                        mm(S, py[:, cc * 128:(cc + 1) * 128], Ys[:, fc, cc * 128:(cc + 1) * 128], s_t[:, fc * 128:(fc + 1) * 128],
                           start=False, stop=(fc == 15), r=[s_t.b(), Ys.b()], w=[py.b()])
                for cc in range(4):
                    tt(S, ycT[:, cc, tc * 128:(tc + 1) * 128], x2T[:, cc, tc * 128:(tc + 1) * 128], py[:, cc * 128:(cc + 1) * 128],
                       ALU.mult, r=[x2T.b(), py.b()], w=[ycT.b()], eng="dve")
    dma(S, C.yc_d.ap.rearrange("(c p) t -> p c t", p=128), ycT[:], r=[ycT.b()], w=[C.yc_d.b()], q="sp")
    S.barrier()
    S.release()


RW_R, RW_V, RW_KK, RW_G, RW_BONUS = 0, 1, 2, 3, 4
RW_E, RW_B, RW_KD = 5, 7, 9
GN_EPS = 64e-5
RW_STOP = 0


def rw_consts():
    c = {}
    j = np.arange(128)[:, None]
    t = np.arange(128)[None, :]
    MU_s = (t > j).astype(np.float32)
    ML_s = (t < j).astype(np.float32)
    MU_i = (t >= j).astype(np.float32)
    ML_i = (t <= j).astype(np.float32)
    c['c_rw_m4'] = np.ascontiguousarray(np.stack([np.concatenate([MU_s, MU_s, ML_s, ML_s], 1),
                                                  np.concatenate([ML_s, ML_s, MU_s, MU_s], 1)], 0))
    c['c_rw_m3'] = np.ascontiguousarray(np.stack([np.concatenate([MU_s, MU_i, MU_i], 1),
                                                  np.concatenate([ML_s, ML_i, ML_i], 1)], 0))
    c['c_rw_tri'] = np.ascontiguousarray(np.stack([MU_i, ML_i], 0))
    blk = np.zeros((128, 128), np.float32)
    blk[:64, :64] = 1.0
    blk[64:, 64:] = 1.0
    c['c_rw_blk'] = blk
    return c


def phase_rwkv(C, l):
    S = C.S
    P = C.P
    nm = "rw%d_" % l
    T_ = S_LEN
    nti = T_ // NT
    rw = C.rw_d

    def col(dst, src_ap, n):
        S.op("sp", lambda e: e.dma_start(out=dst, in_=src_ap.rearrange("(p o) -> p o", o=1), allow_slow_non_contiguous=True),
             w=[], dma=True)

    blk = sb(S, nm + "blk", [128, 128])
    dma(S, blk[:], C.c_rw_blk[:], w=[blk.b()])
    pin = [sb(S, nm + "pin%d" % i, [128, T_]) for i in range(2)]
    mcol = [sb(S, nm + "mcol%d" % i, [128, 4]) for i in range(2)]
    npiece = [0]

    def shift_piece(row0, nrows, dst, dbuf):
        i = npiece[0] % 2
        npiece[0] += 1
        p_in, mc = pin[i], mcol[i]
        dma(S, p_in[0:nrows, :], C.pm_d[row0:row0 + nrows, :], w=[p_in.b()], q="sp" if i == 0 else "act")
        S.op("sp", lambda e: e.dma_start(out=mc[0:nrows, 0:1], in_=P['rwkv_mu_prev'].ap[l][row0:row0 + nrows].rearrange("(p o) -> p o", o=1),
                                         allow_slow_non_contiguous=True), w=[mc.b()], dma=True)
        S.op("sp", lambda e: e.dma_start(out=mc[0:nrows, 1:2], in_=P['rwkv_mu_next'].ap[l][row0:row0 + nrows].rearrange("(p o) -> p o", o=1),
                                         allow_slow_non_contiguous=True), w=[mc.b()], dma=True)
        tt(S, mc[0:nrows, 2:3], mc[0:nrows, 0:1], mc[0:nrows, 1:2], ALU.add, r=[mc.b()], w=[mc.b()])
        ts(S, mc[0:nrows, 2:3], mc[0:nrows, 2:3], -1.0, 1.0, ALU.mult, ALU.add, r=[mc.b()], w=[mc.b()])
        ts(S, dst[0:nrows, :], p_in[0:nrows, :], mc[0:nrows, 2:3], None, ALU.mult, r=[p_in.b(), mc.b()], w=[dbuf])
        stt(S, dst[0:nrows, 1:T_], p_in[0:nrows, 0:T_ - 1], mc[0:nrows, 0:1], dst[0:nrows, 1:T_], ALU.mult, ALU.add,
            r=[p_in.b(), mc.b(), dbuf], w=[dbuf])
        stt(S, dst[0:nrows, 0:T_ - 1], p_in[0:nrows, 1:T_], mc[0:nrows, 1:2], dst[0:nrows, 0:T_ - 1], ALU.mult, ALU.add,
            r=[p_in.b(), mc.b(), dbuf], w=[dbuf])

    lw = [sb(S, nm + "lw%d" % d, [96, T_]) for d in range(2)]
    la = [sb(S, nm + "la%d" % d, [96, T_]) for d in range(2)]
    lg = [sb(S, nm + "lg%d" % i, [128, T_]) for i in range(2)]
    for d in range(2):
        shift_piece(1536 + 96 * d, 96, lw[d], lw[d].b())
        act(S, lw[d][:], lw[d][:], AF.Tanh, r=[lw[d].b()], w=[lw[d].b()])
        shift_piece(1728 + 96 * d, 96, la[d], la[d].b())
    for i in range(2):
        shift_piece(1920 + 128 * i, 128, lg[i], lg[i].b())
        act(S, lg[i][:], lg[i][:], AF.Sigmoid, r=[lg[i].b()], w=[lg[i].b()])
    w2 = sb(S, nm + "w2", [96, 2, 512])
    a2 = sb(S, nm + "a2", [96, 2, 512])
    g2 = sb(S, nm + "g2", [128, 2, 512])
    dma(S, w2[:], P['rwkv_w2'].ap[l].rearrange("d k c -> k d c"), w=[w2.b()])
    dma(S, a2[:], P['rwkv_a2'].ap[l].rearrange("d k c -> k d c"), w=[a2.b()], q="act")
    dma(S, g2[:], P['rwkv_g2'].ap[l].rearrange("(i p) c -> p i c", p=128), w=[g2.b()])
    pc = sb(S, nm + "pc", [128, 4, 12])
    srcs = [P['rwkv_w0'].ap[l][0], P['rwkv_w0'].ap[l][1], P['rwkv_a0'].ap[l][0], P['rwkv_a0'].ap[l][1], P['rwkv_k_k'].ap[l],
            P['rwkv_k_a'].ap[l], P['rwkv_r_k'].ap[l], P['rwkv_ln_w'].ap[l], P['rwkv_ln_b'].ap[l]]
    for j, s_ap in enumerate(srcs):
        S.op("sp", lambda e, j=j, s_ap=s_ap: e.dma_start(out=pc[:, :, j:j + 1], in_=s_ap.rearrange("(c p o) -> p c o", p=128, o=1),
                                                         allow_slow_non_contiguous=True), w=[pc.b()], dma=True)
    ts(S, pc[:, :, 9:10], pc[:, :, 5:6], -1.0, 1.0, ALU.mult, ALU.add, r=[pc.b()], w=[pc.b()])
    rs_ = sb(S, nm + "rs", [128, T_])
    ks_ = sb(S, nm + "ks", [128, T_])
    vs_ = sb(S, nm + "vs", [128, T_])
    kk_ = sb(S, nm + "kk", [128, T_])
    tl = {k: sb(S, nm + "tl_" + k, [128, NT]) for k in ["sq", "den", "e0", "e1", "a0", "a1", "t0", "kd0", "kd1", "b0", "b1", "kds", "pr", "bo", "g"]}
    pp = [ps(S, nm + "pp%d" % i, [128, NT]) for i in range(6)]
    kq = [0]

    def nps():
        kq[0] += 1
        return pp[kq[0] % 6]

    for cc in range(4):
        shift_piece(cc * 128, 128, rs_, rs_.b())
        shift_piece(512 + cc * 128, 128, ks_, ks_.b())
        shift_piece(1024 + cc * 128, 128, vs_, vs_.b())
        csl = slice(cc * 128, (cc + 1) * 128)
        dma(S, rw.ap[RW_R][csl, :], rs_[:], r=[rs_.b()], w=[rw.b((RW_R, cc))], q="sp")
        dma(S, rw.ap[RW_V][csl, :], vs_[:], r=[vs_.b()], w=[rw.b((RW_V, cc))], q="act")
        for ti in range(nti):
            tsl = slice(ti * NT, (ti + 1) * NT)
            ts(S, kk_[:, tsl], ks_[:, tsl], pc[:, cc, 4:5], None, ALU.mult, r=[ks_.b(), pc.b()], w=[kk_.b()])
            act(S, tl["sq"][:], kk_[:, tsl], AF.Square, r=[kk_.b()], w=[tl["sq"].b()])
            p = nps()
            mm(S, p[:], blk[:], tl["sq"][:], r=[blk.b(), tl["sq"].b()], w=[p.b()])
            act(S, tl["den"][:], p[:], AF.Sqrt, r=[p.b()], w=[tl["den"].b()])
            ts(S, tl["den"][:], tl["den"][:], 1e-12, None, ALU.max, r=[tl["den"].b()], w=[tl["den"].b()])
            S.op("dve", lambda e: e.reciprocal(tl["den"][:], tl["den"][:]), r=[tl["den"].b()], w=[tl["den"].b()])
            tt(S, kk_[:, tsl], kk_[:, tsl], tl["den"][:], ALU.mult, r=[kk_.b(), tl["den"].b()], w=[kk_.b()])
            for d in range(2):
                e_t, a_t, kd_t, b_t = tl["e%d" % d], tl["a%d" % d], tl["kd%d" % d], tl["b%d" % d]
                p = nps()
                mm(S, p[:], w2[:, d, csl], lw[d][:, tsl], r=[w2.b(), lw[d].b()], w=[p.b()])
                act(S, e_t[:], p[:], AF.Sigmoid, bias=pc[:, cc, d:d + 1], r=[p.b(), pc.b()], w=[e_t.b()])
                ts(S, e_t[:], e_t[:], -math.exp(-0.5), None, ALU.mult, r=[e_t.b()], w=[e_t.b()], eng="pool")
                dma(S, rw.ap[RW_E + d][csl, tsl], e_t[:], r=[e_t.b()], w=[rw.b((RW_E + d, cc))], q="sp")
                p = nps()
                mm(S, p[:], a2[:, d, csl], la[d][:, tsl], r=[a2.b(), la[d].b()], w=[p.b()])
                act(S, a_t[:], p[:], AF.Sigmoid, bias=pc[:, cc, 2 + d:3 + d], r=[p.b(), pc.b()], w=[a_t.b()])
                ts(S, tl["t0"][:], a_t[:], pc[:, cc, 5:6], pc[:, cc, 9:10], ALU.mult, ALU.add, r=[a_t.b(), pc.b()], w=[tl["t0"].b()])
                tt(S, kd_t[:], ks_[:, tsl], tl["t0"][:], ALU.mult, r=[ks_.b(), tl["t0"].b()], w=[kd_t.b()])
                dma(S, rw.ap[RW_KD + d][csl, tsl], kd_t[:], r=[kd_t.b()], w=[rw.b((RW_KD + d, cc))], q="act")
                tt(S, b_t[:], a_t[:], kk_[:, tsl], ALU.mult, r=[a_t.b(), kk_.b()], w=[b_t.b()], eng="pool")
                dma(S, rw.ap[RW_B + d][csl, tsl], b_t[:], r=[b_t.b()], w=[rw.b((RW_B + d, cc))], q="sp")
            tt(S, tl["kds"][:], tl["kd0"][:], tl["kd1"][:], ALU.add, r=[tl["kd0"].b(), tl["kd1"].b()], w=[tl["kds"].b()], eng="pool")
            stt(S, tl["pr"][:], rs_[:, tsl], pc[:, cc, 6:7], tl["kds"][:], ALU.mult, ALU.mult, r=[rs_.b(), pc.b(), tl["kds"].b()], w=[tl["pr"].b()])
            p = nps()
            mm(S, p[:], blk[:], tl["pr"][:], r=[blk.b(), tl["pr"].b()], w=[p.b()])
            tt(S, tl["bo"][:], p[:], vs_[:, tsl], ALU.mult, r=[p.b(), vs_.b()], w=[tl["bo"].b()])
            dma(S, rw.ap[RW_BONUS][csl, tsl], tl["bo"][:], r=[tl["bo"].b()], w=[rw.b((RW_BONUS, cc))], q="act")
            p = nps()
            for i in range(2):
                mm(S, p[:], g2[:, i, csl], lg[i][:, tsl], start=(i == 0), stop=(i == 1), r=[g2.b(), lg[i].b()], w=[p.b()])
            cp(S, tl["g"][:], p[:], r=[p.b()], w=[tl["g"].b()], eng="act")
            dma(S, rw.ap[RW_G][csl, tsl], tl["g"][:], r=[tl["g"].b()], w=[rw.b((RW_G, cc))], q="sp")
        dma(S, rw.ap[RW_KK][csl, :], kk_[:], r=[kk_.b()], w=[rw.b((RW_KK, cc))], q="sp")
    S.barrier()
    S.release()
    if getattr(C, "rw_stop", 0) == 1:
        return
    CH = 128
    NCH = T_ // CH
    m4 = [sb(S, nm + "m4_%d" % d, [128, 512]) for d in range(2)]
    m3 = [sb(S, nm + "m3_%d" % d, [128, 384]) for d in range(2)]
    tri = [sb(S, nm + "tri%d" % d, [128, 128]) for d in range(2)]
    for d in range(2):
        dma(S, m4[d][:], C.c_rw_m4.ap[d], w=[m4[d].b()])
        dma(S, m3[d][:], C.c_rw_m3.ap[d], w=[m3[d].b()], q="act")
        dma(S, tri[d][:], C.c_rw_tri.ap[d], w=[tri[d].b()])
    hmk = sb(S, nm + "hmk", [128, 128])
    dma(S, hmk[:], C.c_rw_blk[:], w=[hmk.b()])
    Yacc = sb(S, nm + "Yacc", [128, NCH, 512])
    ST = {}
    for d in range(2):
        for cc in range(4):
            ST[(d, cc)] = [sb(S, nm + "ST%d%d%d" % (d, cc, i), [64, 2, 64], BF16) for i in range(2)]
            mset(S, ST[(d, cc)][0][:], 0.0, w=[ST[(d, cc)][0].b()])
    slots = {}
    for d in range(2):
        for cc in range(4):
            sl = {}
            sn = nm + "s%d%d_" % (d, cc)
            sl["in"] = sb(S, sn + "in", [128, 6, CH])
            sl["etm"] = sb(S, sn + "etm", [128, 128])
            sl["cx"] = sb(S, sn + "cx", [128, 128])
            sl["gp"] = sb(S, sn + "gp", [128, 128])
            sl["gn"] = sb(S, sn + "gn", [128, 128])
            sl["gx"] = sb(S, sn + "gx", [128, 128])
            sl["cm"] = sb(S, sn + "cm", [128, 7, 128], BF16)
            sl["AR"] = sb(S, sn + "AR", [128, 4, 128], BF16)
            sl["Bm"] = sb(S, sn + "Bm", [128, 2, 128], BF16)
            sl["tm"] = sb(S, sn + "tm", [128, 4, 128], BF16)
            sl["Xt"] = [sb(S, sn + "Xt%d" % i, [128, 2, 128], BF16) for i in range(2)]
            sl["XTT"] = [sb(S, sn + "XTT%d" % i, [128, 2, 2, 128], BF16) for i in range(2)]
            sl["TTf"] = sb(S, sn + "TTf", [128, 2, 128], BF16)
            sl["L3"] = sb(S, sn + "L3", [128, 2, 3, 128], BF16)
            sl["W1A"] = sb(S, sn + "W1A", [128, 2, 128], BF16)
            sl["QP"] = sb(S, sn + "QP", [128, 2, 128], BF16)
            sl["GT"] = sb(S, sn + "GT", [64, 2, 64], BF16)
            sl["RyT"] = sb(S, sn + "RyT", [64, 2, 128], BF16)
            sl["Dg"] = sb(S, sn + "Dg", [128, 128], BF16)
            slots[(d, cc)] = sl
    bank = [ps(S, nm + "bk%d" % i, [128, NT]) for i in range(6)]
    bankb = [ps(S, nm + "bkb%d" % i, [128, 1024], BF16) for i in range(2)]
    kb = [0]

    def nb():
        kb[0] += 1
        return bank[kb[0] % 6]

    IDB = C.identb
    touched = set()
    par = {}
    for step in range(NCH):
        units = []
        for d in range(2):
            n = step if d == 0 else NCH - 1 - step
            for cc in range(4):
                units.append((d, cc, n, slots[(d, cc)]))
        for d, cc, n, sl in units:
            t0 = n * CH
            csl = slice(cc * 128, (cc + 1) * 128)
            srcs = [RW_R, RW_KK, RW_V, RW_E + d, RW_B + d, RW_KD + d]
            for j, k_ in enumerate(srcs):
                dma(S, sl["in"][:, j, :], rw.ap[k_][csl, t0:t0 + CH], r=[rw.b((k_, cc))], w=[sl["in"].b()], q="sp")
        pbank = {}
        for half in (units[0:4], units[4:8]):
            for u in half:
                d, cc, n, sl = u
                p = nb()
                pbank[(d, cc)] = p
                tr(S, p[:, 0:128], sl["in"][:, 3, :], C.ident[:], r=[sl["in"].b(), C.ident.b()], w=[p.b()])
            for u in half:
                d, cc, n, sl = u
                p = pbank[(d, cc)]
                cp(S, sl["etm"][:], p[:, 0:128], r=[p.b()], w=[sl["etm"].b()], eng="act")
            for u in half:
                d, cc, n, sl = u
                p2 = nb()
                pbank[(d, cc)] = p2
                mm(S, p2[:, 0:128], sl["etm"][:], tri[d][:], r=[sl["etm"].b(), tri[d].b()], w=[p2.b()])
            for u in half:
                d, cc, n, sl = u
                p2 = pbank[(d, cc)]
                act(S, sl["gp"][:], p2[:, 0:128], AF.Exp, r=[p2.b()], w=[sl["gp"].b()])
                act(S, sl["gn"][:], p2[:, 0:128], AF.Exp, scale=-1.0, r=[p2.b()], w=[sl["gn"].b()])
            for u in half:
                d, cc, n, sl = u
                p2 = pbank[(d, cc)]
                tt(S, sl["cx"][:], p2[:, 0:128], sl["in"][:, 3, :], ALU.subtract, r=[p2.b(), sl["in"].b()], w=[sl["cx"].b()])
            for u in half:
                d, cc, n, sl = u
                act(S, sl["gx"][:], sl["cx"][:], AF.Exp, r=[sl["cx"].b()], w=[sl["gx"].b()])
            for u in half:
                d, cc, n, sl = u
                inb, cm = sl["in"], sl["cm"]
                gcol_ap = sl["gp"][:, 127:128] if d == 0 else sl["gp"][:, 0:1]
                tt(S, cm[:, 1, :], inb[:, 4, :], sl["gn"][:], ALU.mult, r=[inb.b(), sl["gn"].b()], w=[cm.b(1)], eng="pool")
                tt(S, cm[:, 2, :], inb[:, 5, :], sl["gn"][:], ALU.mult, r=[inb.b(), sl["gn"].b()], w=[cm.b(2)])
                tt(S, cm[:, 3, :], inb[:, 0, :], sl["gp"][:], ALU.mult, r=[inb.b(), sl["gp"].b()], w=[cm.b(3)], eng="pool")
                stt(S, cm[:, 4, :], inb[:, 4, :], gcol_ap, sl["gn"][:], ALU.mult, ALU.mult, r=[inb.b(), sl["gn"].b(), sl["gp"].b()], w=[cm.b(4)])
                stt(S, cm[:, 5, :], inb[:, 5, :], gcol_ap, sl["gn"][:], ALU.mult, ALU.mult, r=[inb.b(), sl["gn"].b(), sl["gp"].b()], w=[cm.b(5)])
                cp(S, cm[:, 6, :], inb[:, 2, :], r=[inb.b()], w=[cm.b(6)], eng="pool")
                ts(S, sl["Dg"][:], C.ident[:], gcol_ap, None, ALU.mult, r=[C.ident.b(), sl["gp"].b()], w=[sl["Dg"].b()], eng="pool")
            for u in half:
                d, cc, n, sl = u
                inb, cm = sl["in"], sl["cm"]
                stt(S, cm[:, 0, :], inb[:, 1, :], -1.0, sl["gx"][:], ALU.mult, ALU.mult, r=[inb.b(), sl["gx"].b()], w=[cm.b(0)])
            for u in half:
                d, cc, n, sl = u
                cm, AR, Bm = sl["cm"], sl["AR"], sl["Bm"]
                for hp in range(2):
                    hcol = hmk[:, 64 * hp:64 * hp + 1]
                    ts(S, AR[:, hp, :], cm[:, 0, :], hcol, None, ALU.mult, r=[cm.b(0), hmk.b()], w=[AR.b(hp)], eng="pool")
                    ts(S, AR[:, 2 + hp, :], cm[:, 3, :], hcol, None, ALU.mult, r=[cm.b(3), hmk.b()], w=[AR.b(2 + hp)])
                    ts(S, Bm[:, hp, :], cm[:, 1, :], hcol, None, ALU.mult, r=[cm.b(1), hmk.b()], w=[Bm.b(hp)],
                       eng="pool" if hp == 0 else "dve")
            for ui, u in enumerate(half):
                d, cc, n, sl = u
                cm = sl["cm"]
                bb = bankb[ui % 2]
                for j, jj in enumerate([0, 4, 5, 6]):
                    tr(S, bb[:, j * 128:(j + 1) * 128], cm[:, jj, :], IDB[:], r=[cm.b(jj), IDB.b()], w=[bb.b()])
                cp(S, sl["tm"][:].rearrange("p a b -> p (a b)"), bb[:, 0:512], r=[bb.b()], w=[sl["tm"].b()], eng="act")
            pbk = {}
            for u in half:
                d, cc, n, sl = u
                cm, AR, Bm = sl["cm"], sl["AR"], sl["Bm"]
                pB, pK, pA = nb(), nb(), nb()
                pbk[(d, cc)] = (pB, pK, pA)
                ARf = AR[:].rearrange("p a b -> p (a b)")
                mm(S, pB[:], cm[:, 1, :], ARf, r=[cm.b(1)] + [AR.b(i) for i in range(4)], w=[pB.b()])
                mm(S, pK[:], cm[:, 2, :], ARf, r=[cm.b(2)] + [AR.b(i) for i in range(4)], w=[pK.b()])
                mm(S, pA[:, 0:256], cm[:, 0, :], Bm[:].rearrange("p a b -> p (a b)"), r=[cm.b(0), Bm.b(0), Bm.b(1)], w=[pA.b()])
                Xt0, XTT0, XTT1, L3 = sl["Xt"][0], sl["XTT"][0], sl["XTT"][1], sl["L3"]
                v3 = lambda ap: ap.rearrange("p (a b) -> p a b", a=2)
                tt(S, XTT0[:, :, 0, :], v3(pB[:, 0:256]), v3(m4[d][:, 0:256]), ALU.mult, r=[pB.b(), m4[d].b()], w=[XTT0.b()])
                tt(S, L3[:, :, 1, :], v3(pB[:, 256:512]), v3(m3[d][:, 128:384]), ALU.mult, r=[pB.b(), m3[d].b()], w=[L3.b()])
                tt(S, L3[:, :, 0, :], v3(pK[:, 0:256]), v3(m4[d][:, 0:256]), ALU.mult, r=[pK.b(), m4[d].b()], w=[L3.b()])
                tt(S, L3[:, :, 2, :], v3(pK[:, 256:512]), v3(m3[d][:, 128:384]), ALU.mult, r=[pK.b(), m3[d].b()], w=[L3.b()])
                tt(S, Xt0[:].rearrange("p a b -> p (a b)"), pA[:, 0:256], m4[d][:, 256:512], ALU.mult, r=[pA.b(), m4[d].b()], w=[Xt0.b()])
                for hp in range(2):
                    tt(S, XTT1[:, hp, 1, :], XTT0[:, hp, 0, :], IDB[:], ALU.add, r=[XTT0.b(), IDB.b()], w=[XTT1.b()], eng="pool")
        for m_ in range(0, 7):
            cur, nxt = m_ % 2, (m_ + 1) % 2
            for half in (units[0:4], units[4:8]):
                lv = {}
                for u in half:
                    d, cc, n, sl = u
                    Xc, XTc = sl["Xt"][cur], sl["XTT"][cur]
                    pa_, pb_ = nb(), None
                    if m_ == 0:
                        for hp in range(2):
                            mm(S, pa_[:, hp * 256:hp * 256 + 128], Xc[:, hp, :], XTc[:, hp, 0, :], r=[Xc.b(), XTc.b()], w=[pa_.b()])
                    elif m_ < 6:
                        for hp in range(2):
                            mm(S, pa_[:, hp * 256:(hp + 1) * 256], Xc[:, hp, :], XTc[:, hp, :, :].rearrange("p a b -> p (a b)"),
                               r=[Xc.b(), XTc.b()], w=[pa_.b()])
                    else:
                        for hp in range(2):
                            mm(S, pa_[:, hp * 256 + 128:(hp + 1) * 256], Xc[:, hp, :], XTc[:, hp, 1, :], r=[Xc.b(), XTc.b()], w=[pa_.b()])
                    if m_ < 6:
                        pb_ = nb()
                        for hp in range(2):
                            mm(S, pb_[:, hp * 128:(hp + 1) * 128], XTc[:, hp, 0, :], Xc[:, hp, :], r=[Xc.b(), XTc.b()], w=[pb_.b()])
                    Xn, XTn = sl["Xt"][nxt], sl["XTT"][nxt]
                    pv = pa_[:].rearrange("p (h k b) -> p h k b", h=2, k=2)
                    if m_ < 5:
                        cp(S, XTn[:, :, 0, :], pv[:, :, 0, :], r=[pa_.b()], w=[XTn.b()], eng="act")
                    if m_ < 6:
                        cp(S, Xn[:].rearrange("p a b -> p (a b)"), pb_[:, 0:256], r=[pb_.b()], w=[Xn.b()], eng="act")
                    if 1 <= m_ < 6:
                        tt(S, XTn[:, :, 1, :], XTc[:, :, 1, :], pv[:, :, 1, :], ALU.add, r=[XTc.b(), pa_.b()], w=[XTn.b()])
                    elif m_ == 6:
                        tt(S, sl["TTf"][:], XTc[:, :, 1, :], pv[:, :, 1, :], ALU.add, r=[XTc.b(), pa_.b()], w=[sl["TTf"].b()])
        sold = {}
        for half in (units[0:4], units[4:8]):
            for u in half:
                d, cc, n, sl = u
                tm, L3 = sl["tm"], sl["L3"]
                p = nb()
                pbank[(d, cc)] = p
                for hp in range(2):
                    fs = slice(64 * hp, 64 * hp + 64)
                    mm(S, p[:, hp * 64:(hp + 1) * 64], L3[:, hp, 0, :], tm[:, 3, fs], r=[L3.b(), tm.b()], w=[p.b()])
            for u in half:
                d, cc, n, sl = u
                p = pbank[(d, cc)]
                tm, W1A = sl["tm"], sl["W1A"]
                for hp in range(2):
                    fs = slice(64 * hp, 64 * hp + 64)
                    cp(S, W1A[:, hp, 0:64], p[:, hp * 64:(hp + 1) * 64], r=[p.b()], w=[W1A.b()], eng="act")
                    cp(S, W1A[:, hp, 64:128], tm[:, 0, fs], r=[tm.b()], w=[W1A.b()], eng="pool")
            for u in half:
                d, cc, n, sl = u
                TTf, W1A = sl["TTf"], sl["W1A"]
                p2 = nb()
                pbank[(d, cc)] = p2
                for hp in range(2):
                    mm(S, p2[:, hp * 128:(hp + 1) * 128], TTf[:, hp, :], W1A[:, hp, :], r=[TTf.b(), W1A.b()], w=[p2.b()])
            for u in half:
                d, cc, n, sl = u
                p2 = pbank[(d, cc)]
                cp(S, sl["QP"][:].rearrange("p a b -> p (a b)"), p2[:, 0:256], r=[p2.b()], w=[sl["QP"].b()], eng="act")
            for u in half:
                d, cc, n, sl = u
                tm, L3, QP, cm = sl["tm"], sl["L3"], sl["QP"], sl["cm"]
                p3 = nb()
                pbank[(d, cc)] = p3
                for hp in range(2):
                    fs = slice(64 * hp, 64 * hp + 64)
                    mm(S, p3[0:64, hp * 64:(hp + 1) * 64], QP[:, hp, 64:128], tm[:, 1, fs], start=True, stop=False,
                       r=[QP.b(), tm.b()], w=[p3.b()])
                    mm(S, p3[0:64, hp * 64:(hp + 1) * 64], IDB[:, fs], sl["Dg"][:, fs], start=False, stop=True,
                       r=[IDB.b(), sl["Dg"].b()], w=[p3.b()])
                    mm(S, p3[0:64, 128 + hp * 128:256 + hp * 128], QP[:, hp, 64:128], L3[:, hp, 1, :], start=True, stop=False,
                       r=[QP.b(), L3.b()], w=[p3.b()])
                    mm(S, p3[0:64, 128 + hp * 128:256 + hp * 128], IDB[:, fs], cm[:, 3, :], start=False, stop=True,
                       r=[IDB.b(), cm.b(3)], w=[p3.b()])
            for u in half:
                d, cc, n, sl = u
                p3 = pbank[(d, cc)]
                cp(S, sl["GT"][:].rearrange("p a b -> p (a b)"), p3[0:64, 0:128], r=[p3.b()], w=[sl["GT"].b()], eng="act")
                cp(S, sl["RyT"][:].rearrange("p a b -> p (a b)"), p3[0:64, 128:384], r=[p3.b()], w=[sl["RyT"].b()], eng="act")
            for u in half:
                d, cc, n, sl = u
                tm, L3, QP = sl["tm"], sl["L3"], sl["QP"]
                k_ = par.get((d, cc), 0)
                S_old, S_new = ST[(d, cc)][k_], ST[(d, cc)][1 - k_]
                par[(d, cc)] = 1 - k_
                sold[(d, cc)] = (S_old, S_new)
                pz = nb()
                pbank[(d, cc)] = pz
                for hp in range(2):
                    fs = slice(64 * hp, 64 * hp + 64)
                    mm(S, pz[0:64, fs], sl["GT"][:, hp, :], S_old[:, hp, :], start=True, stop=False, r=[sl["GT"].b(), S_old.b()], w=[pz.b()])
                    mm(S, pz[0:64, fs], tm[:, 1, fs], QP[:, hp, 0:64], start=False, stop=False, r=[tm.b(), QP.b()], w=[pz.b()])
                    mm(S, pz[0:64, fs], tm[:, 2, fs], tm[:, 3, fs], start=False, stop=True, r=[tm.b()], w=[pz.b()])
                    mm(S, pz[:, 128 + 64 * hp:192 + 64 * hp], L3[:, hp, 1, :], QP[:, hp, 0:64], start=True, stop=False, r=[L3.b(), QP.b()], w=[pz.b()])
                    mm(S, pz[:, 128 + 64 * hp:192 + 64 * hp], L3[:, hp, 2, :], tm[:, 3, fs], start=False, stop=False, r=[L3.b(), tm.b()], w=[pz.b()])
                    mm(S, pz[:, 128 + 64 * hp:192 + 64 * hp], sl["RyT"][:, hp, :], S_old[:, hp, :], start=False, stop=True, r=[sl["RyT"].b(), S_old.b()], w=[pz.b()])
            for u in half:
                d, cc, n, sl = u
                pz = pbank[(d, cc)]
                S_old, S_new = sold[(d, cc)]
                cp(S, S_new[:].rearrange("p a b -> p (a b)"), pz[0:64, 0:128], r=[pz.b()], w=[S_new.b()], eng="act")
                if (n, cc) not in touched:
                    touched.add((n, cc))
                    cp(S, Yacc[:, n, cc * 128:(cc + 1) * 128], pz[:, 128:256], r=[pz.b()], w=[Yacc.b((n, cc))])
                else:
                    tt(S, Yacc[:, n, cc * 128:(cc + 1) * 128], Yacc[:, n, cc * 128:(cc + 1) * 128], pz[:, 128:256], ALU.add,
                       r=[pz.b(), Yacc.b((n, cc))], w=[Yacc.b((n, cc))])
    if C.rw_stop == 6:
        return
    pcs = sb(S, nm + "pcs", [128, 4, 4])
    for j, k_ in enumerate(['rwkv_ln_w', 'rwkv_ln_b']):
        S.op("sp", lambda e, j=j, k_=k_: e.dma_start(out=pcs[:, :, j:j + 1], in_=P[k_].ap[l].rearrange("(c p o) -> p c o", p=128, o=1),
                                                     allow_slow_non_contiguous=True), w=[pcs.b()], dma=True)
    blk2 = sb(S, nm + "blk2", [128, 128])
    dma(S, blk2[:], C.c_rw_blk[:], w=[blk2.b()])
    gne = sb(S, nm + "gne", [128, 1])
    mset(S, gne[:], GN_EPS, w=[gne.b()])
    ycm = sb(S, nm + "ycm", [128, NT])
    yc2 = sb(S, nm + "yc2", [128, NT])
    sq2 = sb(S, nm + "sq2", [128, NT])
    rstd2 = sb(S, nm + "rstd2", [128, NT])
    bo_t = [sb(S, nm + "bo%d" % i, [128, NT]) for i in range(2)]
    g_t = [sb(S, nm + "gt%d" % i, [128, NT]) for i in range(2)]
    yo = [sb(S, nm + "yo%d" % i, [128, NT], BF16) for i in range(2)]
    k3 = 0
    for cc in range(4):
        csl = slice(cc * 128, (cc + 1) * 128)
        for ti in range(nti):
            tsl = slice(ti * NT, (ti + 1) * NT)
            b_t, gg, y_o = bo_t[k3 % 2], g_t[k3 % 2], yo[k3 % 2]
            k3 += 1
            dma(S, b_t[:], rw.ap[RW_BONUS][csl, tsl], r=[rw.b((RW_BONUS, cc))], w=[b_t.b()], q="sp")
            dma(S, gg[:], rw.ap[RW_G][csl, tsl], r=[rw.b((RW_G, cc))], w=[gg.b()], q="act")
            p = nb()
            for j in range(4):
                n = ti * 4 + j
                tr(S, p[:, j * 128:(j + 1) * 128], Yacc[:, n, csl], C.ident[:], r=[Yacc.b((n, cc)), C.ident.b()], w=[p.b()])
            cp(S, ycm[:], p[:], r=[p.b()], w=[ycm.b()], eng="act")
            p2 = nb()
            mm(S, p2[:], blk2[:], ycm[:], r=[blk2.b(), ycm.b()], w=[p2.b()])
            stt(S, yc2[:], p2[:], -1.0 / 64.0, ycm[:], ALU.mult, ALU.add, r=[p2.b(), ycm.b()], w=[yc2.b()])
            act(S, sq2[:], yc2[:], AF.Square, r=[yc2.b()], w=[sq2.b()])
            p3 = nb()
            mm(S, p3[:], blk2[:], sq2[:], r=[blk2.b(), sq2.b()], w=[p3.b()])
            rsqrt(S, rstd2[:], p3[:], 1.0 / 64.0, gne[:, 0:1], r=[p3.b(), gne.b()], w=[rstd2.b()])
            tt(S, yc2[:], yc2[:], rstd2[:], ALU.mult, r=[yc2.b(), rstd2.b()], w=[yc2.b()])
            ts(S, yc2[:], yc2[:], pcs[:, cc, 0:1], pcs[:, cc, 1:2], ALU.mult, ALU.add, r=[yc2.b(), pcs.b()], w=[yc2.b()])
            tt(S, yc2[:], yc2[:], b_t[:], ALU.add, r=[yc2.b(), b_t.b()], w=[yc2.b()], eng="pool")
            tt(S, y_o[:], yc2[:], gg[:], ALU.mult, r=[yc2.b(), gg.b()], w=[y_o.b()])
            dma(S, C.ya_d[csl, tsl], y_o[:], r=[y_o.b()], w=[C.ya_d.b((cc, ti))], q="sp")
    S.barrier()
    S.release()
```

```python
import math
from contextlib import ExitStack

import numpy as np
import ml_dtypes
import concourse.bass as bass
import concourse.mybir as mybir
from concourse.bass_utils import run_bass_kernel_spmd

F32 = mybir.dt.float32
BF16 = mybir.dt.bfloat16
ALU = mybir.AluOpType
AF = mybir.ActivationFunctionType
AX = mybir.AxisListType

D = 2048
S_LEN = 2048
DEPTH = 2
FFN = 5632
EPS = 1e-6
NT = 512
RW_COLS, MLA_COLS, HY_COLS, GQA_COLS = 2176, 576, 1536, 1024
OFF_RW, OFF_MLA, OFF_HY, OFF_GQA, OFF_GATE = 0, 2176, 2752, 4288, 5312
IN_COLS = 13504
PM_ROWS = 5056


class Buf:
    __slots__ = ("w", "r", "name", "excl")

    def __init__(self, name="", excl=False):
        self.w = None
        self.r = []
        self.name = name
        self.excl = excl


class Rec:
    __slots__ = ("eng", "fn", "deps", "dma", "semkey", "val", "needs_inc", "idx")


class Sched:
    ENG = ("pe", "act", "dve", "pool", "sp")
    BLK = {"pe": "tensor", "act": "scalar", "dve": "vector", "pool": "gpsimd", "sp": "sync"}
    CAP = 8000
    NRING = 8

    def __init__(self, nc):
        self.nc = nc
        self.streams = {e: [] for e in self.ENG}
        self.all = []
        self.ndma = {e: 0 for e in self.ENG}
        self.ring_last = {}
        self.es = ExitStack()

    def sbuf(self, name, shape, dt):
        return self.es.enter_context(self.nc.sbuf_tensor(name, list(shape), dt))

    def psum(self, name, shape, dt=F32):
        return self.es.enter_context(self.nc.psum_tensor(name, list(shape), dt))

    def release(self):
        self.es.close()
        self.es = ExitStack()

    def op(self, eng, fn, r=(), w=(), dma=False):
        rec = Rec()
        rec.eng, rec.fn, rec.dma = eng, fn, dma
        rec.needs_inc = False
        rec.val = None
        rec.semkey = None
        rec.idx = len(self.all)
        deps = {}
        for b in r:
            if b.w is not None:
                deps[id(b.w)] = b.w
            if b.excl:
                for rr in b.r:
                    if rr.eng != eng:
                        deps[id(rr)] = rr
        for b in w:
            if b.w is not None:
                lw = b.w
                if not (eng == "pe" and lw.eng == "pe" and not lw.dma and not dma):
                    deps[id(lw)] = lw
            for rr in b.r:
                if rr.eng != eng or rr.dma or dma:
                    deps[id(rr)] = rr
        if dma:
            i = self.ndma[eng]
            self.ndma[eng] += 1
            slot = i % self.NRING
            rec.semkey = ("dma", eng, slot)
            rec.val = 16 * (i // self.NRING + 1)
            prev = self.ring_last.get((eng, slot))
            if prev is not None:
                deps[id(prev)] = prev
            self.ring_last[(eng, slot)] = rec
        for d in deps.values():
            d.needs_inc = True
        rec.deps = list(deps.values())
        for b in r:
            if not dma:
                b.r = [x for x in b.r if x.dma or x.eng != eng]
            b.r.append(rec)
        for b in w:
            b.w = rec
            b.r = []
        self.streams[eng].append(rec)
        self.all.append(rec)
        return rec

    def barrier(self):
        lasts = []
        for e in self.ENG:
            for rec in reversed(self.streams[e]):
                if rec.fn is not None:
                    lasts.append(rec)
                    break
        for (e, slot), rec in self.ring_last.items():
            lasts.append(rec)
        fence = Buf("fence")
        for e in self.ENG:
            rec = Rec()
            rec.eng, rec.fn, rec.dma = e, None, False
            rec.needs_inc = False
            rec.val = None
            rec.semkey = None
            rec.idx = len(self.all)
            rec.deps = [d for d in lasts]
            for d in lasts:
                d.needs_inc = True
            self.streams[e].append(rec)
            self.all.append(rec)

    def finalize(self):
        nc = self.nc
        cnt = {e: 0 for e in self.ENG}
        for rec in self.all:
            if rec.dma or not rec.needs_inc or rec.fn is None:
                continue
            c = cnt[rec.eng]
            rec.semkey = ("eng", rec.eng, c // self.CAP)
            rec.val = c % self.CAP + 1
            cnt[rec.eng] = c + 1
        keys = set()
        for rec in self.all:
            if rec.semkey is not None:
                keys.add(rec.semkey)
        with ExitStack() as es:
            sems = {}
            for k in sorted(keys):
                sems[k] = es.enter_context(nc.semaphore("s_%s_%s_%d" % k))
            block = es.enter_context(nc.Block())
            for e in self.ENG:
                stream = self.streams[e]

                def body(eng, stream=stream):
                    waited = {}
                    for rec in stream:
                        for d in rec.deps:
                            if d.semkey is None:
                                continue
                            if waited.get(d.semkey, 0) >= d.val:
                                continue
                            eng.wait_ge(sems[d.semkey], d.val)
                            waited[d.semkey] = d.val
                        if rec.fn is None:
                            continue
                        ins = rec.fn(eng)
                        if rec.dma:
                            ins.then_inc(sems[rec.semkey], 16)
                        elif rec.needs_inc:
                            ins.then_inc(sems[rec.semkey], 1)

                getattr(block, self.BLK[e])(body)
        self.es.close()


class T:
    def __init__(self, h, excl=False):
        self.h = h
        self.bufs = {}
        self.excl = excl

    def __getitem__(self, idx):
        return self.h[idx]

    def b(self, key=0):
        bb = self.bufs.get(key)
        if bb is None:
            bb = self.bufs[key] = Buf(excl=self.excl)
        return bb


def sb(S, name, shape, dt=F32):
    return T(S.sbuf(name, shape, dt))


def ps(S, name, shape, dt=F32):
    return T(S.psum(name, shape, dt), excl=True)


class DT:
    def __init__(self, nc, name, shape, dt, kind="Internal"):
        self.t = nc.dram_tensor(name, list(shape), dt, kind=kind)
        self.ap = self.t.ap()
        self.bufs = {}

    def __getitem__(self, idx):
        return self.ap[idx]

    def b(self, key=0):
        bb = self.bufs.get(key)
        if bb is None:
            bb = self.bufs[key] = Buf()
        return bb


def mm(S, out, lhsT, rhs, start=True, stop=True, r=(), w=()):
    return S.op("pe", lambda e: e.matmul(out, lhsT, rhs, start=start, stop=stop), r=r, w=w)


def tr(S, out, in_, ident, r=(), w=()):
    return S.op("pe", lambda e: e.transpose(out, in_, ident), r=r, w=w)


def act(S, out, in_, func, bias=None, scale=None, accum_out=None, r=(), w=(), eng="act"):
    kw = {}
    if bias is not None:
        kw["bias"] = bias
    if scale is not None:
        kw["scale"] = scale
    if accum_out is not None:
        kw["accum_out"] = accum_out
    return S.op(eng, lambda e: e.activation(out, in_, func, **kw), r=r, w=w)


def tt(S, out, in0, in1, op, r=(), w=(), eng="dve"):
    return S.op(eng, lambda e: e.tensor_tensor(out, in0, in1, op), r=r, w=w)


def ts(S, out, in0, s1, s2, op0, op1=None, r=(), w=(), eng="dve", accum_out=None):
    if op1 is None:
        return S.op(eng, lambda e: e.tensor_single_scalar(out, in0, s1, op0), r=r, w=w)
    if accum_out is not None:
        return S.op(eng, lambda e: e.tensor_scalar(out, in0, s1, s2, op0, op1, accum_out), r=r, w=w)
    return S.op(eng, lambda e: e.tensor_scalar(out, in0, s1, s2, op0, op1), r=r, w=w)


def stt(S, out, in0, scalar, in1, op0, op1, r=(), w=(), eng="dve"):
    return S.op(eng, lambda e: e.scalar_tensor_tensor(out, in0, scalar, in1, op0, op1), r=r, w=w)


def rsqrt(S, out, in_, scale, bias, r=(), w=()):
    S.op("act", lambda e: e.activation(out, in_, AF.Sqrt, bias=bias, scale=scale), r=r, w=w)
    S.op("dve", lambda e: e.reciprocal(out, out), r=w, w=w)


def cp(S, out, in_, r=(), w=(), eng="dve"):
    if eng == "act":
        return S.op(eng, lambda e: e.copy(out, in_), r=r, w=w)
    return S.op(eng, lambda e: e.tensor_copy(out, in_), r=r, w=w)


def mset(S, ap, val, w=(), eng="dve"):
    return S.op(eng, lambda e: e.memset(ap, val), w=w)


def dma(S, out, in_, r=(), w=(), q="sp"):
    return S.op(q, lambda e: e.dma_start(out=out, in_=in_), r=r, w=w, dma=True)


class Ctx:
    pass


def load_consts(C):
    S = C.S
    nc = S.nc
    C.cst = ExitStack()
    C.ident = T(C.cst.enter_context(nc.sbuf_tensor("ident", [128, 128], F32)))
    C.identb = T(C.cst.enter_context(nc.sbuf_tensor("identb", [128, 128], BF16)))
    C.ones = T(C.cst.enter_context(nc.sbuf_tensor("ones", [128, 128], F32)))
    dma(S, C.ident[:], C.d_ident[:], w=[C.ident.b()])
    mset(S, C.ones[:], 1.0, w=[C.ones.b()])
    C.epsc = T(C.cst.enter_context(nc.sbuf_tensor("epsc", [128, 4], F32)))
    mset(S, C.epsc[:, 0:1], EPS, w=[C.epsc.b()])
    mset(S, C.epsc[:, 1:2], 0.0, w=[C.epsc.b()])
    cp(S, C.identb[:], C.ident[:], r=[C.ident.b()], w=[C.identb.b()])


def phase_load_x(C):
    S = C.S
    xt = [sb(S, "lx_xt%d" % i, [128, 4, D]) for i in range(2)]
    st = [sb(S, "lx_st%d" % i, [128, NT]) for i in range(3)]
    pp = [ps(S, "lx_ps%d" % i, [128, NT]) for i in range(3)]
    k = 0
    for tg in range(S_LEN // NT):
        x_t = xt[tg % 2]
        for j in range(4):
            t0 = tg * NT + j * 128
            dma(S, x_t[:, j, :], C.x[t0:t0 + 128, :], w=[x_t.b(j)], q="sp" if j % 2 == 0 else "act")
        for dc in range(D // 128):
            p = pp[k % 3]
            s = st[k % 3]
            for j in range(4):
                tr(S, p[:, j * 128:(j + 1) * 128], x_t[:, j, dc * 128:(dc + 1) * 128], C.ident[:],
                   r=[x_t.b(j), C.ident.b()], w=[p.b()])
            cp(S, s[:], p[:], r=[p.b()], w=[s.b()], eng="dve" if k % 2 == 0 else "act")
            dma(S, C.xres[dc * 128:(dc + 1) * 128, tg * NT:(tg + 1) * NT], s[:], r=[s.b()],
                w=[C.xres.b((dc, tg // 2))], q="sp")
            k += 1
    S.barrier()
    S.release()


def phase_final_norm(C):
    S = C.S
    gb = sb(S, "fn_g", [128, D])
    dma(S, gb[:], C.final_norm.ap.partition_broadcast(128), w=[gb.b()])
    xin = [sb(S, "fn_xin%d" % i, [128, 16, 128]) for i in range(2)]
    xt = [sb(S, "fn_xt%d" % i, [128, D]) for i in range(2)]
    sq = sb(S, "fn_sq", [128, D])
    ot = [sb(S, "fn_ot%d" % i, [128, D]) for i in range(2)]
    ss = [sb(S, "fn_ss%d" % i, [128, 1]) for i in range(2)]
    pp = [ps(S, "fn_ps%d" % i, [128, NT]) for i in range(4)]
    for tc in range(S_LEN // 128):
        xi = xin[tc % 2]
        x_t = xt[tc % 2]
        o_t = ot[tc % 2]
        s_ = ss[tc % 2]
        dma(S, xi[:], C.xres.ap[:, tc * 128:(tc + 1) * 128].rearrange("(c p) t -> p c t", p=128),
            r=[C.xres.b((dc, tc // 8)) for dc in range(16)], w=[xi.b()], q="sp" if tc % 2 == 0 else "act")
        for g in range(4):
            p = pp[g]
            for j in range(4):
                dc = g * 4 + j
                tr(S, p[:, j * 128:(j + 1) * 128], xi[:, dc, :], C.ident[:], r=[xi.b(), C.ident.b()], w=[p.b()])
            cp(S, x_t[:, g * NT:(g + 1) * NT], p[:], r=[p.b()], w=[x_t.b(g)], eng="dve" if g % 2 == 0 else "act")
        act(S, sq[:], x_t[:], AF.Square, accum_out=s_[:], r=[x_t.b(g) for g in range(4)], w=[sq.b(), s_.b()])
        rsqrt(S, s_[:], s_[:], 1.0 / D, C.epsc[:, 0:1], r=[s_.b(), C.epsc.b()], w=[s_.b()])
        stt(S, o_t[:], x_t[:], s_[:, 0:1], gb[:], ALU.mult, ALU.mult,
            r=[x_t.b(g) for g in range(4)] + [s_.b(), gb.b()], w=[o_t.b()])
        dma(S, C.out[tc * 128:(tc + 1) * 128, :], o_t[:], r=[o_t.b()], w=[C.out.b(tc)], q="sp")
    S.barrier()
    S.release()


def phase_ffn(C, l, which):
    S = C.S
    nm = "f%d%d_" % (l, which)
    g_d = C.ffn_norm[which][l]
    wg_d, wu_d, wd_d = C.ffn_wg[which].ap[l], C.ffn_wu[which].ap[l], C.ffn_wd[which].ap[l]
    TT = 1024
    G = 2
    NG = FFN // (128 * G)
    gcol = sb(S, nm + "gcol", [128, 16])
    S.op("sp", lambda e: e.dma_start(out=gcol[:], in_=g_d.rearrange("(c p) -> p c", p=128),
                                     allow_slow_non_contiguous=True), w=[gcol.b()], dma=True)
    xa = sb(S, nm + "xa", [128, 16, TT])
    hT = sb(S, nm + "hT", [128, 16, TT], BF16)
    rstd = sb(S, nm + "rstd", [128, TT])
    sq = [sb(S, nm + "sq%d" % i, [128, NT]) for i in range(2)]
    wg = [sb(S, nm + "wg%d" % i, [128, 16, 128 * G], BF16) for i in range(2)]
    wu = [sb(S, nm + "wu%d" % i, [128, 16, 128 * G], BF16) for i in range(2)]
    wd = [sb(S, nm + "wd%d" % i, [128, G, D], BF16) for i in range(2)]
    aT = [sb(S, nm + "aT%d" % i, [128, G, TT], BF16) for i in range(2)]
    sg = [sb(S, nm + "sg%d" % i, [128, NT]) for i in range(2)]
    p_g = [ps(S, nm + "pg%d" % i, [128, NT]) for i in range(2)]
    p_u = [ps(S, nm + "pu%d" % i, [128, NT]) for i in range(2)]
    p_d = [ps(S, nm + "pd%d" % i, [128, NT]) for i in range(3)]
    dtmp = [sb(S, nm + "dtmp%d" % i, [128, NT]) for i in range(2)]
    kt = 0
    p_s = ps(S, nm + "pss", [128, NT])
    NSUB = TT // NT
    it = 0
    for tt_i in range(S_LEN // TT):
        t0 = tt_i * TT
        for c in range(16):
            dma(S, xa[:, c, :], C.xres[c * 128:(c + 1) * 128, t0:t0 + TT], r=[C.xres.b((c, tt_i))],
                w=[xa.b((c, s)) for s in range(NSUB)], q="sp" if c % 2 == 0 else "act")
        for s_i in range(NSUB):
            for c in range(16):
                q_ = sq[c % 2]
                act(S, q_[:], xa[:, c, s_i * NT:(s_i + 1) * NT], AF.Square, r=[xa.b((c, s_i))], w=[q_.b()])
                mm(S, p_s[:], C.ones[:], q_[:], start=(c == 0), stop=(c == 15), r=[C.ones.b(), q_.b()], w=[p_s.b()])
            rsqrt(S, rstd[:, s_i * NT:(s_i + 1) * NT], p_s[:], 1.0 / D, C.epsc[:, 0:1], r=[p_s.b(), C.epsc.b()],
                  w=[rstd.b(s_i)])
        for c in range(16):
            stt(S, hT[:, c, :], xa[:, c, :], gcol[:, c:c + 1], rstd[:], ALU.mult, ALU.mult,
                r=[xa.b((c, s)) for s in range(NSUB)] + [gcol.b()] + [rstd.b(i) for i in range(NSUB)], w=[hT.b(c)])
        hT_r = [hT.b(c) for c in range(16)]
        for fg in range(NG):
            wg_t, wu_t, wd_t, a_t = wg[it % 2], wu[it % 2], wd[it % 2], aT[it % 2]
            it += 1
            f0 = fg * 128 * G
            dma(S, wg_t[:], wg_d[:, f0:f0 + 128 * G].rearrange("(c p) f -> p c f", p=128), w=[wg_t.b()], q="pool")
            dma(S, wu_t[:], wu_d[:, f0:f0 + 128 * G].rearrange("(c p) f -> p c f", p=128), w=[wu_t.b()], q="pool")
            dma(S, wd_t[:], wd_d[f0:f0 + 128 * G, :].rearrange("(c p) d -> p c d", p=128), w=[wd_t.b()], q="pool")
            k = 0
            for fc in range(G):
                for s_i in range(NSUB):
                    pg, pu, sg_t = p_g[k % 2], p_u[k % 2], sg[k % 2]
                    k += 1
                    tsl = slice(s_i * NT, (s_i + 1) * NT)
                    for c in range(16):
                        mm(S, pg[:], wg_t[:, c, fc * 128:(fc + 1) * 128], hT[:, c, tsl], start=(c == 0), stop=(c == 15),
                           r=[wg_t.b(), hT_r[c]], w=[pg.b()])
                    for c in range(16):
                        mm(S, pu[:], wu_t[:, c, fc * 128:(fc + 1) * 128], hT[:, c, tsl], start=(c == 0), stop=(c == 15),
                           r=[wu_t.b(), hT_r[c]], w=[pu.b()])
                    act(S, sg_t[:], pg[:], AF.Silu, r=[pg.b()], w=[sg_t.b()])
                    tt(S, a_t[:, fc, tsl], sg_t[:], pu[:], ALU.mult, r=[sg_t.b(), pu.b()], w=[a_t.b((fc, s_i))])
            k = 0
            for dc in range(16):
                for s_i in range(NSUB):
                    pd = p_d[k % 3]
                    k += 1
                    tsl = slice(s_i * NT, (s_i + 1) * NT)
                    for fc in range(G):
                        mm(S, pd[:], wd_t[:, fc, dc * 128:(dc + 1) * 128], a_t[:, fc, tsl], start=(fc == 0), stop=(fc == G - 1),
                           r=[wd_t.b(), a_t.b((fc, s_i))], w=[pd.b()])
                    if s_i % 2 == 0:
                        stt(S, xa[:, dc, tsl], pd[:], 0.5, xa[:, dc, tsl], ALU.mult, ALU.add, r=[pd.b(), xa.b((dc, s_i))],
                            w=[xa.b((dc, s_i))])
                    else:
                        t_ = dtmp[kt % 2]
                        kt += 1
                        act(S, t_[:], pd[:], AF.Copy, scale=0.5, r=[pd.b()], w=[t_.b()])
                        tt(S, xa[:, dc, tsl], xa[:, dc, tsl], t_[:], ALU.add, r=[xa.b((dc, s_i)), t_.b()], w=[xa.b((dc, s_i))])
        for c in range(16):
            dma(S, C.xres[c * 128:(c + 1) * 128, t0:t0 + TT], xa[:, c, :], r=[xa.b((c, s)) for s in range(NSUB)],
                w=[C.xres.b((c, tt_i))], q="sp" if c % 2 == 0 else "act")
    S.barrier()
    S.release()


PARAM_SHAPES = {
    'ffn1_norm': (DEPTH, D), 'ffn1_w_gate': (DEPTH, D, FFN), 'ffn1_w_up': (DEPTH, D, FFN), 'ffn1_w_down': (DEPTH, FFN, D),
    'mix_norm': (DEPTH, D), 'w_in': (DEPTH, D, IN_COLS),
    'rwkv_mu_prev': (DEPTH, RW_COLS), 'rwkv_mu_next': (DEPTH, RW_COLS), 'rwkv_w0': (DEPTH, 2, 512),
    'rwkv_w2': (DEPTH, 2, 96, 512), 'rwkv_a0': (DEPTH, 2, 512), 'rwkv_a2': (DEPTH, 2, 96, 512),
    'rwkv_g2': (DEPTH, 256, 512), 'rwkv_k_k': (DEPTH, 512), 'rwkv_k_a': (DEPTH, 512), 'rwkv_r_k': (DEPTH, 512),
    'rwkv_ln_w': (DEPTH, 512), 'rwkv_ln_b': (DEPTH, 512),
    'mla_q_norm': (DEPTH, 384), 'mla_w_q_up': (DEPTH, 384, 768), 'mla_kv_norm': (DEPTH, 128),
    'mla_w_kv_up': (DEPTH, 128, 1024),
    'hy_short_w': (DEPTH, 3, 1536), 'hy_short_b': (DEPTH, 1536), 'hy_w1': (DEPTH, 33, 64), 'hy_b1': (DEPTH, 64),
    'hy_w2': (DEPTH, 64, 64), 'hy_b2': (DEPTH, 64), 'hy_w3': (DEPTH, 64, 64), 'hy_b3': (DEPTH, 64),
    'hy_w4': (DEPTH, 64, 2048), 'hy_freq': (DEPTH, 3, 64), 'hy_bias': (DEPTH, 2, 512),
    'gqa_q_norm': (DEPTH, 128), 'gqa_k_norm': (DEPTH, 128), 'w_branch': (DEPTH, 4, 512, D), 'w_out': (DEPTH, D, D),
    'ffn2_norm': (DEPTH, D), 'ffn2_w_gate': (DEPTH, D, FFN), 'ffn2_w_up': (DEPTH, D, FFN), 'ffn2_w_down': (DEPTH, FFN, D),
    'final_norm': (D,),
}


def rope_tab(pos, dim):
    inv = (10000.0 ** (-np.arange(0, dim, 2, dtype=np.float32) / np.float32(dim))).astype(np.float32)
    ang = pos.astype(np.float32)[:, None] * inv[None, :]
    ang = np.concatenate([ang, ang], axis=-1)
    return np.cos(ang).astype(np.float32), np.sin(ang).astype(np.float32)


def rot_lhsT(blocks):
    n = sum(b for b in blocks)
    Rm = np.zeros((n, n), np.float32)
    o = 0
    for size in blocks:
        half = size // 2
        for i in range(half):
            Rm[o + i, o + i + half] = -1.0
            Rm[o + i + half, o + i] = 1.0
        o += size
    return np.ascontiguousarray(Rm.T)


def host_consts():
    c = {}
    c['c_ident'] = np.eye(128, dtype=np.float32)
    pos = np.arange(S_LEN)
    cr, sr = rope_tab(pos // 64, 64)
    cc, sc = rope_tab(pos % 64, 64)
    c['c_gq_cos'] = np.ascontiguousarray(np.concatenate([cr, cc], axis=1).T)
    c['c_gq_sin'] = np.ascontiguousarray(np.concatenate([sr, sc], axis=1).T)
    c['c_gq_RT'] = rot_lhsT([64, 64])
    c1, s1 = rope_tab(pos, 64)
    c['c_ml_cos'] = np.ascontiguousarray(c1.T)
    c['c_ml_sin'] = np.ascontiguousarray(s1.T)
    c['c_ml_RT'] = rot_lhsT([64])
    c.update(hy_consts())
    c.update(rw_consts())
    return c


SCRATCH = {
    'xres': ([D, S_LEN], F32), 'hT_d': ([D, S_LEN], BF16), 'pm_d': ([PM_ROWS, S_LEN], F32),
    'pdv_d': ([S_LEN, 256], BF16), 'ya_d': ([512, S_LEN], BF16), 'yb_d': ([512, S_LEN], BF16),
    'yc_d': ([512, S_LEN], BF16), 'yd_d': ([512, S_LEN], BF16), 'hyH_d': ([S_LEN, 2048], F32),
    'rw_d': ([11, 512, S_LEN], F32),
}


def build(phases=("load", "ffn1", "proj", "rwkv", "mla", "hyena", "gqa", "merge", "ffn2", "final"), depth=DEPTH,
          inject=(), expose=()):
    nc = bass.Bass("TRN2", target_bir_lowering=False)
    C = Ctx()
    C.nc = nc
    C.S = Sched(nc)
    C.x = DT(nc, "x", [S_LEN, D], F32, kind="ExternalInput")
    C.P = {}
    for k, shp in PARAM_SHAPES.items():
        C.P[k] = DT(nc, k, shp, F32, kind="ExternalInput")
    hc = host_consts()
    for k, v in hc.items():
        dt_ = F32 if v.dtype == np.float32 else BF16
        setattr(C, k, DT(nc, k, list(v.shape), dt_, kind="ExternalInput"))
    C.d_ident = C.c_ident
    C.out = DT(nc, "out", [S_LEN, D], F32, kind="ExternalOutput")
    for k, (shp, dt_) in SCRATCH.items():
        kind = "ExternalInput" if k in inject else ("ExternalOutput" if k in expose else "Internal")
        setattr(C, k, DT(nc, k, shp, dt_, kind=kind))
    C.final_norm = C.P['final_norm']
    C.ffn_norm = {1: C.P['ffn1_norm'].ap, 2: C.P['ffn2_norm'].ap}
    C.ffn_wg = {1: C.P['ffn1_w_gate'], 2: C.P['ffn2_w_gate']}
    C.ffn_wu = {1: C.P['ffn1_w_up'], 2: C.P['ffn2_w_up']}
    C.ffn_wd = {1: C.P['ffn1_w_down'], 2: C.P['ffn2_w_down']}
    load_consts(C)
    C.rw_stop = RW_STOP
    if "load" in phases:
        phase_load_x(C)
    for l in range(depth):
        if "ffn1" in phases:
            phase_ffn(C, l, 1)
        if "proj" in phases:
            phase_proj(C, l)
        if "rwkv" in phases:
            phase_rwkv(C, l)
        if "mla" in phases:
            phase_mla(C, l)
        if "hyena" in phases:
            phase_hyena(C, l)
        if "gqa" in phases:
            phase_gqa(C, l)
        if "merge" in phases:
            phase_merge(C, l)
        if "ffn2" in phases:
            phase_ffn(C, l, 2)
    if "final" in phases:
        phase_final_norm(C)
    C.S.barrier()
    C.S.finalize()
    return nc, hc


def kernel(**inputs):
    nc, hc = build()
    x = np.ascontiguousarray(np.asarray(inputs['x'], dtype=np.float32))
    n = x.shape[0]
    base = {k: np.ascontiguousarray(np.asarray(inputs[k], dtype=np.float32)) for k in PARAM_SHAPES}
    base.update(hc)
    in_maps = []
    for b in range(n):
        m = dict(base)
        m['x'] = x[b]
        in_maps.append(m)
    res = run_bass_kernel_spmd(nc, in_maps, core_ids=list(range(n)))
    return np.stack([r['out'] for r in res.results], axis=0)


def norm_tile(C, xa, hT, gcol, rstd, sq, p_s, nsub):
    S = C.S
    for s_i in range(nsub):
        for c in range(16):
            q_ = sq[c % 2]
            act(S, q_[:], xa[:, c, s_i * NT:(s_i + 1) * NT], AF.Square, r=[xa.b(c)], w=[q_.b()])
            mm(S, p_s[:], C.ones[:], q_[:], start=(c == 0), stop=(c == 15), r=[C.ones.b(), q_.b()], w=[p_s.b()])
        rsqrt(S, rstd[:, s_i * NT:(s_i + 1) * NT], p_s[:], 1.0 / D, C.epsc[:, 0:1], r=[p_s.b(), C.epsc.b()],
              w=[rstd.b(s_i)])
    for c in range(16):
        stt(S, hT[:, c, :], xa[:, c, :], gcol[:, c:c + 1], rstd[:], ALU.mult, ALU.mult,
            r=[xa.b(c), gcol.b()] + [rstd.b(i) for i in range(nsub)], w=[hT.b(c)])


def phase_proj(C, l):
    S = C.S
    nm = "pj%d_" % l
    TT = 1024
    NSUB = TT // NT
    win = C.P['w_in'].ap[l]
    gcol = sb(S, nm + "gcol", [128, 16])
    S.op("sp", lambda e: e.dma_start(out=gcol[:], in_=C.P['mix_norm'].ap[l].rearrange("(c p) -> p c", p=128),
                                     allow_slow_non_contiguous=True), w=[gcol.b()], dma=True)
    xa = sb(S, nm + "xa", [128, 16, TT])
    hT = sb(S, nm + "hT", [128, 16, TT], BF16)
    rstd = sb(S, nm + "rstd", [128, TT])
    sq = [sb(S, nm + "sq%d" % i, [128, NT]) for i in range(2)]
    wt = [sb(S, nm + "wt%d" % i, [128, 16, 512], BF16) for i in range(2)]
    stg = [sb(S, nm + "stg%d" % i, [128, TT]) for i in range(2)]
    stv = [sb(S, nm + "stv%d" % i, [128, 256], BF16) for i in range(2)]
    pp = [ps(S, nm + "pp%d" % i, [128, NT]) for i in range(4)]
    p_s = ps(S, nm + "pss", [128, NT])
    groups = []
    c0 = 0
    while c0 < PM_ROWS:
        n = min(512, PM_ROWS - c0)
        groups.append((c0, n))
        c0 += n
    it = 0
    kk = 0
    ks = 0
    for tt_i in range(S_LEN // TT):
        t0 = tt_i * TT
        for c in range(16):
            dma(S, xa[:, c, :], C.xres[c * 128:(c + 1) * 128, t0:t0 + TT], r=[C.xres.b((c, tt_i))],
                w=[xa.b(c)], q="sp" if c % 2 == 0 else "act")
        norm_tile(C, xa, hT, gcol, rstd, sq, p_s, NSUB)
        hT_r = [hT.b(c) for c in range(16)]
        for c in range(16):
            dma(S, C.hT_d[c * 128:(c + 1) * 128, t0:t0 + TT], hT[:, c, :], r=[hT.b(c)], w=[C.hT_d.b((c, tt_i))], q="sp")
        for (c0, n) in groups:
            w_t = wt[it % 2]
            it += 1
            dma(S, w_t[:, :, 0:n], win[:, c0:c0 + n].rearrange("(c p) f -> p c f", p=128), w=[w_t.b()], q="pool")
            for j in range((n + 127) // 128):
                m = min(128, n - j * 128)
                st_ = stg[ks % 2]
                ks += 1
                for s_i in range(NSUB):
                    p = pp[kk % 4]
                    kk += 1
                    tsl = slice(s_i * NT, (s_i + 1) * NT)
                    for c in range(16):
                        mm(S, p[0:m, :], w_t[:, c, j * 128:j * 128 + m], hT[:, c, tsl], start=(c == 0), stop=(c == 15),
                           r=[w_t.b(), hT_r[c]], w=[p.b()])
                    cp(S, st_[0:m, tsl], p[0:m, :], r=[p.b()], w=[st_.b()], eng="dve" if kk % 2 == 0 else "act")
                dma(S, C.pm_d[c0 + j * 128:c0 + j * 128 + m, t0:t0 + TT], st_[0:m, :], r=[st_.b()],
                    w=[C.pm_d.b((c0 + j * 128, tt_i))], q="sp")
        w_t = wt[it % 2]
        it += 1
        dma(S, w_t[:, :, 0:256], win[:, PM_ROWS:PM_ROWS + 256].rearrange("(c p) f -> p c f", p=128), w=[w_t.b()], q="pool")
        for tc in range(TT // 128):
            p = pp[kk % 4]
            kk += 1
            sv = stv[tc % 2]
            for c in range(16):
                mm(S, p[:, 0:256], hT[:, c, tc * 128:(tc + 1) * 128], w_t[:, c, 0:256], start=(c == 0), stop=(c == 15),
                   r=[w_t.b(), hT_r[c]], w=[p.b()])
            cp(S, sv[:], p[:, 0:256], r=[p.b()], w=[sv.b()], eng="dve" if tc % 2 == 0 else "act")
            dma(S, C.pdv_d[t0 + tc * 128:t0 + (tc + 1) * 128, :], sv[:], r=[sv.b()], w=[C.pdv_d.b(tt_i * 8 + tc)], q="sp")
    S.barrier()
    S.release()


def attn_core(C, nm, kq_parts, v_t, v_off, scale, y_d, row0, onesb, bufs):
    S = C.S
    p_sc, p_o, p_r, pT, rinv, yst, racc = bufs
    ksc = 0
    for ti in range(S_LEN // NT):
        tsl = slice(ti * NT, (ti + 1) * NT)
        po = p_o[ti % 2]
        pr = p_r[ti % 2]
        acc = racc[ti % 2]
        for sc in range(16):
            psc = p_sc[ksc % 2]
            p_t = pT[ksc % 3]
            ksc += 1
            for i, (kT, qT, K) in enumerate(kq_parts):
                mm(S, psc[:], kT[0:K, sc * 128:(sc + 1) * 128], qT[0:K, tsl], start=(i == 0), stop=(i == len(kq_parts) - 1),
                   r=[kT.b(), qT.b()], w=[psc.b()])
            act(S, p_t[:], psc[:], AF.Exp, scale=scale, r=[psc.b()], w=[p_t.b()])
            mm(S, po[:], v_t[:, sc, v_off:v_off + 128], p_t[:], start=(sc == 0), stop=(sc == 15), r=[v_t.b(), p_t.b()], w=[po.b()])
            if sc == 0:
                cp(S, acc[:], p_t[:], r=[p_t.b()], w=[acc.b()])
            else:
                tt(S, acc[:], acc[:], p_t[:], ALU.add, r=[acc.b(), p_t.b()], w=[acc.b()])
        mm(S, pr[:], C.ones[:], acc[:], r=[C.ones.b(), acc.b()], w=[pr.b()])
        ri = rinv[ti % 2]
        ys = yst[ti % 2]
        S.op("dve", lambda e, ri=ri, pr=pr: e.reciprocal(ri[:], pr[:]), r=[pr.b()], w=[ri.b()])
        tt(S, ys[:], po[:], ri[:], ALU.mult, r=[po.b(), ri.b()], w=[ys.b()])
        dma(S, y_d[row0:row0 + 128, tsl], ys[:], r=[ys.b()], w=[y_d.b((row0, ti))], q="sp")


def attn_bufs(S, nm):
    p_sc = [ps(S, nm + "psc%d" % i, [128, NT]) for i in range(2)]
    p_o = [ps(S, nm + "po%d" % i, [128, NT]) for i in range(2)]
    p_r = [ps(S, nm + "pr%d" % i, [128, NT]) for i in range(2)]
    pT = [sb(S, nm + "pT%d" % i, [128, NT], BF16) for i in range(3)]
    rinv = [sb(S, nm + "ri%d" % i, [128, NT]) for i in range(2)]
    yst = [sb(S, nm + "ys%d" % i, [128, NT], BF16) for i in range(2)]
    racc = [sb(S, nm + "racc%d" % i, [128, NT]) for i in range(2)]
    return p_sc, p_o, p_r, pT, rinv, yst, racc


def rope_norm_head(C, nm, src_rows, nrow, gain_col, cosT, sinT, RT, dst, tmp, p_a, p_b, do_norm):
    S = C.S
    xin, xn, t1, rs = tmp
    dma(S, xin[0:nrow, :], C.pm_d[src_rows:src_rows + nrow, :], w=[xin.b()], q="act")
    for ti in range(S_LEN // NT):
        tsl = slice(ti * NT, (ti + 1) * NT)
        if do_norm:
            act(S, t1[0:nrow, :], xin[0:nrow, tsl], AF.Square, r=[xin.b()], w=[t1.b()])
            mm(S, p_a[0:nrow, :], C.ones[0:nrow, 0:nrow], t1[0:nrow, :], r=[C.ones.b(), t1.b()], w=[p_a.b()])
            rsqrt(S, rs[0:nrow, :], p_a[0:nrow, :], 1.0 / nrow, C.epsc[0:nrow, 0:1], r=[p_a.b(), C.epsc.b()], w=[rs.b()])
            stt(S, xn[0:nrow, :], xin[0:nrow, tsl], gain_col[0:nrow, 0:1], rs[0:nrow, :], ALU.mult, ALU.mult,
                r=[xin.b(), gain_col.b(), rs.b()], w=[xn.b()])
            src = xn[0:nrow, :]
        else:
            cp(S, xn[0:nrow, :], xin[0:nrow, tsl], r=[xin.b()], w=[xn.b()], eng="pool")
            src = xn[0:nrow, :]
        mm(S, p_b[0:nrow, :], RT[0:nrow, 0:nrow], src, r=[RT.b(), xn.b()], w=[p_b.b()])
        tt(S, t1[0:nrow, :], p_b[0:nrow, :], sinT[0:nrow, tsl], ALU.mult, r=[p_b.b(), sinT.b()], w=[t1.b()])
        tt(S, xn[0:nrow, :], src, cosT[0:nrow, tsl], ALU.mult, r=[xn.b(), cosT.b()], w=[xn.b()], eng="pool")
        tt(S, dst[0:nrow, tsl], xn[0:nrow, :], t1[0:nrow, :], ALU.add, r=[xn.b(), t1.b()], w=[dst.b()])


def phase_gqa(C, l):
    S = C.S
    nm = "gq%d_" % l
    cosT = sb(S, nm + "cos", [128, S_LEN])
    sinT = sb(S, nm + "sin", [128, S_LEN])
    RT = sb(S, nm + "RT", [128, 128])
    dma(S, cosT[:], C.c_gq_cos[:], w=[cosT.b()])
    dma(S, sinT[:], C.c_gq_sin[:], w=[sinT.b()], q="act")
    dma(S, RT[:], C.c_gq_RT[:], w=[RT.b()])
    gq = sb(S, nm + "gq", [128, 2])
    S.op("sp", lambda e: e.dma_start(out=gq[:, 0:1], in_=C.P['gqa_q_norm'].ap[l].rearrange("(p o) -> p o", o=1),
                                     allow_slow_non_contiguous=True), w=[gq.b()], dma=True)
    gk = sb(S, nm + "gk", [128, 2])
    S.op("sp", lambda e: e.dma_start(out=gk[:, 0:1], in_=C.P['gqa_k_norm'].ap[l].rearrange("(p o) -> p o", o=1),
                                     allow_slow_non_contiguous=True), w=[gk.b()], dma=True)
    onesb = sb(S, nm + "onesb", [128, 128], BF16)
    mset(S, onesb[:], 1.0, w=[onesb.b()])
    tmp = (sb(S, nm + "xin", [128, S_LEN]), sb(S, nm + "xn", [128, NT]), sb(S, nm + "t1", [128, NT]), sb(S, nm + "rs", [128, NT]))
    p_a = ps(S, nm + "pa", [128, NT])
    p_b = ps(S, nm + "pb", [128, NT])
    qT = [sb(S, nm + "qT%d" % h, [128, S_LEN], BF16) for h in range(4)]
    kT = [sb(S, nm + "kT%d" % g, [128, S_LEN], BF16) for g in range(2)]
    v_t = sb(S, nm + "v", [128, 16, 256], BF16)
    dma(S, v_t[:], C.pdv_d.ap.rearrange("(c p) f -> p c f", p=128), w=[v_t.b()], q="act")
    for h in range(4):
        rope_norm_head(C, nm, OFF_GQA + h * 128, 128, gq, cosT, sinT, RT, qT[h], tmp, p_a, p_b, True)
    for g in range(2):
        rope_norm_head(C, nm, OFF_GQA + 512 + g * 128, 128, gk, cosT, sinT, RT, kT[g], tmp, p_a, p_b, True)
    bufs = attn_bufs(S, nm)
    for h in range(4):
        g = h // 2
        attn_core(C, nm, [(kT[g], qT[h], 128)], v_t, g * 128, 128.0 ** -0.5, C.yd_d, h * 128, onesb, bufs)
    S.barrier()
    S.release()


def phase_mla(C, l):
    S = C.S
    nm = "ml%d_" % l
    cosT = sb(S, nm + "cos", [64, S_LEN])
    sinT = sb(S, nm + "sin", [64, S_LEN])
    RT = sb(S, nm + "RT", [64, 64])
    dma(S, cosT[:], C.c_ml_cos[:], w=[cosT.b()])
    dma(S, sinT[:], C.c_ml_sin[:], w=[sinT.b()], q="act")
    dma(S, RT[:], C.c_ml_RT[:], w=[RT.b()])
    wq = sb(S, nm + "wq", [128, 3, 768], BF16)
    dma(S, wq[:], C.P['mla_w_q_up'].ap[l].rearrange("(c p) f -> p c f", p=128), w=[wq.b()], q="pool")
    wkv = sb(S, nm + "wkv", [128, 1024], BF16)
    dma(S, wkv[:], C.P['mla_w_kv_up'].ap[l], w=[wkv.b()], q="pool")
    gq = sb(S, nm + "gq", [128, 4])
    S.op("sp", lambda e: e.dma_start(out=gq[:, 0:3], in_=C.P['mla_q_norm'].ap[l].rearrange("(c p) -> p c", p=128),
                                     allow_slow_non_contiguous=True), w=[gq.b()], dma=True)
    S.op("sp", lambda e: e.dma_start(out=gq[:, 3:4], in_=C.P['mla_kv_norm'].ap[l].rearrange("(p o) -> p o", o=1),
                                     allow_slow_non_contiguous=True), w=[gq.b()], dma=True)
    onesb = sb(S, nm + "onesb", [128, 128], BF16)
    mset(S, onesb[:], 1.0, w=[onesb.b()])
    xq = sb(S, nm + "xq", [128, 3, S_LEN])
    dma(S, xq[:], C.pm_d.ap[OFF_MLA:OFF_MLA + 384, :].rearrange("(c p) t -> p c t", p=128), w=[xq.b()])
    xkv = sb(S, nm + "xkv", [128, S_LEN])
    dma(S, xkv[:], C.pm_d[OFF_MLA + 384:OFF_MLA + 512, :], w=[xkv.b()], q="act")
    qn = sb(S, nm + "qn", [128, 3, S_LEN], BF16)
    kvn = sb(S, nm + "kvn", [128, S_LEN], BF16)
    t1 = sb(S, nm + "t1", [128, NT])
    t2 = sb(S, nm + "t2", [128, NT])
    xn = sb(S, nm + "xn", [128, NT])
    rs = sb(S, nm + "rs", [128, NT])
    p_a = ps(S, nm + "pa", [128, NT])
    p_b = ps(S, nm + "pb", [128, NT])
    nti = S_LEN // NT
    for ti in range(nti):
        tsl = slice(ti * NT, (ti + 1) * NT)
        for c in range(3):
            act(S, t1[:], xq[:, c, tsl], AF.Square, r=[xq.b()], w=[t1.b()])
            mm(S, p_a[:], C.ones[:], t1[:], start=(c == 0), stop=(c == 2), r=[C.ones.b(), t1.b()], w=[p_a.b()])
        rsqrt(S, rs[:], p_a[:], 1.0 / 384, C.epsc[:, 0:1], r=[p_a.b(), C.epsc.b()], w=[rs.b()])
        for c in range(3):
            stt(S, qn[:, c, tsl], xq[:, c, tsl], gq[:, c:c + 1], rs[:], ALU.mult, ALU.mult, r=[xq.b(), gq.b(), rs.b()], w=[qn.b()])
        act(S, t1[:], xkv[:, tsl], AF.Square, r=[xkv.b()], w=[t1.b()])
        mm(S, p_a[:], C.ones[:], t1[:], r=[C.ones.b(), t1.b()], w=[p_a.b()])
        rsqrt(S, rs[:], p_a[:], 1.0 / 128, C.epsc[:, 0:1], r=[p_a.b(), C.epsc.b()], w=[rs.b()])
        stt(S, kvn[:, tsl], xkv[:, tsl], gq[:, 3:4], rs[:], ALU.mult, ALU.mult, r=[xkv.b(), gq.b(), rs.b()], w=[kvn.b()])
    qnope = [sb(S, nm + "qnope%d" % h, [128, S_LEN], BF16) for h in range(4)]
    qrope = [sb(S, nm + "qrope%d" % h, [64, S_LEN], BF16) for h in range(4)]
    knope = [sb(S, nm + "knope%d" % h, [128, S_LEN], BF16) for h in range(4)]
    krope = sb(S, nm + "krope", [64, S_LEN], BF16)
    v_t = sb(S, nm + "v", [128, 16, 512], BF16)
    k = 0
    for h in range(4):
        for ti in range(nti):
            tsl = slice(ti * NT, (ti + 1) * NT)
            for c in range(3):
                mm(S, p_a[:], wq[:, c, h * 192:h * 192 + 128], qn[:, c, tsl], start=(c == 0), stop=(c == 2), r=[wq.b(), qn.b()], w=[p_a.b()])
            cp(S, qnope[h][:, tsl], p_a[:], r=[p_a.b()], w=[qnope[h].b()], eng="act")
            mm(S, p_a[:], wkv[:, h * 256:h * 256 + 128], kvn[:, tsl], r=[wkv.b(), kvn.b()], w=[p_a.b()])
            cp(S, knope[h][:, tsl], p_a[:], r=[p_a.b()], w=[knope[h].b()], eng="act")
            for c in range(3):
                mm(S, p_b[0:64, :], wq[:, c, h * 192 + 128:h * 192 + 192], qn[:, c, tsl], start=(c == 0), stop=(c == 2), r=[wq.b(), qn.b()], w=[p_b.b()])
            cp(S, xn[0:64, :], p_b[0:64, :], r=[p_b.b()], w=[xn.b()])
            mm(S, p_b[0:64, :], RT[:, :], xn[0:64, :], r=[RT.b(), xn.b()], w=[p_b.b()])
            tt(S, t1[0:64, :], p_b[0:64, :], sinT[:, tsl], ALU.mult, r=[p_b.b(), sinT.b()], w=[t1.b()])
            tt(S, t2[0:64, :], xn[0:64, :], cosT[:, tsl], ALU.mult, r=[xn.b(), cosT.b()], w=[t2.b()], eng="pool")
            tt(S, qrope[h][:, tsl], t2[0:64, :], t1[0:64, :], ALU.add, r=[t2.b(), t1.b()], w=[qrope[h].b()])
    xkr = sb(S, nm + "xkr", [64, S_LEN])
    dma(S, xkr[:], C.pm_d[OFF_MLA + 512:OFF_MLA + 576, :], w=[xkr.b()])
    for ti in range(nti):
        tsl = slice(ti * NT, (ti + 1) * NT)
        mm(S, p_b[0:64, :], RT[:, :], xkr[:, tsl], r=[RT.b(), xkr.b()], w=[p_b.b()])
        tt(S, t1[0:64, :], p_b[0:64, :], sinT[:, tsl], ALU.mult, r=[p_b.b(), sinT.b()], w=[t1.b()])
        tt(S, t2[0:64, :], xkr[:, tsl], cosT[:, tsl], ALU.mult, r=[xkr.b(), cosT.b()], w=[t2.b()], eng="pool")
        tt(S, krope[:, tsl], t2[0:64, :], t1[0:64, :], ALU.add, r=[t2.b(), t1.b()], w=[krope.b()])
    for sc in range(16):
        for h in range(4):
            mm(S, p_a[:, h * 128:(h + 1) * 128], kvn[:, sc * 128:(sc + 1) * 128], wkv[:, h * 256 + 128:h * 256 + 256],
               r=[wkv.b(), kvn.b()], w=[p_a.b()])
        cp(S, v_t[:, sc, :], p_a[:], r=[p_a.b()], w=[v_t.b()], eng="dve" if sc % 2 == 0 else "act")
    bufs = attn_bufs(S, nm)
    for h in range(4):
        attn_core(C, nm, [(knope[h], qnope[h], 128), (krope, qrope[h], 64)], v_t, h * 128, 192.0 ** -0.5, C.yb_d, h * 128, onesb, bufs)
    S.barrier()
    S.release()


def phase_merge(C, l):
    S = C.S
    nm = "mg%d_" % l
    TT = 1024
    NSUB = TT // NT
    win = C.P['w_in'].ap[l]
    wbr = C.P['w_branch'].ap[l]
    wout = C.P['w_out'].ap[l]
    ys_d = [C.ya_d, C.yb_d, C.yc_d, C.yd_d]
    hT = sb(S, nm + "hT", [128, 16, TT], BF16)
    yT = [sb(S, nm + "yT%d" % n, [128, 4, TT], BF16) for n in range(4)]
    mT = sb(S, nm + "mT", [128, 16, TT], BF16)
    C.mg_acc = {(j, s_i): sb(S, nm + "acc%d_%d" % (j, s_i), [128, NT]) for j in range(4) for s_i in range(NSUB)}
    gw = [sb(S, nm + "gw%d" % i, [128, 16, 512], BF16) for i in range(2)]
    wb = [sb(S, nm + "wb%d" % i, [128, 4, 512], BF16) for i in range(2)]
    sg = [sb(S, nm + "sg%d" % i, [128, NT]) for i in range(2)]
    tmp = [sb(S, nm + "tmp%d" % i, [128, NT]) for i in range(2)]
    xa = [sb(S, nm + "xa%d" % i, [128, TT]) for i in range(2)]
    p_g = [ps(S, nm + "pg%d" % i, [128, NT]) for i in range(2)]
    p_b = [ps(S, nm + "pb%d" % i, [128, NT]) for i in range(2)]
    p_o = [ps(S, nm + "po%d" % i, [128, NT]) for i in range(2)]
    it = 0
    kk = 0
    for tt_i in range(S_LEN // TT):
        t0 = tt_i * TT
        for c in range(16):
            dma(S, hT[:, c, :], C.hT_d[c * 128:(c + 1) * 128, t0:t0 + TT], r=[C.hT_d.b((c, tt_i))], w=[hT.b()],
                q="sp" if c % 2 == 0 else "act")
        for n in range(4):
            dma(S, yT[n][:], ys_d[n].ap[:, t0:t0 + TT].rearrange("(c p) t -> p c t", p=128), w=[yT[n].b()], q="act")
        for dg in range(4):
            gws = []
            for n in range(4):
                g_t, b_t = gw[it % 2], wb[it % 2]
                it += 1
                col = OFF_GATE + n * D + dg * 512
                dma(S, g_t[:], win[:, col:col + 512].rearrange("(c p) f -> p c f", p=128), w=[g_t.b()], q="pool")
                dma(S, b_t[:], wbr[n][:, dg * 512:(dg + 1) * 512].rearrange("(c p) f -> p c f", p=128), w=[b_t.b()], q="pool")
                for j in range(4):
                    dc = dg * 4 + j
                    for s_i in range(NSUB):
                        tsl = slice(s_i * NT, (s_i + 1) * NT)
                        pg, pb = p_g[kk % 2], p_b[kk % 2]
                        sg_t, tm_t = sg[kk % 2], tmp[kk % 2]
                        kk += 1
                        for c in range(16):
                            mm(S, pg[:], g_t[:, c, j * 128:(j + 1) * 128], hT[:, c, tsl], start=(c == 0), stop=(c == 15),
                               r=[g_t.b(), hT.b()], w=[pg.b()])
                        for c in range(4):
                            mm(S, pb[:], b_t[:, c, j * 128:(j + 1) * 128], yT[n][:, c, tsl], start=(c == 0), stop=(c == 3),
                               r=[b_t.b(), yT[n].b()], w=[pb.b()])
                        act(S, sg_t[:], pg[:], AF.Sigmoid, r=[pg.b()], w=[sg_t.b()])
                        acc = C.mg_acc[(j, s_i)]
                        if n == 0:
                            tt(S, acc[:], sg_t[:], pb[:], ALU.mult, r=[sg_t.b(), pb.b()], w=[acc.b()])
                        else:
                            tt(S, tm_t[:], sg_t[:], pb[:], ALU.mult, r=[sg_t.b(), pb.b()], w=[tm_t.b()])
                            if n < 3:
                                tt(S, acc[:], acc[:], tm_t[:], ALU.add, r=[acc.b(), tm_t.b()], w=[acc.b()], eng="pool")
                            else:
                                tt(S, mT[:, dc, tsl], acc[:], tm_t[:], ALU.add, r=[acc.b(), tm_t.b()], w=[mT.b(dc)], eng="pool")
        for og in range(4):
            w_t = gw[it % 2]
            it += 1
            dma(S, w_t[:], wout[:, og * 512:(og + 1) * 512].rearrange("(c p) f -> p c f", p=128), w=[w_t.b()], q="pool")
            for j in range(4):
                dc = og * 4 + j
                x_t = xa[dc % 2]
                dma(S, x_t[:], C.xres[dc * 128:(dc + 1) * 128, t0:t0 + TT], r=[C.xres.b((dc, tt_i))], w=[x_t.b()], q="sp")
                for s_i in range(NSUB):
                    tsl = slice(s_i * NT, (s_i + 1) * NT)
                    po = p_o[kk % 2]
                    kk += 1
                    for c in range(16):
                        mm(S, po[:], w_t[:, c, j * 128:(j + 1) * 128], mT[:, c, tsl], start=(c == 0), stop=(c == 15),
                           r=[w_t.b(), mT.b(c)], w=[po.b()])
                    tt(S, x_t[:, tsl], x_t[:, tsl], po[:], ALU.add, r=[x_t.b(), po.b()], w=[x_t.b()])
                dma(S, C.xres[dc * 128:(dc + 1) * 128, t0:t0 + TT], x_t[:], r=[x_t.b()], w=[C.xres.b((dc, tt_i))], q="sp")
    S.barrier()
    S.release()


def hy_consts():
    L = S_LEN
    c = {}
    t = np.linspace(0.0, 1.0, L, dtype=np.float32)[:, None]
    w_ang = (2.0 * math.pi * np.arange(L, dtype=np.float32) / L).astype(np.float32)
    fr = np.linspace(1e-4, 15.0, 16, dtype=np.float32)
    ang = w_ang[:, None] * fr[None, :]
    z = np.concatenate([t, np.cos(ang), -np.sin(ang)], axis=-1).astype(np.float32)
    c['c_hy_z'] = np.ascontiguousarray(z.T)
    deltas = np.abs(np.linspace(math.log(1e-2) / 1.5, math.log(1e-2) / 0.3, 512)).astype(np.float32)
    win = np.exp(-t * deltas[None, :]).astype(np.float32)
    c['c_hy_win'] = np.ascontiguousarray(win.reshape(16, 128, 512).transpose(1, 0, 2))
    n = np.arange(L, dtype=np.int64)
    k = np.mod(np.outer(n, 2 * n + 1), 8192)
    angm = (2.0 * math.pi / 8192.0) * k.astype(np.float64)
    Cm = np.cos(angm)
    Sm = np.sin(angm)
    bf = ml_dtypes.bfloat16

    def t_major(M):
        return np.ascontiguousarray(M.reshape(16, 128, 16, 128).transpose(2, 1, 0, 3).reshape(16, 128, 2048).astype(bf))

    def f_major(M):
        MT = M.T
        return np.ascontiguousarray(MT.reshape(16, 128, 16, 128).transpose(2, 1, 0, 3).reshape(16, 128, 2048).astype(bf))

    c['c_hy_Ct'] = t_major(Cm)
    c['c_hy_St'] = t_major(Sm)
    c['c_hy_Cf'] = f_major(Cm)
    c['c_hy_Sf'] = f_major(Sm)
    return c


def sin_act(S, out, arg, tmp, bufs_r, w):
    s4, s8, q = tmp
    act(S, s4, arg, AF.Sin, scale=0.25, r=bufs_r, w=[w[1]])
    act(S, s8, arg, AF.Sin, scale=0.125, r=bufs_r, w=[w[2]])
    tt(S, q, s8, s8, ALU.mult, r=[w[2]], w=[w[3]])
    ts(S, q, q, -2.0, 1.0, ALU.mult, ALU.add, r=[w[3]], w=[w[3]])
    tt(S, s8, s4, q, ALU.mult, r=[w[1], w[3]], w=[w[2]])
    tt(S, q, s4, s4, ALU.mult, r=[w[1]], w=[w[3]])
    ts(S, q, q, -2.0, 1.0, ALU.mult, ALU.add, r=[w[3]], w=[w[3]])
    stt(S, out, s8, 4.0, q, ALU.mult, ALU.mult, r=[w[2], w[3]], w=[w[0]])


def phase_hyena(C, l):
    S = C.S
    nm = "hy%d_" % l
    P = C.P
    nti = S_LEN // NT
    zT = sb(S, nm + "zT", [33, S_LEN])
    dma(S, zT[:], C.c_hy_z[:], w=[zT.b()])
    w1 = sb(S, nm + "w1", [33, 64])
    dma(S, w1[:], P['hy_w1'].ap[l], w=[w1.b()])
    w2 = sb(S, nm + "w2", [64, 64])
    dma(S, w2[:], P['hy_w2'].ap[l], w=[w2.b()])
    w3 = sb(S, nm + "w3", [64, 64])
    dma(S, w3[:], P['hy_w3'].ap[l], w=[w3.b()])
    w4 = sb(S, nm + "w4", [64, 2048])
    dma(S, w4[:], P['hy_w4'].ap[l], w=[w4.b()], q="act")
    cols = sb(S, nm + "cols", [64, 8])
    for i, k in enumerate(['hy_b1', 'hy_b2', 'hy_b3']):
        S.op("sp", lambda e, i=i, k=k: e.dma_start(out=cols[:, i:i + 1], in_=P[k].ap[l].rearrange("(p o) -> p o", o=1),
                                                   allow_slow_non_contiguous=True), w=[cols.b()], dma=True)
    S.op("sp", lambda e: e.dma_start(out=cols[:, 3:6], in_=P['hy_freq'].ap[l].rearrange("k c -> c k"),
                                     allow_slow_non_contiguous=True), w=[cols.b()], dma=True)
    bias_s = sb(S, nm + "bias", [128, 1024])
    dma(S, bias_s[:], P['hy_bias'].ap[l].rearrange("o c -> (o c)").partition_broadcast(128), w=[bias_s.b()])
    ts(S, bias_s[:], bias_s[:], 1.0 / 2048.0, None, ALU.mult, r=[bias_s.b()], w=[bias_s.b()])
    win = sb(S, nm + "win", [128, 16, 512])
    dma(S, win[:], C.c_hy_win[:], w=[win.b()], q="act")
    hA = sb(S, nm + "hA", [64, S_LEN])
    hB = sb(S, nm + "hB", [64, S_LEN])
    arg = sb(S, nm + "arg", [64, NT])
    s4 = sb(S, nm + "s4", [64, NT])
    s8 = sb(S, nm + "s8", [64, NT])
    qq = sb(S, nm + "qq", [64, NT])
    p_a = ps(S, nm + "pa", [128, NT])
    p_b = ps(S, nm + "pb", [128, NT])
    p_c = ps(S, nm + "pc", [128, NT])
    p_d = ps(S, nm + "pd", [128, NT])
    layers = [(w1, zT, 33, hA, 0), (w2, hA, 64, hB, 1), (w3, hB, 64, hA, 2)]
    for (w_, src, K, dst, li) in layers:
        for ti in range(nti):
            tsl = slice(ti * NT, (ti + 1) * NT)
            mm(S, p_a[0:64, :], w_[0:K, :], src[0:K, tsl], r=[w_.b(), src.b()], w=[p_a.b()])
            ts(S, arg[:], p_a[0:64, :], cols[:, li:li + 1], cols[:, 3 + li:4 + li], ALU.add, ALU.mult,
               r=[p_a.b(), cols.b()], w=[arg.b()])
            sin_act(S, dst[:, tsl], arg[:], (s4[:], s8[:], qq[:]), [arg.b()], [dst.b(), s4.b(), s8.b(), qq.b()])
    h3 = hA
    hs = sb(S, nm + "hs", [128, 16, 1024], BF16)
    hd = sb(S, nm + "hd", [128, 16, 1024], BF16)
    f0 = sb(S, nm + "f0", [128, 1024])
    f1 = sb(S, nm + "f1", [128, 1024])
    pg = [p_a, p_b, p_c, p_d]
    for tc in range(16):
        for g in range(4):
            mm(S, pg[g][:], h3[:, tc * 128:(tc + 1) * 128], w4[:, g * 512:(g + 1) * 512], r=[h3.b(), w4.b()], w=[pg[g].b()])
        for o in range(2):
            tt(S, f0[:, o * 512:(o + 1) * 512], pg[o][:], win[:, tc, :], ALU.mult, r=[pg[o].b(), win.b()], w=[f0.b()])
            tt(S, f1[:, o * 512:(o + 1) * 512], pg[2 + o][:], win[:, tc, :], ALU.mult, r=[pg[2 + o].b(), win.b()], w=[f1.b()])
        if tc == 0:
            mset(S, f1[0:1, :], 0.0, w=[f1.b()])
        tt(S, hs[:, tc, :], f0[:], f1[:], ALU.add, r=[f0.b(), f1.b()], w=[hs.b()], eng="pool")
        tt(S, hd[:, tc, :], f0[:], f1[:], ALU.subtract, r=[f0.b(), f1.b()], w=[hd.b()])
    ct = [sb(S, nm + "ct%d" % i, [128, 2048], BF16) for i in range(2)]
    st = [sb(S, nm + "st%d" % i, [128, 2048], BF16) for i in range(2)]
    hst = [sb(S, nm + "hst%d" % i, [128, 4, 512]) for i in range(2)]
    for fc in range(16):
        c_t, s_t, h_t = ct[fc % 2], st[fc % 2], hst[fc % 2]
        dma(S, c_t[:], C.c_hy_Ct.ap[fc], w=[c_t.b()], q="sp")
        dma(S, s_t[:], C.c_hy_St.ap[fc], w=[s_t.b()], q="act")
        for o in range(2):
            pr, pi = pg[o * 2], pg[o * 2 + 1]
            for tc in range(16):
                mm(S, pr[:], c_t[:, tc * 128:(tc + 1) * 128], hs[:, tc, o * 512:(o + 1) * 512], start=(tc == 0), stop=(tc == 15),
                   r=[c_t.b(), hs.b()], w=[pr.b()])
            for tc in range(16):
                mm(S, pi[:], s_t[:, tc * 128:(tc + 1) * 128], hd[:, tc, o * 512:(o + 1) * 512], start=(tc == 0), stop=(tc == 15),
                   r=[s_t.b(), hd.b()], w=[pi.b()])
            stt(S, h_t[:, o * 2, :], pr[:], 1.0 / 2048.0, bias_s[:, o * 512:(o + 1) * 512], ALU.mult, ALU.add,
                r=[pr.b(), bias_s.b()], w=[h_t.b()])
            act(S, h_t[:, o * 2 + 1, :], pi[:], AF.Copy, scale=1.0 / 2048.0, r=[pi.b()], w=[h_t.b()])
        dma(S, C.hyH_d.ap[fc * 128:(fc + 1) * 128, :].rearrange("p (k c) -> p k c", k=4), h_t[:], r=[h_t.b()],
            w=[C.hyH_d.b(fc)], q="sp")
    S.barrier()
    S.release()
    swc = sb(S, nm + "swc", [128, 12, 4])
    for k in range(3):
        S.op("sp", lambda e, k=k: e.dma_start(out=swc[:, :, k:k + 1],
                                              in_=P['hy_short_w'].ap[l][k].rearrange("(c p o) -> p c o", p=128, o=1),
                                              allow_slow_non_contiguous=True), w=[swc.b()], dma=True)
    S.op("sp", lambda e: e.dma_start(out=swc[:, :, 3:4], in_=P['hy_short_b'].ap[l].rearrange("(c p o) -> p c o", p=128, o=1),
                                     allow_slow_non_contiguous=True), w=[swc.b()], dma=True)
    x1_tm = sb(S, nm + "x1tm", [128, 16, 512])
    x2T = sb(S, nm + "x2T", [128, 4, S_LEN])
    v_tm = sb(S, nm + "vtm", [128, 16, 512], BF16)
    Yr = sb(S, nm + "Yr", [128, 16, 512], BF16)
    Ys = sb(S, nm + "Ys", [128, 16, 512], BF16)
    ycT = sb(S, nm + "ycT", [128, 4, S_LEN], BF16)
    pin = [sb(S, nm + "pin%d" % i, [128, S_LEN]) for i in range(2)]
    u = sb(S, nm + "u", [128, S_LEN])
    pt = [ps(S, nm + "pt%d" % i, [128, NT]) for i in range(2)]
    kk = 0
    for ch in range(12):
        p_in = pin[ch % 2]
        dma(S, p_in[:], C.pm_d[OFF_HY + ch * 128:OFF_HY + (ch + 1) * 128, :], w=[p_in.b()], q="sp" if ch % 2 == 0 else "act")
        dst = x2T[:, ch - 4, :] if 4 <= ch < 8 else u[:]
        dbuf = x2T.b() if 4 <= ch < 8 else u.b()
        ts(S, dst, p_in[:], swc[:, ch, 1:2], swc[:, ch, 3:4], ALU.mult, ALU.add, r=[p_in.b(), swc.b()], w=[dbuf])
        d1 = x2T[:, ch - 4, 1:S_LEN] if 4 <= ch < 8 else u[:, 1:S_LEN]
        d2 = x2T[:, ch - 4, 0:S_LEN - 1] if 4 <= ch < 8 else u[:, 0:S_LEN - 1]
        stt(S, d1, p_in[:, 0:S_LEN - 1], swc[:, ch, 0:1], d1, ALU.mult, ALU.add, r=[p_in.b(), swc.b(), dbuf], w=[dbuf])
        stt(S, d2, p_in[:, 1:S_LEN], swc[:, ch, 2:3], d2, ALU.mult, ALU.add, r=[p_in.b(), swc.b(), dbuf], w=[dbuf])
        if ch < 4 or ch >= 8:
            cc = ch if ch < 4 else ch - 8
            tgt = x1_tm if ch < 4 else v_tm
            for tc in range(16):
                p = pt[kk % 2]
                kk += 1
                tr(S, p[:, 0:128], u[:, tc * 128:(tc + 1) * 128], C.ident[:], r=[u.b(), C.ident.b()], w=[p.b()])
                cp(S, tgt[:, tc, cc * 128:(cc + 1) * 128], p[:, 0:128], r=[p.b()], w=[tgt.b()], eng="dve" if kk % 2 == 0 else "act")
    ct = [sb(S, nm + "dct%d" % i, [128, 2048], BF16) for i in range(2)]
    st = [sb(S, nm + "dst%d" % i, [128, 2048], BF16) for i in range(2)]
    hst = [sb(S, nm + "dhst%d" % i, [128, 2, 512]) for i in range(2)]
    ur = [sb(S, nm + "ur%d" % i, [128, 512]) for i in range(2)]
    us = [sb(S, nm + "us%d" % i, [128, 512]) for i in range(2)]
    m1 = sb(S, nm + "m1", [128, 512])
    m2 = sb(S, nm + "m2", [128, 512])
    m3 = sb(S, nm + "m3", [128, 512])
    m4 = sb(S, nm + "m4", [128, 512])
    p_r = [ps(S, nm + "pr%d" % i, [128, NT]) for i in range(2)]
    p_s = [ps(S, nm + "psn%d" % i, [128, NT]) for i in range(2)]
    p_y = [ps(S, nm + "py%d" % i, [128, NT]) for i in range(2)]
    it = 0
    for o in range(2):
        src_tm = v_tm
        for fc in range(16):
            c_t, s_t, h_t = ct[it % 2], st[it % 2], hst[it % 2]
            u_r, u_s = ur[it % 2], us[it % 2]
            pr, pi = p_r[it % 2], p_s[it % 2]
            it += 1
            dma(S, c_t[:], C.c_hy_Ct.ap[fc], w=[c_t.b()], q="sp")
            dma(S, s_t[:], C.c_hy_St.ap[fc], w=[s_t.b()], q="act")
            dma(S, h_t[:], C.hyH_d.ap[fc * 128:(fc + 1) * 128, o * 1024:(o + 1) * 1024].rearrange("p (k c) -> p k c", k=2),
                r=[C.hyH_d.b(fc)], w=[h_t.b()], q="sp")
            for tc in range(16):
                mm(S, pr[:], c_t[:, tc * 128:(tc + 1) * 128], src_tm[:, tc, :], start=(tc == 0), stop=(tc == 15),
                   r=[c_t.b(), src_tm.b()], w=[pr.b()])
            for tc in range(16):
                mm(S, pi[:], s_t[:, tc * 128:(tc + 1) * 128], src_tm[:, tc, :], start=(tc == 0), stop=(tc == 15),
                   r=[s_t.b(), src_tm.b()], w=[pi.b()])
            cp(S, u_r[:], pr[:], r=[pr.b()], w=[u_r.b()], eng="act")
            cp(S, u_s[:], pi[:], r=[pi.b()], w=[u_s.b()], eng="act")
            tt(S, m1[:], u_r[:], h_t[:, 0, :], ALU.mult, r=[u_r.b(), h_t.b()], w=[m1.b()])
            tt(S, m2[:], u_s[:], h_t[:, 1, :], ALU.mult, r=[u_s.b(), h_t.b()], w=[m2.b()], eng="pool")
            tt(S, Yr[:, fc, :], m1[:], m2[:], ALU.subtract, r=[m1.b(), m2.b()], w=[Yr.b()])
            tt(S, m3[:], u_r[:], h_t[:, 1, :], ALU.mult, r=[u_r.b(), h_t.b()], w=[m3.b()], eng="pool")
            tt(S, m4[:], u_s[:], h_t[:, 0, :], ALU.mult, r=[u_s.b(), h_t.b()], w=[m4.b()])
            tt(S, Ys[:, fc, :], m3[:], m4[:], ALU.add, r=[m3.b(), m4.b()], w=[Ys.b()], eng="pool")
        for tc in range(16):
            c_t, s_t = ct[it % 2], st[it % 2]
            it += 1
            dma(S, c_t[:], C.c_hy_Cf.ap[tc], w=[c_t.b()], q="sp")
            dma(S, s_t[:], C.c_hy_Sf.ap[tc], w=[s_t.b()], q="act")
            if o == 0:
                py = p_y[tc % 2]
                for fc in range(16):
                    mm(S, py[:], c_t[:, fc * 128:(fc + 1) * 128], Yr[:, fc, :], start=(fc == 0), stop=False,
                       r=[c_t.b(), Yr.b()], w=[py.b()])
                for fc in range(16):
                    mm(S, py[:], s_t[:, fc * 128:(fc + 1) * 128], Ys[:, fc, :], start=False, stop=(fc == 15),
                       r=[s_t.b(), Ys.b()], w=[py.b()])
                tt(S, v_tm[:, tc, :], x1_tm[:, tc, :], py[:], ALU.mult, r=[x1_tm.b(), py.b()], w=[v_tm.b()])
            else:
                py = p_y[tc % 2]
                for cc in range(4):
                    for fc in range(16):
                        mm(S, py[:, cc * 128:(cc + 1) * 128], Yr[:, fc, cc * 128:(cc + 1) * 128], c_t[:, fc * 128:(fc + 1) * 128],
                           start=(fc == 0), stop=False, r=[c_t.b(), Yr.b()], w=[py.b()])
                    for fc in range(16):
                        mm(S, py[:, cc * 128:(cc + 1) * 128], Ys[:, fc, cc * 128:(cc + 1) * 128], s_t[:, fc * 128:(fc + 1) * 128],
                           start=False, stop=(fc == 15), r=[s_t.b(), Ys.b()], w=[py.b()])
                for cc in range(4):
                    tt(S, ycT[:, cc, tc * 128:(tc + 1) * 128], x2T[:, cc, tc * 128:(tc + 1) * 128], py[:, cc * 128:(cc + 1) * 128],
                       ALU.mult, r=[x2T.b(), py.b()], w=[ycT.b()], eng="dve")
    dma(S, C.yc_d.ap.rearrange("(c p) t -> p c t", p=128), ycT[:], r=[ycT.b()], w=[C.yc_d.b()], q="sp")
    S.barrier()
    S.release()


RW_R, RW_V, RW_KK, RW_G, RW_BONUS = 0, 1, 2, 3, 4
RW_E, RW_B, RW_KD = 5, 7, 9
GN_EPS = 64e-5
RW_STOP = 0


def rw_consts():
    c = {}
    j = np.arange(128)[:, None]
    t = np.arange(128)[None, :]
    MU_s = (t > j).astype(np.float32)
    ML_s = (t < j).astype(np.float32)
    MU_i = (t >= j).astype(np.float32)
    ML_i = (t <= j).astype(np.float32)
    c['c_rw_m4'] = np.ascontiguousarray(np.stack([np.concatenate([MU_s, MU_s, ML_s, ML_s], 1),
                                                  np.concatenate([ML_s, ML_s, MU_s, MU_s], 1)], 0))
    c['c_rw_m3'] = np.ascontiguousarray(np.stack([np.concatenate([MU_s, MU_i, MU_i], 1),
                                                  np.concatenate([ML_s, ML_i, ML_i], 1)], 0))
    c['c_rw_tri'] = np.ascontiguousarray(np.stack([MU_i, ML_i], 0))
    blk = np.zeros((128, 128), np.float32)
    blk[:64, :64] = 1.0
    blk[64:, 64:] = 1.0
    c['c_rw_blk'] = blk
    return c


def phase_rwkv(C, l):
    S = C.S
    P = C.P
    nm = "rw%d_" % l
    T_ = S_LEN
    nti = T_ // NT
    rw = C.rw_d

    def col(dst, src_ap, n):
        S.op("sp", lambda e: e.dma_start(out=dst, in_=src_ap.rearrange("(p o) -> p o", o=1), allow_slow_non_contiguous=True),
             w=[], dma=True)

    blk = sb(S, nm + "blk", [128, 128])
    dma(S, blk[:], C.c_rw_blk[:], w=[blk.b()])
    pin = [sb(S, nm + "pin%d" % i, [128, T_]) for i in range(2)]
    mcol = [sb(S, nm + "mcol%d" % i, [128, 4]) for i in range(2)]
    npiece = [0]

    def shift_piece(row0, nrows, dst, dbuf):
        i = npiece[0] % 2
        npiece[0] += 1
        p_in, mc = pin[i], mcol[i]
        dma(S, p_in[0:nrows, :], C.pm_d[row0:row0 + nrows, :], w=[p_in.b()], q="sp" if i == 0 else "act")
        S.op("sp", lambda e: e.dma_start(out=mc[0:nrows, 0:1], in_=P['rwkv_mu_prev'].ap[l][row0:row0 + nrows].rearrange("(p o) -> p o", o=1),
                                         allow_slow_non_contiguous=True), w=[mc.b()], dma=True)
        S.op("sp", lambda e: e.dma_start(out=mc[0:nrows, 1:2], in_=P['rwkv_mu_next'].ap[l][row0:row0 + nrows].rearrange("(p o) -> p o", o=1),
                                         allow_slow_non_contiguous=True), w=[mc.b()], dma=True)
        tt(S, mc[0:nrows, 2:3], mc[0:nrows, 0:1], mc[0:nrows, 1:2], ALU.add, r=[mc.b()], w=[mc.b()])
        ts(S, mc[0:nrows, 2:3], mc[0:nrows, 2:3], -1.0, 1.0, ALU.mult, ALU.add, r=[mc.b()], w=[mc.b()])
        ts(S, dst[0:nrows, :], p_in[0:nrows, :], mc[0:nrows, 2:3], None, ALU.mult, r=[p_in.b(), mc.b()], w=[dbuf])
        stt(S, dst[0:nrows, 1:T_], p_in[0:nrows, 0:T_ - 1], mc[0:nrows, 0:1], dst[0:nrows, 1:T_], ALU.mult, ALU.add,
            r=[p_in.b(), mc.b(), dbuf], w=[dbuf])
        stt(S, dst[0:nrows, 0:T_ - 1], p_in[0:nrows, 1:T_], mc[0:nrows, 1:2], dst[0:nrows, 0:T_ - 1], ALU.mult, ALU.add,
            r=[p_in.b(), mc.b(), dbuf], w=[dbuf])

    lw = [sb(S, nm + "lw%d" % d, [96, T_]) for d in range(2)]
    la = [sb(S, nm + "la%d" % d, [96, T_]) for d in range(2)]
    lg = [sb(S, nm + "lg%d" % i, [128, T_]) for i in range(2)]
    for d in range(2):
        shift_piece(1536 + 96 * d, 96, lw[d], lw[d].b())
        act(S, lw[d][:], lw[d][:], AF.Tanh, r=[lw[d].b()], w=[lw[d].b()])
        shift_piece(1728 + 96 * d, 96, la[d], la[d].b())
    for i in range(2):
        shift_piece(1920 + 128 * i, 128, lg[i], lg[i].b())
        act(S, lg[i][:], lg[i][:], AF.Sigmoid, r=[lg[i].b()], w=[lg[i].b()])
    w2 = sb(S, nm + "w2", [96, 2, 512])
    a2 = sb(S, nm + "a2", [96, 2, 512])
    g2 = sb(S, nm + "g2", [128, 2, 512])
    dma(S, w2[:], P['rwkv_w2'].ap[l].rearrange("d k c -> k d c"), w=[w2.b()])
    dma(S, a2[:], P['rwkv_a2'].ap[l].rearrange("d k c -> k d c"), w=[a2.b()], q="act")
    dma(S, g2[:], P['rwkv_g2'].ap[l].rearrange("(i p) c -> p i c", p=128), w=[g2.b()])
    pc = sb(S, nm + "pc", [128, 4, 12])
    srcs = [P['rwkv_w0'].ap[l][0], P['rwkv_w0'].ap[l][1], P['rwkv_a0'].ap[l][0], P['rwkv_a0'].ap[l][1], P['rwkv_k_k'].ap[l],
            P['rwkv_k_a'].ap[l], P['rwkv_r_k'].ap[l], P['rwkv_ln_w'].ap[l], P['rwkv_ln_b'].ap[l]]
    for j, s_ap in enumerate(srcs):
        S.op("sp", lambda e, j=j, s_ap=s_ap: e.dma_start(out=pc[:, :, j:j + 1], in_=s_ap.rearrange("(c p o) -> p c o", p=128, o=1),
                                                         allow_slow_non_contiguous=True), w=[pc.b()], dma=True)
    ts(S, pc[:, :, 9:10], pc[:, :, 5:6], -1.0, 1.0, ALU.mult, ALU.add, r=[pc.b()], w=[pc.b()])
    rs_ = sb(S, nm + "rs", [128, T_])
    ks_ = sb(S, nm + "ks", [128, T_])
    vs_ = sb(S, nm + "vs", [128, T_])
    kk_ = sb(S, nm + "kk", [128, T_])
    tl = {k: sb(S, nm + "tl_" + k, [128, NT]) for k in ["sq", "den", "e0", "e1", "a0", "a1", "t0", "kd0", "kd1", "b0", "b1", "kds", "pr", "bo", "g"]}
    pp = [ps(S, nm + "pp%d" % i, [128, NT]) for i in range(6)]
    kq = [0]

    def nps():
        kq[0] += 1
        return pp[kq[0] % 6]

    for cc in range(4):
        shift_piece(cc * 128, 128, rs_, rs_.b())
        shift_piece(512 + cc * 128, 128, ks_, ks_.b())
        shift_piece(1024 + cc * 128, 128, vs_, vs_.b())
        csl = slice(cc * 128, (cc + 1) * 128)
        dma(S, rw.ap[RW_R][csl, :], rs_[:], r=[rs_.b()], w=[rw.b((RW_R, cc))], q="sp")
        dma(S, rw.ap[RW_V][csl, :], vs_[:], r=[vs_.b()], w=[rw.b((RW_V, cc))], q="act")
        for ti in range(nti):
            tsl = slice(ti * NT, (ti + 1) * NT)
            ts(S, kk_[:, tsl], ks_[:, tsl], pc[:, cc, 4:5], None, ALU.mult, r=[ks_.b(), pc.b()], w=[kk_.b()])
            act(S, tl["sq"][:], kk_[:, tsl], AF.Square, r=[kk_.b()], w=[tl["sq"].b()])
            p = nps()
            mm(S, p[:], blk[:], tl["sq"][:], r=[blk.b(), tl["sq"].b()], w=[p.b()])
            act(S, tl["den"][:], p[:], AF.Sqrt, r=[p.b()], w=[tl["den"].b()])
            ts(S, tl["den"][:], tl["den"][:], 1e-12, None, ALU.max, r=[tl["den"].b()], w=[tl["den"].b()])
            S.op("dve", lambda e: e.reciprocal(tl["den"][:], tl["den"][:]), r=[tl["den"].b()], w=[tl["den"].b()])
            tt(S, kk_[:, tsl], kk_[:, tsl], tl["den"][:], ALU.mult, r=[kk_.b(), tl["den"].b()], w=[kk_.b()])
            for d in range(2):
                e_t, a_t, kd_t, b_t = tl["e%d" % d], tl["a%d" % d], tl["kd%d" % d], tl["b%d" % d]
                p = nps()
                mm(S, p[:], w2[:, d, csl], lw[d][:, tsl], r=[w2.b(), lw[d].b()], w=[p.b()])
                act(S, e_t[:], p[:], AF.Sigmoid, bias=pc[:, cc, d:d + 1], r=[p.b(), pc.b()], w=[e_t.b()])
                ts(S, e_t[:], e_t[:], -math.exp(-0.5), None, ALU.mult, r=[e_t.b()], w=[e_t.b()], eng="pool")
                dma(S, rw.ap[RW_E + d][csl, tsl], e_t[:], r=[e_t.b()], w=[rw.b((RW_E + d, cc))], q="sp")
                p = nps()
                mm(S, p[:], a2[:, d, csl], la[d][:, tsl], r=[a2.b(), la[d].b()], w=[p.b()])
                act(S, a_t[:], p[:], AF.Sigmoid, bias=pc[:, cc, 2 + d:3 + d], r=[p.b(), pc.b()], w=[a_t.b()])
                ts(S, tl["t0"][:], a_t[:], pc[:, cc, 5:6], pc[:, cc, 9:10], ALU.mult, ALU.add, r=[a_t.b(), pc.b()], w=[tl["t0"].b()])
                tt(S, kd_t[:], ks_[:, tsl], tl["t0"][:], ALU.mult, r=[ks_.b(), tl["t0"].b()], w=[kd_t.b()])
                dma(S, rw.ap[RW_KD + d][csl, tsl], kd_t[:], r=[kd_t.b()], w=[rw.b((RW_KD + d, cc))], q="act")
                tt(S, b_t[:], a_t[:], kk_[:, tsl], ALU.mult, r=[a_t.b(), kk_.b()], w=[b_t.b()], eng="pool")
                dma(S, rw.ap[RW_B + d][csl, tsl], b_t[:], r=[b_t.b()], w=[rw.b((RW_B + d, cc))], q="sp")
            tt(S, tl["kds"][:], tl["kd0"][:], tl["kd1"][:], ALU.add, r=[tl["kd0"].b(), tl["kd1"].b()], w=[tl["kds"].b()], eng="pool")
            stt(S, tl["pr"][:], rs_[:, tsl], pc[:, cc, 6:7], tl["kds"][:], ALU.mult, ALU.mult, r=[rs_.b(), pc.b(), tl["kds"].b()], w=[tl["pr"].b()])
            p = nps()
            mm(S, p[:], blk[:], tl["pr"][:], r=[blk.b(), tl["pr"].b()], w=[p.b()])
            tt(S, tl["bo"][:], p[:], vs_[:, tsl], ALU.mult, r=[p.b(), vs_.b()], w=[tl["bo"].b()])
            dma(S, rw.ap[RW_BONUS][csl, tsl], tl["bo"][:], r=[tl["bo"].b()], w=[rw.b((RW_BONUS, cc))], q="act")
            p = nps()
            for i in range(2):
                mm(S, p[:], g2[:, i, csl], lg[i][:, tsl], start=(i == 0), stop=(i == 1), r=[g2.b(), lg[i].b()], w=[p.b()])
            cp(S, tl["g"][:], p[:], r=[p.b()], w=[tl["g"].b()], eng="act")
            dma(S, rw.ap[RW_G][csl, tsl], tl["g"][:], r=[tl["g"].b()], w=[rw.b((RW_G, cc))], q="sp")
        dma(S, rw.ap[RW_KK][csl, :], kk_[:], r=[kk_.b()], w=[rw.b((RW_KK, cc))], q="sp")
    S.barrier()
    S.release()
    if getattr(C, "rw_stop", 0) == 1:
        return
    CH = 128
    NCH = T_ // CH
    m4 = [sb(S, nm + "m4_%d" % d, [128, 512]) for d in range(2)]
    m3 = [sb(S, nm + "m3_%d" % d, [128, 384]) for d in range(2)]
    tri = [sb(S, nm + "tri%d" % d, [128, 128]) for d in range(2)]
    for d in range(2):
        dma(S, m4[d][:], C.c_rw_m4.ap[d], w=[m4[d].b()])
        dma(S, m3[d][:], C.c_rw_m3.ap[d], w=[m3[d].b()], q="act")
        dma(S, tri[d][:], C.c_rw_tri.ap[d], w=[tri[d].b()])
    hmk = sb(S, nm + "hmk", [128, 128])
    dma(S, hmk[:], C.c_rw_blk[:], w=[hmk.b()])
    Yacc = sb(S, nm + "Yacc", [128, NCH, 512])
    ST = {}
    for d in range(2):
        for cc in range(4):
            ST[(d, cc)] = [sb(S, nm + "ST%d%d%d" % (d, cc, i), [64, 2, 64], BF16) for i in range(2)]
            mset(S, ST[(d, cc)][0][:], 0.0, w=[ST[(d, cc)][0].b()])
    slots = {}
    for d in range(2):
        for cc in range(4):
            sl = {}
            sn = nm + "s%d%d_" % (d, cc)
            sl["in"] = sb(S, sn + "in", [128, 6, CH])
            sl["etm"] = sb(S, sn + "etm", [128, 128])
            sl["cx"] = sb(S, sn + "cx", [128, 128])
            sl["gp"] = sb(S, sn + "gp", [128, 128])
            sl["gn"] = sb(S, sn + "gn", [128, 128])
            sl["gx"] = sb(S, sn + "gx", [128, 128])
            sl["cm"] = sb(S, sn + "cm", [128, 7, 128], BF16)
            sl["AR"] = sb(S, sn + "AR", [128, 4, 128], BF16)
            sl["Bm"] = sb(S, sn + "Bm", [128, 2, 128], BF16)
            sl["tm"] = sb(S, sn + "tm", [128, 4, 128], BF16)
            sl["Xt"] = [sb(S, sn + "Xt%d" % i, [128, 2, 128], BF16) for i in range(2)]
            sl["XTT"] = [sb(S, sn + "XTT%d" % i, [128, 2, 2, 128], BF16) for i in range(2)]
            sl["TTf"] = sb(S, sn + "TTf", [128, 2, 128], BF16)
            sl["L3"] = sb(S, sn + "L3", [128, 2, 3, 128], BF16)
            sl["W1A"] = sb(S, sn + "W1A", [128, 2, 128], BF16)
            sl["QP"] = sb(S, sn + "QP", [128, 2, 128], BF16)
            sl["GT"] = sb(S, sn + "GT", [64, 2, 64], BF16)
            sl["RyT"] = sb(S, sn + "RyT", [64, 2, 128], BF16)
            sl["Dg"] = sb(S, sn + "Dg", [128, 128], BF16)
            slots[(d, cc)] = sl
    bank = [ps(S, nm + "bk%d" % i, [128, NT]) for i in range(6)]
    bankb = [ps(S, nm + "bkb%d" % i, [128, 1024], BF16) for i in range(2)]
    kb = [0]

    def nb():
        kb[0] += 1
        return bank[kb[0] % 6]

    IDB = C.identb
    touched = set()
    par = {}
    for step in range(NCH):
        units = []
        for d in range(2):
            n = step if d == 0 else NCH - 1 - step
            for cc in range(4):
                units.append((d, cc, n, slots[(d, cc)]))
        for d, cc, n, sl in units:
            t0 = n * CH
            csl = slice(cc * 128, (cc + 1) * 128)
            srcs = [RW_R, RW_KK, RW_V, RW_E + d, RW_B + d, RW_KD + d]
            for j, k_ in enumerate(srcs):
                dma(S, sl["in"][:, j, :], rw.ap[k_][csl, t0:t0 + CH], r=[rw.b((k_, cc))], w=[sl["in"].b()], q="sp")
        pbank = {}
        for half in (units[0:4], units[4:8]):
            for u in half:
                d, cc, n, sl = u
                p = nb()
                pbank[(d, cc)] = p
                tr(S, p[:, 0:128], sl["in"][:, 3, :], C.ident[:], r=[sl["in"].b(), C.ident.b()], w=[p.b()])
            for u in half:
                d, cc, n, sl = u
                p = pbank[(d, cc)]
                cp(S, sl["etm"][:], p[:, 0:128], r=[p.b()], w=[sl["etm"].b()], eng="act")
            for u in half:
                d, cc, n, sl = u
                p2 = nb()
                pbank[(d, cc)] = p2
                mm(S, p2[:, 0:128], sl["etm"][:], tri[d][:], r=[sl["etm"].b(), tri[d].b()], w=[p2.b()])
            for u in half:
                d, cc, n, sl = u
                p2 = pbank[(d, cc)]
                act(S, sl["gp"][:], p2[:, 0:128], AF.Exp, r=[p2.b()], w=[sl["gp"].b()])
                act(S, sl["gn"][:], p2[:, 0:128], AF.Exp, scale=-1.0, r=[p2.b()], w=[sl["gn"].b()])
            for u in half:
                d, cc, n, sl = u
                p2 = pbank[(d, cc)]
                tt(S, sl["cx"][:], p2[:, 0:128], sl["in"][:, 3, :], ALU.subtract, r=[p2.b(), sl["in"].b()], w=[sl["cx"].b()])
            for u in half:
                d, cc, n, sl = u
                act(S, sl["gx"][:], sl["cx"][:], AF.Exp, r=[sl["cx"].b()], w=[sl["gx"].b()])
            for u in half:
                d, cc, n, sl = u
                inb, cm = sl["in"], sl["cm"]
                gcol_ap = sl["gp"][:, 127:128] if d == 0 else sl["gp"][:, 0:1]
                tt(S, cm[:, 1, :], inb[:, 4, :], sl["gn"][:], ALU.mult, r=[inb.b(), sl["gn"].b()], w=[cm.b(1)], eng="pool")
                tt(S, cm[:, 2, :], inb[:, 5, :], sl["gn"][:], ALU.mult, r=[inb.b(), sl["gn"].b()], w=[cm.b(2)])
                tt(S, cm[:, 3, :], inb[:, 0, :], sl["gp"][:], ALU.mult, r=[inb.b(), sl["gp"].b()], w=[cm.b(3)], eng="pool")
                stt(S, cm[:, 4, :], inb[:, 4, :], gcol_ap, sl["gn"][:], ALU.mult, ALU.mult, r=[inb.b(), sl["gn"].b(), sl["gp"].b()], w=[cm.b(4)])
                stt(S, cm[:, 5, :], inb[:, 5, :], gcol_ap, sl["gn"][:], ALU.mult, ALU.mult, r=[inb.b(), sl["gn"].b(), sl["gp"].b()], w=[cm.b(5)])
                cp(S, cm[:, 6, :], inb[:, 2, :], r=[inb.b()], w=[cm.b(6)], eng="pool")
                ts(S, sl["Dg"][:], C.ident[:], gcol_ap, None, ALU.mult, r=[C.ident.b(), sl["gp"].b()], w=[sl["Dg"].b()], eng="pool")
            for u in half:
                d, cc, n, sl = u
                inb, cm = sl["in"], sl["cm"]
                stt(S, cm[:, 0, :], inb[:, 1, :], -1.0, sl["gx"][:], ALU.mult, ALU.mult, r=[inb.b(), sl["gx"].b()], w=[cm.b(0)])
            for u in half:
                d, cc, n, sl = u
                cm, AR, Bm = sl["cm"], sl["AR"], sl["Bm"]
                for hp in range(2):
                    hcol = hmk[:, 64 * hp:64 * hp + 1]
                    ts(S, AR[:, hp, :], cm[:, 0, :], hcol, None, ALU.mult, r=[cm.b(0), hmk.b()], w=[AR.b(hp)], eng="pool")
                    ts(S, AR[:, 2 + hp, :], cm[:, 3, :], hcol, None, ALU.mult, r=[cm.b(3), hmk.b()], w=[AR.b(2 + hp)])
                    ts(S, Bm[:, hp, :], cm[:, 1, :], hcol, None, ALU.mult, r=[cm.b(1), hmk.b()], w=[Bm.b(hp)],
                       eng="pool" if hp == 0 else "dve")
            for ui, u in enumerate(half):
                d, cc, n, sl = u
                cm = sl["cm"]
                bb = bankb[ui % 2]
                for j, jj in enumerate([0, 4, 5, 6]):
                    tr(S, bb[:, j * 128:(j + 1) * 128], cm[:, jj, :], IDB[:], r=[cm.b(jj), IDB.b()], w=[bb.b()])
                cp(S, sl["tm"][:].rearrange("p a b -> p (a b)"), bb[:, 0:512], r=[bb.b()], w=[sl["tm"].b()], eng="act")
            pbk = {}
            for u in half:
                d, cc, n, sl = u
                cm, AR, Bm = sl["cm"], sl["AR"], sl["Bm"]
                pB, pK, pA = nb(), nb(), nb()
                pbk[(d, cc)] = (pB, pK, pA)
                ARf = AR[:].rearrange("p a b -> p (a b)")
                mm(S, pB[:], cm[:, 1, :], ARf, r=[cm.b(1)] + [AR.b(i) for i in range(4)], w=[pB.b()])
                mm(S, pK[:], cm[:, 2, :], ARf, r=[cm.b(2)] + [AR.b(i) for i in range(4)], w=[pK.b()])
                mm(S, pA[:, 0:256], cm[:, 0, :], Bm[:].rearrange("p a b -> p (a b)"), r=[cm.b(0), Bm.b(0), Bm.b(1)], w=[pA.b()])
                Xt0, XTT0, XTT1, L3 = sl["Xt"][0], sl["XTT"][0], sl["XTT"][1], sl["L3"]
                v3 = lambda ap: ap.rearrange("p (a b) -> p a b", a=2)
                tt(S, XTT0[:, :, 0, :], v3(pB[:, 0:256]), v3(m4[d][:, 0:256]), ALU.mult, r=[pB.b(), m4[d].b()], w=[XTT0.b()])
                tt(S, L3[:, :, 1, :], v3(pB[:, 256:512]), v3(m3[d][:, 128:384]), ALU.mult, r=[pB.b(), m3[d].b()], w=[L3.b()])
                tt(S, L3[:, :, 0, :], v3(pK[:, 0:256]), v3(m4[d][:, 0:256]), ALU.mult, r=[pK.b(), m4[d].b()], w=[L3.b()])
                tt(S, L3[:, :, 2, :], v3(pK[:, 256:512]), v3(m3[d][:, 128:384]), ALU.mult, r=[pK.b(), m3[d].b()], w=[L3.b()])
                tt(S, Xt0[:].rearrange("p a b -> p (a b)"), pA[:, 0:256], m4[d][:, 256:512], ALU.mult, r=[pA.b(), m4[d].b()], w=[Xt0.b()])
                for hp in range(2):
                    tt(S, XTT1[:, hp, 1, :], XTT0[:, hp, 0, :], IDB[:], ALU.add, r=[XTT0.b(), IDB.b()], w=[XTT1.b()], eng="pool")
        for m_ in range(0, 7):
            cur, nxt = m_ % 2, (m_ + 1) % 2
            for half in (units[0:4], units[4:8]):
                lv = {}
                for u in half:
                    d, cc, n, sl = u
                    Xc, XTc = sl["Xt"][cur], sl["XTT"][cur]
                    pa_, pb_ = nb(), None
                    if m_ == 0:
                        for hp in range(2):
                            mm(S, pa_[:, hp * 256:hp * 256 + 128], Xc[:, hp, :], XTc[:, hp, 0, :], r=[Xc.b(), XTc.b()], w=[pa_.b()])
                    elif m_ < 6:
                        for hp in range(2):
                            mm(S, pa_[:, hp * 256:(hp + 1) * 256], Xc[:, hp, :], XTc[:, hp, :, :].rearrange("p a b -> p (a b)"),
                               r=[Xc.b(), XTc.b()], w=[pa_.b()])
                    else:
                        for hp in range(2):
                            mm(S, pa_[:, hp * 256 + 128:(hp + 1) * 256], Xc[:, hp, :], XTc[:, hp, 1, :], r=[Xc.b(), XTc.b()], w=[pa_.b()])
                    if m_ < 6:
                        pb_ = nb()
                        for hp in range(2):
                            mm(S, pb_[:, hp * 128:(hp + 1) * 128], XTc[:, hp, 0, :], Xc[:, hp, :], r=[Xc.b(), XTc.b()], w=[pb_.b()])
                    Xn, XTn = sl["Xt"][nxt], sl["XTT"][nxt]
                    pv = pa_[:].rearrange("p (h k b) -> p h k b", h=2, k=2)
                    if m_ < 5:
                        cp(S, XTn[:, :, 0, :], pv[:, :, 0, :], r=[pa_.b()], w=[XTn.b()], eng="act")
                    if m_ < 6:
                        cp(S, Xn[:].rearrange("p a b -> p (a b)"), pb_[:, 0:256], r=[pb_.b()], w=[Xn.b()], eng="act")
                    if 1 <= m_ < 6:
                        tt(S, XTn[:, :, 1, :], XTc[:, :, 1, :], pv[:, :, 1, :], ALU.add, r=[XTc.b(), pa_.b()], w=[XTn.b()])
                    elif m_ == 6:
                        tt(S, sl["TTf"][:], XTc[:, :, 1, :], pv[:, :, 1, :], ALU.add, r=[XTc.b(), pa_.b()], w=[sl["TTf"].b()])
        sold = {}
        for half in (units[0:4], units[4:8]):
            for u in half:
                d, cc, n, sl = u
                tm, L3 = sl["tm"], sl["L3"]
                p = nb()
                pbank[(d, cc)] = p
                for hp in range(2):
                    fs = slice(64 * hp, 64 * hp + 64)
                    mm(S, p[:, hp * 64:(hp + 1) * 64], L3[:, hp, 0, :], tm[:, 3, fs], r=[L3.b(), tm.b()], w=[p.b()])
            for u in half:
                d, cc, n, sl = u
                p = pbank[(d, cc)]
                tm, W1A = sl["tm"], sl["W1A"]
                for hp in range(2):
                    fs = slice(64 * hp, 64 * hp + 64)
                    cp(S, W1A[:, hp, 0:64], p[:, hp * 64:(hp + 1) * 64], r=[p.b()], w=[W1A.b()], eng="act")
                    cp(S, W1A[:, hp, 64:128], tm[:, 0, fs], r=[tm.b()], w=[W1A.b()], eng="pool")
            for u in half:
                d, cc, n, sl = u
                TTf, W1A = sl["TTf"], sl["W1A"]
                p2 = nb()
                pbank[(d, cc)] = p2
                for hp in range(2):
                    mm(S, p2[:, hp * 128:(hp + 1) * 128], TTf[:, hp, :], W1A[:, hp, :], r=[TTf.b(), W1A.b()], w=[p2.b()])
            for u in half:
                d, cc, n, sl = u
                p2 = pbank[(d, cc)]
                cp(S, sl["QP"][:].rearrange("p a b -> p (a b)"), p2[:, 0:256], r=[p2.b()], w=[sl["QP"].b()], eng="act")
            for u in half:
                d, cc, n, sl = u
                tm, L3, QP, cm = sl["tm"], sl["L3"], sl["QP"], sl["cm"]
                p3 = nb()
                pbank[(d, cc)] = p3
                for hp in range(2):
                    fs = slice(64 * hp, 64 * hp + 64)
                    mm(S, p3[0:64, hp * 64:(hp + 1) * 64], QP[:, hp, 64:128], tm[:, 1, fs], start=True, stop=False,
                       r=[QP.b(), tm.b()], w=[p3.b()])
                    mm(S, p3[0:64, hp * 64:(hp + 1) * 64], IDB[:, fs], sl["Dg"][:, fs], start=False, stop=True,
                       r=[IDB.b(), sl["Dg"].b()], w=[p3.b()])
                    mm(S, p3[0:64, 128 + hp * 128:256 + hp * 128], QP[:, hp, 64:128], L3[:, hp, 1, :], start=True, stop=False,
                       r=[QP.b(), L3.b()], w=[p3.b()])
                    mm(S, p3[0:64, 128 + hp * 128:256 + hp * 128], IDB[:, fs], cm[:, 3, :], start=False, stop=True,
                       r=[IDB.b(), cm.b(3)], w=[p3.b()])
            for u in half:
                d, cc, n, sl = u
                p3 = pbank[(d, cc)]
                cp(S, sl["GT"][:].rearrange("p a b -> p (a b)"), p3[0:64, 0:128], r=[p3.b()], w=[sl["GT"].b()], eng="act")
                cp(S, sl["RyT"][:].rearrange("p a b -> p (a b)"), p3[0:64, 128:384], r=[p3.b()], w=[sl["RyT"].b()], eng="act")
            for u in half:
                d, cc, n, sl = u
                tm, L3, QP = sl["tm"], sl["L3"], sl["QP"]
                k_ = par.get((d, cc), 0)
                S_old, S_new = ST[(d, cc)][k_], ST[(d, cc)][1 - k_]
                par[(d, cc)] = 1 - k_
                sold[(d, cc)] = (S_old, S_new)
                pz = nb()
                pbank[(d, cc)] = pz
                for hp in range(2):
                    fs = slice(64 * hp, 64 * hp + 64)
                    mm(S, pz[0:64, fs], sl["GT"][:, hp, :], S_old[:, hp, :], start=True, stop=False, r=[sl["GT"].b(), S_old.b()], w=[pz.b()])
                    mm(S, pz[0:64, fs], tm[:, 1, fs], QP[:, hp, 0:64], start=False, stop=False, r=[tm.b(), QP.b()], w=[pz.b()])
                    mm(S, pz[0:64, fs], tm[:, 2, fs], tm[:, 3, fs], start=False, stop=True, r=[tm.b()], w=[pz.b()])
                    mm(S, pz[:, 128 + 64 * hp:192 + 64 * hp], L3[:, hp, 1, :], QP[:, hp, 0:64], start=True, stop=False, r=[L3.b(), QP.b()], w=[pz.b()])
                    mm(S, pz[:, 128 + 64 * hp:192 + 64 * hp], L3[:, hp, 2, :], tm[:, 3, fs], start=False, stop=False, r=[L3.b(), tm.b()], w=[pz.b()])
                    mm(S, pz[:, 128 + 64 * hp:192 + 64 * hp], sl["RyT"][:, hp, :], S_old[:, hp, :], start=False, stop=True, r=[sl["RyT"].b(), S_old.b()], w=[pz.b()])
            for u in half:
                d, cc, n, sl = u
                pz = pbank[(d, cc)]
                S_old, S_new = sold[(d, cc)]
                cp(S, S_new[:].rearrange("p a b -> p (a b)"), pz[0:64, 0:128], r=[pz.b()], w=[S_new.b()], eng="act")
                if (n, cc) not in touched:
                    touched.add((n, cc))
                    cp(S, Yacc[:, n, cc * 128:(cc + 1) * 128], pz[:, 128:256], r=[pz.b()], w=[Yacc.b((n, cc))])
                else:
                    tt(S, Yacc[:, n, cc * 128:(cc + 1) * 128], Yacc[:, n, cc * 128:(cc + 1) * 128], pz[:, 128:256], ALU.add,
                       r=[pz.b(), Yacc.b((n, cc))], w=[Yacc.b((n, cc))])
    if C.rw_stop == 6:
        return
    pcs = sb(S, nm + "pcs", [128, 4, 4])
    for j, k_ in enumerate(['rwkv_ln_w', 'rwkv_ln_b']):
        S.op("sp", lambda e, j=j, k_=k_: e.dma_start(out=pcs[:, :, j:j + 1], in_=P[k_].ap[l].rearrange("(c p o) -> p c o", p=128, o=1),
                                                     allow_slow_non_contiguous=True), w=[pcs.b()], dma=True)
    blk2 = sb(S, nm + "blk2", [128, 128])
    dma(S, blk2[:], C.c_rw_blk[:], w=[blk2.b()])
    gne = sb(S, nm + "gne", [128, 1])
    mset(S, gne[:], GN_EPS, w=[gne.b()])
    ycm = sb(S, nm + "ycm", [128, NT])
    yc2 = sb(S, nm + "yc2", [128, NT])
    sq2 = sb(S, nm + "sq2", [128, NT])
    rstd2 = sb(S, nm + "rstd2", [128, NT])
    bo_t = [sb(S, nm + "bo%d" % i, [128, NT]) for i in range(2)]
    g_t = [sb(S, nm + "gt%d" % i, [128, NT]) for i in range(2)]
    yo = [sb(S, nm + "yo%d" % i, [128, NT], BF16) for i in range(2)]
    k3 = 0
    for cc in range(4):
        csl = slice(cc * 128, (cc + 1) * 128)
        for ti in range(nti):
            tsl = slice(ti * NT, (ti + 1) * NT)
            b_t, gg, y_o = bo_t[k3 % 2], g_t[k3 % 2], yo[k3 % 2]
            k3 += 1
            dma(S, b_t[:], rw.ap[RW_BONUS][csl, tsl], r=[rw.b((RW_BONUS, cc))], w=[b_t.b()], q="sp")
            dma(S, gg[:], rw.ap[RW_G][csl, tsl], r=[rw.b((RW_G, cc))], w=[gg.b()], q="act")
            p = nb()
            for j in range(4):
                n = ti * 4 + j
                tr(S, p[:, j * 128:(j + 1) * 128], Yacc[:, n, csl], C.ident[:], r=[Yacc.b((n, cc)), C.ident.b()], w=[p.b()])
            cp(S, ycm[:], p[:], r=[p.b()], w=[ycm.b()], eng="act")
            p2 = nb()
            mm(S, p2[:], blk2[:], ycm[:], r=[blk2.b(), ycm.b()], w=[p2.b()])
            stt(S, yc2[:], p2[:], -1.0 / 64.0, ycm[:], ALU.mult, ALU.add, r=[p2.b(), ycm.b()], w=[yc2.b()])
            act(S, sq2[:], yc2[:], AF.Square, r=[yc2.b()], w=[sq2.b()])
            p3 = nb()
            mm(S, p3[:], blk2[:], sq2[:], r=[blk2.b(), sq2.b()], w=[p3.b()])
            rsqrt(S, rstd2[:], p3[:], 1.0 / 64.0, gne[:, 0:1], r=[p3.b(), gne.b()], w=[rstd2.b()])
            tt(S, yc2[:], yc2[:], rstd2[:], ALU.mult, r=[yc2.b(), rstd2.b()], w=[yc2.b()])
            ts(S, yc2[:], yc2[:], pcs[:, cc, 0:1], pcs[:, cc, 1:2], ALU.mult, ALU.add, r=[yc2.b(), pcs.b()], w=[yc2.b()])
            tt(S, yc2[:], yc2[:], b_t[:], ALU.add, r=[yc2.b(), b_t.b()], w=[yc2.b()], eng="pool")
            tt(S, y_o[:], yc2[:], gg[:], ALU.mult, r=[yc2.b(), gg.b()], w=[y_o.b()])
            dma(S, C.ya_d[csl, tsl], y_o[:], r=[y_o.b()], w=[C.ya_d.b((cc, ti))], q="sp")
    S.barrier()
    S.release()
```

```python
import math
from contextlib import ExitStack

import numpy as np
import ml_dtypes
import concourse.bass as bass
import concourse.mybir as mybir
from concourse.bass_utils import run_bass_kernel_spmd

F32 = mybir.dt.float32
BF16 = mybir.dt.bfloat16
ALU = mybir.AluOpType
AF = mybir.ActivationFunctionType
AX = mybir.AxisListType

D = 2048
S_LEN = 2048
DEPTH = 2
FFN = 5632
EPS = 1e-6
NT = 512
RW_COLS, MLA_COLS, HY_COLS, GQA_COLS = 2176, 576, 1536, 1024
OFF_RW, OFF_MLA, OFF_HY, OFF_GQA, OFF_GATE = 0, 2176, 2752, 4288, 5312
IN_COLS = 13504
PM_ROWS = 5056


class Buf:
    __slots__ = ("w", "r", "name", "excl")

    def __init__(self, name="", excl=False):
        self.w = None
        self.r = []
        self.name = name
        self.excl = excl


class Rec:
    __slots__ = ("eng", "fn", "deps", "dma", "semkey", "val", "needs_inc", "idx")


class Sched:
    ENG = ("pe", "act", "dve", "pool", "sp")
    BLK = {"pe": "tensor", "act": "scalar", "dve": "vector", "pool": "gpsimd", "sp": "sync"}
    CAP = 8000
    NRING = 8

    def __init__(self, nc):
        self.nc = nc
        self.streams = {e: [] for e in self.ENG}
        self.all = []
        self.ndma = {e: 0 for e in self.ENG}
        self.ring_last = {}
        self.es = ExitStack()

    def sbuf(self, name, shape, dt):
        return self.es.enter_context(self.nc.sbuf_tensor(name, list(shape), dt))

    def psum(self, name, shape, dt=F32):
        return self.es.enter_context(self.nc.psum_tensor(name, list(shape), dt))

    def release(self):
        self.es.close()
        self.es = ExitStack()

    def op(self, eng, fn, r=(), w=(), dma=False):
        rec = Rec()
        rec.eng, rec.fn, rec.dma = eng, fn, dma
        rec.needs_inc = False
        rec.val = None
        rec.semkey = None
        rec.idx = len(self.all)
        deps = {}
        for b in r:
            if b.w is not None:
                deps[id(b.w)] = b.w
            if b.excl:
                for rr in b.r:
                    if rr.eng != eng:
                        deps[id(rr)] = rr
        for b in w:
            if b.w is not None:
                lw = b.w
                if not (eng == "pe" and lw.eng == "pe" and not lw.dma and not dma):
                    deps[id(lw)] = lw
            for rr in b.r:
                if rr.eng != eng or rr.dma or dma:
                    deps[id(rr)] = rr
        if dma:
            i = self.ndma[eng]
            self.ndma[eng] += 1
            slot = i % self.NRING
            rec.semkey = ("dma", eng, slot)
            rec.val = 16 * (i // self.NRING + 1)
            prev = self.ring_last.get((eng, slot))
            if prev is not None:
                deps[id(prev)] = prev
            self.ring_last[(eng, slot)] = rec
        for d in deps.values():
            d.needs_inc = True
        rec.deps = list(deps.values())
        for b in r:
            if not dma:
                b.r = [x for x in b.r if x.dma or x.eng != eng]
            b.r.append(rec)
        for b in w:
            b.w = rec
            b.r = []
        self.streams[eng].append(rec)
        self.all.append(rec)
        return rec

    def barrier(self):
        lasts = []
        for e in self.ENG:
            for rec in reversed(self.streams[e]):
                if rec.fn is not None:
                    lasts.append(rec)
                    break
        for (e, slot), rec in self.ring_last.items():
            lasts.append(rec)
        fence = Buf("fence")
        for e in self.ENG:
            rec = Rec()
            rec.eng, rec.fn, rec.dma = e, None, False
            rec.needs_inc = False
            rec.val = None
            rec.semkey = None
            rec.idx = len(self.all)
            rec.deps = [d for d in lasts]
            for d in lasts:
                d.needs_inc = True
            self.streams[e].append(rec)
            self.all.append(rec)

    def finalize(self):
        nc = self.nc
        cnt = {e: 0 for e in self.ENG}
        for rec in self.all:
            if rec.dma or not rec.needs_inc or rec.fn is None:
                continue
            c = cnt[rec.eng]
            rec.semkey = ("eng", rec.eng, c // self.CAP)
            rec.val = c % self.CAP + 1
            cnt[rec.eng] = c + 1
        keys = set()
        for rec in self.all:
            if rec.semkey is not None:
                keys.add(rec.semkey)
        with ExitStack() as es:
            sems = {}
            for k in sorted(keys):
                sems[k] = es.enter_context(nc.semaphore("s_%s_%s_%d" % k))
            block = es.enter_context(nc.Block())
            for e in self.ENG:
                stream = self.streams[e]

                def body(eng, stream=stream):
                    waited = {}
                    for rec in stream:
                        for d in rec.deps:
                            if d.semkey is None:
                                continue
                            if waited.get(d.semkey, 0) >= d.val:
                                continue
                            eng.wait_ge(sems[d.semkey], d.val)
                            waited[d.semkey] = d.val
                        if rec.fn is None:
                            continue
                        ins = rec.fn(eng)
                        if rec.dma:
                            ins.then_inc(sems[rec.semkey], 16)
                        elif rec.needs_inc:
                            ins.then_inc(sems[rec.semkey], 1)

                getattr(block, self.BLK[e])(body)
        self.es.close()


class T:
    def __init__(self, h, excl=False):
        self.h = h
        self.bufs = {}
        self.excl = excl

    def __getitem__(self, idx):
        return self.h[idx]

    def b(self, key=0):
        bb = self.bufs.get(key)
        if bb is None:
            bb = self.bufs[key] = Buf(excl=self.excl)
        return bb


def sb(S, name, shape, dt=F32):
    return T(S.sbuf(name, shape, dt))


def ps(S, name, shape, dt=F32):
    return T(S.psum(name, shape, dt), excl=True)


class DT:
    def __init__(self, nc, name, shape, dt, kind="Internal"):
        self.t = nc.dram_tensor(name, list(shape), dt, kind=kind)
        self.ap = self.t.ap()
        self.bufs = {}

    def __getitem__(self, idx):
        return self.ap[idx]

    def b(self, key=0):
        bb = self.bufs.get(key)
        if bb is None:
            bb = self.bufs[key] = Buf()
        return bb


def mm(S, out, lhsT, rhs, start=True, stop=True, r=(), w=()):
    return S.op("pe", lambda e: e.matmul(out, lhsT, rhs, start=start, stop=stop), r=r, w=w)


def tr(S, out, in_, ident, r=(), w=()):
    return S.op("pe", lambda e: e.transpose(out, in_, ident), r=r, w=w)


def act(S, out, in_, func, bias=None, scale=None, accum_out=None, r=(), w=(), eng="act"):
    kw = {}
    if bias is not None:
        kw["bias"] = bias
    if scale is not None:
        kw["scale"] = scale
    if accum_out is not None:
        kw["accum_out"] = accum_out
    return S.op(eng, lambda e: e.activation(out, in_, func, **kw), r=r, w=w)


def tt(S, out, in0, in1, op, r=(), w=(), eng="dve"):
    return S.op(eng, lambda e: e.tensor_tensor(out, in0, in1, op), r=r, w=w)


def ts(S, out, in0, s1, s2, op0, op1=None, r=(), w=(), eng="dve", accum_out=None):
    if op1 is None:
        return S.op(eng, lambda e: e.tensor_single_scalar(out, in0, s1, op0), r=r, w=w)
    if accum_out is not None:
        return S.op(eng, lambda e: e.tensor_scalar(out, in0, s1, s2, op0, op1, accum_out), r=r, w=w)
    return S.op(eng, lambda e: e.tensor_scalar(out, in0, s1, s2, op0, op1), r=r, w=w)


def stt(S, out, in0, scalar, in1, op0, op1, r=(), w=(), eng="dve"):
    return S.op(eng, lambda e: e.scalar_tensor_tensor(out, in0, scalar, in1, op0, op1), r=r, w=w)


def rsqrt(S, out, in_, scale, bias, r=(), w=()):
    S.op("act", lambda e: e.activation(out, in_, AF.Sqrt, bias=bias, scale=scale), r=r, w=w)
    S.op("dve", lambda e: e.reciprocal(out, out), r=w, w=w)


def cp(S, out, in_, r=(), w=(), eng="dve"):
    if eng == "act":
        return S.op(eng, lambda e: e.copy(out, in_), r=r, w=w)
    return S.op(eng, lambda e: e.tensor_copy(out, in_), r=r, w=w)


def mset(S, ap, val, w=(), eng="dve"):
    return S.op(eng, lambda e: e.memset(ap, val), w=w)


def dma(S, out, in_, r=(), w=(), q="sp"):
    return S.op(q, lambda e: e.dma_start(out=out, in_=in_), r=r, w=w, dma=True)


class Ctx:
    pass


def load_consts(C):
    S = C.S
    nc = S.nc
    C.cst = ExitStack()
    C.ident = T(C.cst.enter_context(nc.sbuf_tensor("ident", [128, 128], F32)))
    C.identb = T(C.cst.enter_context(nc.sbuf_tensor("identb", [128, 128], BF16)))
    C.ones = T(C.cst.enter_context(nc.sbuf_tensor("ones", [128, 128], F32)))
    dma(S, C.ident[:], C.d_ident[:], w=[C.ident.b()])
    mset(S, C.ones[:], 1.0, w=[C.ones.b()])
    C.epsc = T(C.cst.enter_context(nc.sbuf_tensor("epsc", [128, 4], F32)))
    mset(S, C.epsc[:, 0:1], EPS, w=[C.epsc.b()])
    mset(S, C.epsc[:, 1:2], 0.0, w=[C.epsc.b()])
    cp(S, C.identb[:], C.ident[:], r=[C.ident.b()], w=[C.identb.b()])


def phase_load_x(C):
    S = C.S
    xt = [sb(S, "lx_xt%d" % i, [128, 4, D]) for i in range(2)]
    st = [sb(S, "lx_st%d" % i, [128, NT]) for i in range(3)]
    pp = [ps(S, "lx_ps%d" % i, [128, NT]) for i in range(3)]
    k = 0
    for tg in range(S_LEN // NT):
        x_t = xt[tg % 2]
        for j in range(4):
            t0 = tg * NT + j * 128
            dma(S, x_t[:, j, :], C.x[t0:t0 + 128, :], w=[x_t.b(j)], q="sp" if j % 2 == 0 else "act")
        for dc in range(D // 128):
            p = pp[k % 3]
            s = st[k % 3]
            for j in range(4):
                tr(S, p[:, j * 128:(j + 1) * 128], x_t[:, j, dc * 128:(dc + 1) * 128], C.ident[:],
                   r=[x_t.b(j), C.ident.b()], w=[p.b()])
            cp(S, s[:], p[:], r=[p.b()], w=[s.b()], eng="dve" if k % 2 == 0 else "act")
            dma(S, C.xres[dc * 128:(dc + 1) * 128, tg * NT:(tg + 1) * NT], s[:], r=[s.b()],
                w=[C.xres.b((dc, tg // 2))], q="sp")
            k += 1
    S.barrier()
    S.release()


def phase_final_norm(C):
    S = C.S
    gb = sb(S, "fn_g", [128, D])
    dma(S, gb[:], C.final_norm.ap.partition_broadcast(128), w=[gb.b()])
    xin = [sb(S, "fn_xin%d" % i, [128, 16, 128]) for i in range(2)]
    xt = [sb(S, "fn_xt%d" % i, [128, D]) for i in range(2)]
    sq = sb(S, "fn_sq", [128, D])
    ot = [sb(S, "fn_ot%d" % i, [128, D]) for i in range(2)]
    ss = [sb(S, "fn_ss%d" % i, [128, 1]) for i in range(2)]
    pp = [ps(S, "fn_ps%d" % i, [128, NT]) for i in range(4)]
    for tc in range(S_LEN // 128):
        xi = xin[tc % 2]
        x_t = xt[tc % 2]
        o_t = ot[tc % 2]
        s_ = ss[tc % 2]
        dma(S, xi[:], C.xres.ap[:, tc * 128:(tc + 1) * 128].rearrange("(c p) t -> p c t", p=128),
            r=[C.xres.b((dc, tc // 8)) for dc in range(16)], w=[xi.b()], q="sp" if tc % 2 == 0 else "act")
        for g in range(4):
            p = pp[g]
            for j in range(4):
                dc = g * 4 + j
                tr(S, p[:, j * 128:(j + 1) * 128], xi[:, dc, :], C.ident[:], r=[xi.b(), C.ident.b()], w=[p.b()])
            cp(S, x_t[:, g * NT:(g + 1) * NT], p[:], r=[p.b()], w=[x_t.b(g)], eng="dve" if g % 2 == 0 else "act")
        act(S, sq[:], x_t[:], AF.Square, accum_out=s_[:], r=[x_t.b(g) for g in range(4)], w=[sq.b(), s_.b()])
        rsqrt(S, s_[:], s_[:], 1.0 / D, C.epsc[:, 0:1], r=[s_.b(), C.epsc.b()], w=[s_.b()])
        stt(S, o_t[:], x_t[:], s_[:, 0:1], gb[:], ALU.mult, ALU.mult,
            r=[x_t.b(g) for g in range(4)] + [s_.b(), gb.b()], w=[o_t.b()])
        dma(S, C.out[tc * 128:(tc + 1) * 128, :], o_t[:], r=[o_t.b()], w=[C.out.b(tc)], q="sp")
    S.barrier()
    S.release()


def phase_ffn(C, l, which):
    S = C.S
    nm = "f%d%d_" % (l, which)
    g_d = C.ffn_norm[which][l]
    wg_d, wu_d, wd_d = C.ffn_wg[which].ap[l], C.ffn_wu[which].ap[l], C.ffn_wd[which].ap[l]
    TT = 1024
    G = 2
    NG = FFN // (128 * G)
    gcol = sb(S, nm + "gcol", [128, 16])
    S.op("sp", lambda e: e.dma_start(out=gcol[:], in_=g_d.rearrange("(c p) -> p c", p=128),
                                     allow_slow_non_contiguous=True), w=[gcol.b()], dma=True)
    xa = sb(S, nm + "xa", [128, 16, TT])
    hT = sb(S, nm + "hT", [128, 16, TT], BF16)
    rstd = sb(S, nm + "rstd", [128, TT])
    sq = [sb(S, nm + "sq%d" % i, [128, NT]) for i in range(2)]
    wg = [sb(S, nm + "wg%d" % i, [128, 16, 128 * G], BF16) for i in range(2)]
    wu = [sb(S, nm + "wu%d" % i, [128, 16, 128 * G], BF16) for i in range(2)]
    wd = [sb(S, nm + "wd%d" % i, [128, G, D], BF16) for i in range(2)]
    aT = [sb(S, nm + "aT%d" % i, [128, G, TT], BF16) for i in range(2)]
    sg = [sb(S, nm + "sg%d" % i, [128, NT]) for i in range(2)]
    p_g = [ps(S, nm + "pg%d" % i, [128, NT]) for i in range(2)]
    p_u = [ps(S, nm + "pu%d" % i, [128, NT]) for i in range(2)]
    p_d = [ps(S, nm + "pd%d" % i, [128, NT]) for i in range(3)]
    dtmp = [sb(S, nm + "dtmp%d" % i, [128, NT]) for i in range(2)]
    kt = 0
    p_s = ps(S, nm + "pss", [128, NT])
    NSUB = TT // NT
    it = 0
    for tt_i in range(S_LEN // TT):
        t0 = tt_i * TT
        for c in range(16):
            dma(S, xa[:, c, :], C.xres[c * 128:(c + 1) * 128, t0:t0 + TT], r=[C.xres.b((c, tt_i))],
                w=[xa.b((c, s)) for s in range(NSUB)], q="sp" if c % 2 == 0 else "act")
        for s_i in range(NSUB):
            for c in range(16):
                q_ = sq[c % 2]
                act(S, q_[:], xa[:, c, s_i * NT:(s_i + 1) * NT], AF.Square, r=[xa.b((c, s_i))], w=[q_.b()])
                mm(S, p_s[:], C.ones[:], q_[:], start=(c == 0), stop=(c == 15), r=[C.ones.b(), q_.b()], w=[p_s.b()])
            rsqrt(S, rstd[:, s_i * NT:(s_i + 1) * NT], p_s[:], 1.0 / D, C.epsc[:, 0:1], r=[p_s.b(), C.epsc.b()],
                  w=[rstd.b(s_i)])
        for c in range(16):
            stt(S, hT[:, c, :], xa[:, c, :], gcol[:, c:c + 1], rstd[:], ALU.mult, ALU.mult,
                r=[xa.b((c, s)) for s in range(NSUB)] + [gcol.b()] + [rstd.b(i) for i in range(NSUB)], w=[hT.b(c)])
        hT_r = [hT.b(c) for c in range(16)]
        for fg in range(NG):
            wg_t, wu_t, wd_t, a_t = wg[it % 2], wu[it % 2], wd[it % 2], aT[it % 2]
            it += 1
            f0 = fg * 128 * G
            dma(S, wg_t[:], wg_d[:, f0:f0 + 128 * G].rearrange("(c p) f -> p c f", p=128), w=[wg_t.b()], q="pool")
            dma(S, wu_t[:], wu_d[:, f0:f0 + 128 * G].rearrange("(c p) f -> p c f", p=128), w=[wu_t.b()], q="pool")
            dma(S, wd_t[:], wd_d[f0:f0 + 128 * G, :].rearrange("(c p) d -> p c d", p=128), w=[wd_t.b()], q="pool")
            k = 0
            for fc in range(G):
                for s_i in range(NSUB):
                    pg, pu, sg_t = p_g[k % 2], p_u[k % 2], sg[k % 2]
                    k += 1
                    tsl = slice(s_i * NT, (s_i + 1) * NT)
                    for c in range(16):
                        mm(S, pg[:], wg_t[:, c, fc * 128:(fc + 1) * 128], hT[:, c, tsl], start=(c == 0), stop=(c == 15),
                           r=[wg_t.b(), hT_r[c]], w=[pg.b()])
                    for c in range(16):
                        mm(S, pu[:], wu_t[:, c, fc * 128:(fc + 1) * 128], hT[:, c, tsl], start=(c == 0), stop=(c == 15),
                           r=[wu_t.b(), hT_r[c]], w=[pu.b()])
                    act(S, sg_t[:], pg[:], AF.Silu, r=[pg.b()], w=[sg_t.b()])
                    tt(S, a_t[:, fc, tsl], sg_t[:], pu[:], ALU.mult, r=[sg_t.b(), pu.b()], w=[a_t.b((fc, s_i))])
            k = 0
            for dc in range(16):
                for s_i in range(NSUB):
                    pd = p_d[k % 3]
                    k += 1
                    tsl = slice(s_i * NT, (s_i + 1) * NT)
                    for fc in range(G):
                        mm(S, pd[:], wd_t[:, fc, dc * 128:(dc + 1) * 128], a_t[:, fc, tsl], start=(fc == 0), stop=(fc == G - 1),
                           r=[wd_t.b(), a_t.b((fc, s_i))], w=[pd.b()])
                    if s_i % 2 == 0:
                        stt(S, xa[:, dc, tsl], pd[:], 0.5, xa[:, dc, tsl], ALU.mult, ALU.add, r=[pd.b(), xa.b((dc, s_i))],
                            w=[xa.b((dc, s_i))])
                    else:
                        t_ = dtmp[kt % 2]
                        kt += 1
                        act(S, t_[:], pd[:], AF.Copy, scale=0.5, r=[pd.b()], w=[t_.b()])
                        tt(S, xa[:, dc, tsl], xa[:, dc, tsl], t_[:], ALU.add, r=[xa.b((dc, s_i)), t_.b()], w=[xa.b((dc, s_i))])
        for c in range(16):
            dma(S, C.xres[c * 128:(c + 1) * 128, t0:t0 + TT], xa[:, c, :], r=[xa.b((c, s)) for s in range(NSUB)],
                w=[C.xres.b((c, tt_i))], q="sp" if c % 2 == 0 else "act")
    S.barrier()
    S.release()


PARAM_SHAPES = {
    'ffn1_norm': (DEPTH, D), 'ffn1_w_gate': (DEPTH, D, FFN), 'ffn1_w_up': (DEPTH, D, FFN), 'ffn1_w_down': (DEPTH, FFN, D),
    'mix_norm': (DEPTH, D), 'w_in': (DEPTH, D, IN_COLS),
    'rwkv_mu_prev': (DEPTH, RW_COLS), 'rwkv_mu_next': (DEPTH, RW_COLS), 'rwkv_w0': (DEPTH, 2, 512),
    'rwkv_w2': (DEPTH, 2, 96, 512), 'rwkv_a0': (DEPTH, 2, 512), 'rwkv_a2': (DEPTH, 2, 96, 512),
    'rwkv_g2': (DEPTH, 256, 512), 'rwkv_k_k': (DEPTH, 512), 'rwkv_k_a': (DEPTH, 512), 'rwkv_r_k': (DEPTH, 512),
    'rwkv_ln_w': (DEPTH, 512), 'rwkv_ln_b': (DEPTH, 512),
    'mla_q_norm': (DEPTH, 384), 'mla_w_q_up': (DEPTH, 384, 768), 'mla_kv_norm': (DEPTH, 128),
    'mla_w_kv_up': (DEPTH, 128, 1024),
    'hy_short_w': (DEPTH, 3, 1536), 'hy_short_b': (DEPTH, 1536), 'hy_w1': (DEPTH, 33, 64), 'hy_b1': (DEPTH, 64),
    'hy_w2': (DEPTH, 64, 64), 'hy_b2': (DEPTH, 64), 'hy_w3': (DEPTH, 64, 64), 'hy_b3': (DEPTH, 64),
    'hy_w4': (DEPTH, 64, 2048), 'hy_freq': (DEPTH, 3, 64), 'hy_bias': (DEPTH, 2, 512),
    'gqa_q_norm': (DEPTH, 128), 'gqa_k_norm': (DEPTH, 128), 'w_branch': (DEPTH, 4, 512, D), 'w_out': (DEPTH, D, D),
    'ffn2_norm': (DEPTH, D), 'ffn2_w_gate': (DEPTH, D, FFN), 'ffn2_w_up': (DEPTH, D, FFN), 'ffn2_w_down': (DEPTH, FFN, D),
    'final_norm': (D,),
}


def rope_tab(pos, dim):
    inv = (10000.0 ** (-np.arange(0, dim, 2, dtype=np.float32) / np.float32(dim))).astype(np.float32)
    ang = pos.astype(np.float32)[:, None] * inv[None, :]
    ang = np.concatenate([ang, ang], axis=-1)
    return np.cos(ang).astype(np.float32), np.sin(ang).astype(np.float32)


def rot_lhsT(blocks):
    n = sum(b for b in blocks)
    Rm = np.zeros((n, n), np.float32)
    o = 0
    for size in blocks:
        half = size // 2
        for i in range(half):
            Rm[o + i, o + i + half] = -1.0
            Rm[o + i + half, o + i] = 1.0
        o += size
    return np.ascontiguousarray(Rm.T)


def host_consts():
    c = {}
    c['c_ident'] = np.eye(128, dtype=np.float32)
    pos = np.arange(S_LEN)
    cr, sr = rope_tab(pos // 64, 64)
    cc, sc = rope_tab(pos % 64, 64)
    c['c_gq_cos'] = np.ascontiguousarray(np.concatenate([cr, cc], axis=1).T)
    c['c_gq_sin'] = np.ascontiguousarray(np.concatenate([sr, sc], axis=1).T)
    c['c_gq_RT'] = rot_lhsT([64, 64])
    c1, s1 = rope_tab(pos, 64)
    c['c_ml_cos'] = np.ascontiguousarray(c1.T)
    c['c_ml_sin'] = np.ascontiguousarray(s1.T)
    c['c_ml_RT'] = rot_lhsT([64])
    c.update(hy_consts())
    c.update(rw_consts())
    return c


SCRATCH = {
    'xres': ([D, S_LEN], F32), 'hT_d': ([D, S_LEN], BF16), 'pm_d': ([PM_ROWS, S_LEN], F32),
    'pdv_d': ([S_LEN, 256], BF16), 'ya_d': ([512, S_LEN], BF16), 'yb_d': ([512, S_LEN], BF16),
    'yc_d': ([512, S_LEN], BF16), 'yd_d': ([512, S_LEN], BF16), 'hyH_d': ([S_LEN, 2048], F32), 'mT_d': ([D, S_LEN], BF16),
    'rw_d': ([11, 512, S_LEN], F32),
}


def build(phases=("load", "ffn1", "proj", "rwkv", "mla", "hyena", "gqa", "merge", "ffn2", "final"), depth=DEPTH,
          inject=(), expose=()):
    nc = bass.Bass("TRN2", target_bir_lowering=False)
    C = Ctx()
    C.nc = nc
    C.S = Sched(nc)
    C.x = DT(nc, "x", [S_LEN, D], F32, kind="ExternalInput")
    C.P = {}
    for k, shp in PARAM_SHAPES.items():
        C.P[k] = DT(nc, k, shp, F32, kind="ExternalInput")
    hc = host_consts()
    for k, v in hc.items():
        dt_ = F32 if v.dtype == np.float32 else BF16
        setattr(C, k, DT(nc, k, list(v.shape), dt_, kind="ExternalInput"))
    C.d_ident = C.c_ident
    C.out = DT(nc, "out", [S_LEN, D], F32, kind="ExternalOutput")
    for k, (shp, dt_) in SCRATCH.items():
        kind = "ExternalInput" if k in inject else ("ExternalOutput" if k in expose else "Internal")
        setattr(C, k, DT(nc, k, shp, dt_, kind=kind))
    C.final_norm = C.P['final_norm']
    C.ffn_norm = {1: C.P['ffn1_norm'].ap, 2: C.P['ffn2_norm'].ap}
    C.ffn_wg = {1: C.P['ffn1_w_gate'], 2: C.P['ffn2_w_gate']}
    C.ffn_wu = {1: C.P['ffn1_w_up'], 2: C.P['ffn2_w_up']}
    C.ffn_wd = {1: C.P['ffn1_w_down'], 2: C.P['ffn2_w_down']}
    load_consts(C)
    C.rw_stop = RW_STOP
    if "load" in phases:
        phase_load_x(C)
    for l in range(depth):
        if "ffn1" in phases:
            phase_ffn(C, l, 1)
        if "proj" in phases:
            phase_proj(C, l)
        if "rwkv" in phases:
            phase_rwkv(C, l)
        if "mla" in phases:
            phase_mla(C, l)
        if "hyena" in phases:
            phase_hyena(C, l)
        if "gqa" in phases:
            phase_gqa(C, l)
        if "merge" in phases:
            phase_merge(C, l)
        if "ffn2" in phases:
            phase_ffn(C, l, 2)
    if "final" in phases:
        phase_final_norm(C)
    C.S.barrier()
    C.S.finalize()
    return nc, hc


def kernel(**inputs):
    nc, hc = build()
    x = np.ascontiguousarray(np.asarray(inputs['x'], dtype=np.float32))
    n = x.shape[0]
    base = {k: np.ascontiguousarray(np.asarray(inputs[k], dtype=np.float32)) for k in PARAM_SHAPES}
    base.update(hc)
    in_maps = []
    for b in range(n):
        m = dict(base)
        m['x'] = x[b]
        in_maps.append(m)
    res = run_bass_kernel_spmd(nc, in_maps, core_ids=list(range(n)))
    return np.stack([r['out'] for r in res.results], axis=0)


def norm_tile(C, xa, hT, gcol, rstd, sq, p_s, nsub):
    S = C.S
    for s_i in range(nsub):
        for c in range(16):
            q_ = sq[c % 2]
            act(S, q_[:], xa[:, c, s_i * NT:(s_i + 1) * NT], AF.Square, r=[xa.b(c)], w=[q_.b()])
            mm(S, p_s[:], C.ones[:], q_[:], start=(c == 0), stop=(c == 15), r=[C.ones.b(), q_.b()], w=[p_s.b()])
        rsqrt(S, rstd[:, s_i * NT:(s_i + 1) * NT], p_s[:], 1.0 / D, C.epsc[:, 0:1], r=[p_s.b(), C.epsc.b()],
              w=[rstd.b(s_i)])
    for c in range(16):
        stt(S, hT[:, c, :], xa[:, c, :], gcol[:, c:c + 1], rstd[:], ALU.mult, ALU.mult,
            r=[xa.b(c), gcol.b()] + [rstd.b(i) for i in range(nsub)], w=[hT.b(c)])


def phase_proj(C, l):
    S = C.S
    nm = "pj%d_" % l
    TT = 1024
    NSUB = TT // NT
    win = C.P['w_in'].ap[l]
    gcol = sb(S, nm + "gcol", [128, 16])
    S.op("sp", lambda e: e.dma_start(out=gcol[:], in_=C.P['mix_norm'].ap[l].rearrange("(c p) -> p c", p=128),
                                     allow_slow_non_contiguous=True), w=[gcol.b()], dma=True)
    xa = sb(S, nm + "xa", [128, 16, TT])
    hT = sb(S, nm + "hT", [128, 16, TT], BF16)
    rstd = sb(S, nm + "rstd", [128, TT])
    sq = [sb(S, nm + "sq%d" % i, [128, NT]) for i in range(2)]
    wt = [sb(S, nm + "wt%d" % i, [128, 16, 512], BF16) for i in range(2)]
    stg = [sb(S, nm + "stg%d" % i, [128, TT]) for i in range(2)]
    stv = [sb(S, nm + "stv%d" % i, [128, 256], BF16) for i in range(2)]
    pp = [ps(S, nm + "pp%d" % i, [128, NT]) for i in range(4)]
    p_s = ps(S, nm + "pss", [128, NT])
    groups = []
    c0 = 0
    while c0 < PM_ROWS:
        n = min(512, PM_ROWS - c0)
        groups.append((c0, n))
        c0 += n
    it = 0
    kk = 0
    ks = 0
    for tt_i in range(S_LEN // TT):
        t0 = tt_i * TT
        for c in range(16):
            dma(S, xa[:, c, :], C.xres[c * 128:(c + 1) * 128, t0:t0 + TT], r=[C.xres.b((c, tt_i))],
                w=[xa.b(c)], q="sp" if c % 2 == 0 else "act")
        norm_tile(C, xa, hT, gcol, rstd, sq, p_s, NSUB)
        hT_r = [hT.b(c) for c in range(16)]
        for c in range(16):
            dma(S, C.hT_d[c * 128:(c + 1) * 128, t0:t0 + TT], hT[:, c, :], r=[hT.b(c)], w=[C.hT_d.b((c, tt_i))], q="sp")
        for (c0, n) in groups:
            w_t = wt[it % 2]
            it += 1
            dma(S, w_t[:, :, 0:n], win[:, c0:c0 + n].rearrange("(c p) f -> p c f", p=128), w=[w_t.b()], q="pool")
            for j in range((n + 127) // 128):
                m = min(128, n - j * 128)
                st_ = stg[ks % 2]
                ks += 1
                for s_i in range(NSUB):
                    p = pp[kk % 4]
                    kk += 1
                    tsl = slice(s_i * NT, (s_i + 1) * NT)
                    for c in range(16):
                        mm(S, p[0:m, :], w_t[:, c, j * 128:j * 128 + m], hT[:, c, tsl], start=(c == 0), stop=(c == 15),
                           r=[w_t.b(), hT_r[c]], w=[p.b()])
                    cp(S, st_[0:m, tsl], p[0:m, :], r=[p.b()], w=[st_.b()], eng="dve" if kk % 2 == 0 else "act")
                dma(S, C.pm_d[c0 + j * 128:c0 + j * 128 + m, t0:t0 + TT], st_[0:m, :], r=[st_.b()],
                    w=[C.pm_d.b((c0 + j * 128, tt_i))], q="sp")
        w_t = wt[it % 2]
        it += 1
        dma(S, w_t[:, :, 0:256], win[:, PM_ROWS:PM_ROWS + 256].rearrange("(c p) f -> p c f", p=128), w=[w_t.b()], q="pool")
        for tc in range(TT // 128):
            p = pp[kk % 4]
            kk += 1
            sv = stv[tc % 2]
            for c in range(16):
                mm(S, p[:, 0:256], hT[:, c, tc * 128:(tc + 1) * 128], w_t[:, c, 0:256], start=(c == 0), stop=(c == 15),
                   r=[w_t.b(), hT_r[c]], w=[p.b()])
            cp(S, sv[:], p[:, 0:256], r=[p.b()], w=[sv.b()], eng="dve" if tc % 2 == 0 else "act")
            dma(S, C.pdv_d[t0 + tc * 128:t0 + (tc + 1) * 128, :], sv[:], r=[sv.b()], w=[C.pdv_d.b(tt_i * 8 + tc)], q="sp")
    S.barrier()
    S.release()


def attn_core(C, nm, kq_parts, v_t, v_off, scale, y_d, row0, onesb, bufs):
    S = C.S
    p_sc, p_o, p_r, pT, rinv, yst = bufs
    ksc = 0
    for ti in range(S_LEN // NT):
        tsl = slice(ti * NT, (ti + 1) * NT)
        po = p_o[ti % 2]
        pr = p_r[ti % 2]
        for sc in range(16):
            psc = p_sc[ksc % 2]
            p_t = pT[ksc % 3]
            ksc += 1
            for i, (kT, qT, K) in enumerate(kq_parts):
                mm(S, psc[:], kT[0:K, sc * 128:(sc + 1) * 128], qT[0:K, tsl], start=(i == 0), stop=(i == len(kq_parts) - 1),
                   r=[kT.b(), qT.b()], w=[psc.b()])
            act(S, p_t[:], psc[:], AF.Exp, scale=scale, r=[psc.b()], w=[p_t.b()])
            mm(S, po[:], v_t[:, sc, v_off:v_off + 128], p_t[:], start=(sc == 0), stop=(sc == 15), r=[v_t.b(), p_t.b()], w=[po.b()])
            mm(S, pr[:], onesb[:], p_t[:], start=(sc == 0), stop=(sc == 15), r=[onesb.b(), p_t.b()], w=[pr.b()])
        ri = rinv[ti % 2]
        ys = yst[ti % 2]
        S.op("dve", lambda e, ri=ri, pr=pr: e.reciprocal(ri[:], pr[:]), r=[pr.b()], w=[ri.b()])
        tt(S, ys[:], po[:], ri[:], ALU.mult, r=[po.b(), ri.b()], w=[ys.b()])
        dma(S, y_d[row0:row0 + 128, tsl], ys[:], r=[ys.b()], w=[y_d.b((row0, ti))], q="sp")


def attn_bufs(S, nm):
    p_sc = [ps(S, nm + "psc%d" % i, [128, NT]) for i in range(2)]
    p_o = [ps(S, nm + "po%d" % i, [128, NT]) for i in range(2)]
    p_r = [ps(S, nm + "pr%d" % i, [128, NT]) for i in range(2)]
    pT = [sb(S, nm + "pT%d" % i, [128, NT], BF16) for i in range(3)]
    rinv = [sb(S, nm + "ri%d" % i, [128, NT]) for i in range(2)]
    yst = [sb(S, nm + "ys%d" % i, [128, NT], BF16) for i in range(2)]
    return p_sc, p_o, p_r, pT, rinv, yst


def rope_norm_head(C, nm, src_rows, nrow, gain_col, cosT, sinT, RT, dst, tmp, p_a, p_b, do_norm):
    S = C.S
    xin, xn, t1, rs = tmp
    dma(S, xin[0:nrow, :], C.pm_d[src_rows:src_rows + nrow, :], w=[xin.b()], q="act")
    for ti in range(S_LEN // NT):
        tsl = slice(ti * NT, (ti + 1) * NT)
        if do_norm:
            act(S, t1[0:nrow, :], xin[0:nrow, tsl], AF.Square, r=[xin.b()], w=[t1.b()])
            mm(S, p_a[0:nrow, :], C.ones[0:nrow, 0:nrow], t1[0:nrow, :], r=[C.ones.b(), t1.b()], w=[p_a.b()])
            rsqrt(S, rs[0:nrow, :], p_a[0:nrow, :], 1.0 / nrow, C.epsc[0:nrow, 0:1], r=[p_a.b(), C.epsc.b()], w=[rs.b()])
            stt(S, xn[0:nrow, :], xin[0:nrow, tsl], gain_col[0:nrow, 0:1], rs[0:nrow, :], ALU.mult, ALU.mult,
                r=[xin.b(), gain_col.b(), rs.b()], w=[xn.b()])
            src = xn[0:nrow, :]
        else:
            cp(S, xn[0:nrow, :], xin[0:nrow, tsl], r=[xin.b()], w=[xn.b()], eng="pool")
            src = xn[0:nrow, :]
        mm(S, p_b[0:nrow, :], RT[0:nrow, 0:nrow], src, r=[RT.b(), xn.b()], w=[p_b.b()])
        tt(S, t1[0:nrow, :], p_b[0:nrow, :], sinT[0:nrow, tsl], ALU.mult, r=[p_b.b(), sinT.b()], w=[t1.b()])
        tt(S, xn[0:nrow, :], src, cosT[0:nrow, tsl], ALU.mult, r=[xn.b(), cosT.b()], w=[xn.b()], eng="pool")
        tt(S, dst[0:nrow, tsl], xn[0:nrow, :], t1[0:nrow, :], ALU.add, r=[xn.b(), t1.b()], w=[dst.b()])


def phase_gqa(C, l):
    S = C.S
    nm = "gq%d_" % l
    cosT = sb(S, nm + "cos", [128, S_LEN])
    sinT = sb(S, nm + "sin", [128, S_LEN])
    RT = sb(S, nm + "RT", [128, 128])
    dma(S, cosT[:], C.c_gq_cos[:], w=[cosT.b()])
    dma(S, sinT[:], C.c_gq_sin[:], w=[sinT.b()], q="act")
    dma(S, RT[:], C.c_gq_RT[:], w=[RT.b()])
    gq = sb(S, nm + "gq", [128, 2])
    S.op("sp", lambda e: e.dma_start(out=gq[:, 0:1], in_=C.P['gqa_q_norm'].ap[l].rearrange("(p o) -> p o", o=1),
                                     allow_slow_non_contiguous=True), w=[gq.b()], dma=True)
    gk = sb(S, nm + "gk", [128, 2])
    S.op("sp", lambda e: e.dma_start(out=gk[:, 0:1], in_=C.P['gqa_k_norm'].ap[l].rearrange("(p o) -> p o", o=1),
                                     allow_slow_non_contiguous=True), w=[gk.b()], dma=True)
    onesb = sb(S, nm + "onesb", [128, 128], BF16)
    mset(S, onesb[:], 1.0, w=[onesb.b()])
    tmp = (sb(S, nm + "xin", [128, S_LEN]), sb(S, nm + "xn", [128, NT]), sb(S, nm + "t1", [128, NT]), sb(S, nm + "rs", [128, NT]))
    p_a = ps(S, nm + "pa", [128, NT])
    p_b = ps(S, nm + "pb", [128, NT])
    qT = [sb(S, nm + "qT%d" % h, [128, S_LEN], BF16) for h in range(4)]
    kT = [sb(S, nm + "kT%d" % g, [128, S_LEN], BF16) for g in range(2)]
    v_t = sb(S, nm + "v", [128, 16, 256], BF16)
    dma(S, v_t[:], C.pdv_d.ap.rearrange("(c p) f -> p c f", p=128), w=[v_t.b()], q="act")
    for h in range(4):
        rope_norm_head(C, nm, OFF_GQA + h * 128, 128, gq, cosT, sinT, RT, qT[h], tmp, p_a, p_b, True)
    for g in range(2):
        rope_norm_head(C, nm, OFF_GQA + 512 + g * 128, 128, gk, cosT, sinT, RT, kT[g], tmp, p_a, p_b, True)
    bufs = attn_bufs(S, nm)
    for h in range(4):
        g = h // 2
        attn_core(C, nm, [(kT[g], qT[h], 128)], v_t, g * 128, 128.0 ** -0.5, C.yd_d, h * 128, onesb, bufs)
    S.barrier()
    S.release()


def phase_mla(C, l):
    S = C.S
    nm = "ml%d_" % l
    cosT = sb(S, nm + "cos", [64, S_LEN])
    sinT = sb(S, nm + "sin", [64, S_LEN])
    RT = sb(S, nm + "RT", [64, 64])
    dma(S, cosT[:], C.c_ml_cos[:], w=[cosT.b()])
    dma(S, sinT[:], C.c_ml_sin[:], w=[sinT.b()], q="act")
    dma(S, RT[:], C.c_ml_RT[:], w=[RT.b()])
    wq = sb(S, nm + "wq", [128, 3, 768], BF16)
    dma(S, wq[:], C.P['mla_w_q_up'].ap[l].rearrange("(c p) f -> p c f", p=128), w=[wq.b()], q="pool")
    wkv = sb(S, nm + "wkv", [128, 1024], BF16)
    dma(S, wkv[:], C.P['mla_w_kv_up'].ap[l], w=[wkv.b()], q="pool")
    gq = sb(S, nm + "gq", [128, 4])
    S.op("sp", lambda e: e.dma_start(out=gq[:, 0:3], in_=C.P['mla_q_norm'].ap[l].rearrange("(c p) -> p c", p=128),
                                     allow_slow_non_contiguous=True), w=[gq.b()], dma=True)
    S.op("sp", lambda e: e.dma_start(out=gq[:, 3:4], in_=C.P['mla_kv_norm'].ap[l].rearrange("(p o) -> p o", o=1),
                                     allow_slow_non_contiguous=True), w=[gq.b()], dma=True)
    onesb = sb(S, nm + "onesb", [128, 128], BF16)
    mset(S, onesb[:], 1.0, w=[onesb.b()])
    xq = sb(S, nm + "xq", [128, 3, S_LEN])
    dma(S, xq[:], C.pm_d.ap[OFF_MLA:OFF_MLA + 384, :].rearrange("(c p) t -> p c t", p=128), w=[xq.b()])
    xkv = sb(S, nm + "xkv", [128, S_LEN])
    dma(S, xkv[:], C.pm_d[OFF_MLA + 384:OFF_MLA + 512, :], w=[xkv.b()], q="act")
    qn = sb(S, nm + "qn", [128, 3, S_LEN], BF16)
    kvn = sb(S, nm + "kvn", [128, S_LEN], BF16)
    t1 = sb(S, nm + "t1", [128, NT])
    t2 = sb(S, nm + "t2", [128, NT])
    xn = sb(S, nm + "xn", [128, NT])
    rs = sb(S, nm + "rs", [128, NT])
    p_a = ps(S, nm + "pa", [128, NT])
    p_b = ps(S, nm + "pb", [128, NT])
    nti = S_LEN // NT
    for ti in range(nti):
        tsl = slice(ti * NT, (ti + 1) * NT)
        for c in range(3):
            act(S, t1[:], xq[:, c, tsl], AF.Square, r=[xq.b()], w=[t1.b()])
            mm(S, p_a[:], C.ones[:], t1[:], start=(c == 0), stop=(c == 2), r=[C.ones.b(), t1.b()], w=[p_a.b()])
        rsqrt(S, rs[:], p_a[:], 1.0 / 384, C.epsc[:, 0:1], r=[p_a.b(), C.epsc.b()], w=[rs.b()])
        for c in range(3):
            stt(S, qn[:, c, tsl], xq[:, c, tsl], gq[:, c:c + 1], rs[:], ALU.mult, ALU.mult, r=[xq.b(), gq.b(), rs.b()], w=[qn.b()])
        act(S, t1[:], xkv[:, tsl], AF.Square, r=[xkv.b()], w=[t1.b()])
        mm(S, p_a[:], C.ones[:], t1[:], r=[C.ones.b(), t1.b()], w=[p_a.b()])
        rsqrt(S, rs[:], p_a[:], 1.0 / 128, C.epsc[:, 0:1], r=[p_a.b(), C.epsc.b()], w=[rs.b()])
        stt(S, kvn[:, tsl], xkv[:, tsl], gq[:, 3:4], rs[:], ALU.mult, ALU.mult, r=[xkv.b(), gq.b(), rs.b()], w=[kvn.b()])
    qnope = [sb(S, nm + "qnope%d" % h, [128, S_LEN], BF16) for h in range(4)]
    qrope = [sb(S, nm + "qrope%d" % h, [64, S_LEN], BF16) for h in range(4)]
    knope = [sb(S, nm + "knope%d" % h, [128, S_LEN], BF16) for h in range(4)]
    krope = sb(S, nm + "krope", [64, S_LEN], BF16)
    v_t = sb(S, nm + "v", [128, 16, 512], BF16)
    k = 0
    for h in range(4):
        for ti in range(nti):
            tsl = slice(ti * NT, (ti + 1) * NT)
            for c in range(3):
                mm(S, p_a[:], wq[:, c, h * 192:h * 192 + 128], qn[:, c, tsl], start=(c == 0), stop=(c == 2), r=[wq.b(), qn.b()], w=[p_a.b()])
            cp(S, qnope[h][:, tsl], p_a[:], r=[p_a.b()], w=[qnope[h].b()], eng="act")
            mm(S, p_a[:], wkv[:, h * 256:h * 256 + 128], kvn[:, tsl], r=[wkv.b(), kvn.b()], w=[p_a.b()])
            cp(S, knope[h][:, tsl], p_a[:], r=[p_a.b()], w=[knope[h].b()], eng="act")
            for c in range(3):
                mm(S, p_b[0:64, :], wq[:, c, h * 192 + 128:h * 192 + 192], qn[:, c, tsl], start=(c == 0), stop=(c == 2), r=[wq.b(), qn.b()], w=[p_b.b()])
            cp(S, xn[0:64, :], p_b[0:64, :], r=[p_b.b()], w=[xn.b()])
            mm(S, p_b[0:64, :], RT[:, :], xn[0:64, :], r=[RT.b(), xn.b()], w=[p_b.b()])
            tt(S, t1[0:64, :], p_b[0:64, :], sinT[:, tsl], ALU.mult, r=[p_b.b(), sinT.b()], w=[t1.b()])
            tt(S, t2[0:64, :], xn[0:64, :], cosT[:, tsl], ALU.mult, r=[xn.b(), cosT.b()], w=[t2.b()], eng="pool")
            tt(S, qrope[h][:, tsl], t2[0:64, :], t1[0:64, :], ALU.add, r=[t2.b(), t1.b()], w=[qrope[h].b()])
    xkr = sb(S, nm + "xkr", [64, S_LEN])
    dma(S, xkr[:], C.pm_d[OFF_MLA + 512:OFF_MLA + 576, :], w=[xkr.b()])
    for ti in range(nti):
        tsl = slice(ti * NT, (ti + 1) * NT)
        mm(S, p_b[0:64, :], RT[:, :], xkr[:, tsl], r=[RT.b(), xkr.b()], w=[p_b.b()])
        tt(S, t1[0:64, :], p_b[0:64, :], sinT[:, tsl], ALU.mult, r=[p_b.b(), sinT.b()], w=[t1.b()])
        tt(S, t2[0:64, :], xkr[:, tsl], cosT[:, tsl], ALU.mult, r=[xkr.b(), cosT.b()], w=[t2.b()], eng="pool")
        tt(S, krope[:, tsl], t2[0:64, :], t1[0:64, :], ALU.add, r=[t2.b(), t1.b()], w=[krope.b()])
    for sc in range(16):
        for h in range(4):
            mm(S, p_a[:, h * 128:(h + 1) * 128], kvn[:, sc * 128:(sc + 1) * 128], wkv[:, h * 256 + 128:h * 256 + 256],
               r=[wkv.b(), kvn.b()], w=[p_a.b()])
        cp(S, v_t[:, sc, :], p_a[:], r=[p_a.b()], w=[v_t.b()], eng="dve" if sc % 2 == 0 else "act")
    bufs = attn_bufs(S, nm)
    for h in range(4):
        attn_core(C, nm, [(knope[h], qnope[h], 128), (krope, qrope[h], 64)], v_t, h * 128, 192.0 ** -0.5, C.yb_d, h * 128, onesb, bufs)
    S.barrier()
    S.release()


def phase_merge(C, l):
    S = C.S
    nm = "mg%d_" % l
    win = C.P['w_in'].ap[l]
    wbr = C.P['w_branch'].ap[l]
    wout = C.P['w_out'].ap[l]
    ys_d = [C.ya_d, C.yb_d, C.yc_d, C.yd_d]
    NQ = S_LEN // NT
    hT = sb(S, nm + "hT", [128, 16, S_LEN], BF16)
    for c in range(16):
        dma(S, hT[:, c, :], C.hT_d[c * 128:(c + 1) * 128, :], w=[hT.b()], q="sp" if c % 2 == 0 else "act")
    yT = [sb(S, nm + "yT%d" % i, [128, 4, S_LEN], BF16) for i in range(2)]
    acc = {(j, q): sb(S, nm + "acc%d_%d" % (j, q), [128, NT]) for j in range(4) for q in range(NQ)}
    gw = [sb(S, nm + "gw%d" % i, [128, 16, 512], BF16) for i in range(2)]
    wb = [sb(S, nm + "wb%d" % i, [128, 4, 512], BF16) for i in range(2)]
    sg = [sb(S, nm + "sg%d" % i, [128, NT]) for i in range(2)]
    tmp = [sb(S, nm + "tmp%d" % i, [128, NT]) for i in range(2)]
    mst = [sb(S, nm + "mst%d" % i, [128, NT], BF16) for i in range(2)]
    p_g = [ps(S, nm + "pg%d" % i, [128, NT]) for i in range(3)]
    p_b = [ps(S, nm + "pb%d" % i, [128, NT]) for i in range(3)]
    it = 0
    kk = 0
    km = 0
    for dg in range(4):
        for n in range(4):
            g_t, b_t, y_t = gw[it % 2], wb[it % 2], yT[it % 2]
            it += 1
            col = OFF_GATE + n * D + dg * 512
            dma(S, g_t[:], win[:, col:col + 512].rearrange("(c p) f -> p c f", p=128), w=[g_t.b()], q="pool")
            dma(S, b_t[:], wbr[n][:, dg * 512:(dg + 1) * 512].rearrange("(c p) f -> p c f", p=128), w=[b_t.b()], q="pool")
            dma(S, y_t[:], ys_d[n].ap.rearrange("(c p) t -> p c t", p=128), w=[y_t.b()], q="act")
            for j in range(4):
                dc = dg * 4 + j
                for q in range(NQ):
                    tsl = slice(q * NT, (q + 1) * NT)
                    pg, pb = p_g[kk % 3], p_b[kk % 3]
                    sg_t, tm_t = sg[kk % 2], tmp[kk % 2]
                    kk += 1
                    for c in range(16):
                        mm(S, pg[:], g_t[:, c, j * 128:(j + 1) * 128], hT[:, c, tsl], start=(c == 0), stop=(c == 15),
                           r=[g_t.b(), hT.b()], w=[pg.b()])
                    for c in range(4):
                        mm(S, pb[:], b_t[:, c, j * 128:(j + 1) * 128], y_t[:, c, tsl], start=(c == 0), stop=(c == 3),
                           r=[b_t.b(), y_t.b()], w=[pb.b()])
                    act(S, sg_t[:], pg[:], AF.Sigmoid, r=[pg.b()], w=[sg_t.b()])
                    a_ = acc[(j, q)]
                    if n == 0:
                        tt(S, a_[:], sg_t[:], pb[:], ALU.mult, r=[sg_t.b(), pb.b()], w=[a_.b()])
                    else:
                        tt(S, tm_t[:], sg_t[:], pb[:], ALU.mult, r=[sg_t.b(), pb.b()], w=[tm_t.b()])
                        if n < 3:
                            tt(S, a_[:], a_[:], tm_t[:], ALU.add, r=[a_.b(), tm_t.b()], w=[a_.b()])
                        else:
                            m_t = mst[km % 2]
                            km += 1
                            tt(S, m_t[:], a_[:], tm_t[:], ALU.add, r=[a_.b(), tm_t.b()], w=[m_t.b()])
                            dma(S, C.mT_d[dc * 128:(dc + 1) * 128, tsl], m_t[:], r=[m_t.b()], w=[C.mT_d.b((dc, q // 2))], q="sp")
    S.barrier()
    S.release()
    TT = 1024
    NSUB = TT // NT
    mT = sb(S, nm + "mT", [128, 16, TT], BF16)
    wo = [sb(S, nm + "wo%d" % i, [128, 16, 512], BF16) for i in range(2)]
    xa = [sb(S, nm + "xa%d" % i, [128, TT]) for i in range(2)]
    p_o = [ps(S, nm + "po%d" % i, [128, NT]) for i in range(3)]
    it = 0
    kk = 0
    for tt_i in range(S_LEN // TT):
        t0 = tt_i * TT
        for c in range(16):
            dma(S, mT[:, c, :], C.mT_d[c * 128:(c + 1) * 128, t0:t0 + TT], r=[C.mT_d.b((c, tt_i))], w=[mT.b(c)],
                q="sp" if c % 2 == 0 else "act")
        for og in range(4):
            w_t = wo[it % 2]
            it += 1
            dma(S, w_t[:], wout[:, og * 512:(og + 1) * 512].rearrange("(c p) f -> p c f", p=128), w=[w_t.b()], q="pool")
            for j in range(4):
                dc = og * 4 + j
                x_t = xa[dc % 2]
                dma(S, x_t[:], C.xres[dc * 128:(dc + 1) * 128, t0:t0 + TT], r=[C.xres.b((dc, tt_i))], w=[x_t.b()], q="sp")
                for s_i in range(NSUB):
                    tsl = slice(s_i * NT, (s_i + 1) * NT)
                    po = p_o[kk % 3]
                    kk += 1
                    for c in range(16):
                        mm(S, po[:], w_t[:, c, j * 128:(j + 1) * 128], mT[:, c, tsl], start=(c == 0), stop=(c == 15),
                           r=[w_t.b(), mT.b(c)], w=[po.b()])
                    tt(S, x_t[:, tsl], x_t[:, tsl], po[:], ALU.add, r=[x_t.b(), po.b()], w=[x_t.b()])
                dma(S, C.xres[dc * 128:(dc + 1) * 128, t0:t0 + TT], x_t[:], r=[x_t.b()], w=[C.xres.b((dc, tt_i))], q="sp")
    S.barrier()
    S.release()


def hy_consts():
    L = S_LEN
    c = {}
    t = np.linspace(0.0, 1.0, L, dtype=np.float32)[:, None]
    w_ang = (2.0 * math.pi * np.arange(L, dtype=np.float32) / L).astype(np.float32)
    fr = np.linspace(1e-4, 15.0, 16, dtype=np.float32)
    ang = w_ang[:, None] * fr[None, :]
    z = np.concatenate([t, np.cos(ang), -np.sin(ang)], axis=-1).astype(np.float32)
    c['c_hy_z'] = np.ascontiguousarray(z.T)
    deltas = np.abs(np.linspace(math.log(1e-2) / 1.5, math.log(1e-2) / 0.3, 512)).astype(np.float32)
    win = np.exp(-t * deltas[None, :]).astype(np.float32)
    c['c_hy_win'] = np.ascontiguousarray(win.reshape(16, 128, 512).transpose(1, 0, 2))
    n = np.arange(L, dtype=np.int64)
    k = np.mod(np.outer(n, 2 * n + 1), 8192)
    angm = (2.0 * math.pi / 8192.0) * k.astype(np.float64)
    Cm = np.cos(angm)
    Sm = np.sin(angm)
    bf = ml_dtypes.bfloat16

    def t_major(M):
        return np.ascontiguousarray(M.reshape(16, 128, 16, 128).transpose(2, 1, 0, 3).reshape(16, 128, 2048).astype(bf))

    def f_major(M):
        MT = M.T
        return np.ascontiguousarray(MT.reshape(16, 128, 16, 128).transpose(2, 1, 0, 3).reshape(16, 128, 2048).astype(bf))

    c['c_hy_Ct'] = t_major(Cm)
    c['c_hy_St'] = t_major(Sm)
    c['c_hy_Cf'] = f_major(Cm)
    c['c_hy_Sf'] = f_major(Sm)
    return c


def sin_act(S, out, arg, tmp, bufs_r, w):
    s4, s8, q = tmp
    act(S, s4, arg, AF.Sin, scale=0.25, r=bufs_r, w=[w[1]])
    act(S, s8, arg, AF.Sin, scale=0.125, r=bufs_r, w=[w[2]])
    tt(S, q, s8, s8, ALU.mult, r=[w[2]], w=[w[3]])
    ts(S, q, q, -2.0, 1.0, ALU.mult, ALU.add, r=[w[3]], w=[w[3]])
    tt(S, s8, s4, q, ALU.mult, r=[w[1], w[3]], w=[w[2]])
    tt(S, q, s4, s4, ALU.mult, r=[w[1]], w=[w[3]])
    ts(S, q, q, -2.0, 1.0, ALU.mult, ALU.add, r=[w[3]], w=[w[3]])
    stt(S, out, s8, 4.0, q, ALU.mult, ALU.mult, r=[w[2], w[3]], w=[w[0]])


def phase_hyena(C, l):
    S = C.S
    nm = "hy%d_" % l
    P = C.P
    nti = S_LEN // NT
    zT = sb(S, nm + "zT", [33, S_LEN])
    dma(S, zT[:], C.c_hy_z[:], w=[zT.b()])
    w1 = sb(S, nm + "w1", [33, 64])
    dma(S, w1[:], P['hy_w1'].ap[l], w=[w1.b()])
    w2 = sb(S, nm + "w2", [64, 64])
    dma(S, w2[:], P['hy_w2'].ap[l], w=[w2.b()])
    w3 = sb(S, nm + "w3", [64, 64])
    dma(S, w3[:], P['hy_w3'].ap[l], w=[w3.b()])
    w4 = sb(S, nm + "w4", [64, 2048])
    dma(S, w4[:], P['hy_w4'].ap[l], w=[w4.b()], q="act")
    cols = sb(S, nm + "cols", [64, 8])
    for i, k in enumerate(['hy_b1', 'hy_b2', 'hy_b3']):
        S.op("sp", lambda e, i=i, k=k: e.dma_start(out=cols[:, i:i + 1], in_=P[k].ap[l].rearrange("(p o) -> p o", o=1),
                                                   allow_slow_non_contiguous=True), w=[cols.b()], dma=True)
    S.op("sp", lambda e: e.dma_start(out=cols[:, 3:6], in_=P['hy_freq'].ap[l].rearrange("k c -> c k"),
                                     allow_slow_non_contiguous=True), w=[cols.b()], dma=True)
    bias_s = sb(S, nm + "bias", [128, 1024])
    dma(S, bias_s[:], P['hy_bias'].ap[l].rearrange("o c -> (o c)").partition_broadcast(128), w=[bias_s.b()])
    ts(S, bias_s[:], bias_s[:], 1.0 / 2048.0, None, ALU.mult, r=[bias_s.b()], w=[bias_s.b()])
    win = sb(S, nm + "win", [128, 16, 512])
    dma(S, win[:], C.c_hy_win[:], w=[win.b()], q="act")
    hA = sb(S, nm + "hA", [64, S_LEN])
    hB = sb(S, nm + "hB", [64, S_LEN])
    arg = sb(S, nm + "arg", [64, NT])
    s4 = sb(S, nm + "s4", [64, NT])
    s8 = sb(S, nm + "s8", [64, NT])
    qq = sb(S, nm + "qq", [64, NT])
    p_a = ps(S, nm + "pa", [128, NT])
    p_b = ps(S, nm + "pb", [128, NT])
    p_c = ps(S, nm + "pc", [128, NT])
    p_d = ps(S, nm + "pd", [128, NT])
    layers = [(w1, zT, 33, hA, 0), (w2, hA, 64, hB, 1), (w3, hB, 64, hA, 2)]
    for (w_, src, K, dst, li) in layers:
        for ti in range(nti):
            tsl = slice(ti * NT, (ti + 1) * NT)
            mm(S, p_a[0:64, :], w_[0:K, :], src[0:K, tsl], r=[w_.b(), src.b()], w=[p_a.b()])
            ts(S, arg[:], p_a[0:64, :], cols[:, li:li + 1], cols[:, 3 + li:4 + li], ALU.add, ALU.mult,
               r=[p_a.b(), cols.b()], w=[arg.b()])
            sin_act(S, dst[:, tsl], arg[:], (s4[:], s8[:], qq[:]), [arg.b()], [dst.b(), s4.b(), s8.b(), qq.b()])
    h3 = hA
    hs = sb(S, nm + "hs", [128, 16, 1024], BF16)
    hd = sb(S, nm + "hd", [128, 16, 1024], BF16)
    f0 = sb(S, nm + "f0", [128, 1024])
    f1 = sb(S, nm + "f1", [128, 1024])
    pg = [p_a, p_b, p_c, p_d]
    for tc in range(16):
        for g in range(4):
            mm(S, pg[g][:], h3[:, tc * 128:(tc + 1) * 128], w4[:, g * 512:(g + 1) * 512], r=[h3.b(), w4.b()], w=[pg[g].b()])
        for o in range(2):
            tt(S, f0[:, o * 512:(o + 1) * 512], pg[o][:], win[:, tc, :], ALU.mult, r=[pg[o].b(), win.b()], w=[f0.b()])
            tt(S, f1[:, o * 512:(o + 1) * 512], pg[2 + o][:], win[:, tc, :], ALU.mult, r=[pg[2 + o].b(), win.b()], w=[f1.b()])
        if tc == 0:
            mset(S, f1[0:1, :], 0.0, w=[f1.b()])
        tt(S, hs[:, tc, :], f0[:], f1[:], ALU.add, r=[f0.b(), f1.b()], w=[hs.b()], eng="pool")
        tt(S, hd[:, tc, :], f0[:], f1[:], ALU.subtract, r=[f0.b(), f1.b()], w=[hd.b()])
    ct = [sb(S, nm + "ct%d" % i, [128, 2048], BF16) for i in range(2)]
    st = [sb(S, nm + "st%d" % i, [128, 2048], BF16) for i in range(2)]
    hst = [sb(S, nm + "hst%d" % i, [128, 4, 512]) for i in range(2)]
    for fc in range(16):
        c_t, s_t, h_t = ct[fc % 2], st[fc % 2], hst[fc % 2]
        dma(S, c_t[:], C.c_hy_Ct.ap[fc], w=[c_t.b()], q="sp")
        dma(S, s_t[:], C.c_hy_St.ap[fc], w=[s_t.b()], q="act")
        for o in range(2):
            pr, pi = pg[o * 2], pg[o * 2 + 1]
            for tc in range(16):
                mm(S, pr[:], c_t[:, tc * 128:(tc + 1) * 128], hs[:, tc, o * 512:(o + 1) * 512], start=(tc == 0), stop=(tc == 15),
                   r=[c_t.b(), hs.b()], w=[pr.b()])
            for tc in range(16):
                mm(S, pi[:], s_t[:, tc * 128:(tc + 1) * 128], hd[:, tc, o * 512:(o + 1) * 512], start=(tc == 0), stop=(tc == 15),
                   r=[s_t.b(), hd.b()], w=[pi.b()])
            stt(S, h_t[:, o * 2, :], pr[:], 1.0 / 2048.0, bias_s[:, o * 512:(o + 1) * 512], ALU.mult, ALU.add,
                r=[pr.b(), bias_s.b()], w=[h_t.b()])
            act(S, h_t[:, o * 2 + 1, :], pi[:], AF.Copy, scale=1.0 / 2048.0, r=[pi.b()], w=[h_t.b()])
        dma(S, C.hyH_d.ap[fc * 128:(fc + 1) * 128, :].rearrange("p (k c) -> p k c", k=4), h_t[:], r=[h_t.b()],
            w=[C.hyH_d.b(fc)], q="sp")
    S.barrier()
    S.release()
    swc = sb(S, nm + "swc", [128, 12, 4])
    for k in range(3):
        S.op("sp", lambda e, k=k: e.dma_start(out=swc[:, :, k:k + 1],
                                              in_=P['hy_short_w'].ap[l][k].rearrange("(c p o) -> p c o", p=128, o=1),
                                              allow_slow_non_contiguous=True), w=[swc.b()], dma=True)
    S.op("sp", lambda e: e.dma_start(out=swc[:, :, 3:4], in_=P['hy_short_b'].ap[l].rearrange("(c p o) -> p c o", p=128, o=1),
                                     allow_slow_non_contiguous=True), w=[swc.b()], dma=True)
    x1_tm = sb(S, nm + "x1tm", [128, 16, 512])
    x2T = sb(S, nm + "x2T", [128, 4, S_LEN])
    v_tm = sb(S, nm + "vtm", [128, 16, 512], BF16)
    Yr = sb(S, nm + "Yr", [128, 16, 512], BF16)
    Ys = sb(S, nm + "Ys", [128, 16, 512], BF16)
    ycT = sb(S, nm + "ycT", [128, 4, S_LEN], BF16)
    pin = [sb(S, nm + "pin%d" % i, [128, S_LEN]) for i in range(2)]
    u = sb(S, nm + "u", [128, S_LEN])
    pt = [ps(S, nm + "pt%d" % i, [128, NT]) for i in range(2)]
    kk = 0
    for ch in range(12):
        p_in = pin[ch % 2]
        dma(S, p_in[:], C.pm_d[OFF_HY + ch * 128:OFF_HY + (ch + 1) * 128, :], w=[p_in.b()], q="sp" if ch % 2 == 0 else "act")
        dst = x2T[:, ch - 4, :] if 4 <= ch < 8 else u[:]
        dbuf = x2T.b() if 4 <= ch < 8 else u.b()
        ts(S, dst, p_in[:], swc[:, ch, 1:2], swc[:, ch, 3:4], ALU.mult, ALU.add, r=[p_in.b(), swc.b()], w=[dbuf])
        d1 = x2T[:, ch - 4, 1:S_LEN] if 4 <= ch < 8 else u[:, 1:S_LEN]
        d2 = x2T[:, ch - 4, 0:S_LEN - 1] if 4 <= ch < 8 else u[:, 0:S_LEN - 1]
        stt(S, d1, p_in[:, 0:S_LEN - 1], swc[:, ch, 0:1], d1, ALU.mult, ALU.add, r=[p_in.b(), swc.b(), dbuf], w=[dbuf])
        stt(S, d2, p_in[:, 1:S_LEN], swc[:, ch, 2:3], d2, ALU.mult, ALU.add, r=[p_in.b(), swc.b(), dbuf], w=[dbuf])
        if ch < 4 or ch >= 8:
            cc = ch if ch < 4 else ch - 8
            tgt = x1_tm if ch < 4 else v_tm
            for tc in range(16):
                p = pt[kk % 2]
                kk += 1
                tr(S, p[:, 0:128], u[:, tc * 128:(tc + 1) * 128], C.ident[:], r=[u.b(), C.ident.b()], w=[p.b()])
                cp(S, tgt[:, tc, cc * 128:(cc + 1) * 128], p[:, 0:128], r=[p.b()], w=[tgt.b()], eng="dve" if kk % 2 == 0 else "act")
    ct = [sb(S, nm + "dct%d" % i, [128, 2048], BF16) for i in range(2)]
    st = [sb(S, nm + "dst%d" % i, [128, 2048], BF16) for i in range(2)]
    hst = [sb(S, nm + "dhst%d" % i, [128, 2, 512]) for i in range(2)]
    ur = [sb(S, nm + "ur%d" % i, [128, 512]) for i in range(2)]
    us = [sb(S, nm + "us%d" % i, [128, 512]) for i in range(2)]
    m1 = sb(S, nm + "m1", [128, 512])
    m2 = sb(S, nm + "m2", [128, 512])
    m3 = sb(S, nm + "m3", [128, 512])
    m4 = sb(S, nm + "m4", [128, 512])
    p_r = [ps(S, nm + "pr%d" % i, [128, NT]) for i in range(2)]
    p_s = [ps(S, nm + "psn%d" % i, [128, NT]) for i in range(2)]
    p_y = [ps(S, nm + "py%d" % i, [128, NT]) for i in range(2)]
    it = 0
    for o in range(2):
        src_tm = v_tm
        for fc in range(16):
            c_t, s_t, h_t = ct[it % 2], st[it % 2], hst[it % 2]
            u_r, u_s = ur[it % 2], us[it % 2]
            pr, pi = p_r[it % 2], p_s[it % 2]
            it += 1
            dma(S, c_t[:], C.c_hy_Ct.ap[fc], w=[c_t.b()], q="sp")
            dma(S, s_t[:], C.c_hy_St.ap[fc], w=[s_t.b()], q="act")
            dma(S, h_t[:], C.hyH_d.ap[fc * 128:(fc + 1) * 128, o * 1024:(o + 1) * 1024].rearrange("p (k c) -> p k c", k=2),
                r=[C.hyH_d.b(fc)], w=[h_t.b()], q="sp")
            for tc in range(16):
                mm(S, pr[:], c_t[:, tc * 128:(tc + 1) * 128], src_tm[:, tc, :], start=(tc == 0), stop=(tc == 15),
                   r=[c_t.b(), src_tm.b()], w=[pr.b()])
            for tc in range(16):
                mm(S, pi[:], s_t[:, tc * 128:(tc + 1) * 128], src_tm[:, tc, :], start=(tc == 0), stop=(tc == 15),
                   r=[s_t.b(), src_tm.b()], w=[pi.b()])
            cp(S, u_r[:], pr[:], r=[pr.b()], w=[u_r.b()], eng="act")
            cp(S, u_s[:], pi[:], r=[pi.b()], w=[u_s.b()], eng="act")
            tt(S, m1[:], u_r[:], h_t[:, 0, :], ALU.mult, r=[u_r.b(), h_t.b()], w=[m1.b()])
            tt(S, m2[:], u_s[:], h_t[:, 1, :], ALU.mult, r=[u_s.b(), h_t.b()], w=[m2.b()], eng="pool")
            tt(S, Yr[:, fc, :], m1[:], m2[:], ALU.subtract, r=[m1.b(), m2.b()], w=[Yr.b()])
            tt(S, m3[:], u_r[:], h_t[:, 1, :], ALU.mult, r=[u_r.b(), h_t.b()], w=[m3.b()], eng="pool")
            tt(S, m4[:], u_s[:], h_t[:, 0, :], ALU.mult, r=[u_s.b(), h_t.b()], w=[m4.b()])
            tt(S, Ys[:, fc, :], m3[:], m4[:], ALU.add, r=[m3.b(), m4.b()], w=[Ys.b()], eng="pool")
        for tc in range(16):
            c_t, s_t = ct[it % 2], st[it % 2]
            it += 1
            dma(S, c_t[:], C.c_hy_Cf.ap[tc], w=[c_t.b()], q="sp")
            dma(S, s_t[:], C.c_hy_Sf.ap[tc], w=[s_t.b()], q="act")
            if o == 0:
                py = p_y[tc % 2]
                for fc in range(16):
                    mm(S, py[:], c_t[:, fc * 128:(fc + 1) * 128], Yr[:, fc, :], start=(fc == 0), stop=False,
                       r=[c_t.b(), Yr.b()], w=[py.b()])
                for fc in range(16):
                    mm(S, py[:], s_t[:, fc * 128:(fc + 1) * 128], Ys[:, fc, :], start=False, stop=(fc == 15),
                       r=[s_t.b(), Ys.b()], w=[py.b()])
                tt(S, v_tm[:, tc, :], x1_tm[:, tc, :], py[:], ALU.mult, r=[x1_tm.b(), py.b()], w=[v_tm.b()])
            else:
                py = p_y[tc % 2]
                for cc in range(4):
                    for fc in range(16):
                        mm(S, py[:, cc * 128:(cc + 1) * 128], Yr[:, fc, cc * 128:(cc + 1) * 128], c_t[:, fc * 128:(fc + 1) * 128],
                           start=(fc == 0), stop=False, r=[c_t.b(), Yr.b()], w=[py.b()])
                    for fc in range(16):
                        mm(S, py[:, cc * 128:(cc + 1) * 128], Ys[:, fc, cc * 128:(cc + 1) * 128], s_t[:, fc * 128:(fc + 1) * 128],
                           start=False, stop=(fc == 15), r=[s_t.b(), Ys.b()], w=[py.b()])
                for cc in range(4):
                    tt(S, ycT[:, cc, tc * 128:(tc + 1) * 128], x2T[:, cc, tc * 128:(tc + 1) * 128], py[:, cc * 128:(cc + 1) * 128],
                       ALU.mult, r=[x2T.b(), py.b()], w=[ycT.b()], eng="dve")
    dma(S, C.yc_d.ap.rearrange("(c p) t -> p c t", p=128), ycT[:], r=[ycT.b()], w=[C.yc_d.b()], q="sp")
    S.barrier()
    S.release()


RW_R, RW_V, RW_KK, RW_G, RW_BONUS = 0, 1, 2, 3, 4
RW_E, RW_B, RW_KD = 5, 7, 9
GN_EPS = 64e-5
RW_STOP = 0


def rw_consts():
    c = {}
    j = np.arange(128)[:, None]
    t = np.arange(128)[None, :]
    MU_s = (t > j).astype(np.float32)
    ML_s = (t < j).astype(np.float32)
    MU_i = (t >= j).astype(np.float32)
    ML_i = (t <= j).astype(np.float32)
    c['c_rw_m4'] = np.ascontiguousarray(np.stack([np.concatenate([MU_s, MU_s, ML_s, ML_s], 1),
                                                  np.concatenate([ML_s, ML_s, MU_s, MU_s], 1)], 0))
    c['c_rw_m3'] = np.ascontiguousarray(np.stack([np.concatenate([MU_s, MU_i, MU_i], 1),
                                                  np.concatenate([ML_s, ML_i, ML_i], 1)], 0))
    c['c_rw_tri'] = np.ascontiguousarray(np.stack([MU_i, ML_i], 0))
    blk = np.zeros((128, 128), np.float32)
    blk[:64, :64] = 1.0
    blk[64:, 64:] = 1.0
    c['c_rw_blk'] = blk
    return c


def phase_rwkv(C, l):
    S = C.S
    P = C.P
    nm = "rw%d_" % l
    T_ = S_LEN
    nti = T_ // NT
    rw = C.rw_d

    def col(dst, src_ap, n):
        S.op("sp", lambda e: e.dma_start(out=dst, in_=src_ap.rearrange("(p o) -> p o", o=1), allow_slow_non_contiguous=True),
             w=[], dma=True)

    blk = sb(S, nm + "blk", [128, 128])
    dma(S, blk[:], C.c_rw_blk[:], w=[blk.b()])
    pin = [sb(S, nm + "pin%d" % i, [128, T_]) for i in range(2)]
    mcol = [sb(S, nm + "mcol%d" % i, [128, 4]) for i in range(2)]
    npiece = [0]

    def shift_piece(row0, nrows, dst, dbuf):
        i = npiece[0] % 2
        npiece[0] += 1
        p_in, mc = pin[i], mcol[i]
        dma(S, p_in[0:nrows, :], C.pm_d[row0:row0 + nrows, :], w=[p_in.b()], q="sp" if i == 0 else "act")
        S.op("sp", lambda e: e.dma_start(out=mc[0:nrows, 0:1], in_=P['rwkv_mu_prev'].ap[l][row0:row0 + nrows].rearrange("(p o) -> p o", o=1),
                                         allow_slow_non_contiguous=True), w=[mc.b()], dma=True)
        S.op("sp", lambda e: e.dma_start(out=mc[0:nrows, 1:2], in_=P['rwkv_mu_next'].ap[l][row0:row0 + nrows].rearrange("(p o) -> p o", o=1),
                                         allow_slow_non_contiguous=True), w=[mc.b()], dma=True)
        tt(S, mc[0:nrows, 2:3], mc[0:nrows, 0:1], mc[0:nrows, 1:2], ALU.add, r=[mc.b()], w=[mc.b()])
        ts(S, mc[0:nrows, 2:3], mc[0:nrows, 2:3], -1.0, 1.0, ALU.mult, ALU.add, r=[mc.b()], w=[mc.b()])
        ts(S, dst[0:nrows, :], p_in[0:nrows, :], mc[0:nrows, 2:3], None, ALU.mult, r=[p_in.b(), mc.b()], w=[dbuf])
        stt(S, dst[0:nrows, 1:T_], p_in[0:nrows, 0:T_ - 1], mc[0:nrows, 0:1], dst[0:nrows, 1:T_], ALU.mult, ALU.add,
            r=[p_in.b(), mc.b(), dbuf], w=[dbuf])
        stt(S, dst[0:nrows, 0:T_ - 1], p_in[0:nrows, 1:T_], mc[0:nrows, 1:2], dst[0:nrows, 0:T_ - 1], ALU.mult, ALU.add,
            r=[p_in.b(), mc.b(), dbuf], w=[dbuf])

    lw = [sb(S, nm + "lw%d" % d, [96, T_]) for d in range(2)]
    la = [sb(S, nm + "la%d" % d, [96, T_]) for d in range(2)]
    lg = [sb(S, nm + "lg%d" % i, [128, T_]) for i in range(2)]
    for d in range(2):
        shift_piece(1536 + 96 * d, 96, lw[d], lw[d].b())
        act(S, lw[d][:], lw[d][:], AF.Tanh, r=[lw[d].b()], w=[lw[d].b()])
        shift_piece(1728 + 96 * d, 96, la[d], la[d].b())
    for i in range(2):
        shift_piece(1920 + 128 * i, 128, lg[i], lg[i].b())
        act(S, lg[i][:], lg[i][:], AF.Sigmoid, r=[lg[i].b()], w=[lg[i].b()])
    w2 = sb(S, nm + "w2", [96, 2, 512])
    a2 = sb(S, nm + "a2", [96, 2, 512])
    g2 = sb(S, nm + "g2", [128, 2, 512])
    dma(S, w2[:], P['rwkv_w2'].ap[l].rearrange("d k c -> k d c"), w=[w2.b()])
    dma(S, a2[:], P['rwkv_a2'].ap[l].rearrange("d k c -> k d c"), w=[a2.b()], q="act")
    dma(S, g2[:], P['rwkv_g2'].ap[l].rearrange("(i p) c -> p i c", p=128), w=[g2.b()])
    pc = sb(S, nm + "pc", [128, 4, 12])
    srcs = [P['rwkv_w0'].ap[l][0], P['rwkv_w0'].ap[l][1], P['rwkv_a0'].ap[l][0], P['rwkv_a0'].ap[l][1], P['rwkv_k_k'].ap[l],
            P['rwkv_k_a'].ap[l], P['rwkv_r_k'].ap[l], P['rwkv_ln_w'].ap[l], P['rwkv_ln_b'].ap[l]]
    for j, s_ap in enumerate(srcs):
        S.op("sp", lambda e, j=j, s_ap=s_ap: e.dma_start(out=pc[:, :, j:j + 1], in_=s_ap.rearrange("(c p o) -> p c o", p=128, o=1),
                                                         allow_slow_non_contiguous=True), w=[pc.b()], dma=True)
    ts(S, pc[:, :, 9:10], pc[:, :, 5:6], -1.0, 1.0, ALU.mult, ALU.add, r=[pc.b()], w=[pc.b()])
    rs_ = sb(S, nm + "rs", [128, T_])
    ks_ = sb(S, nm + "ks", [128, T_])
    vs_ = sb(S, nm + "vs", [128, T_])
    kk_ = sb(S, nm + "kk", [128, T_])
    tl = {k: sb(S, nm + "tl_" + k, [128, NT]) for k in ["sq", "den", "e0", "e1", "a0", "a1", "t0", "kd0", "kd1", "b0", "b1", "kds", "pr", "bo", "g"]}
    pp = [ps(S, nm + "pp%d" % i, [128, NT]) for i in range(6)]
    kq = [0]

    def nps():
        kq[0] += 1
        return pp[kq[0] % 6]

    for cc in range(4):
        shift_piece(cc * 128, 128, rs_, rs_.b())
        shift_piece(512 + cc * 128, 128, ks_, ks_.b())
        shift_piece(1024 + cc * 128, 128, vs_, vs_.b())
        csl = slice(cc * 128, (cc + 1) * 128)
        dma(S, rw.ap[RW_R][csl, :], rs_[:], r=[rs_.b()], w=[rw.b((RW_R, cc))], q="sp")
        dma(S, rw.ap[RW_V][csl, :], vs_[:], r=[vs_.b()], w=[rw.b((RW_V, cc))], q="act")
        for ti in range(nti):
            tsl = slice(ti * NT, (ti + 1) * NT)
            ts(S, kk_[:, tsl], ks_[:, tsl], pc[:, cc, 4:5], None, ALU.mult, r=[ks_.b(), pc.b()], w=[kk_.b()])
            act(S, tl["sq"][:], kk_[:, tsl], AF.Square, r=[kk_.b()], w=[tl["sq"].b()])
            p = nps()
            mm(S, p[:], blk[:], tl["sq"][:], r=[blk.b(), tl["sq"].b()], w=[p.b()])
            act(S, tl["den"][:], p[:], AF.Sqrt, r=[p.b()], w=[tl["den"].b()])
            ts(S, tl["den"][:], tl["den"][:], 1e-12, None, ALU.max, r=[tl["den"].b()], w=[tl["den"].b()])
            S.op("dve", lambda e: e.reciprocal(tl["den"][:], tl["den"][:]), r=[tl["den"].b()], w=[tl["den"].b()])
            tt(S, kk_[:, tsl], kk_[:, tsl], tl["den"][:], ALU.mult, r=[kk_.b(), tl["den"].b()], w=[kk_.b()])
            for d in range(2):
                e_t, a_t, kd_t, b_t = tl["e%d" % d], tl["a%d" % d], tl["kd%d" % d], tl["b%d" % d]
                p = nps()
                mm(S, p[:], w2[:, d, csl], lw[d][:, tsl], r=[w2.b(), lw[d].b()], w=[p.b()])
                act(S, e_t[:], p[:], AF.Sigmoid, bias=pc[:, cc, d:d + 1], r=[p.b(), pc.b()], w=[e_t.b()])
                ts(S, e_t[:], e_t[:], -math.exp(-0.5), None, ALU.mult, r=[e_t.b()], w=[e_t.b()], eng="pool")
                dma(S, rw.ap[RW_E + d][csl, tsl], e_t[:], r=[e_t.b()], w=[rw.b((RW_E + d, cc))], q="sp")
                p = nps()
                mm(S, p[:], a2[:, d, csl], la[d][:, tsl], r=[a2.b(), la[d].b()], w=[p.b()])
                act(S, a_t[:], p[:], AF.Sigmoid, bias=pc[:, cc, 2 + d:3 + d], r=[p.b(), pc.b()], w=[a_t.b()])
                ts(S, tl["t0"][:], a_t[:], pc[:, cc, 5:6], pc[:, cc, 9:10], ALU.mult, ALU.add, r=[a_t.b(), pc.b()], w=[tl["t0"].b()])
                tt(S, kd_t[:], ks_[:, tsl], tl["t0"][:], ALU.mult, r=[ks_.b(), tl["t0"].b()], w=[kd_t.b()])
                dma(S, rw.ap[RW_KD + d][csl, tsl], kd_t[:], r=[kd_t.b()], w=[rw.b((RW_KD + d, cc))], q="act")
                tt(S, b_t[:], a_t[:], kk_[:, tsl], ALU.mult, r=[a_t.b(), kk_.b()], w=[b_t.b()], eng="pool")
                dma(S, rw.ap[RW_B + d][csl, tsl], b_t[:], r=[b_t.b()], w=[rw.b((RW_B + d, cc))], q="sp")
            tt(S, tl["kds"][:], tl["kd0"][:], tl["kd1"][:], ALU.add, r=[tl["kd0"].b(), tl["kd1"].b()], w=[tl["kds"].b()], eng="pool")
            stt(S, tl["pr"][:], rs_[:, tsl], pc[:, cc, 6:7], tl["kds"][:], ALU.mult, ALU.mult, r=[rs_.b(), pc.b(), tl["kds"].b()], w=[tl["pr"].b()])
            p = nps()
            mm(S, p[:], blk[:], tl["pr"][:], r=[blk.b(), tl["pr"].b()], w=[p.b()])
            tt(S, tl["bo"][:], p[:], vs_[:, tsl], ALU.mult, r=[p.b(), vs_.b()], w=[tl["bo"].b()])
            dma(S, rw.ap[RW_BONUS][csl, tsl], tl["bo"][:], r=[tl["bo"].b()], w=[rw.b((RW_BONUS, cc))], q="act")
            p = nps()
            for i in range(2):
                mm(S, p[:], g2[:, i, csl], lg[i][:, tsl], start=(i == 0), stop=(i == 1), r=[g2.b(), lg[i].b()], w=[p.b()])
            cp(S, tl["g"][:], p[:], r=[p.b()], w=[tl["g"].b()], eng="act")
            dma(S, rw.ap[RW_G][csl, tsl], tl["g"][:], r=[tl["g"].b()], w=[rw.b((RW_G, cc))], q="sp")
        dma(S, rw.ap[RW_KK][csl, :], kk_[:], r=[kk_.b()], w=[rw.b((RW_KK, cc))], q="sp")
    S.barrier()
    S.release()
    if getattr(C, "rw_stop", 0) == 1:
        return
    CH = 128
    NCH = T_ // CH
    m4 = [sb(S, nm + "m4_%d" % d, [128, 512]) for d in range(2)]
    m3 = [sb(S, nm + "m3_%d" % d, [128, 384]) for d in range(2)]
    tri = [sb(S, nm + "tri%d" % d, [128, 128]) for d in range(2)]
    for d in range(2):
        dma(S, m4[d][:], C.c_rw_m4.ap[d], w=[m4[d].b()])
        dma(S, m3[d][:], C.c_rw_m3.ap[d], w=[m3[d].b()], q="act")
        dma(S, tri[d][:], C.c_rw_tri.ap[d], w=[tri[d].b()])
    hmk = sb(S, nm + "hmk", [128, 128])
    dma(S, hmk[:], C.c_rw_blk[:], w=[hmk.b()])
    Yacc = sb(S, nm + "Yacc", [128, NCH, 512])
    ST = {}
    for d in range(2):
        for cc in range(4):
            ST[(d, cc)] = [sb(S, nm + "ST%d%d%d" % (d, cc, i), [64, 2, 64], BF16) for i in range(2)]
            mset(S, ST[(d, cc)][0][:], 0.0, w=[ST[(d, cc)][0].b()])
    slots = {}
    for d in range(2):
        for cc in range(4):
            sl = {}
            sn = nm + "s%d%d_" % (d, cc)
            sl["in"] = sb(S, sn + "in", [128, 6, CH])
            sl["etm"] = sb(S, sn + "etm", [128, 128])
            sl["cx"] = sb(S, sn + "cx", [128, 128])
            sl["gp"] = sb(S, sn + "gp", [128, 128])
            sl["gn"] = sb(S, sn + "gn", [128, 128])
            sl["gx"] = sb(S, sn + "gx", [128, 128])
            sl["cm"] = sb(S, sn + "cm", [128, 7, 128], BF16)
            sl["AR"] = sb(S, sn + "AR", [128, 4, 128], BF16)
            sl["Bm"] = sb(S, sn + "Bm", [128, 2, 128], BF16)
            sl["tm"] = sb(S, sn + "tm", [128, 4, 128], BF16)
            sl["Xt"] = [sb(S, sn + "Xt%d" % i, [128, 2, 128], BF16) for i in range(2)]
            sl["XTT"] = [sb(S, sn + "XTT%d" % i, [128, 2, 2, 128], BF16) for i in range(2)]
            sl["TTf"] = sb(S, sn + "TTf", [128, 2, 128], BF16)
            sl["L3"] = sb(S, sn + "L3", [128, 2, 3, 128], BF16)
            sl["W1A"] = sb(S, sn + "W1A", [128, 2, 128], BF16)
            sl["QP"] = sb(S, sn + "QP", [128, 2, 128], BF16)
            sl["GT"] = sb(S, sn + "GT", [64, 2, 64], BF16)
            sl["RyT"] = sb(S, sn + "RyT", [64, 2, 128], BF16)
            sl["Dg"] = sb(S, sn + "Dg", [128, 128], BF16)
            slots[(d, cc)] = sl
    bank = [ps(S, nm + "bk%d" % i, [128, NT]) for i in range(6)]
    bankb = [ps(S, nm + "bkb%d" % i, [128, 1024], BF16) for i in range(2)]
    kb = [0]

    def nb():
        kb[0] += 1
        return bank[kb[0] % 6]

    IDB = C.identb
    touched = set()
    par = {}
    for step in range(NCH):
        units = []
        for d in range(2):
            n = step if d == 0 else NCH - 1 - step
            for cc in range(4):
                units.append((d, cc, n, slots[(d, cc)]))
        for d, cc, n, sl in units:
            t0 = n * CH
            csl = slice(cc * 128, (cc + 1) * 128)
            srcs = [RW_R, RW_KK, RW_V, RW_E + d, RW_B + d, RW_KD + d]
            for j, k_ in enumerate(srcs):
                dma(S, sl["in"][:, j, :], rw.ap[k_][csl, t0:t0 + CH], r=[rw.b((k_, cc))], w=[sl["in"].b()], q="sp")
        pbank = {}
        for half in (units[0:4], units[4:8]):
            for u in half:
                d, cc, n, sl = u
                p = nb()
                pbank[(d, cc)] = p
                tr(S, p[:, 0:128], sl["in"][:, 3, :], C.ident[:], r=[sl["in"].b(), C.ident.b()], w=[p.b()])
            for u in half:
                d, cc, n, sl = u
                p = pbank[(d, cc)]
                cp(S, sl["etm"][:], p[:, 0:128], r=[p.b()], w=[sl["etm"].b()], eng="act")
            for u in half:
                d, cc, n, sl = u
                p2 = nb()
                pbank[(d, cc)] = p2
                mm(S, p2[:, 0:128], sl["etm"][:], tri[d][:], r=[sl["etm"].b(), tri[d].b()], w=[p2.b()])
            for u in half:
                d, cc, n, sl = u
                p2 = pbank[(d, cc)]
                act(S, sl["gp"][:], p2[:, 0:128], AF.Exp, r=[p2.b()], w=[sl["gp"].b()])
                act(S, sl["gn"][:], p2[:, 0:128], AF.Exp, scale=-1.0, r=[p2.b()], w=[sl["gn"].b()])
            for u in half:
                d, cc, n, sl = u
                p2 = pbank[(d, cc)]
                tt(S, sl["cx"][:], p2[:, 0:128], sl["in"][:, 3, :], ALU.subtract, r=[p2.b(), sl["in"].b()], w=[sl["cx"].b()])
            for u in half:
                d, cc, n, sl = u
                act(S, sl["gx"][:], sl["cx"][:], AF.Exp, r=[sl["cx"].b()], w=[sl["gx"].b()])
            for u in half:
                d, cc, n, sl = u
                inb, cm = sl["in"], sl["cm"]
                gcol_ap = sl["gp"][:, 127:128] if d == 0 else sl["gp"][:, 0:1]
                tt(S, cm[:, 1, :], inb[:, 4, :], sl["gn"][:], ALU.mult, r=[inb.b(), sl["gn"].b()], w=[cm.b(1)], eng="pool")
                tt(S, cm[:, 2, :], inb[:, 5, :], sl["gn"][:], ALU.mult, r=[inb.b(), sl["gn"].b()], w=[cm.b(2)])
                tt(S, cm[:, 3, :], inb[:, 0, :], sl["gp"][:], ALU.mult, r=[inb.b(), sl["gp"].b()], w=[cm.b(3)], eng="pool")
                stt(S, cm[:, 4, :], inb[:, 4, :], gcol_ap, sl["gn"][:], ALU.mult, ALU.mult, r=[inb.b(), sl["gn"].b(), sl["gp"].b()], w=[cm.b(4)])
                stt(S, cm[:, 5, :], inb[:, 5, :], gcol_ap, sl["gn"][:], ALU.mult, ALU.mult, r=[inb.b(), sl["gn"].b(), sl["gp"].b()], w=[cm.b(5)])
                cp(S, cm[:, 6, :], inb[:, 2, :], r=[inb.b()], w=[cm.b(6)], eng="pool")
                ts(S, sl["Dg"][:], C.ident[:], gcol_ap, None, ALU.mult, r=[C.ident.b(), sl["gp"].b()], w=[sl["Dg"].b()], eng="pool")
            for u in half:
                d, cc, n, sl = u
                inb, cm = sl["in"], sl["cm"]
                stt(S, cm[:, 0, :], inb[:, 1, :], -1.0, sl["gx"][:], ALU.mult, ALU.mult, r=[inb.b(), sl["gx"].b()], w=[cm.b(0)])
            for u in half:
                d, cc, n, sl = u
                cm, AR, Bm = sl["cm"], sl["AR"], sl["Bm"]
                for hp in range(2):
                    hcol = hmk[:, 64 * hp:64 * hp + 1]
                    ts(S, AR[:, hp, :], cm[:, 0, :], hcol, None, ALU.mult, r=[cm.b(0), hmk.b()], w=[AR.b(hp)], eng="pool")
                    ts(S, AR[:, 2 + hp, :], cm[:, 3, :], hcol, None, ALU.mult, r=[cm.b(3), hmk.b()], w=[AR.b(2 + hp)])
                    ts(S, Bm[:, hp, :], cm[:, 1, :], hcol, None, ALU.mult, r=[cm.b(1), hmk.b()], w=[Bm.b(hp)],
                       eng="pool" if hp == 0 else "dve")
            for ui, u in enumerate(half):
                d, cc, n, sl = u
                cm = sl["cm"]
                bb = bankb[ui % 2]
                for j, jj in enumerate([0, 4, 5, 6]):
                    tr(S, bb[:, j * 128:(j + 1) * 128], cm[:, jj, :], IDB[:], r=[cm.b(jj), IDB.b()], w=[bb.b()])
                cp(S, sl["tm"][:].rearrange("p a b -> p (a b)"), bb[:, 0:512], r=[bb.b()], w=[sl["tm"].b()], eng="act")
            pbk = {}
            for u in half:
                d, cc, n, sl = u
                cm, AR, Bm = sl["cm"], sl["AR"], sl["Bm"]
                pB, pK, pA = nb(), nb(), nb()
                pbk[(d, cc)] = (pB, pK, pA)
                ARf = AR[:].rearrange("p a b -> p (a b)")
                mm(S, pB[:], cm[:, 1, :], ARf, r=[cm.b(1)] + [AR.b(i) for i in range(4)], w=[pB.b()])
                mm(S, pK[:], cm[:, 2, :], ARf, r=[cm.b(2)] + [AR.b(i) for i in range(4)], w=[pK.b()])
                mm(S, pA[:, 0:256], cm[:, 0, :], Bm[:].rearrange("p a b -> p (a b)"), r=[cm.b(0), Bm.b(0), Bm.b(1)], w=[pA.b()])
                Xt0, XTT0, XTT1, L3 = sl["Xt"][0], sl["XTT"][0], sl["XTT"][1], sl["L3"]
                v3 = lambda ap: ap.rearrange("p (a b) -> p a b", a=2)
                tt(S, XTT0[:, :, 0, :], v3(pB[:, 0:256]), v3(m4[d][:, 0:256]), ALU.mult, r=[pB.b(), m4[d].b()], w=[XTT0.b()])
                tt(S, L3[:, :, 1, :], v3(pB[:, 256:512]), v3(m3[d][:, 128:384]), ALU.mult, r=[pB.b(), m3[d].b()], w=[L3.b()])
                tt(S, L3[:, :, 0, :], v3(pK[:, 0:256]), v3(m4[d][:, 0:256]), ALU.mult, r=[pK.b(), m4[d].b()], w=[L3.b()])
                tt(S, L3[:, :, 2, :], v3(pK[:, 256:512]), v3(m3[d][:, 128:384]), ALU.mult, r=[pK.b(), m3[d].b()], w=[L3.b()])
                tt(S, Xt0[:].rearrange("p a b -> p (a b)"), pA[:, 0:256], m4[d][:, 256:512], ALU.mult, r=[pA.b(), m4[d].b()], w=[Xt0.b()])
                for hp in range(2):
                    tt(S, XTT1[:, hp, 1, :], XTT0[:, hp, 0, :], IDB[:], ALU.add, r=[XTT0.b(), IDB.b()], w=[XTT1.b()], eng="pool")
        for m_ in range(0, 7):
            cur, nxt = m_ % 2, (m_ + 1) % 2
            for half in (units[0:4], units[4:8]):
                lv = {}
                for u in half:
                    d, cc, n, sl = u
                    Xc, XTc = sl["Xt"][cur], sl["XTT"][cur]
                    pa_, pb_ = nb(), None
                    if m_ == 0:
                        for hp in range(2):
                            mm(S, pa_[:, hp * 256:hp * 256 + 128], Xc[:, hp, :], XTc[:, hp, 0, :], r=[Xc.b(), XTc.b()], w=[pa_.b()])
                    elif m_ < 6:
                        for hp in range(2):
                            mm(S, pa_[:, hp * 256:(hp + 1) * 256], Xc[:, hp, :], XTc[:, hp, :, :].rearrange("p a b -> p (a b)"),
                               r=[Xc.b(), XTc.b()], w=[pa_.b()])
                    else:
                        for hp in range(2):
                            mm(S, pa_[:, hp * 256 + 128:(hp + 1) * 256], Xc[:, hp, :], XTc[:, hp, 1, :], r=[Xc.b(), XTc.b()], w=[pa_.b()])
                    if m_ < 6:
                        pb_ = nb()
                        for hp in range(2):
                            mm(S, pb_[:, hp * 128:(hp + 1) * 128], XTc[:, hp, 0, :], Xc[:, hp, :], r=[Xc.b(), XTc.b()], w=[pb_.b()])
                    Xn, XTn = sl["Xt"][nxt], sl["XTT"][nxt]
                    pv = pa_[:].rearrange("p (h k b) -> p h k b", h=2, k=2)
                    if m_ < 5:
                        cp(S, XTn[:, :, 0, :], pv[:, :, 0, :], r=[pa_.b()], w=[XTn.b()], eng="act")
                    if m_ < 6:
                        cp(S, Xn[:].rearrange("p a b -> p (a b)"), pb_[:, 0:256], r=[pb_.b()], w=[Xn.b()], eng="act")
                    if 1 <= m_ < 6:
                        tt(S, XTn[:, :, 1, :], XTc[:, :, 1, :], pv[:, :, 1, :], ALU.add, r=[XTc.b(), pa_.b()], w=[XTn.b()])
                    elif m_ == 6:
                        tt(S, sl["TTf"][:], XTc[:, :, 1, :], pv[:, :, 1, :], ALU.add, r=[XTc.b(), pa_.b()], w=[sl["TTf"].b()])
        sold = {}
        for half in (units[0:4], units[4:8]):
            for u in half:
                d, cc, n, sl = u
                tm, L3 = sl["tm"], sl["L3"]
                p = nb()
                pbank[(d, cc)] = p
                for hp in range(2):
                    fs = slice(64 * hp, 64 * hp + 64)
                    mm(S, p[:, hp * 64:(hp + 1) * 64], L3[:, hp, 0, :], tm[:, 3, fs], r=[L3.b(), tm.b()], w=[p.b()])
            for u in half:
                d, cc, n, sl = u
                p = pbank[(d, cc)]
                tm, W1A = sl["tm"], sl["W1A"]
                for hp in range(2):
                    fs = slice(64 * hp, 64 * hp + 64)
                    cp(S, W1A[:, hp, 0:64], p[:, hp * 64:(hp + 1) * 64], r=[p.b()], w=[W1A.b()], eng="act")
                    cp(S, W1A[:, hp, 64:128], tm[:, 0, fs], r=[tm.b()], w=[W1A.b()], eng="pool")
            for u in half:
                d, cc, n, sl = u
                TTf, W1A = sl["TTf"], sl["W1A"]
                p2 = nb()
                pbank[(d, cc)] = p2
                for hp in range(2):
                    mm(S, p2[:, hp * 128:(hp + 1) * 128], TTf[:, hp, :], W1A[:, hp, :], r=[TTf.b(), W1A.b()], w=[p2.b()])
            for u in half:
                d, cc, n, sl = u
                p2 = pbank[(d, cc)]
                cp(S, sl["QP"][:].rearrange("p a b -> p (a b)"), p2[:, 0:256], r=[p2.b()], w=[sl["QP"].b()], eng="act")
            for u in half:
                d, cc, n, sl = u
                tm, L3, QP, cm = sl["tm"], sl["L3"], sl["QP"], sl["cm"]
                p3 = nb()
                pbank[(d, cc)] = p3
                for hp in range(2):
                    fs = slice(64 * hp, 64 * hp + 64)
                    mm(S, p3[0:64, hp * 64:(hp + 1) * 64], QP[:, hp, 64:128], tm[:, 1, fs], start=True, stop=False,
                       r=[QP.b(), tm.b()], w=[p3.b()])
                    mm(S, p3[0:64, hp * 64:(hp + 1) * 64], IDB[:, fs], sl["Dg"][:, fs], start=False, stop=True,
                       r=[IDB.b(), sl["Dg"].b()], w=[p3.b()])
                    mm(S, p3[0:64, 128 + hp * 128:256 + hp * 128], QP[:, hp, 64:128], L3[:, hp, 1, :], start=True, stop=False,
                       r=[QP.b(), L3.b()], w=[p3.b()])
                    mm(S, p3[0:64, 128 + hp * 128:256 + hp * 128], IDB[:, fs], cm[:, 3, :], start=False, stop=True,
                       r=[IDB.b(), cm.b(3)], w=[p3.b()])
            for u in half:
                d, cc, n, sl = u
                p3 = pbank[(d, cc)]
                cp(S, sl["GT"][:].rearrange("p a b -> p (a b)"), p3[0:64, 0:128], r=[p3.b()], w=[sl["GT"].b()], eng="act")
                cp(S, sl["RyT"][:].rearrange("p a b -> p (a b)"), p3[0:64, 128:384], r=[p3.b()], w=[sl["RyT"].b()], eng="act")
            for u in half:
                d, cc, n, sl = u
                tm, L3, QP = sl["tm"], sl["L3"], sl["QP"]
                k_ = par.get((d, cc), 0)
                S_old, S_new = ST[(d, cc)][k_], ST[(d, cc)][1 - k_]
                par[(d, cc)] = 1 - k_
                sold[(d, cc)] = (S_old, S_new)
                pz = nb()
                pbank[(d, cc)] = pz
                for hp in range(2):
                    fs = slice(64 * hp, 64 * hp + 64)
                    mm(S, pz[0:64, fs], sl["GT"][:, hp, :], S_old[:, hp, :], start=True, stop=False, r=[sl["GT"].b(), S_old.b()], w=[pz.b()])
                    mm(S, pz[0:64, fs], tm[:, 1, fs], QP[:, hp, 0:64], start=False, stop=False, r=[tm.b(), QP.b()], w=[pz.b()])
                    mm(S, pz[0:64, fs], tm[:, 2, fs], tm[:, 3, fs], start=False, stop=True, r=[tm.b()], w=[pz.b()])
                    mm(S, pz[:, 128 + 64 * hp:192 + 64 * hp], L3[:, hp, 1, :], QP[:, hp, 0:64], start=True, stop=False, r=[L3.b(), QP.b()], w=[pz.b()])
                    mm(S, pz[:, 128 + 64 * hp:192 + 64 * hp], L3[:, hp, 2, :], tm[:, 3, fs], start=False, stop=False, r=[L3.b(), tm.b()], w=[pz.b()])
                    mm(S, pz[:, 128 + 64 * hp:192 + 64 * hp], sl["RyT"][:, hp, :], S_old[:, hp, :], start=False, stop=True, r=[sl["RyT"].b(), S_old.b()], w=[pz.b()])
            for u in half:
                d, cc, n, sl = u
                pz = pbank[(d, cc)]
                S_old, S_new = sold[(d, cc)]
                cp(S, S_new[:].rearrange("p a b -> p (a b)"), pz[0:64, 0:128], r=[pz.b()], w=[S_new.b()], eng="act")
                if (n, cc) not in touched:
                    touched.add((n, cc))
                    cp(S, Yacc[:, n, cc * 128:(cc + 1) * 128], pz[:, 128:256], r=[pz.b()], w=[Yacc.b((n, cc))])
                else:
                    tt(S, Yacc[:, n, cc * 128:(cc + 1) * 128], Yacc[:, n, cc * 128:(cc + 1) * 128], pz[:, 128:256], ALU.add,
                       r=[pz.b(), Yacc.b((n, cc))], w=[Yacc.b((n, cc))])
    if C.rw_stop == 6:
        return
    pcs = sb(S, nm + "pcs", [128, 4, 4])
    for j, k_ in enumerate(['rwkv_ln_w', 'rwkv_ln_b']):
        S.op("sp", lambda e, j=j, k_=k_: e.dma_start(out=pcs[:, :, j:j + 1], in_=P[k_].ap[l].rearrange("(c p o) -> p c o", p=128, o=1),
                                                     allow_slow_non_contiguous=True), w=[pcs.b()], dma=True)
    blk2 = sb(S, nm + "blk2", [128, 128])
    dma(S, blk2[:], C.c_rw_blk[:], w=[blk2.b()])
    gne = sb(S, nm + "gne", [128, 1])
    mset(S, gne[:], GN_EPS, w=[gne.b()])
    ycm = sb(S, nm + "ycm", [128, NT])
    yc2 = sb(S, nm + "yc2", [128, NT])
    sq2 = sb(S, nm + "sq2", [128, NT])
    rstd2 = sb(S, nm + "rstd2", [128, NT])
    bo_t = [sb(S, nm + "bo%d" % i, [128, NT]) for i in range(2)]
    g_t = [sb(S, nm + "gt%d" % i, [128, NT]) for i in range(2)]
    yo = [sb(S, nm + "yo%d" % i, [128, NT], BF16) for i in range(2)]
    k3 = 0
    for cc in range(4):
        csl = slice(cc * 128, (cc + 1) * 128)
        for ti in range(nti):
            tsl = slice(ti * NT, (ti + 1) * NT)
            b_t, gg, y_o = bo_t[k3 % 2], g_t[k3 % 2], yo[k3 % 2]
            k3 += 1
            dma(S, b_t[:], rw.ap[RW_BONUS][csl, tsl], r=[rw.b((RW_BONUS, cc))], w=[b_t.b()], q="sp")
            dma(S, gg[:], rw.ap[RW_G][csl, tsl], r=[rw.b((RW_G, cc))], w=[gg.b()], q="act")
            p = nb()
            for j in range(4):
                n = ti * 4 + j
                tr(S, p[:, j * 128:(j + 1) * 128], Yacc[:, n, csl], C.ident[:], r=[Yacc.b((n, cc)), C.ident.b()], w=[p.b()])
            cp(S, ycm[:], p[:], r=[p.b()], w=[ycm.b()], eng="act")
            p2 = nb()
            mm(S, p2[:], blk2[:], ycm[:], r=[blk2.b(), ycm.b()], w=[p2.b()])
            stt(S, yc2[:], p2[:], -1.0 / 64.0, ycm[:], ALU.mult, ALU.add, r=[p2.b(), ycm.b()], w=[yc2.b()])
            act(S, sq2[:], yc2[:], AF.Square, r=[yc2.b()], w=[sq2.b()])
            p3 = nb()
            mm(S, p3[:], blk2[:], sq2[:], r=[blk2.b(), sq2.b()], w=[p3.b()])
            rsqrt(S, rstd2[:], p3[:], 1.0 / 64.0, gne[:, 0:1], r=[p3.b(), gne.b()], w=[rstd2.b()])
            tt(S, yc2[:], yc2[:], rstd2[:], ALU.mult, r=[yc2.b(), rstd2.b()], w=[yc2.b()])
            ts(S, yc2[:], yc2[:], pcs[:, cc, 0:1], pcs[:, cc, 1:2], ALU.mult, ALU.add, r=[yc2.b(), pcs.b()], w=[yc2.b()])
            tt(S, yc2[:], yc2[:], b_t[:], ALU.add, r=[yc2.b(), b_t.b()], w=[yc2.b()], eng="pool")
            tt(S, y_o[:], yc2[:], gg[:], ALU.mult, r=[yc2.b(), gg.b()], w=[y_o.b()])
            dma(S, C.ya_d[csl, tsl], y_o[:], r=[y_o.b()], w=[C.ya_d.b((cc, ti))], q="sp")
    S.barrier()
    S.release()
```
